# Optimizing a Trainium2 kernel written in Bass

```python
import math
import jax, jax.numpy as jnp
from jax import lax
import numpy as np

D_MODEL = 1024
BATCH = 8
SEQ = 2048
DEPTH = 4
DEC_BATCH = 128
DEC_SEQ = 1
PAST_LEN = 16384
PAGE_SIZE = 128

N_META = 16
W_BRANCH = 512
D_MIX = 3 * W_BRANCH
CONV_W = 4
RG_BLOCKS = 8
RG_BLOCK = W_BRANCH // RG_BLOCKS
RG_C = 8.0
RET_HEADS = 4
RET_DK = W_BRANCH // RET_HEADS
RET_CHUNK = 64
ROPE_BASE = 10000.0
GDN_HEADS = 4
GDN_DK = W_BRANCH // GDN_HEADS
GDN_CHUNK = 64
EPS = 1e-6
DEEPNORM_ALPHA = (2.0 * DEPTH) ** 0.25
DEEPNORM_BETA = (8.0 * DEPTH) ** -0.25
D_IN = 10 * W_BRANCH + 2 * GDN_HEADS

kernel_name = "hymba_rglru_retention_gdn_deepnorm_step"


def _layernorm(x, w, b):
    x = x.astype(jnp.float32)
    mu = jnp.mean(x, -1, keepdims=True)
    var = jnp.mean(jnp.square(x - mu), -1, keepdims=True)
    return (x - mu) * lax.rsqrt(var + EPS) * w + b


def _heads(t, h):
    b, s, _ = t.shape
    return t.reshape(b, s, h, -1).transpose(0, 2, 1, 3)


def _merge(t):
    b, h, s, d = t.shape
    return t.transpose(0, 2, 1, 3).reshape(b, s, h * d)


def _l2norm(t):
    return t * lax.rsqrt(jnp.sum(t * t, -1, keepdims=True) + EPS)


def _rope(t, pos):
    half = t.shape[-1] // 2
    inv = ROPE_BASE ** (-jnp.arange(half, dtype=jnp.float32) / half)
    ang = pos[:, None] * inv[None, :]
    cos, sin = jnp.cos(ang), jnp.sin(ang)
    t1, t2 = t[..., :half], t[..., half:]
    return jnp.concatenate([t1 * cos - t2 * sin, t1 * sin + t2 * cos], -1)


def _causal_conv(x, buf, w):
    t = x.shape[1]
    xp = jnp.concatenate([buf.astype(x.dtype), x], axis=1)
    y = sum(w[j].astype(jnp.float32) * xp[:, j:j + t] for j in range(CONV_W))
    return y, xp[:, -(CONV_W - 1):]


def _rglru(x, h0, w_a, b_a, w_x, b_x, lam):
    b, t, _ = x.shape
    xb = x.reshape(b, t, RG_BLOCKS, RG_BLOCK)
    r = jax.nn.sigmoid(jnp.einsum('btnc,ncd->btnd', xb, w_a.astype(jnp.float32)).reshape(b, t, W_BRANCH) + b_a)
    i = jax.nn.sigmoid(jnp.einsum('btnc,ncd->btnd', xb, w_x.astype(jnp.float32)).reshape(b, t, W_BRANCH) + b_x)
    log_a = -RG_C * r * jax.nn.softplus(-lam.astype(jnp.float32))
    a = jnp.exp(log_a)
    u = jnp.sqrt(-jnp.expm1(2.0 * log_a)) * (i * x)
    u = u.at[:, 0].add(a[:, 0] * h0.astype(jnp.float32))

    def comb(l, r_):
        a1, b1 = l
        a2, b2 = r_
        return a1 * a2, a2 * b1 + b2

    _, h = lax.associative_scan(comb, (a, u), axis=1)
    return h, h[:, -1]


def _segments(t, lead, chunk):
    segs = []
    if lead > 0:
        segs.append((lead, lead))
    rest = t - lead
    segs.append((rest, chunk if rest % chunk == 0 else rest))
    return segs


def _chunk_run(make_step, chunk, s, xs, lead):
    t = xs[0].shape[2]
    outs, start = [], 0
    for length, c in _segments(t, lead, chunk):
        n = length // c
        seg = tuple(jnp.moveaxis(a[:, :, start:start + length].reshape(a.shape[:2] + (n, c) + a.shape[3:]), 2, 0)
                    for a in xs)
        s, o = lax.scan(make_step(c), s, seg)
        o = jnp.moveaxis(o, 0, 2)
        outs.append(o.reshape(o.shape[:2] + (length,) + o.shape[4:]))
        start += length
    return jnp.concatenate(outs, axis=2), s


def _ret_step(c):
    lg = jnp.log1p(-jnp.exp2(-5.0 - jnp.arange(RET_HEADS, dtype=jnp.float32)))[:, None, None]
    idx = jnp.arange(c, dtype=jnp.float32)
    diff = idx[:, None] - idx[None, :]
    decay = jnp.where(diff >= 0, jnp.exp(lg * jnp.maximum(diff, 0.0)), 0.0)
    q_dec = jnp.exp(lg[:, 0] * (idx + 1.0))[:, :, None]
    k_dec = jnp.exp(lg[:, 0] * (c - 1.0 - idx))[:, :, None]
    s_dec = jnp.exp(lg * c)

    def step(s, xs):
        q, k, v = xs
        sc = jnp.einsum('bhid,bhjd->bhij', q, k) * decay
        o = jnp.einsum('bhij,bhjv->bhiv', sc, v) + jnp.einsum('bhid,bhdv->bhiv', q * q_dec, s)
        s = s * s_dec + jnp.einsum('bhjd,bhjv->bhdv', k * k_dec, v)
        return s, o
    return step


def _gdn_step(c):
    tril = jnp.tril(jnp.ones((c, c), bool))
    strict = jnp.tril(jnp.ones((c, c), bool), -1)
    eye = jnp.eye(c, dtype=jnp.float32)

    def step(s, xs):
        q, k, v, g, beta = xs
        gc = jnp.cumsum(g, axis=-1)
        decay = jnp.exp(jnp.where(tril, gc[..., :, None] - gc[..., None, :], -jnp.inf))
        kb = k * beta[..., None]
        a_low = jnp.where(strict, jnp.einsum('bhid,bhjd->bhij', kb, k) * decay, 0.0)
        rhs = jnp.concatenate([v * beta[..., None], kb * jnp.exp(gc)[..., None]], -1)
        sol = lax.linalg.triangular_solve(eye + a_low, rhs, left_side=True, lower=True)
        u, w = sol[..., :GDN_DK], sol[..., GDN_DK:]
        v_new = u - jnp.einsum('bhik,bhkv->bhiv', w, s)
        att = jnp.einsum('bhid,bhjd->bhij', q, k) * decay
        o = jnp.einsum('bhik,bhkv->bhiv', q * jnp.exp(gc)[..., None], s) + jnp.einsum('bhij,bhjv->bhiv', att, v_new)
        g_last = gc[..., -1:]
        s = s * jnp.exp(g_last)[..., None] + jnp.einsum('bhik,bhiv->bhkv', k * jnp.exp(g_last - gc)[..., None], v_new)
        return s, o
    return step


def _layer(x, pos, lead, h0, rg_buf, s_ret, gdn_buf, s_gdn,
           w_in, rg_conv_w, rg_conv_b, rg_w_a, rg_b_a, rg_w_x, rg_b_x, rg_lambda,
           ret_gn_w, ret_gn_b, gdn_conv_w, gdn_a_log, gdn_dt_bias, gdn_norm_w,
           w_out, ln_w, ln_b):
    f32 = jnp.float32
    W = W_BRANCH
    u = jnp.einsum('btd,de->bte', x, w_in).astype(f32)
    rg_in, rg_z = u[..., 0:W], u[..., W:2 * W]
    r_q, r_k, r_v, r_z = (u[..., (2 + j) * W:(3 + j) * W] for j in range(4))
    g_qkv, g_z = u[..., 6 * W:9 * W], u[..., 9 * W:10 * W]
    g_a, g_b = u[..., 10 * W:10 * W + GDN_HEADS], u[..., 10 * W + GDN_HEADS:]

    rg_c, rg_buf_new = _causal_conv(rg_in, rg_buf, rg_conv_w)
    rg_h, h_new = _rglru(rg_c + rg_conv_b, h0, rg_w_a, rg_b_a, rg_w_x, rg_b_x, rg_lambda)
    y_rg = rg_h * jax.nn.silu(rg_z)

    q = _rope(_heads(r_q, RET_HEADS), pos)
    k = _rope(_heads(r_k, RET_HEADS), pos) * RET_DK ** -0.5
    v = _heads(r_v, RET_HEADS)
    o, s_ret_new = _chunk_run(_ret_step, RET_CHUNK, s_ret.astype(f32), (q, k, v), lead)
    mu = jnp.mean(o, -1, keepdims=True)
    o = (o - mu) * lax.rsqrt(jnp.mean(jnp.square(o - mu), -1, keepdims=True) + EPS)
    y_ret = (_merge(o) * ret_gn_w + ret_gn_b) * jax.nn.silu(r_z)

    gc_, gdn_buf_new = _causal_conv(g_qkv, gdn_buf, gdn_conv_w)
    gc_ = jax.nn.silu(gc_)
    gq = _l2norm(_heads(gc_[..., :W], GDN_HEADS)) * GDN_DK ** -0.5
    gk = _l2norm(_heads(gc_[..., W:2 * W], GDN_HEADS))
    gv = _heads(gc_[..., 2 * W:], GDN_HEADS)
    g_log = -jnp.exp(gdn_a_log.astype(f32)) * jax.nn.softplus(g_a + gdn_dt_bias)
    beta = jax.nn.sigmoid(g_b)
    o, s_gdn_new = _chunk_run(_gdn_step, GDN_CHUNK, s_gdn.astype(f32),
                              (gq, gk, gv, g_log.transpose(0, 2, 1), beta.transpose(0, 2, 1)), lead)
    o = o * lax.rsqrt(jnp.mean(jnp.square(o), -1, keepdims=True) + EPS) * gdn_norm_w
    y_gdn = _merge(o) * jax.nn.silu(g_z)

    mix = jnp.concatenate([y_rg, y_ret, y_gdn], -1)
    out = jnp.einsum('bte,ed->btd', mix, w_out.astype(f32))
    y = _layernorm(DEEPNORM_ALPHA * x.astype(f32) + out, ln_w, ln_b).astype(x.dtype)
    return y, h_new, rg_buf_new, s_ret_new, gdn_buf_new, s_gdn_new


def setup_inputs(seed: int = 0) -> dict:
    key = jax.random.key(seed)
    ks = jax.random.split(key, 32)
    nrm = jax.random.normal
    f32 = jnp.float32
    a_init = jax.random.uniform(ks[12], (DEPTH, W_BRANCH), f32, 0.9, 0.999)
    dt = jnp.exp(jax.random.uniform(ks[16], (DEPTH, GDN_HEADS), f32, math.log(1e-3), math.log(1e-1)))
    return {
        "x_prompt": nrm(ks[0], (BATCH, SEQ, D_MODEL), f32),
        "x_sample": nrm(ks[1], (DEC_BATCH, DEC_SEQ, D_MODEL), f32),
        "state_rglru_h": 0.5 * nrm(ks[2], (DEPTH, DEC_BATCH, W_BRANCH), f32),
        "state_rglru_conv": nrm(ks[3], (DEPTH, DEC_BATCH, CONV_W - 1, W_BRANCH), f32),
        "state_ret": 0.5 * nrm(ks[4], (DEPTH, DEC_BATCH, RET_HEADS, RET_DK, RET_DK), f32),
        "state_gdn_conv": nrm(ks[5], (DEPTH, DEC_BATCH, CONV_W - 1, 3 * W_BRANCH), f32),
        "state_gdn": 0.1 * nrm(ks[6], (DEPTH, DEC_BATCH, GDN_HEADS, GDN_DK, GDN_DK), f32),
        "meta_tokens": nrm(ks[7], (N_META, D_MODEL), f32),
        "w_in": nrm(ks[8], (DEPTH, D_MODEL, D_IN), f32) * D_MODEL ** -0.5,
        "rg_conv_w": nrm(ks[9], (DEPTH, CONV_W, W_BRANCH), f32) * CONV_W ** -0.5,
        "rg_conv_b": 0.01 * nrm(ks[10], (DEPTH, W_BRANCH), f32),
        "rg_w_a": nrm(ks[11], (DEPTH, RG_BLOCKS, RG_BLOCK, RG_BLOCK), f32) * RG_BLOCK ** -0.5,
        "rg_b_a": 0.01 * nrm(ks[13], (DEPTH, W_BRANCH), f32),
        "rg_w_x": nrm(ks[14], (DEPTH, RG_BLOCKS, RG_BLOCK, RG_BLOCK), f32) * RG_BLOCK ** -0.5,
        "rg_b_x": 0.01 * nrm(ks[15], (DEPTH, W_BRANCH), f32),
        "rg_lambda": jnp.log(a_init) - jnp.log1p(-a_init),
        "ret_gn_w": 1.0 + 0.02 * nrm(ks[17], (DEPTH, W_BRANCH), f32),
        "ret_gn_b": 0.01 * nrm(ks[18], (DEPTH, W_BRANCH), f32),
        "gdn_conv_w": nrm(ks[19], (DEPTH, CONV_W, 3 * W_BRANCH), f32) * CONV_W ** -0.5,
        "gdn_a_log": jnp.log(jax.random.uniform(ks[20], (DEPTH, GDN_HEADS), f32, 1.0, 16.0)),
        "gdn_dt_bias": dt + jnp.log(-jnp.expm1(-dt)),
        "gdn_norm_w": 1.0 + 0.02 * nrm(ks[21], (DEPTH, GDN_DK), f32),
        "w_out": nrm(ks[22], (DEPTH, D_MIX, D_MODEL), f32) * (D_MIX ** -0.5) * DEEPNORM_BETA,
        "ln_w": 1.0 + 0.02 * nrm(ks[23], (DEPTH, D_MODEL), f32),
        "ln_b": 0.01 * nrm(ks[24], (DEPTH, D_MODEL), f32),
    }


def reference(x_prompt, x_sample, state_rglru_h, state_rglru_conv, state_ret, state_gdn_conv, state_gdn,
              meta_tokens, w_in, rg_conv_w, rg_conv_b, rg_w_a, rg_b_a, rg_w_x, rg_b_x, rg_lambda,
              ret_gn_w, ret_gn_b, gdn_conv_w, gdn_a_log, gdn_dt_bias, gdn_norm_w, w_out, ln_w, ln_b):
    f32 = jnp.float32
    bp = x_prompt.shape[0]
    meta = jnp.broadcast_to(meta_tokens.astype(x_prompt.dtype)[None], (bp, N_META, D_MODEL))
    xp = jnp.concatenate([meta, x_prompt], axis=1)
    xs = x_sample
    pos_p = jnp.arange(xp.shape[1], dtype=f32)
    pos_s = jnp.arange(xs.shape[1], dtype=f32) + float(PAST_LEN)
    new_p = [[], [], [], [], []]
    new_s = [[], [], [], [], []]
    for l in range(DEPTH):
        params = (w_in[l], rg_conv_w[l], rg_conv_b[l], rg_w_a[l], rg_b_a[l], rg_w_x[l], rg_b_x[l], rg_lambda[l],
                  ret_gn_w[l], ret_gn_b[l], gdn_conv_w[l], gdn_a_log[l], gdn_dt_bias[l], gdn_norm_w[l],
                  w_out[l], ln_w[l], ln_b[l])
        xp, *st_p = _layer(xp, pos_p, N_META,
                           jnp.zeros((bp, W_BRANCH), f32),
                           jnp.zeros((bp, CONV_W - 1, W_BRANCH), f32),
                           jnp.zeros((bp, RET_HEADS, RET_DK, RET_DK), f32),
                           jnp.zeros((bp, CONV_W - 1, 3 * W_BRANCH), f32),
                           jnp.zeros((bp, GDN_HEADS, GDN_DK, GDN_DK), f32),
                           *params)
        xs, *st_s = _layer(xs, pos_s, 0, state_rglru_h[l], state_rglru_conv[l], state_ret[l],
                           state_gdn_conv[l], state_gdn[l], *params)
        for j in range(5):
            new_p[j].append(st_p[j])
            new_s[j].append(st_s[j])
    y_prompt = xp[:, N_META:]
    y_sample = xs
    dts = (state_rglru_h.dtype, state_rglru_conv.dtype, state_ret.dtype, state_gdn_conv.dtype, state_gdn.dtype)
    sp = [jnp.stack(new_p[j]).astype(dts[j]) for j in range(5)]
    ss = [jnp.stack(new_s[j]).astype(dts[j]) for j in range(5)]
    return (y_prompt, y_sample, sp[0], sp[1], sp[2], sp[3], sp[4], ss[0], ss[1], ss[2], ss[3], ss[4])
```

```python
import numpy as np
import concourse.bass as bass
import concourse.mybir as mybir
from concourse.bass_utils import run_bass_kernel_spmd

F32 = mybir.dt.float32
BF16 = mybir.dt.bfloat16
ALU = mybir.AluOpType
AF = mybir.ActivationFunctionType
AX = mybir.AxisListType

EPOCH = 12000
POOL_CONV = True
DEFCOST = {"pe": 0.2, "act": 0.4, "dve": 0.3, "pool": 0.05, "sp": 0.05}
GDN_STOP = 100000
ENABLE = [True, True, True]
NET = 6
SKIP_SAMPLE = False
PER_TILE_SEMS = True
MERGE = "sim"
SAME_SYNC = True

NL = 4
DM = 1024
DIN = 5128
NTOK = 2064
NBLK = 17
NS = 16
ALPHA = 8.0 ** 0.25
EPS = 1e-6
GAM = [1.0 - 2.0 ** (-5.0 - h) for h in range(4)]
NCST = 540
NPF = 89
COL = dict(rgx=0, rgz=512, rq=1024, rk=1536, rv=2048, rz=2560, gq=3072, gk=3584, gv=4096, gz=4608, gab=5120)


class Tile:
    def __init__(self, nc, name, shape, dtype, psum=False):
        if psum:
            self.h = nc.alloc_psum_tensor("T_" + name, list(shape), dtype)
        else:
            self.h = nc.alloc_sbuf_tensor("T_" + name, list(shape), dtype)
        self.ap = self.h.ap()
        self.name = name

    def __getitem__(self, k):
        return self.ap[k]


class Sched:
    def __init__(self, nc):
        self.nc = nc
        self.eng = {"pe": nc.tensor, "act": nc.scalar, "dve": nc.vector, "pool": nc.gpsimd, "sp": nc.sync}
        self.sem = {}
        self.cnt = {}
        self.pend = {}
        self.nsem = 0
        for e in self.eng:
            self._new_sem(e)
            self.pend[e] = False
        self.lastw = {}
        self.readers = {}
        self.waited = {e: {} for e in self.eng}
        self.dstream = {}
        self.nwaits = 0
        self.nops = 0
        self.rec = None

    def _new_sem(self, e):
        self.sem[e] = self.nc.alloc_semaphore(f"s_{e}_{self.nsem}")
        self.nsem += 1
        self.cnt[e] = 0

    def _deps(self, r, w):
        evs = []
        for k in r:
            if k in self.lastw:
                evs.append(self.lastw[k] + (True,))
        for k in w:
            if k in self.lastw:
                evs.append(self.lastw[k] + (False,))
            evs.extend(v + (False,) for v in self.readers.get(k, {}).values())
        return evs

    def _do_waits(self, e, evs):
        need = {}
        for sem, val, src, raw in evs:
            if src == e and not (SAME_SYNC or raw):
                continue
            if src.startswith("dma:"):
                val = self.dstream[src[4:]][1]
            if val > need.get(sem, (0, None))[0]:
                need[sem] = (val, src)
        for sem, (val, src) in need.items():
            if self.waited[e].get(sem, 0) >= val:
                continue
            if src == e and sem is self.sem[e] and val > self.cnt[e]:
                continue
            self.eng[e].wait_ge(sem, val)
            self.waited[e][sem] = val
            self.nwaits += 1

    def _register(self, ev, r, w):
        sem = ev[0]
        for k in r:
            self.readers.setdefault(k, {})[sem] = ev
        for k in w:
            self.lastw[k] = ev
            self.readers[k] = {}

    def op(self, e, fn, r=(), w=(), inc=True, cost=None):
        if self.rec is not None:
            self.rec.append(("op", e, fn, tuple(r), tuple(w), inc, cost if cost else DEFCOST[e]))
            return None
        self._do_waits(e, self._deps(r, w))
        if self.cnt[e] >= EPOCH and not self.pend[e]:
            self._new_sem(e)
        ins = fn()
        self.nops += 1
        if inc:
            self.cnt[e] += 1
            ins.then_inc(self.sem[e], 1)
            ev = (self.sem[e], self.cnt[e], e)
            self.pend[e] = False
        else:
            ev = (self.sem[e], self.cnt[e] + 1, e)
            self.pend[e] = True
        self._register(ev, r, w)
        return ins

    def dma(self, q, out, in_, r=(), w=(), stream="d", **kw):
        if self.rec is not None:
            self.rec.append(("dma", q, (out, in_, kw), tuple(r), tuple(w), True, 0.05))
            return None
        tl = [k for k in w if isinstance(k, Tile)]
        if not PER_TILE_SEMS:
            pass
        elif tl:
            stream = "ld_" + tl[0].name
        else:
            stream = "st_" + [k for k in r if isinstance(k, Tile)][0].name
        self._do_waits(q, self._deps(r, w))
        if stream not in self.dstream:
            self.dstream[stream] = [self.nc.alloc_semaphore(f"d_{stream}"), 0]
        st = self.dstream[stream]
        ins = self.eng[q].dma_start(out=out, in_=in_, **kw)
        st[1] += 16
        ins.then_inc(st[0], 16)
        ev = (st[0], st[1], "dma:" + stream)
        self._register(ev, r, w)
        return ins

    def finish(self, e="sp"):
        for name, (sem, tot) in self.dstream.items():
            if tot > 0:
                self.eng[e].wait_ge(sem, tot)


def v3(ap, c=4):
    return ap.rearrange("p (c n) -> p c n", c=c)


def build_nc():
    nc = bass.Bass("TRN2", target_bir_lowering=False)

    def din(name, shape):
        return nc.dram_tensor(name, list(shape), F32, kind="ExternalInput").ap()

    def dout(name, shape):
        return nc.dram_tensor(name, list(shape), F32, kind="ExternalOutput").ap()

    xp_d = din("xp", [NTOK, DM])
    xs_d = din("xs", [NS, DM])
    s_h = din("s_h", [NL, 128, 4, NS])
    s_rgc = din("s_rgc", [NL, 128, 4, 3, NS])
    s_gc = din("s_gc", [NL, 128, 12, 3, NS])
    s_ret = din("s_ret", [NL, NS, 4, 128, 128])
    s_gdn = din("s_gdn", [NL, NS, 4, 128, 128])
    w_in = din("w_in", [NL, DM, DIN])
    w_out = din("w_out", [NL, 1536, DM])
    pf_d = din("pf", [NL, 128, NPF])
    rgw_d = din("rgw", [NL, 128, 2 * 4 * 128])
    rows_d = din("rows", [NL, 1, 2056])
    cst_d = din("cst", [128, NCST])
    rope_d = din("ropet", [18, 128, 128])
    esel_d = din("esel", [1, 256])

    y_p = dout("y_p", [2048, DM])
    y_s = dout("y_s", [NS, DM])
    o_h_p = dout("o_h_p", [NL, 128, 4])
    o_rgc_p = dout("o_rgc_p", [NL, 128, 4, 3])
    o_ret_p = dout("o_ret_p", [NL, 4, 128, 128])
    o_gc_p = dout("o_gc_p", [NL, 128, 12, 3])
    o_gdn_p = dout("o_gdn_p", [NL, 4, 128, 128])
    o_h_s = dout("o_h_s", [NL, 128, 4, NS])
    o_rgc_s = dout("o_rgc_s", [NL, 128, 4, 3, NS])
    o_ret_s = dout("o_ret_s", [NL, NS, 4, 128, 128])
    o_gc_s = dout("o_gc_s", [NL, 128, 12, 3, NS])
    o_gdn_s = dout("o_gdn_s", [NL, NS, 4, 128, 128])
    xscr = nc.dram_tensor("xscr", [NTOK, DM], F32, kind="Internal").ap()
    xsscr = nc.dram_tensor("xsscr", [NS, DM], F32, kind="Internal").ap()

    S = Sched(nc)

    def TL(name, shape, dt=F32):
        return Tile(nc, name, shape, dt)

    Win = TL("Win", [128, 8, DIN], BF16)
    Wout = TL("Wout", [128, 12, DM], BF16)
    cst = TL("cst", [128, NCST])
    identb = TL("identb", [128, 128], BF16)
    esel = TL("esel", [128, 16, 16])
    pft = TL("pft", [128, NPF])
    rgwt = TL("rgwt", [128, 2, 4, 128], BF16)
    rowt = TL("rowt", [128, 2056])
    nc8sp = TL("nc8sp", [128, 4])
    negA = TL("negA", [128, 4])
    ropeb = TL("ropeb", [128, 128])
    X = TL("X", [128, 4, 131])
    GX = TL("GX", [128, 12, 131])
    h0s = TL("h0s", [128, 4, NS])
    Sret = TL("Sret", [128, 512])
    Sretb = TL("Sretb", [128, 512], BF16)
    Sgdn = TL("Sgdn", [128, 512])
    hprev = TL("hprev", [128, 4])
    mix = TL("mix", [128, 12, 128], BF16)
    xt = TL("xt", [128, DM])
    xT = TL("xT", [128, 8, 128], BF16)
    ztr = TL("ztr", [128, 512])
    zte = TL("zte", [128, 512])
    ztg = TL("ztg", [128, 512])
    xcb = TL("xcb", [128, 512], BF16)
    mixr, mixe, mixg = "mixr", "mixe", "mixg"
    Vb = TL("Vb", [128, 512])
    Kbg = TL("Kbg", [128, 512])
    kd = TL("kd", [128, 512])
    qgT = TL("qgT", [128, 512])
    attT = TL("attT", [128, 512])
    Y = TL("Y", [128, 512])
    gabt = TL("gabt", [128, 8])
    gt = TL("gt", [128, 4])
    betat = TL("betat", [128, 4])
    gct = TL("gct", [128, 4])
    egt = TL("egt", [128, 4])
    eglt = TL("eglt", [128, 4])
    eglast = TL("eglast", [128, 4])
    sm1 = TL("sm1", [128, 4])
    sm2 = TL("sm2", [128, 4])
    bst = TL("bst", [128, 4, 6])
    bmv = TL("bmv", [128, 4, 2])
    ebs = TL("ebs", [128, 4, NS])
    vb = TL("vb", [128, 512], BF16)
    k2b = TL("k2b", [128, 512], BF16)
    sm3 = TL("sm3", [128, 4])
    bst2 = TL("bst2", [128, 2, 6])
    bmv2 = TL("bmv2", [128, 2])
    RT = [TL(f"rt{i}", [128, 512]) for i in range(4)]
    ET = [TL(f"et{i}", [128, 512]) for i in range(NET)]
    GT = [TL(f"gt{i}", [128, 512]) for i in range(9)]
    B = [TL(f"bb{i}", [128, 1024], BF16) for i in range(3)]
    ps = [Tile(nc, f"ps{i}", [128, 512], F32, psum=True) for i in range(8)]
    print("sbuf bytes remaining", nc.sbuf_bytes_remaining)
    GDN_W = 2

    def fs(ap):
        n = 1
        for d in ap.shape[1:]:
            n *= d
        return n

    def A(out, in_, func, r, w, bias=0.0, scale=1.0):
        S.op("act", lambda: nc.scalar.activation(out=out, in_=in_, func=func, bias=bias, scale=scale), r, w,
             cost=0.22 + fs(out) / 1000.0)

    def TT(out, a, b, op, r, w):
        S.op("dve", lambda: nc.vector.tensor_tensor(out=out, in0=a, in1=b, op=op), r, w, cost=0.2 + fs(out) / 1000.0)

    def TS(out, a, s1, s2, op0, op1, r, w):
        if s2 is None:
            S.op("dve", lambda: nc.vector.tensor_scalar(out=out, in0=a, scalar1=s1, scalar2=None, op0=op0), r, w,
                 cost=0.2 + fs(out) / 1000.0)
        else:
            S.op("dve", lambda: nc.vector.tensor_scalar(out=out, in0=a, scalar1=s1, scalar2=s2, op0=op0, op1=op1), r, w,
                 cost=0.2 + fs(out) / 1000.0)

    def STT(out, a, s, b, op0, op1, r, w):
        S.op("dve", lambda: nc.vector.scalar_tensor_tensor(out=out, in0=a, scalar=s, in1=b, op0=op0, op1=op1), r, w,
             cost=0.2 + fs(out) / 1000.0)

    def CP(out, in_, r, w, e="dve"):
        if e == "dve":
            S.op("dve", lambda: nc.vector.tensor_copy(out=out, in_=in_), r, w, cost=0.2 + fs(out) / 1000.0)
        else:
            S.op("act", lambda: nc.scalar.activation(out=out, in_=in_, func=AF.Copy), r, w, cost=0.22 + fs(out) / 1000.0)

    def MS(out, val, w):
        S.op("dve", lambda: nc.vector.memset(out, val), (), w)

    def MM(out, lhsT, rhs, st, sp, r, w, inc=True):
        S.op("pe", lambda: nc.tensor.matmul(out, lhsT=lhsT, rhs=rhs, start=st, stop=sp), r, w, inc=inc,
             cost=(0.11 + fs(out) / 1200.0) * (2.0 if lhsT.dtype == F32 else 1.0))

    def TR(out, in_, ident, r, w, inc=True):
        S.op("pe", lambda: nc.tensor.transpose(out=out, in_=in_, identity=ident), r, w, inc=inc,
             cost=(0.11 + fs(out) / 1200.0) * (2.0 if in_.dtype == F32 else 1.0))

    def RSQ(out, in_, r, w, bias=EPS, scale=1.0):
        A(out, in_, AF.Sqrt, r, w, bias=bias, scale=scale)
        S.op("dve", lambda: nc.vector.reciprocal(out=out, in_=out), w, w)

    ident = cst[:, 0:128]
    maskT = cst[:, 128:256]
    strictL = cst[:, 256:384]
    ones = cst[:, 384:512]

    S.dma("sp", cst[:], cst_d, w=[cst], stream="c")
    S.dma("sp", esel[:].rearrange("p a b -> p (a b)"), esel_d.partition_broadcast(128), w=[esel], stream="c")
    CP(identb[:], ident, [cst], [identb], e="act")

    def bc(ap2, n, L):
        return ap2.unsqueeze(2).to_broadcast([L, 4, n])

    def layer_setup(l):
        for kc in range(8):
            S.dma("pool", Win[:, kc, :], w_in[l, kc * 128:(kc + 1) * 128, :], w=[Win], stream="w")
        for kc in range(12):
            S.dma("pool", Wout[:, kc, :], w_out[l, kc * 128:(kc + 1) * 128, :], w=[Wout], stream="w")
        S.dma("pool", rgwt[:].rearrange("p a c n -> p (a c n)"), rgw_d[l], w=[rgwt], stream="w")
        S.dma("sp", pft[:], pf_d[l], w=[pft], stream="c")
        S.dma("sp", rowt[:], rows_d[l].partition_broadcast(128), w=[rowt], stream="c")
        A(nc8sp[:], pft[:, 28:32], AF.Exp, [pft], [nc8sp], scale=-1.0)
        A(nc8sp[:], nc8sp[:], AF.Ln, [nc8sp], [nc8sp], bias=1.0)
        TS(nc8sp[:], nc8sp[:], -8.0, None, ALU.mult, None, [nc8sp], [nc8sp])
        A(negA[:], rowt[:, 2048:2052], AF.Exp, [rowt], [negA])
        TS(negA[:], negA[:], -1.0, None, ALU.mult, None, [negA], [negA])
        MS(Sret[:], 0.0, [Sret])
        MS(Sretb[:], 0.0, [Sretb])
        MS(Sgdn[:], 0.0, [Sgdn])
        MS(hprev[:], 0.0, [hprev])
        MS(X[:, :, 0:3], 0.0, [X])
        MS(GX[:, :, 0:3], 0.0, [GX])

    def block(l, mode, b):
        smp = mode == "s"
        if smp:
            L = NS
            t0 = 0
            src = xs_d if l == 0 else xsscr
            S.dma("sp", xt[:L, :], src, r=[("xsscr", 0)], w=[xt], stream="x")
        else:
            L = 16 if b == 0 else 128
            t0 = 0 if b == 0 else 16 + 128 * (b - 1)
            src = xp_d if l == 0 else xscr
            S.dma("sp", xt[:L, :], src[t0:t0 + L, :], r=[("xscr", b)], w=[xt], stream="x")
        xsrc = xt
        rb = 17 if smp else b
        S.dma("sp", ropeb[:L, :], rope_d[rb, 0:L, :], w=[ropeb], stream="x")
        mT = ident if smp else maskT
        ci = 528 if smp else 512
        qdec = cst[:L, ci:ci + 4]
        kdecp = cst[:L, ci + 4:ci + 8]
        if smp:
            k2dec = cst[:L, 536:540]
        elif L == 128:
            k2dec = cst[:L, 520:524]
        else:
            k2dec = cst[:L, 524:528]
        Xs = X.ap.rearrange("p c n -> p (c n)")[:, 0:4 * 4 * NS].rearrange("p (c j s) -> p c j s", c=4, j=4)
        GXs = GX.ap.rearrange("p c n -> p (c n)")[:, 0:12 * 4 * NS].rearrange("p (c j s) -> p c j s", c=12, j=4)

        xb = B[0]
        CP(xb[:L, :], xsrc[:L, :], [xsrc], [xb], e="act")
        pt = ps[0]
        ptb = pt.ap.bitcast(BF16)
        for kc in range(8):
            TR(ptb[:, kc * 128:kc * 128 + L], xb[:L, kc * 128:(kc + 1) * 128], identb[:L, :L], [xb, identb], [pt],
               inc=(kc == 7))
        CP(xT[:, :, :L], v3(ptb, 8)[:, :, :L], [pt], [xT])

        def fm_group(bank, c0, dst_ap, dst_key, e="act"):
            b3 = v3(bank.ap)
            for c in range(4):
                for kc in range(8):
                    MM(b3[:, c, :L], Win[:, kc, c0 + c * 128:c0 + (c + 1) * 128], xT[:, kc, :L], kc == 0, kc == 7,
                       [Win, xT], [bank], inc=(c == 3 and kc == 7))
            CP(dst_ap, b3[:, :, :L], [bank], [dst_key], e=e)

        def tm_group(bank, c0, n, dst_ap, dst_key, e="dve"):
            for kc in range(8):
                MM(bank[:L, :n], xT[:, kc, :L], Win[:, kc, c0:c0 + n], kc == 0, kc == 7, [Win, xT], [bank],
                   inc=(kc == 7))
            CP(dst_ap, bank[:L, :n], [bank], [dst_key], e=e)

        def conv_chunk(src_tap, wcol0, c, o, dst_key, src_key, bias_col=None):
            w0 = pft[:, wcol0 + c * 4:wcol0 + c * 4 + 1]
            if bias_col is not None:
                TS(o, src_tap(c, 0), w0, pft[:, bias_col + c:bias_col + c + 1], ALU.mult, ALU.add,
                   [src_key, pft], [dst_key])
            else:
                TS(o, src_tap(c, 0), w0, None, ALU.mult, None, [src_key, pft], [dst_key])
            for j in range(1, 4):
                STT(o, src_tap(c, j), pft[:, wcol0 + c * 4 + j:wcol0 + c * 4 + j + 1], o, ALU.mult, ALU.add,
                    [src_key, pft, dst_key], [dst_key])

        def gen_rg():
            pa, pb = ps[0], ps[1]
            if smp:
                S.dma("sp", Xs[:, :, 0:3, :], s_rgc[l], w=[X], stream="st")
                S.dma("sp", h0s[:], s_h[l], w=[h0s], stream="st")
                fm_group(pa, COL["rgx"], Xs[:, :, 3, :], X)
                tap = lambda c, j: Xs[:, c, j, :]
            else:
                fm_group(pa, COL["rgx"], X[:, :, 3:3 + L], X)
                tap = lambda c, j: X[:, c, j:j + L]
            yield
            z3 = v3(ztr.ap)
            fm_group(pb, COL["rgz"], z3[:, :, :L], ztr, e="dve")
            yield
            xc = RT[0]
            xc3 = v3(xc.ap)
            for c in range(4):
                conv_chunk(tap, 0, c, xc3[:, c, :L], xc, X, bias_col=16)
                yield
            xcb3 = v3(xcb.ap)
            CP(xcb3[:, :, :L], xc3[:, :, :L], [xc], [xcb], e="act")
            rt = RT[1]; it = RT[2]; at = RT[3]
            r3 = v3(rt.ap); i3 = v3(it.ap); a3 = v3(at.ap)
            for which, dst3, dkey, bcol, bank in ((0, r3, rt, 20, pa), (1, i3, it, 24, pb)):
                b3 = v3(bank.ap)
                for c in range(4):
                    MM(b3[:, c, :L], rgwt[:, which, c, :], xcb3[:, c, :L], True, True, [rgwt, xcb], [bank], inc=(c == 3))
                yield
                for c in range(4):
                    A(dst3[:, c, :L], b3[:, c, :L], AF.Sigmoid, [bank, pft], [dkey], bias=pft[:, bcol + c:bcol + c + 1])
                yield
            for c in range(4):
                A(a3[:, c, :L], r3[:, c, :L], AF.Exp, [rt, nc8sp], [at], scale=nc8sp[:, c:c + 1])
            yield
            mt = RT[1]
            m3 = v3(mt.ap)
            TT(m3[:, :, :L], a3[:, :, :L], a3[:, :, :L], ALU.mult, [at], [mt])
            A(m3[:, :, :L], m3[:, :, :L], AF.Sqrt, [mt], [mt], bias=1.0, scale=-1.0)
            yield
            TT(i3[:, :, :L], i3[:, :, :L], xc3[:, :, :L], ALU.mult, [it, xc], [it])
            yield
            TT(i3[:, :, :L], i3[:, :, :L], m3[:, :, :L], ALU.mult, [it, mt], [it])
            yield
            ht = RT[0]
            h3 = v3(ht.ap)
            if smp:
                TT(h3[:, :, :L], a3[:, :, :L], h0s[:], ALU.mult, [at, h0s], [ht])
                TT(h3[:, :, :L], h3[:, :, :L], i3[:, :, :L], ALU.add, [ht, it], [ht])
                S.dma("pool", o_h_s[l], h3[:, :, :L], r=[ht], stream="o")
                S.dma("pool", o_rgc_s[l], Xs[:, :, 1:4, :], r=[X], stream="o")
            else:
                for c in range(4):
                    S.op("dve", lambda c=c: nc.vector.tensor_tensor_scan(
                        out=h3[:, c, :L], data0=a3[:, c, :L], data1=i3[:, c, :L], initial=hprev[:, c:c + 1],
                        op0=ALU.mult, op1=ALU.add), [at, it, hprev], [ht])
                    yield
                CP(hprev[:].unsqueeze(2), h3[:, :, L - 1:L], [ht], [hprev])
                if b == NBLK - 1:
                    S.dma("pool", o_h_p[l], hprev[:], r=[hprev], stream="o")
                    S.dma("pool", o_rgc_p[l], X[:, :, L:L + 3], r=[X], stream="o")
                CP(X[:, :, 0:3], X[:, :, L:L + 3], [X], [X])
            yield
            A(z3[:, :, :L], z3[:, :, :L], AF.Silu, [ztr], [ztr])
            TT(mix[:, 0:4, :L], h3[:, :, :L], z3[:, :, :L], ALU.mult, [ht, ztr], [mixr])

        def gen_ret():
            bk = [ps[2], ps[3], ps[4]]
            rq = ET[0]; rk = ET[1]
            tm_group(bk[0], COL["rq"], 512, rq[:L, :], rq)
            yield
            tm_group(bk[1], COL["rk"], 512, rk[:L, :], rk)
            yield
            tm_group(bk[2], COL["rv"], 512, vb[:L, 0:512], vb)
            yield
            z3 = v3(zte.ap)
            fm_group(bk[0], COL["rz"], z3[:, :, :L], zte, e="dve")
            yield
            cosb = ropeb[:L, 0:64].unsqueeze(1).to_broadcast([L, 4, 64])
            sinb = ropeb[:L, 64:128].unsqueeze(1).to_broadcast([L, 4, 64])

            def rope(src, dst, tmp):
                s3 = v3(src[:L, :]); d3 = v3(dst[:L, :]); t3 = v3(tmp[:L, :])
                t1 = s3[:, :, 0:64]; t2 = s3[:, :, 64:128]
                TT(d3[:, :, 0:64], t1, cosb, ALU.mult, [src, ropeb], [dst])
                TT(t3[:, :, 0:64], t2, sinb, ALU.mult, [src, ropeb], [tmp])
                yield
                TT(d3[:, :, 0:64], d3[:, :, 0:64], t3[:, :, 0:64], ALU.subtract, [dst, tmp], [dst])
                TT(d3[:, :, 64:128], t1, sinb, ALU.mult, [src, ropeb], [dst])
                yield
                TT(t3[:, :, 64:128], t2, cosb, ALU.mult, [src, ropeb], [tmp])
                TT(d3[:, :, 64:128], d3[:, :, 64:128], t3[:, :, 64:128], ALU.add, [dst, tmp], [dst])
                yield

            rqr = ET[2]; rkr = ET[4]
            yield from rope(rq, rqr, ET[3])
            yield from rope(rk, rkr, ET[3])
            qkb = B[0]
            TT(v3(qkb[:L, 0:512]), v3(rqr[:L, :]), bc(qdec, 128, L), ALU.mult, [rqr, cst], [qkb])
            TT(v3(qkb[:L, 512:1024]), v3(rkr[:L, :]), bc(kdecp, 128, L), ALU.mult, [rkr, cst], [qkb])
            yield
            if smp:
                k2 = ET[5]
                TT(v3(k2[:L, :]), v3(rkr[:L, :]), bc(k2dec, 128, L), ALU.mult, [rkr, cst], [k2])
            else:
                k2 = k2b
                TT(v3(k2[:L, 0:512]), v3(rkr[:L, :]), bc(k2dec, 128, L), ALU.mult, [rkr, cst], [k2])
            yield
            pt = bk[1]
            ptb = pt.ap.bitcast(BF16)
            for j in range(8):
                TR(ptb[:, j * 128:j * 128 + L], qkb[:L, j * 128:(j + 1) * 128], identb[:L, :L], [qkb, identb], [pt],
                   inc=(j == 7))
            qkT = B[1]
            qkT3 = v3(qkT.ap, 8)
            CP(qkT3[:, :, :L], v3(ptb, 8)[:, :, :L], [pt], [qkT], e="act")
            yield
            bank = bk[2]
            b3 = v3(bank.ap)
            for h in range(4):
                MM(b3[:L, h, :L], qkT3[:, 4 + h, :L], qkT3[:, h, :L], True, True, [qkT], [bank], inc=(h == 3))
            scb = B[2]
            sc3 = v3(scb[:, 0:512])
            TT(sc3[:L, :, :L], b3[:L, :, :L], mT[:L, :L].unsqueeze(1).to_broadcast([L, 4, L]), ALU.mult, [bank, cst], [scb])
            yield
            ob = bk[0]
            ob3 = v3(ob.ap)
            for h in range(4):
                MM(ob3[:L, h, :], sc3[:L, h, :L], vb[:L, h * 128:(h + 1) * 128], True, smp, [scb, vb], [ob],
                   inc=(smp and h == 3))
                if not smp:
                    MM(ob3[:L, h, :], qkT3[:, h, :L], v3(Sretb.ap)[:, h, :], False, True, [qkT, Sretb], [ob], inc=(h == 3))
            yield
            if not smp:
                sb = bk[1]
                sb3 = v3(sb.ap)
                for h in range(4):
                    MM(sb3[:, h, :], k2[:L, h * 128:(h + 1) * 128], vb[:L, h * 128:(h + 1) * 128], True, True, [k2, vb], [sb],
                       inc=(h == 3))
                yield
                for h in range(4):
                    STT(v3(Sret.ap)[:, h, :], v3(Sret.ap)[:, h, :], float(GAM[h] ** L), sb3[:, h, :], ALU.mult, ALU.add,
                        [Sret, sb], [Sret])
                yield
                CP(Sretb[:], Sret[:], [Sret], [Sretb], e="act")
                if b == NBLK - 1:
                    S.dma("pool", o_ret_p[l].rearrange("h d v -> d h v"), v3(Sret.ap), r=[Sret], stream="o")
                osrc, okey = ob3, ob
            else:
                oacc = ET[1]
                CP(oacc[:L, :], ob[:L, :], [ob], [oacc])
                for s in range(NS):
                    St = ET[2]
                    S.dma("sp", v3(St.ap), s_ret[l, s].rearrange("h d v -> d h v"), w=[St], stream="st")
                    qm = ET[3]
                    TT(v3(qm[:, 0:64], 4), qkT3[:, 0:4, :NS], esel[:, s, :].unsqueeze(1).to_broadcast([128, 4, NS]),
                       ALU.mult, [qkT, esel], [qm])
                    tb_ = bk[1]
                    tb3 = v3(tb_.ap)
                    for h in range(4):
                        MM(tb3[:NS, h, :], v3(qm[:, 0:64], 4)[:, h, :], v3(St.ap)[:, h, :], True, True, [qm, St], [tb_],
                           inc=(h == 3))
                    TT(oacc[:L, :], oacc[:L, :], tb_[:NS, :], ALU.add, [oacc, tb_], [oacc])
                    yield
                    vm = ET[4]
                    TT(vm[:NS, :], vb[:NS, 0:512], ident[:NS, s:s + 1].to_broadcast([NS, 512]), ALU.mult, [vb, cst], [vm])
                    sb = bk[2]
                    sb3 = v3(sb.ap)
                    for h in range(4):
                        MM(sb3[:, h, :], k2[:NS, h * 128:(h + 1) * 128], vm[:NS, h * 128:(h + 1) * 128], True, True,
                           [k2, vm], [sb], inc=(h == 3))
                    So = ET[0]
                    for h in range(4):
                        STT(v3(So.ap)[:, h, :], v3(St.ap)[:, h, :], float(GAM[h]), sb3[:, h, :], ALU.mult, ALU.add,
                            [St, sb], [So])
                    S.dma("pool", o_ret_s[l, s].rearrange("h d v -> d h v"), v3(So.ap), r=[So], stream="o")
                    yield
                osrc, okey = v3(oacc.ap), oacc
            for h in range(4):
                S.op("dve", lambda h=h: nc.vector.bn_stats(out=bst[:L, h, :], in_=osrc[:L, h, :]), [okey], [bst])
            yield
            for h in range(4):
                S.op("dve", lambda h=h: nc.vector.bn_aggr(out=bmv[:L, h, :], in_=bst[:L, h, :]), [bst], [bmv])
            RSQ(sm1[:L, :], bmv[:L, :, 1], [bmv], [sm1])
            yield
            onb = B[2]
            for h in range(4):
                TS(onb[:L, 512 + h * 128:512 + (h + 1) * 128], osrc[:L, h, :], bmv[:L, h, 0:1], sm1[:L, h:h + 1],
                   ALU.subtract, ALU.mult, [okey, bmv, sm1], [onb])
            yield
            pt = bk[1]
            ptb = pt.ap.bitcast(BF16)
            for h in range(4):
                TR(ptb[:, h * 128:h * 128 + L], onb[:L, 512 + h * 128:512 + (h + 1) * 128], identb[:L, :L], [onb, identb],
                   [pt], inc=(h == 3))
            yt = ET[0]
            y3 = v3(yt.ap)
            for h in range(4):
                A(y3[:, h, :L], ptb[:, h * 128:h * 128 + L], AF.Identity, [pt, pft], [yt], bias=pft[:, 36 + h:37 + h],
                  scale=pft[:, 32 + h:33 + h])
            yield
            A(z3[:, :, :L], z3[:, :, :L], AF.Silu, [zte], [zte])
            TT(mix[:, 4:8, :L], y3[:, :, :L], z3[:, :, :L], ALU.mult, [yt, zte], [mixe])

        def gen_gdn():
            bk = [ps[5], ps[6], ps[7]]
            nb = [0]

            def PB():
                nb[0] = (nb[0] + 1) % 3
                return bk[nb[0]]

            if smp:
                S.dma("sp", GXs[:, :, 0:3, :], s_gc[l], w=[GX], stream="st")
                for g in range(3):
                    fm_group(PB(), COL["gq"] + 512 * g, GXs[:, 4 * g:4 * g + 4, 3, :], GX)
                    yield
                gtap = lambda c, j: GXs[:, c, j, :]
            else:
                for g in range(3):
                    fm_group(PB(), COL["gq"] + 512 * g, GX[:, 4 * g:4 * g + 4, 3:3 + L], GX)
                    yield
                gtap = lambda c, j: GX[:, c, j:j + L]
            tm_group(PB(), COL["gab"], 8, gabt[:L, :], gabt)
            z3 = v3(ztg.ap)
            fm_group(PB(), COL["gz"], z3[:, :, :L], ztg, e="dve")
            yield
            TT(gt[:L, :], gabt[:L, 0:4], rowt[:L, 2052:2056], ALU.add, [gabt, rowt], [gt])
            A(gt[:L, :], gt[:L, :], AF.Exp, [gt], [gt])
            A(gt[:L, :], gt[:L, :], AF.Ln, [gt], [gt], bias=1.0)
            TT(gt[:L, :], gt[:L, :], negA[:L, :], ALU.mult, [gt, negA], [gt])
            A(betat[:L, :], gabt[:L, 4:8], AF.Sigmoid, [gabt], [betat])
            yield
            bank = PB()
            MM(bank[:L, 0:4], mT[:L, :L], gt[:L, :], True, True, [cst, gt], [bank])
            CP(gct[:L, :], bank[:L, 0:4], [bank], [gct])
            A(egt[:L, :], gct[:L, :], AF.Exp, [gct], [egt])
            yield
            Rt = GT[5]
            R3 = v3(Rt.ap)
            for h in range(4):
                TS(R3[:L, h, :L], mT[:L, :L], gt[:L, h:h + 1], None, ALU.mult, None, [cst, gt], [Rt])
            yield
            gcB = PB()
            g3 = v3(gcB.ap)
            for h in range(4):
                MM(g3[:, h, :L], ones[:L, :], R3[:L, h, :L], True, True, [cst, Rt], [gcB], inc=(h == 3))
            EB = GT[6]
            EB3 = v3(EB.ap)
            A(EB3[:, :, :L], g3[:, :, :L], AF.Exp, [gcB], [EB])
            yield
            dT = GT[7]
            dT3 = v3(dT.ap)
            for h in range(4):
                TS(dT3[:L, h, :L], g3[:L, h, :L], gct[:L, h:h + 1], 0.0, ALU.subtract, ALU.min, [gcB, gct, EB], [dT])
            yield
            if not smp:
                dl = GT[8]
                dl3 = v3(dl.ap)
                for h in range(4):
                    TS(dl3[:L, h, :L], g3[:L, h, :L], gct[:L, h:h + 1], 0.0, ALU.subtract, ALU.max, [gcB, gct], [dl])
                yield
                CP(eglast[:].unsqueeze(2), EB3[:, :, L - 1:L], [EB], [eglast])
                TT(eglt[:L, :].unsqueeze(2), g3[:L, :, L - 1:L], gct[:L, :].unsqueeze(2), ALU.subtract, [gcB, gct], [eglt])
                A(eglt[:L, :], eglt[:L, :], AF.Exp, [eglt], [eglt])
                A(dl3[:L, :, :L], dl3[:L, :, :L], AF.Exp, [dl], [dl], scale=-1.0)
                TT(dl3[:L, :, :L], dl3[:L, :, :L], strictL[:L, :L].unsqueeze(1).to_broadcast([L, 4, L]), ALU.mult,
                   [dl, cst], [dl])
                yield
            else:
                CP(ebs[:], EB3[:, :, :NS], [EB], [ebs])
            A(dT3[:L, :, :L], dT3[:L, :, :L], AF.Exp, [dT], [dT])
            TT(dT3[:L, :, :L], dT3[:L, :, :L], mT[:L, :L].unsqueeze(1).to_broadcast([L, 4, L]), ALU.mult, [dT, cst], [dT])
            yield
            cq = GT[0]; ck = GT[1]; cv = GT[2]
            cqk = [cq, ck, cv]
            for g in range(3):
                cg3 = v3(cqk[g].ap)
                for c in range(4):
                    if g == 2 and POOL_CONV:
                        tp3 = v3(GT[3].ap)
                        o = cg3[:, c, :L]
                        wc = 40 + 16 * g + c * 4
                        S.op("pool", lambda o=o, c=c, wc=wc: nc.gpsimd.tensor_scalar(
                            out=o, in0=gtap(8 + c, 0), scalar1=pft[:, wc:wc + 1], scalar2=None, op0=ALU.mult),
                            [GX, pft], [cv], cost=0.5)
                        for j in range(1, 4):
                            S.op("pool", lambda c=c, j=j, wc=wc: nc.gpsimd.tensor_scalar(
                                out=tp3[:, c, :L], in0=gtap(8 + c, j), scalar1=pft[:, wc + j:wc + j + 1], scalar2=None,
                                op0=ALU.mult), [GX, pft], [GT[3]], cost=0.5)
                            S.op("pool", lambda o=o, c=c: nc.gpsimd.tensor_tensor(
                                out=o, in0=o, in1=tp3[:, c, :L], op=ALU.add), [cv, GT[3]], [cv], cost=0.5)
                    else:
                        conv_chunk(lambda c_, j, g=g: gtap(4 * g + c_, j), 40 + 16 * g, c, cg3[:, c, :L], cqk[g], GX)
                    yield
                A(cg3[:, :, :L], cg3[:, :, :L], AF.Silu, [cqk[g]], [cqk[g]])
            if smp:
                S.dma("pool", o_gc_s[l], GXs[:, :, 1:4, :], r=[GX], stream="o")
            else:
                if b == NBLK - 1:
                    S.dma("pool", o_gc_p[l], GX[:, :, L:L + 3], r=[GX], stream="o")
                CP(GX[:, :, 0:3], GX[:, :, L:L + 3], [GX], [GX])
            yield
            for g in range(2):
                cg3 = v3(cqk[g].ap)
                sq = GT[3]
                sq3 = v3(sq.ap)
                TT(sq3[:, :, :L], cg3[:, :, :L], cg3[:, :, :L], ALU.mult, [cqk[g]], [sq])
                bank = PB()
                b3 = v3(bank.ap)
                for h in range(4):
                    MM(b3[:, h, :L], ones, sq3[:, h, :L], True, True, [cst, sq], [bank], inc=(h == 3))
                yield
                rn = GT[4]
                rn3 = v3(rn.ap)
                RSQ(rn3[:, :, :L], b3[:, :, :L], [bank], [rn])
                if g == 0:
                    STT(cg3[:, :, :L], cg3[:, :, :L], float(128 ** -0.5), rn3[:, :, :L], ALU.mult, ALU.mult, [cq, rn], [cq])
                else:
                    TT(cg3[:, :, :L], cg3[:, :, :L], rn3[:, :, :L], ALU.mult, [ck, rn], [ck])
                yield
            q3 = v3(cq.ap); k3 = v3(ck.ap); cv3 = v3(cv.ap)
            TT(v3(qgT.ap)[:, :, :L], q3[:, :, :L], EB3[:, :, :L], ALU.mult, [cq, EB], [qgT])
            bank = PB()
            b3 = v3(bank.ap)
            for h in range(4):
                MM(b3[:L, h, :L], k3[:, h, :L], q3[:, h, :L], True, True, [ck, cq], [bank], inc=(h == 3))
            at3 = v3(attT.ap)
            TT(at3[:L, :, :L], b3[:L, :, :L], dT3[:L, :, :L], ALU.mult, [bank, dT], [attT])
            yield
            kTM = GT[3]; vTM = GT[4]
            for srcT, s3_, dstT in ((ck, k3, kTM), (cv, cv3, vTM)):
                bank = PB()
                for h in range(4):
                    TR(bank[:L, h * 128:(h + 1) * 128], s3_[:, h, :L], ident, [srcT, cst], [bank], inc=(h == 3))
                CP(dstT[:L, :], bank[:L, :], [bank], [dstT], e="act")
                yield
            if not smp:
                TT(v3(kd[:L, :]), v3(kTM[:L, :]), bc(eglt[:L, :], 128, L), ALU.mult, [kTM, eglt], [kd])
            else:
                CP(kd[:L, :], kTM[:L, :], [kTM], [kd])
            TT(v3(Vb[:L, :]), v3(vTM[:L, :]), bc(betat[:L, :], 128, L), ALU.mult, [vTM, betat], [Vb])
            yield
            TT(sm2[:L, :], betat[:L, :], egt[:L, :], ALU.mult, [betat, egt], [sm2])
            TT(v3(Kbg[:L, :]), v3(kTM[:L, :]), bc(sm2[:L, :], 128, L), ALU.mult, [kTM, sm2], [Kbg])
            yield
            Y3 = v3(Y.ap)
            if smp:
                CP(Y3[:L, :, :L], ident[:L, :L].unsqueeze(1).to_broadcast([L, 4, L]), [cst], [Y])
            else:
                bank = PB()
                b3 = v3(bank.ap)
                for h in range(4):
                    MM(b3[:L, h, :L], k3[:, h, :L], k3[:, h, :L], True, True, [ck], [bank], inc=(h == 3))
                P = GT[2]
                P3 = v3(P.ap)
                for h in range(4):
                    STT(P3[:L, h, :L], b3[:L, h, :L], betat[:L, h:h + 1], dl3[:L, h, :L], ALU.mult, ALU.mult,
                        [bank, betat, dl], [P])
                yield
                bank = PB()
                b3 = v3(bank.ap)
                for h in range(4):
                    TR(b3[:L, h, :L], P3[:L, h, :L], ident[:L, :L], [P, cst], [bank], inc=(h == 3))
                Q = GT[5]
                Q3 = v3(Q.ap)
                CP(Q3[:L, :, :L], b3[:L, :, :L], [bank], [Q], e="act")
                STT(Y3[:L, :, :L], Q3[:L, :, :L], -1.0, ident[:L, :L].unsqueeze(1).to_broadcast([L, 4, L]), ALU.mult, ALU.add,
                    [Q, cst], [Y])
                yield
                nlev = 6 if L == 128 else 3
                for lev in range(nlev + 1):
                    last = lev == nlev
                    if not last:
                        bq = PB(); bp = PB()
                        bq3 = v3(bq.ap); bp3 = v3(bp.ap)
                    if lev > 0:
                        by = PB()
                        by3 = v3(by.ap)
                    for h in range(4):
                        if not last:
                            MM(bq3[:L, h, :L], P3[:L, h, :L], Q3[:L, h, :L], True, True, [P, Q], [bq], inc=False)
                        if lev > 0:
                            MM(by3[:L, h, :L], P3[:L, h, :L], Y3[:L, h, :L], True, True, [P, Y], [by],
                               inc=(last and h == 3))
                        if not last:
                            MM(bp3[:L, h, :L], Q3[:L, h, :L], P3[:L, h, :L], True, True, [P, Q], [bp], inc=(h == 3))
                    yield
                    if lev > 0:
                        TT(Y3[:L, :, :L], Y3[:L, :, :L], by3[:L, :, :L], ALU.add, [Y, by], [Y])
                    if not last:
                        Pn, Qn = (GT[3], GT[4]) if lev % 2 == 0 else (GT[2], GT[5])
                        CP(v3(Qn.ap)[:L, :, :L], bq3[:L, :, :L], [bq], [Qn], e="act")
                        CP(v3(Pn.ap)[:L, :, :L], bp3[:L, :, :L], [bp], [Pn], e="dve")
                        P, Q = Pn, Qn
                        P3, Q3 = v3(P.ap), v3(Q.ap)
                    yield
            bank = PB()
            b3 = v3(bank.ap)
            for h in range(4):
                MM(b3[:, h, :L], Kbg[:L, h * 128:(h + 1) * 128], Y3[:L, h, :L], True, True, [Kbg, Y], [bank], inc=(h == 3))
            nWT = GT[6]
            nW3 = v3(nWT.ap)
            A(nW3[:, :, :L], b3[:, :, :L], AF.Copy, [bank], [nWT], scale=-1.0)
            yield
            Sg3 = v3(Sgdn.ap)
            vnb = PB()
            vn3 = v3(vnb.ap)
            for h in range(4):
                MM(vn3[:L, h, :], Y3[:L, h, :L], Vb[:L, h * 128:(h + 1) * 128], True, smp, [Y, Vb], [vnb],
                   inc=(smp and h == 3))
                if not smp:
                    MM(vn3[:L, h, :], nW3[:, h, :L], Sg3[:, h, :], False, True, [nWT, Sgdn], [vnb], inc=(h == 3))
            vnew = GT[7]
            if not smp:
                CP(vnew[:L, :], vnb[:L, :], [vnb], [vnew], e="act")
                yield
            else:
                wacc = GT[0]; qacc = GT[1]
                CP(wacc[:L, :], vnb[:L, :], [vnb], [wacc], e="act")
                MS(qacc[:L, :], 0.0, [qacc])
                for s in range(NS):
                    St = GT[2]
                    S.dma("sp", v3(St.ap), s_gdn[l, s].rearrange("h d v -> d h v"), w=[St], stream="st")
                    for srcT, s3_, acc in ((qgT, v3(qgT.ap), qacc), (nWT, nW3, wacc)):
                        qm = GT[3]
                        TT(v3(qm[:, 0:64], 4), s3_[:, :, :NS], esel[:, s, :].unsqueeze(1).to_broadcast([128, 4, NS]),
                           ALU.mult, [srcT, esel], [qm])
                        tb_ = PB()
                        tb3 = v3(tb_.ap)
                        for h in range(4):
                            MM(tb3[:NS, h, :], v3(qm[:, 0:64], 4)[:, h, :], v3(St.ap)[:, h, :], True, True, [qm, St], [tb_],
                               inc=(h == 3))
                        TT(acc[:L, :], acc[:L, :], tb_[:NS, :], ALU.add, [acc, tb_], [acc])
                        yield
                CP(vnew[:L, :], wacc[:L, :], [wacc], [vnew])
            ob = PB()
            ob3 = v3(ob.ap)
            for h in range(4):
                MM(ob3[:L, h, :], at3[:L, h, :L], vnew[:L, h * 128:(h + 1) * 128], True, smp, [attT, vnew], [ob],
                   inc=(smp and h == 3))
                if not smp:
                    MM(ob3[:L, h, :], v3(qgT.ap)[:, h, :L], Sg3[:, h, :], False, True, [qgT, Sgdn], [ob], inc=(h == 3))
            yield
            if smp:
                TT(qacc[:L, :], qacc[:L, :], ob[:L, :], ALU.add, [qacc, ob], [qacc])
                osrc, okey = v3(qacc.ap), qacc
                for s in range(NS):
                    St = GT[2]
                    S.dma("sp", v3(St.ap), s_gdn[l, s].rearrange("h d v -> d h v"), w=[St], stream="st")
                    vm = GT[3]
                    TS(vm[:NS, :], vnew[:NS, :], ident[:NS, s:s + 1], None, ALU.mult, None, [vnew, cst], [vm])
                    sb = PB()
                    sb3 = v3(sb.ap)
                    for h in range(4):
                        MM(sb3[:, h, :], kd[:NS, h * 128:(h + 1) * 128], vm[:NS, h * 128:(h + 1) * 128], True, True,
                           [kd, vm], [sb], inc=(h == 3))
                    So = GT[4]
                    for h in range(4):
                        STT(v3(So.ap)[:, h, :], v3(St.ap)[:, h, :], ebs[:, h, s:s + 1], sb3[:, h, :], ALU.mult, ALU.add,
                            [St, sb, ebs], [So])
                    S.dma("pool", o_gdn_s[l, s].rearrange("h d v -> d h v"), v3(So.ap), r=[So], stream="o")
                    yield
            else:
                osrc, okey = ob3, ob
                sb = PB()
                sb3 = v3(sb.ap)
                for h in range(4):
                    MM(sb3[:, h, :], kd[:L, h * 128:(h + 1) * 128], vnew[:L, h * 128:(h + 1) * 128], True, True, [kd, vnew], [sb],
                       inc=(h == 3))
                yield
                for h in range(4):
                    STT(Sg3[:, h, :], Sg3[:, h, :], eglast[:, h:h + 1], sb3[:, h, :], ALU.mult, ALU.add,
                        [Sgdn, sb, eglast], [Sgdn])
                if b == NBLK - 1:
                    S.dma("pool", o_gdn_p[l].rearrange("h d v -> d h v"), Sg3, r=[Sgdn], stream="o")
                yield
            osq = GT[8]
            A(v3(osq.ap)[:L, :, :], osrc[:L, :, :], AF.Square, [okey], [osq])
            S.op("dve", lambda: nc.vector.reduce_sum(out=sm3[:L, :], in_=v3(osq.ap)[:L, :, :], axis=AX.X), [osq], [sm3])
            RSQ(sm3[:L, :], sm3[:L, :], [sm3], [sm3], bias=EPS, scale=1.0 / 128.0)
            yield
            on = GT[5]
            for h in range(4):
                TS(on[:L, h * 128:(h + 1) * 128], osrc[:L, h, :], sm3[:L, h:h + 1], None, ALU.mult, None, [okey, sm3], [on])
            yield
            bank = PB()
            b3 = v3(bank.ap)
            for h in range(4):
                TR(b3[:, h, :L], on[:L, h * 128:(h + 1) * 128], ident[:L, :L], [on, cst], [bank], inc=(h == 3))
            A(z3[:, :, :L], z3[:, :, :L], AF.Silu, [ztg], [ztg])
            STT(mix[:, 8:12, :L], b3[:, :, :L], pft[:, 88:89], z3[:, :, :L], ALU.mult, ALU.mult, [bank, pft, ztg], [mixg])

        if MERGE == "sim":
            branches = []
            for gen in (gen_gdn, gen_ret, gen_rg):
                S.rec = []
                for _ in gen():
                    pass
                ops, S.rec = S.rec, None
                units, cur = [], []
                for o in ops:
                    cur.append(o)
                    if o[5]:
                        units.append(cur)
                        cur = []
                assert not cur
                branches.append(units)
            clock = {}
            wr = {}
            rd = {}
            ptr = [0] * len(branches)
            HOP = 0.3
            while True:
                best = None
                for bi, units in enumerate(branches):
                    if ptr[bi] >= len(units):
                        continue
                    u = units[ptr[bi]]
                    e = u[0][1]
                    t = clock.get(e, 0.0)
                    for o in u:
                        for k in o[3]:
                            if k in wr:
                                t = max(t, wr[k][0] + (HOP if wr[k][1] != e else 0.0))
                        for k in o[4]:
                            if k in wr:
                                t = max(t, wr[k][0] + (HOP if wr[k][1] != e else 0.0))
                            if k in rd:
                                t = max(t, rd[k][0] + (HOP if rd[k][1] != e else 0.0))
                    if best is None or t < best[0] - 1e-9:
                        best = (t, bi)
                if best is None:
                    break
                t, bi = best
                u = branches[bi][ptr[bi]]
                ptr[bi] += 1
                e = u[0][1]
                for o in u:
                    kind, eng, fn, r_, w_, inc, cost = o
                    if kind == "dma":
                        out_, in_, kw = fn
                        S.dma(eng, out_, in_, r=r_, w=w_, **kw)
                        done = t + 2.5
                        t += cost
                        who = "dma"
                    else:
                        S.op(eng, fn, r_, w_, inc=inc)
                        t += cost
                        done = t
                        who = eng
                    for k in r_:
                        if k not in rd or rd[k][0] < done:
                            rd[k] = (done, who)
                    for k in w_:
                        wr[k] = (done, who)
                        rd.pop(k, None)
                clock[e] = t
            gens = []
        else:
            gens = [(gen_rg(), 1), (gen_ret(), 1), (gen_gdn(), GDN_W)]
        if MERGE == "seq":
            for g, _ in gens:
                for _ in g:
                    pass
            gens = []
        while gens:
            for item in list(gens):
                g, wgt = item
                for _ in range(wgt):
                    try:
                        next(g)
                    except StopIteration:
                        gens.remove(item)
                        break

        z = [RT[0], RT[1]]
        for n in range(2):
            bank = ps[n]
            for kc in range(12):
                MM(bank[:L, :], mix[:, kc, :L], Wout[:, kc, n * 512:(n + 1) * 512], kc == 0, kc == 11,
                   [mixr, mixe, mixg, Wout], [bank], inc=(kc == 11))
            STT(z[n][:L, :], xsrc[:L, n * 512:(n + 1) * 512], float(ALPHA), bank[:L, :], ALU.mult, ALU.add, [xsrc, bank],
                [z[n]])
            S.op("dve", lambda n=n: nc.vector.bn_stats(out=bst2[:L, n, :], in_=z[n][:L, :]), [z[n]], [bst2])
        S.op("dve", lambda: nc.vector.bn_aggr(out=bmv2[:L, :], in_=bst2[:L, 0:2, :]), [bst2], [bmv2])
        RSQ(sm2[:L, 0:1], bmv2[:L, 1:2], [bmv2], [sm2])
        for n in range(2):
            sl = slice(n * 512, (n + 1) * 512)
            TS(z[n][:L, :], z[n][:L, :], bmv2[:L, 0:1], sm2[:L, 0:1], ALU.subtract, ALU.mult, [z[n], bmv2, sm2], [z[n]])
            TT(z[n][:L, :], z[n][:L, :], rowt[:L, sl], ALU.mult, [z[n], rowt], [z[n]])
            TT(z[n][:L, :], z[n][:L, :], rowt[:L, 1024 + n * 512:1024 + (n + 1) * 512], ALU.add, [z[n], rowt], [z[n]])
            if smp:
                if l == NL - 1:
                    S.dma("pool", y_s[:, sl], z[n][:NS, :], r=[z[n]], stream="o")
                else:
                    S.dma("pool", xsscr[:, sl], z[n][:NS, :], r=[z[n]], w=[("xsscr", 0)], stream="xo")
            else:
                if l == NL - 1:
                    if b > 0:
                        S.dma("pool", y_p[t0 - 16:t0 - 16 + L, sl], z[n][:L, :], r=[z[n]], stream="o")
                else:
                    S.dma("pool", xscr[t0:t0 + L, sl], z[n][:L, :], r=[z[n]], w=[("xscr", b)], stream="xo")


    for l in range(NL):
        layer_setup(l)
        for b in range(NBLK):
            block(l, "p", b)
        if not SKIP_SAMPLE:
            block(l, "s", 0)
    S.finish("sp")
    print("ops", S.nops, "waits", S.nwaits, "sems", S.nsem + len(S.dstream))
    return nc


def _consts():
    cst = np.zeros((128, NCST), np.float32)
    i = np.arange(128)
    cst[:, 0:128] = np.eye(128)
    cst[:, 128:256] = (i[None, :] >= i[:, None])
    cst[:, 256:384] = (i[None, :] < i[:, None])
    cst[:, 384:512] = 1.0
    sc = 128.0 ** -0.5
    for h in range(4):
        g = np.float64(GAM[h])
        cst[:, 512 + h] = g ** (i + 1.0)
        cst[:, 516 + h] = g ** (-(i + 1.0)) * sc
        cst[:, 520 + h] = g ** (127.0 - i) * sc
        cst[:16, 524 + h] = g ** (15.0 - i[:16]) * sc
        cst[:, 528 + h] = g
        cst[:, 532 + h] = sc / g
        cst[:, 536 + h] = sc
    half = 64
    inv = (np.float32(10000.0) ** (-np.arange(half, dtype=np.float32) / np.float32(half))).astype(np.float32)
    rope = np.zeros((18, 128, 128), np.float32)
    for b in range(18):
        if b == 0:
            pos = np.arange(16, dtype=np.float32)
        elif b < 17:
            pos = 16 + 128 * (b - 1) + np.arange(128, dtype=np.float32)
        else:
            pos = np.full(16, 16384.0, np.float32)
        ang = (pos[:, None].astype(np.float32) * inv[None, :]).astype(np.float32)
        rope[b, :len(pos), 0:64] = np.cos(ang.astype(np.float64))
        rope[b, :len(pos), 64:128] = np.sin(ang.astype(np.float64))
    esel = np.eye(16, dtype=np.float32).reshape(1, 256)
    return cst, rope, esel


_NC_CACHE = {}


def kernel(x_prompt, x_sample, state_rglru_h, state_rglru_conv, state_ret, state_gdn_conv, state_gdn,
           meta_tokens, w_in, rg_conv_w, rg_conv_b, rg_w_a, rg_b_a, rg_w_x, rg_b_x, rg_lambda,
           ret_gn_w, ret_gn_b, gdn_conv_w, gdn_a_log, gdn_dt_bias, gdn_norm_w, w_out, ln_w, ln_b):
    f = lambda a: np.ascontiguousarray(np.asarray(a, dtype=np.float32))
    x_prompt, x_sample, meta_tokens = f(x_prompt), f(x_sample), f(meta_tokens)
    w_in, w_out = f(w_in), f(w_out)
    pf = np.zeros((NL, 128, NPF), np.float32)

    def fm(v, nch):
        return f(v).reshape(NL, nch, 128).transpose(0, 2, 1)

    pf[:, :, 0:16] = f(rg_conv_w).reshape(NL, 4, 4, 128).transpose(0, 3, 2, 1).reshape(NL, 128, 16)
    pf[:, :, 16:20] = fm(rg_conv_b, 4)
    pf[:, :, 20:24] = fm(rg_b_a, 4)
    pf[:, :, 24:28] = fm(rg_b_x, 4)
    pf[:, :, 28:32] = fm(rg_lambda, 4)
    pf[:, :, 32:36] = fm(ret_gn_w, 4)
    pf[:, :, 36:40] = fm(ret_gn_b, 4)
    pf[:, :, 40:88] = f(gdn_conv_w).reshape(NL, 4, 12, 128).transpose(0, 3, 2, 1).reshape(NL, 128, 48)
    pf[:, :, 88] = f(gdn_norm_w)
    rgw = np.zeros((NL, 128, 2, 4, 128), np.float32)
    for which, wsrc in ((0, f(rg_w_a)), (1, f(rg_w_x))):
        for n in range(8):
            c, o = n // 2, (n % 2) * 64
            rgw[:, o:o + 64, which, c, o:o + 64] = wsrc[:, n]
    rgw = rgw.reshape(NL, 128, 1024)
    rows = np.concatenate([f(ln_w), f(ln_b), f(gdn_a_log), f(gdn_dt_bias)], axis=1).reshape(NL, 1, 2056)
    cst, rope, esel = _consts()
    if "nc" not in _NC_CACHE:
        _NC_CACHE["nc"] = build_nc()
    nc = _NC_CACHE["nc"]
    in_maps = []
    for c in range(8):
        sl = slice(NS * c, NS * (c + 1))
        m = {
            "xp": np.ascontiguousarray(np.concatenate([meta_tokens, x_prompt[c]], axis=0)),
            "xs": np.ascontiguousarray(x_sample[sl, 0, :]),
            "s_h": np.ascontiguousarray(f(state_rglru_h)[:, sl].reshape(NL, NS, 4, 128).transpose(0, 3, 2, 1)),
            "s_rgc": np.ascontiguousarray(f(state_rglru_conv)[:, sl].reshape(NL, NS, 3, 4, 128).transpose(0, 4, 3, 2, 1)),
            "s_gc": np.ascontiguousarray(f(state_gdn_conv)[:, sl].reshape(NL, NS, 3, 12, 128).transpose(0, 4, 3, 2, 1)),
            "s_ret": np.ascontiguousarray(f(state_ret)[:, sl]),
            "s_gdn": np.ascontiguousarray(f(state_gdn)[:, sl]),
            "w_in": w_in, "w_out": w_out, "pf": pf, "rgw": rgw, "rows": rows,
            "cst": cst, "ropet": rope, "esel": esel,
        }
        in_maps.append(m)
    res = run_bass_kernel_spmd(nc, in_maps, core_ids=list(range(8)))
    R = res.results
    g = lambda k: [np.asarray(R[c][k], dtype=np.float32) for c in range(8)]
    y_prompt = np.stack(g("y_p"), 0)
    y_sample = np.concatenate(g("y_s"), 0)[:, None, :]
    hp = np.stack([a.transpose(0, 2, 1).reshape(NL, 512) for a in g("o_h_p")], 1)
    rgcp = np.stack([a.transpose(0, 3, 2, 1).reshape(NL, 3, 512) for a in g("o_rgc_p")], 1)
    retp = np.stack(g("o_ret_p"), 1)
    gcp = np.stack([a.transpose(0, 3, 2, 1).reshape(NL, 3, 1536) for a in g("o_gc_p")], 1)
    gdnp = np.stack(g("o_gdn_p"), 1)
    hs = np.concatenate([a.transpose(0, 3, 2, 1).reshape(NL, NS, 512) for a in g("o_h_s")], 1)
    rgcs = np.concatenate([a.transpose(0, 4, 3, 2, 1).reshape(NL, NS, 3, 512) for a in g("o_rgc_s")], 1)
    rets = np.concatenate(g("o_ret_s"), 1)
    gcs = np.concatenate([a.transpose(0, 4, 3, 2, 1).reshape(NL, NS, 3, 1536) for a in g("o_gc_s")], 1)
    gdns = np.concatenate(g("o_gdn_s"), 1)
    c = np.ascontiguousarray
    return (c(y_prompt), c(y_sample), c(hp), c(rgcp), c(retp), c(gcp), c(gdnp), c(hs), c(rgcs), c(rets), c(gcs), c(gdns))
```

```python
import numpy as np
import concourse.bass as bass
import concourse.mybir as mybir
from concourse.bass_utils import run_bass_kernel_spmd

F32 = mybir.dt.float32
BF16 = mybir.dt.bfloat16
ALU = mybir.AluOpType
AF = mybir.ActivationFunctionType
AX = mybir.AxisListType

EPOCH = 12000
POOL_CONV = False
DEFCOST = {"pe": 0.2, "act": 0.4, "dve": 0.3, "pool": 0.05, "sp": 0.05}
GDN_STOP = 100000
ENABLE = [True, True, True]
NET = 6
SKIP_SAMPLE = False
PER_TILE_SEMS = True
MERGE = "sim"
SAME_SYNC = True

NL = 4
DM = 1024
DIN = 5128
NTOK = 2064
NBLK = 17
NS = 16
ALPHA = 8.0 ** 0.25
EPS = 1e-6
GAM = [1.0 - 2.0 ** (-5.0 - h) for h in range(4)]
NCST = 540
NPF = 89
COL = dict(rgx=0, rgz=512, rq=1024, rk=1536, rv=2048, rz=2560, gq=3072, gk=3584, gv=4096, gz=4608, gab=5120)


class Tile:
    def __init__(self, nc, name, shape, dtype, psum=False):
        if psum:
            self.h = nc.alloc_psum_tensor("T_" + name, list(shape), dtype)
        else:
            self.h = nc.alloc_sbuf_tensor("T_" + name, list(shape), dtype)
        self.ap = self.h.ap()
        self.name = name

    def __getitem__(self, k):
        return self.ap[k]


class Sched:
    def __init__(self, nc):
        self.nc = nc
        self.eng = {"pe": nc.tensor, "act": nc.scalar, "dve": nc.vector, "pool": nc.gpsimd, "sp": nc.sync}
        self.sem = {}
        self.cnt = {}
        self.pend = {}
        self.nsem = 0
        for e in self.eng:
            self._new_sem(e)
            self.pend[e] = False
        self.lastw = {}
        self.readers = {}
        self.waited = {e: {} for e in self.eng}
        self.dstream = {}
        self.nwaits = 0
        self.nops = 0
        self.rec = None

    def _new_sem(self, e):
        self.sem[e] = self.nc.alloc_semaphore(f"s_{e}_{self.nsem}")
        self.nsem += 1
        self.cnt[e] = 0

    def _deps(self, r, w):
        evs = []
        for k in r:
            if k in self.lastw:
                evs.append(self.lastw[k] + (True,))
        for k in w:
            if k in self.lastw:
                evs.append(self.lastw[k] + (False,))
            evs.extend(v + (False,) for v in self.readers.get(k, {}).values())
        return evs

    def _do_waits(self, e, evs):
        need = {}
        for sem, val, src, raw in evs:
            if src == e and not (SAME_SYNC or raw):
                continue
            if src.startswith("dma:"):
                val = self.dstream[src[4:]][1]
            if val > need.get(sem, (0, None))[0]:
                need[sem] = (val, src)
        for sem, (val, src) in need.items():
            if self.waited[e].get(sem, 0) >= val:
                continue
            if src == e and sem is self.sem[e] and val > self.cnt[e]:
                continue
            self.eng[e].wait_ge(sem, val)
            self.waited[e][sem] = val
            self.nwaits += 1

    def _register(self, ev, r, w):
        sem = ev[0]
        for k in r:
            self.readers.setdefault(k, {})[sem] = ev
        for k in w:
            self.lastw[k] = ev
            self.readers[k] = {}

    def op(self, e, fn, r=(), w=(), inc=True, cost=None):
        if self.rec is not None:
            self.rec.append(("op", e, fn, tuple(r), tuple(w), inc, cost if cost else DEFCOST[e]))
            return None
        self._do_waits(e, self._deps(r, w))
        if self.cnt[e] >= EPOCH and not self.pend[e]:
            self._new_sem(e)
        ins = fn()
        self.nops += 1
        if inc:
            self.cnt[e] += 1
            ins.then_inc(self.sem[e], 1)
            ev = (self.sem[e], self.cnt[e], e)
            self.pend[e] = False
        else:
            ev = (self.sem[e], self.cnt[e] + 1, e)
            self.pend[e] = True
        self._register(ev, r, w)
        return ins

    def dma(self, q, out, in_, r=(), w=(), stream="d", **kw):
        if self.rec is not None:
            self.rec.append(("dma", q, (out, in_, kw), tuple(r), tuple(w), True, 0.05))
            return None
        tl = [k for k in w if isinstance(k, Tile)]
        if not PER_TILE_SEMS:
            pass
        elif tl:
            stream = "ld_" + tl[0].name
        else:
            stream = "st_" + [k for k in r if isinstance(k, Tile)][0].name
        self._do_waits(q, self._deps(r, w))
        if stream not in self.dstream:
            self.dstream[stream] = [self.nc.alloc_semaphore(f"d_{stream}"), 0]
        st = self.dstream[stream]
        ins = self.eng[q].dma_start(out=out, in_=in_, **kw)
        st[1] += 16
        ins.then_inc(st[0], 16)
        ev = (st[0], st[1], "dma:" + stream)
        self._register(ev, r, w)
        return ins

    def finish(self, e="sp"):
        for name, (sem, tot) in self.dstream.items():
            if tot > 0:
                self.eng[e].wait_ge(sem, tot)


def v3(ap, c=4):
    return ap.rearrange("p (c n) -> p c n", c=c)


def build_nc():
    nc = bass.Bass("TRN2", target_bir_lowering=False)

    def din(name, shape):
        return nc.dram_tensor(name, list(shape), F32, kind="ExternalInput").ap()

    def dout(name, shape):
        return nc.dram_tensor(name, list(shape), F32, kind="ExternalOutput").ap()

    xp_d = din("xp", [NTOK, DM])
    xs_d = din("xs", [NS, DM])
    s_h = din("s_h", [NL, 128, 4, NS])
    s_rgc = din("s_rgc", [NL, 128, 4, 3, NS])
    s_gc = din("s_gc", [NL, 128, 12, 3, NS])
    s_ret = din("s_ret", [NL, NS, 4, 128, 128])
    s_gdn = din("s_gdn", [NL, NS, 4, 128, 128])
    w_in = din("w_in", [NL, DM, DIN])
    w_out = din("w_out", [NL, 1536, DM])
    pf_d = din("pf", [NL, 128, NPF])
    rgw_d = din("rgw", [NL, 128, 2 * 4 * 128])
    rows_d = din("rows", [NL, 1, 2056])
    cst_d = din("cst", [128, NCST])
    rope_d = din("ropet", [18, 128, 128])
    esel_d = din("esel", [1, 256])

    y_p = dout("y_p", [2048, DM])
    y_s = dout("y_s", [NS, DM])
    o_h_p = dout("o_h_p", [NL, 128, 4])
    o_rgc_p = dout("o_rgc_p", [NL, 128, 4, 3])
    o_ret_p = dout("o_ret_p", [NL, 4, 128, 128])
    o_gc_p = dout("o_gc_p", [NL, 128, 12, 3])
    o_gdn_p = dout("o_gdn_p", [NL, 4, 128, 128])
    o_h_s = dout("o_h_s", [NL, 128, 4, NS])
    o_rgc_s = dout("o_rgc_s", [NL, 128, 4, 3, NS])
    o_ret_s = dout("o_ret_s", [NL, NS, 4, 128, 128])
    o_gc_s = dout("o_gc_s", [NL, 128, 12, 3, NS])
    o_gdn_s = dout("o_gdn_s", [NL, NS, 4, 128, 128])
    xscr = nc.dram_tensor("xscr", [NTOK, DM], F32, kind="Internal").ap()
    xsscr = nc.dram_tensor("xsscr", [NS, DM], F32, kind="Internal").ap()

    S = Sched(nc)

    def TL(name, shape, dt=F32):
        return Tile(nc, name, shape, dt)

    Win = TL("Win", [128, 8, DIN], BF16)
    Wout = TL("Wout", [128, 12, DM], BF16)
    cst = TL("cst", [128, NCST])
    identb = TL("identb", [128, 128], BF16)
    esel = TL("esel", [128, 16, 16])
    pft = TL("pft", [128, NPF])
    rgwt = TL("rgwt", [128, 2, 4, 128], BF16)
    rowt = TL("rowt", [128, 2056])
    nc8sp = TL("nc8sp", [128, 4])
    negA = TL("negA", [128, 4])
    ropeb = TL("ropeb", [128, 128])
    X = TL("X", [128, 4, 131])
    GX = TL("GX", [128, 12, 131])
    h0s = TL("h0s", [128, 4, NS])
    Sret = TL("Sret", [128, 512])
    Sretb = TL("Sretb", [128, 512], BF16)
    Sgdn = TL("Sgdn", [128, 512])
    hprev = TL("hprev", [128, 4])
    mix = TL("mix", [128, 12, 128], BF16)
    xt = TL("xt", [128, DM])
    xT = TL("xT", [128, 8, 128], BF16)
    ztr = TL("ztr", [128, 512])
    zte = TL("zte", [128, 512])
    ztg = TL("ztg", [128, 512])
    xcb = TL("xcb", [128, 512], BF16)
    mixr, mixe, mixg = "mixr", "mixe", "mixg"
    Vb = TL("Vb", [128, 512])
    Kbg = TL("Kbg", [128, 512])
    kd = TL("kd", [128, 512])
    qgT = TL("qgT", [128, 512])
    attT = TL("attT", [128, 512])
    Y = TL("Y", [128, 512])
    gabt = TL("gabt", [128, 8])
    gt = TL("gt", [128, 4])
    betat = TL("betat", [128, 4])
    gct = TL("gct", [128, 4])
    egt = TL("egt", [128, 4])
    eglt = TL("eglt", [128, 4])
    eglast = TL("eglast", [128, 4])
    sm1 = TL("sm1", [128, 4])
    sm2 = TL("sm2", [128, 4])
    bst = TL("bst", [128, 4, 6])
    bmv = TL("bmv", [128, 4, 2])
    ebs = TL("ebs", [128, 4, NS])
    vb = TL("vb", [128, 512], BF16)
    k2b = TL("k2b", [128, 512], BF16)
    sm3 = TL("sm3", [128, 4])
    bst2 = TL("bst2", [128, 2, 6])
    bmv2 = TL("bmv2", [128, 2])
    RT = [TL(f"rt{i}", [128, 512]) for i in range(4)]
    ET = [TL(f"et{i}", [128, 512]) for i in range(NET)]
    GT = [TL(f"gt{i}", [128, 512]) for i in range(9)]
    B = [TL(f"bb{i}", [128, 1024], BF16) for i in range(3)]
    ps = [Tile(nc, f"ps{i}", [128, 512], F32, psum=True) for i in range(8)]
    print("sbuf bytes remaining", nc.sbuf_bytes_remaining)
    GDN_W = 2

    def fs(ap):
        n = 1
        for d in ap.shape[1:]:
            n *= d
        return n

    def A(out, in_, func, r, w, bias=0.0, scale=1.0):
        S.op("act", lambda: nc.scalar.activation(out=out, in_=in_, func=func, bias=bias, scale=scale), r, w,
             cost=0.22 + fs(out) / 1000.0)

    def TT(out, a, b, op, r, w):
        S.op("dve", lambda: nc.vector.tensor_tensor(out=out, in0=a, in1=b, op=op), r, w, cost=0.2 + fs(out) / 1000.0)

    def TS(out, a, s1, s2, op0, op1, r, w):
        if s2 is None:
            S.op("dve", lambda: nc.vector.tensor_scalar(out=out, in0=a, scalar1=s1, scalar2=None, op0=op0), r, w,
                 cost=0.2 + fs(out) / 1000.0)
        else:
            S.op("dve", lambda: nc.vector.tensor_scalar(out=out, in0=a, scalar1=s1, scalar2=s2, op0=op0, op1=op1), r, w,
                 cost=0.2 + fs(out) / 1000.0)

    def STT(out, a, s, b, op0, op1, r, w):
        S.op("dve", lambda: nc.vector.scalar_tensor_tensor(out=out, in0=a, scalar=s, in1=b, op0=op0, op1=op1), r, w,
             cost=0.2 + fs(out) / 1000.0)

    def CP(out, in_, r, w, e="dve"):
        if e == "dve":
            S.op("dve", lambda: nc.vector.tensor_copy(out=out, in_=in_), r, w, cost=0.2 + fs(out) / 1000.0)
        else:
            S.op("act", lambda: nc.scalar.activation(out=out, in_=in_, func=AF.Copy), r, w, cost=0.22 + fs(out) / 1000.0)

    def MS(out, val, w):
        S.op("dve", lambda: nc.vector.memset(out, val), (), w)

    def MM(out, lhsT, rhs, st, sp, r, w, inc=True):
        S.op("pe", lambda: nc.tensor.matmul(out, lhsT=lhsT, rhs=rhs, start=st, stop=sp), r, w, inc=inc,
             cost=(0.11 + fs(out) / 1200.0) * (2.0 if lhsT.dtype == F32 else 1.0))

    def TR(out, in_, ident, r, w, inc=True):
        S.op("pe", lambda: nc.tensor.transpose(out=out, in_=in_, identity=ident), r, w, inc=inc,
             cost=(0.11 + fs(out) / 1200.0) * (2.0 if in_.dtype == F32 else 1.0))

    def RSQ(out, in_, r, w, bias=EPS, scale=1.0):
        A(out, in_, AF.Sqrt, r, w, bias=bias, scale=scale)
        S.op("dve", lambda: nc.vector.reciprocal(out=out, in_=out), w, w)

    ident = cst[:, 0:128]
    maskT = cst[:, 128:256]
    strictL = cst[:, 256:384]
    ones = cst[:, 384:512]

    S.dma("sp", cst[:], cst_d, w=[cst], stream="c")
    S.dma("sp", esel[:].rearrange("p a b -> p (a b)"), esel_d.partition_broadcast(128), w=[esel], stream="c")
    CP(identb[:], ident, [cst], [identb], e="act")

    def bc(ap2, n, L):
        return ap2.unsqueeze(2).to_broadcast([L, 4, n])

    def layer_setup(l):
        for kc in range(8):
            S.dma("pool", Win[:, kc, :], w_in[l, kc * 128:(kc + 1) * 128, :], w=[Win], stream="w")
        for kc in range(12):
            S.dma("pool", Wout[:, kc, :], w_out[l, kc * 128:(kc + 1) * 128, :], w=[Wout], stream="w")
        S.dma("pool", rgwt[:].rearrange("p a c n -> p (a c n)"), rgw_d[l], w=[rgwt], stream="w")
        S.dma("sp", pft[:], pf_d[l], w=[pft], stream="c")
        S.dma("sp", rowt[:], rows_d[l].partition_broadcast(128), w=[rowt], stream="c")
        A(nc8sp[:], pft[:, 28:32], AF.Exp, [pft], [nc8sp], scale=-1.0)
        A(nc8sp[:], nc8sp[:], AF.Ln, [nc8sp], [nc8sp], bias=1.0)
        TS(nc8sp[:], nc8sp[:], -8.0, None, ALU.mult, None, [nc8sp], [nc8sp])
        A(negA[:], rowt[:, 2048:2052], AF.Exp, [rowt], [negA])
        TS(negA[:], negA[:], -1.0, None, ALU.mult, None, [negA], [negA])
        MS(Sret[:], 0.0, [Sret])
        MS(Sretb[:], 0.0, [Sretb])
        MS(Sgdn[:], 0.0, [Sgdn])
        MS(hprev[:], 0.0, [hprev])
        MS(X[:, :, 0:3], 0.0, [X])
        MS(GX[:, :, 0:3], 0.0, [GX])

    def block(l, mode, b):
        smp = mode == "s"
        if smp:
            L = NS
            t0 = 0
            src = xs_d if l == 0 else xsscr
            S.dma("sp", xt[:L, :], src, r=[("xsscr", 0)], w=[xt], stream="x")
        else:
            L = 16 if b == 0 else 128
            t0 = 0 if b == 0 else 16 + 128 * (b - 1)
            src = xp_d if l == 0 else xscr
            S.dma("sp", xt[:L, :], src[t0:t0 + L, :], r=[("xscr", b)], w=[xt], stream="x")
        xsrc = xt
        rb = 17 if smp else b
        S.dma("sp", ropeb[:L, :], rope_d[rb, 0:L, :], w=[ropeb], stream="x")
        mT = ident if smp else maskT
        ci = 528 if smp else 512
        qdec = cst[:L, ci:ci + 4]
        kdecp = cst[:L, ci + 4:ci + 8]
        if smp:
            k2dec = cst[:L, 536:540]
        elif L == 128:
            k2dec = cst[:L, 520:524]
        else:
            k2dec = cst[:L, 524:528]
        Xs = X.ap.rearrange("p c n -> p (c n)")[:, 0:4 * 4 * NS].rearrange("p (c j s) -> p c j s", c=4, j=4)
        GXs = GX.ap.rearrange("p c n -> p (c n)")[:, 0:12 * 4 * NS].rearrange("p (c j s) -> p c j s", c=12, j=4)

        xb = B[0]
        CP(xb[:L, :], xsrc[:L, :], [xsrc], [xb], e="act")
        pt = ps[0]
        ptb = pt.ap.bitcast(BF16)
        for kc in range(8):
            TR(ptb[:, kc * 128:kc * 128 + L], xb[:L, kc * 128:(kc + 1) * 128], identb[:L, :L], [xb, identb], [pt],
               inc=(kc == 7))
        CP(xT[:, :, :L], v3(ptb, 8)[:, :, :L], [pt], [xT], e="act")

        def fm_group(bank, c0, dst_ap, dst_key, e="act"):
            b3 = v3(bank.ap)
            for c in range(4):
                for kc in range(8):
                    MM(b3[:, c, :L], Win[:, kc, c0 + c * 128:c0 + (c + 1) * 128], xT[:, kc, :L], kc == 0, kc == 7,
                       [Win, xT], [bank], inc=(c == 3 and kc == 7))
            CP(dst_ap, b3[:, :, :L], [bank], [dst_key], e=e)

        def tm_group(bank, c0, n, dst_ap, dst_key, e="act"):
            for kc in range(8):
                MM(bank[:L, :n], xT[:, kc, :L], Win[:, kc, c0:c0 + n], kc == 0, kc == 7, [Win, xT], [bank],
                   inc=(kc == 7))
            CP(dst_ap, bank[:L, :n], [bank], [dst_key], e=e)

        def conv_chunk(src_tap, wcol0, c, o, dst_key, src_key, bias_col=None):
            w0 = pft[:, wcol0 + c * 4:wcol0 + c * 4 + 1]
            if bias_col is not None:
                TS(o, src_tap(c, 0), w0, pft[:, bias_col + c:bias_col + c + 1], ALU.mult, ALU.add,
                   [src_key, pft], [dst_key])
            else:
                TS(o, src_tap(c, 0), w0, None, ALU.mult, None, [src_key, pft], [dst_key])
            for j in range(1, 4):
                STT(o, src_tap(c, j), pft[:, wcol0 + c * 4 + j:wcol0 + c * 4 + j + 1], o, ALU.mult, ALU.add,
                    [src_key, pft, dst_key], [dst_key])

        def gen_rg():
            pa, pb = ps[0], ps[1]
            if smp:
                S.dma("sp", Xs[:, :, 0:3, :], s_rgc[l], w=[X], stream="st")
                S.dma("sp", h0s[:], s_h[l], w=[h0s], stream="st")
                fm_group(pa, COL["rgx"], Xs[:, :, 3, :], X)
                tap = lambda c, j: Xs[:, c, j, :]
            else:
                fm_group(pa, COL["rgx"], X[:, :, 3:3 + L], X)
                tap = lambda c, j: X[:, c, j:j + L]
            yield
            z3 = v3(ztr.ap)
            fm_group(pb, COL["rgz"], z3[:, :, :L], ztr)
            yield
            xc = RT[0]
            xc3 = v3(xc.ap)
            for c in range(4):
                conv_chunk(tap, 0, c, xc3[:, c, :L], xc, X, bias_col=16)
                yield
            xcb3 = v3(xcb.ap)
            CP(xcb3[:, :, :L], xc3[:, :, :L], [xc], [xcb], e="act")
            rt = RT[1]; it = RT[2]; at = RT[3]
            r3 = v3(rt.ap); i3 = v3(it.ap); a3 = v3(at.ap)
            for which, dst3, dkey, bcol, bank in ((0, r3, rt, 20, pa), (1, i3, it, 24, pb)):
                b3 = v3(bank.ap)
                for c in range(4):
                    MM(b3[:, c, :L], rgwt[:, which, c, :], xcb3[:, c, :L], True, True, [rgwt, xcb], [bank], inc=(c == 3))
                yield
                for c in range(4):
                    A(dst3[:, c, :L], b3[:, c, :L], AF.Sigmoid, [bank, pft], [dkey], bias=pft[:, bcol + c:bcol + c + 1])
                yield
            for c in range(4):
                A(a3[:, c, :L], r3[:, c, :L], AF.Exp, [rt, nc8sp], [at], scale=nc8sp[:, c:c + 1])
            yield
            mt = RT[1]
            m3 = v3(mt.ap)
            TT(m3[:, :, :L], a3[:, :, :L], a3[:, :, :L], ALU.mult, [at], [mt])
            A(m3[:, :, :L], m3[:, :, :L], AF.Sqrt, [mt], [mt], bias=1.0, scale=-1.0)
            yield
            TT(i3[:, :, :L], i3[:, :, :L], xc3[:, :, :L], ALU.mult, [it, xc], [it])
            yield
            TT(i3[:, :, :L], i3[:, :, :L], m3[:, :, :L], ALU.mult, [it, mt], [it])
            yield
            ht = RT[0]
            h3 = v3(ht.ap)
            if smp:
                TT(h3[:, :, :L], a3[:, :, :L], h0s[:], ALU.mult, [at, h0s], [ht])
                TT(h3[:, :, :L], h3[:, :, :L], i3[:, :, :L], ALU.add, [ht, it], [ht])
                S.dma("pool", o_h_s[l], h3[:, :, :L], r=[ht], stream="o")
                S.dma("pool", o_rgc_s[l], Xs[:, :, 1:4, :], r=[X], stream="o")
            else:
                for c in range(4):
                    S.op("dve", lambda c=c: nc.vector.tensor_tensor_scan(
                        out=h3[:, c, :L], data0=a3[:, c, :L], data1=i3[:, c, :L], initial=hprev[:, c:c + 1],
                        op0=ALU.mult, op1=ALU.add), [at, it, hprev], [ht])
                    yield
                CP(hprev[:].unsqueeze(2), h3[:, :, L - 1:L], [ht], [hprev])
                if b == NBLK - 1:
                    S.dma("pool", o_h_p[l], hprev[:], r=[hprev], stream="o")
                    S.dma("pool", o_rgc_p[l], X[:, :, L:L + 3], r=[X], stream="o")
                CP(X[:, :, 0:3], X[:, :, L:L + 3], [X], [X])
            yield
            A(z3[:, :, :L], z3[:, :, :L], AF.Silu, [ztr], [ztr])
            TT(mix[:, 0:4, :L], h3[:, :, :L], z3[:, :, :L], ALU.mult, [ht, ztr], [mixr])

        def gen_ret():
            bk = [ps[2], ps[3], ps[4]]
            rq = ET[0]; rk = ET[1]
            tm_group(bk[0], COL["rq"], 512, rq[:L, :], rq)
            yield
            tm_group(bk[1], COL["rk"], 512, rk[:L, :], rk)
            yield
            tm_group(bk[2], COL["rv"], 512, vb[:L, 0:512], vb)
            yield
            z3 = v3(zte.ap)
            fm_group(bk[0], COL["rz"], z3[:, :, :L], zte)
            yield
            cosb = ropeb[:L, 0:64].unsqueeze(1).to_broadcast([L, 4, 64])
            sinb = ropeb[:L, 64:128].unsqueeze(1).to_broadcast([L, 4, 64])

            def rope(src, dst, tmp):
                s3 = v3(src[:L, :]); d3 = v3(dst[:L, :]); t3 = v3(tmp[:L, :])
                t1 = s3[:, :, 0:64]; t2 = s3[:, :, 64:128]
                TT(d3[:, :, 0:64], t1, cosb, ALU.mult, [src, ropeb], [dst])
                TT(t3[:, :, 0:64], t2, sinb, ALU.mult, [src, ropeb], [tmp])
                yield
                TT(d3[:, :, 0:64], d3[:, :, 0:64], t3[:, :, 0:64], ALU.subtract, [dst, tmp], [dst])
                TT(d3[:, :, 64:128], t1, sinb, ALU.mult, [src, ropeb], [dst])
                yield
                TT(t3[:, :, 64:128], t2, cosb, ALU.mult, [src, ropeb], [tmp])
                TT(d3[:, :, 64:128], d3[:, :, 64:128], t3[:, :, 64:128], ALU.add, [dst, tmp], [dst])
                yield

            rqr = ET[2]; rkr = ET[4]
            yield from rope(rq, rqr, ET[3])
            yield from rope(rk, rkr, ET[3])
            qkb = B[0]
            TT(v3(qkb[:L, 0:512]), v3(rqr[:L, :]), bc(qdec, 128, L), ALU.mult, [rqr, cst], [qkb])
            TT(v3(qkb[:L, 512:1024]), v3(rkr[:L, :]), bc(kdecp, 128, L), ALU.mult, [rkr, cst], [qkb])
            yield
            if smp:
                k2 = ET[5]
                TT(v3(k2[:L, :]), v3(rkr[:L, :]), bc(k2dec, 128, L), ALU.mult, [rkr, cst], [k2])
            else:
                k2 = k2b
                TT(v3(k2[:L, 0:512]), v3(rkr[:L, :]), bc(k2dec, 128, L), ALU.mult, [rkr, cst], [k2])
            yield
            pt = bk[1]
            ptb = pt.ap.bitcast(BF16)
            for j in range(8):
                TR(ptb[:, j * 128:j * 128 + L], qkb[:L, j * 128:(j + 1) * 128], identb[:L, :L], [qkb, identb], [pt],
                   inc=(j == 7))
            qkT = B[1]
            qkT3 = v3(qkT.ap, 8)
            CP(qkT3[:, :, :L], v3(ptb, 8)[:, :, :L], [pt], [qkT], e="act")
            yield
            bank = bk[2]
            b3 = v3(bank.ap)
            for h in range(4):
                MM(b3[:L, h, :L], qkT3[:, 4 + h, :L], qkT3[:, h, :L], True, True, [qkT], [bank], inc=(h == 3))
            scb = B[2]
            sc3 = v3(scb[:, 0:512])
            TT(sc3[:L, :, :L], b3[:L, :, :L], mT[:L, :L].unsqueeze(1).to_broadcast([L, 4, L]), ALU.mult, [bank, cst], [scb])
            yield
            ob = bk[0]
            ob3 = v3(ob.ap)
            for h in range(4):
                MM(ob3[:L, h, :], sc3[:L, h, :L], vb[:L, h * 128:(h + 1) * 128], True, smp, [scb, vb], [ob],
                   inc=(smp and h == 3))
                if not smp:
                    MM(ob3[:L, h, :], qkT3[:, h, :L], v3(Sretb.ap)[:, h, :], False, True, [qkT, Sretb], [ob], inc=(h == 3))
            yield
            if not smp:
                sb = bk[1]
                sb3 = v3(sb.ap)
                for h in range(4):
                    MM(sb3[:, h, :], k2[:L, h * 128:(h + 1) * 128], vb[:L, h * 128:(h + 1) * 128], True, True, [k2, vb], [sb],
                       inc=(h == 3))
                yield
                for h in range(4):
                    STT(v3(Sret.ap)[:, h, :], v3(Sret.ap)[:, h, :], float(GAM[h] ** L), sb3[:, h, :], ALU.mult, ALU.add,
                        [Sret, sb], [Sret])
                yield
                CP(Sretb[:], Sret[:], [Sret], [Sretb], e="act")
                if b == NBLK - 1:
                    S.dma("pool", o_ret_p[l].rearrange("h d v -> d h v"), v3(Sret.ap), r=[Sret], stream="o")
                osrc, okey = ob3, ob
            else:
                oacc = ET[1]
                CP(oacc[:L, :], ob[:L, :], [ob], [oacc])
                for s in range(NS):
                    St = ET[2]
                    S.dma("sp", v3(St.ap), s_ret[l, s].rearrange("h d v -> d h v"), w=[St], stream="st")
                    qm = ET[3]
                    TT(v3(qm[:, 0:64], 4), qkT3[:, 0:4, :NS], esel[:, s, :].unsqueeze(1).to_broadcast([128, 4, NS]),
                       ALU.mult, [qkT, esel], [qm])
                    tb_ = bk[1]
                    tb3 = v3(tb_.ap)
                    for h in range(4):
                        MM(tb3[:NS, h, :], v3(qm[:, 0:64], 4)[:, h, :], v3(St.ap)[:, h, :], True, True, [qm, St], [tb_],
                           inc=(h == 3))
                    TT(oacc[:L, :], oacc[:L, :], tb_[:NS, :], ALU.add, [oacc, tb_], [oacc])
                    yield
                    vm = ET[4]
                    TT(vm[:NS, :], vb[:NS, 0:512], ident[:NS, s:s + 1].to_broadcast([NS, 512]), ALU.mult, [vb, cst], [vm])
                    sb = bk[2]
                    sb3 = v3(sb.ap)
                    for h in range(4):
                        MM(sb3[:, h, :], k2[:NS, h * 128:(h + 1) * 128], vm[:NS, h * 128:(h + 1) * 128], True, True,
                           [k2, vm], [sb], inc=(h == 3))
                    So = ET[0]
                    for h in range(4):
                        STT(v3(So.ap)[:, h, :], v3(St.ap)[:, h, :], float(GAM[h]), sb3[:, h, :], ALU.mult, ALU.add,
                            [St, sb], [So])
                    S.dma("pool", o_ret_s[l, s].rearrange("h d v -> d h v"), v3(So.ap), r=[So], stream="o")
                    yield
                osrc, okey = v3(oacc.ap), oacc
            for h in range(4):
                S.op("dve", lambda h=h: nc.vector.bn_stats(out=bst[:L, h, :], in_=osrc[:L, h, :]), [okey], [bst])
            yield
            for h in range(4):
                S.op("dve", lambda h=h: nc.vector.bn_aggr(out=bmv[:L, h, :], in_=bst[:L, h, :]), [bst], [bmv])
            RSQ(sm1[:L, :], bmv[:L, :, 1], [bmv], [sm1])
            yield
            onb = B[2]
            for h in range(4):
                TS(onb[:L, 512 + h * 128:512 + (h + 1) * 128], osrc[:L, h, :], bmv[:L, h, 0:1], sm1[:L, h:h + 1],
                   ALU.subtract, ALU.mult, [okey, bmv, sm1], [onb])
            yield
            pt = bk[1]
            ptb = pt.ap.bitcast(BF16)
            for h in range(4):
                TR(ptb[:, h * 128:h * 128 + L], onb[:L, 512 + h * 128:512 + (h + 1) * 128], identb[:L, :L], [onb, identb],
                   [pt], inc=(h == 3))
            yt = ET[0]
            y3 = v3(yt.ap)
            for h in range(4):
                A(y3[:, h, :L], ptb[:, h * 128:h * 128 + L], AF.Identity, [pt, pft], [yt], bias=pft[:, 36 + h:37 + h],
                  scale=pft[:, 32 + h:33 + h])
            yield
            A(z3[:, :, :L], z3[:, :, :L], AF.Silu, [zte], [zte])
            TT(mix[:, 4:8, :L], y3[:, :, :L], z3[:, :, :L], ALU.mult, [yt, zte], [mixe])

        def gen_gdn():
            bk = [ps[5], ps[6], ps[7]]
            nb = [0]

            def PB():
                nb[0] = (nb[0] + 1) % 3
                return bk[nb[0]]

            if smp:
                S.dma("sp", GXs[:, :, 0:3, :], s_gc[l], w=[GX], stream="st")
                for g in range(3):
                    fm_group(PB(), COL["gq"] + 512 * g, GXs[:, 4 * g:4 * g + 4, 3, :], GX)
                    yield
                gtap = lambda c, j: GXs[:, c, j, :]
            else:
                for g in range(3):
                    fm_group(PB(), COL["gq"] + 512 * g, GX[:, 4 * g:4 * g + 4, 3:3 + L], GX)
                    yield
                gtap = lambda c, j: GX[:, c, j:j + L]
            tm_group(PB(), COL["gab"], 8, gabt[:L, :], gabt)
            z3 = v3(ztg.ap)
            fm_group(PB(), COL["gz"], z3[:, :, :L], ztg)
            yield
            TT(gt[:L, :], gabt[:L, 0:4], rowt[:L, 2052:2056], ALU.add, [gabt, rowt], [gt])
            A(gt[:L, :], gt[:L, :], AF.Exp, [gt], [gt])
            A(gt[:L, :], gt[:L, :], AF.Ln, [gt], [gt], bias=1.0)
            TT(gt[:L, :], gt[:L, :], negA[:L, :], ALU.mult, [gt, negA], [gt])
            A(betat[:L, :], gabt[:L, 4:8], AF.Sigmoid, [gabt], [betat])
            yield
            bank = PB()
            MM(bank[:L, 0:4], mT[:L, :L], gt[:L, :], True, True, [cst, gt], [bank])
            CP(gct[:L, :], bank[:L, 0:4], [bank], [gct])
            A(egt[:L, :], gct[:L, :], AF.Exp, [gct], [egt])
            yield
            Rt = GT[5]
            R3 = v3(Rt.ap)
            for h in range(4):
                TS(R3[:L, h, :L], mT[:L, :L], gt[:L, h:h + 1], None, ALU.mult, None, [cst, gt], [Rt])
            yield
            gcB = PB()
            g3 = v3(gcB.ap)
            for h in range(4):
                MM(g3[:, h, :L], ones[:L, :], R3[:L, h, :L], True, True, [cst, Rt], [gcB], inc=(h == 3))
            EB = GT[6]
            EB3 = v3(EB.ap)
            A(EB3[:, :, :L], g3[:, :, :L], AF.Exp, [gcB], [EB])
            yield
            dT = GT[7]
            dT3 = v3(dT.ap)
            for h in range(4):
                TS(dT3[:L, h, :L], g3[:L, h, :L], gct[:L, h:h + 1], 0.0, ALU.subtract, ALU.min, [gcB, gct, EB], [dT])
            yield
            if not smp:
                dl = GT[8]
                dl3 = v3(dl.ap)
                for h in range(4):
                    TS(dl3[:L, h, :L], g3[:L, h, :L], gct[:L, h:h + 1], 0.0, ALU.subtract, ALU.max, [gcB, gct], [dl])
                yield
                CP(eglast[:].unsqueeze(2), EB3[:, :, L - 1:L], [EB], [eglast])
                TT(eglt[:L, :].unsqueeze(2), g3[:L, :, L - 1:L], gct[:L, :].unsqueeze(2), ALU.subtract, [gcB, gct], [eglt])
                A(eglt[:L, :], eglt[:L, :], AF.Exp, [eglt], [eglt])
                A(dl3[:L, :, :L], dl3[:L, :, :L], AF.Exp, [dl], [dl], scale=-1.0)
                TT(dl3[:L, :, :L], dl3[:L, :, :L], strictL[:L, :L].unsqueeze(1).to_broadcast([L, 4, L]), ALU.mult,
                   [dl, cst], [dl])
                yield
            else:
                CP(ebs[:], EB3[:, :, :NS], [EB], [ebs])
            A(dT3[:L, :, :L], dT3[:L, :, :L], AF.Exp, [dT], [dT])
            TT(dT3[:L, :, :L], dT3[:L, :, :L], mT[:L, :L].unsqueeze(1).to_broadcast([L, 4, L]), ALU.mult, [dT, cst], [dT])
            yield
            cq = GT[0]; ck = GT[1]; cv = GT[2]
            cqk = [cq, ck, cv]
            for g in range(3):
                cg3 = v3(cqk[g].ap)
                for c in range(4):
                    if g == 2 and POOL_CONV:
                        tp3 = v3(GT[3].ap)
                        o = cg3[:, c, :L]
                        wc = 40 + 16 * g + c * 4
                        S.op("pool", lambda o=o, c=c, wc=wc: nc.gpsimd.tensor_scalar(
                            out=o, in0=gtap(8 + c, 0), scalar1=pft[:, wc:wc + 1], scalar2=None, op0=ALU.mult),
                            [GX, pft], [cv], cost=0.5)
                        for j in range(1, 4):
                            S.op("pool", lambda c=c, j=j, wc=wc: nc.gpsimd.tensor_scalar(
                                out=tp3[:, c, :L], in0=gtap(8 + c, j), scalar1=pft[:, wc + j:wc + j + 1], scalar2=None,
                                op0=ALU.mult), [GX, pft], [GT[3]], cost=0.5)
                            S.op("pool", lambda o=o, c=c: nc.gpsimd.tensor_tensor(
                                out=o, in0=o, in1=tp3[:, c, :L], op=ALU.add), [cv, GT[3]], [cv], cost=0.5)
                    else:
                        conv_chunk(lambda c_, j, g=g: gtap(4 * g + c_, j), 40 + 16 * g, c, cg3[:, c, :L], cqk[g], GX)
                    yield
                A(cg3[:, :, :L], cg3[:, :, :L], AF.Silu, [cqk[g]], [cqk[g]])
            if smp:
                S.dma("pool", o_gc_s[l], GXs[:, :, 1:4, :], r=[GX], stream="o")
            else:
                if b == NBLK - 1:
                    S.dma("pool", o_gc_p[l], GX[:, :, L:L + 3], r=[GX], stream="o")
                CP(GX[:, :, 0:3], GX[:, :, L:L + 3], [GX], [GX])
            yield
            for g in range(2):
                cg3 = v3(cqk[g].ap)
                sq = GT[3]
                sq3 = v3(sq.ap)
                TT(sq3[:, :, :L], cg3[:, :, :L], cg3[:, :, :L], ALU.mult, [cqk[g]], [sq])
                bank = PB()
                b3 = v3(bank.ap)
                for h in range(4):
                    MM(b3[:, h, :L], ones, sq3[:, h, :L], True, True, [cst, sq], [bank], inc=(h == 3))
                yield
                rn = GT[4]
                rn3 = v3(rn.ap)
                RSQ(rn3[:, :, :L], b3[:, :, :L], [bank], [rn])
                if g == 0:
                    STT(cg3[:, :, :L], cg3[:, :, :L], float(128 ** -0.5), rn3[:, :, :L], ALU.mult, ALU.mult, [cq, rn], [cq])
                else:
                    TT(cg3[:, :, :L], cg3[:, :, :L], rn3[:, :, :L], ALU.mult, [ck, rn], [ck])
                yield
            q3 = v3(cq.ap); k3 = v3(ck.ap); cv3 = v3(cv.ap)
            TT(v3(qgT.ap)[:, :, :L], q3[:, :, :L], EB3[:, :, :L], ALU.mult, [cq, EB], [qgT])
            bank = PB()
            b3 = v3(bank.ap)
            for h in range(4):
                MM(b3[:L, h, :L], k3[:, h, :L], q3[:, h, :L], True, True, [ck, cq], [bank], inc=(h == 3))
            at3 = v3(attT.ap)
            TT(at3[:L, :, :L], b3[:L, :, :L], dT3[:L, :, :L], ALU.mult, [bank, dT], [attT])
            yield
            kTM = GT[3]; vTM = GT[4]
            for srcT, s3_, dstT in ((ck, k3, kTM), (cv, cv3, vTM)):
                bank = PB()
                for h in range(4):
                    TR(bank[:L, h * 128:(h + 1) * 128], s3_[:, h, :L], ident, [srcT, cst], [bank], inc=(h == 3))
                CP(dstT[:L, :], bank[:L, :], [bank], [dstT], e="act")
                yield
            if not smp:
                TT(v3(kd[:L, :]), v3(kTM[:L, :]), bc(eglt[:L, :], 128, L), ALU.mult, [kTM, eglt], [kd])
            else:
                CP(kd[:L, :], kTM[:L, :], [kTM], [kd])
            TT(v3(Vb[:L, :]), v3(vTM[:L, :]), bc(betat[:L, :], 128, L), ALU.mult, [vTM, betat], [Vb])
            yield
            TT(sm2[:L, :], betat[:L, :], egt[:L, :], ALU.mult, [betat, egt], [sm2])
            TT(v3(Kbg[:L, :]), v3(kTM[:L, :]), bc(sm2[:L, :], 128, L), ALU.mult, [kTM, sm2], [Kbg])
            yield
            Y3 = v3(Y.ap)
            if smp:
                CP(Y3[:L, :, :L], ident[:L, :L].unsqueeze(1).to_broadcast([L, 4, L]), [cst], [Y])
            else:
                bank = PB()
                b3 = v3(bank.ap)
                for h in range(4):
                    MM(b3[:L, h, :L], k3[:, h, :L], k3[:, h, :L], True, True, [ck], [bank], inc=(h == 3))
                P = GT[2]
                P3 = v3(P.ap)
                for h in range(4):
                    STT(P3[:L, h, :L], b3[:L, h, :L], betat[:L, h:h + 1], dl3[:L, h, :L], ALU.mult, ALU.mult,
                        [bank, betat, dl], [P])
                yield
                bank = PB()
                b3 = v3(bank.ap)
                for h in range(4):
                    TR(b3[:L, h, :L], P3[:L, h, :L], ident[:L, :L], [P, cst], [bank], inc=(h == 3))
                Q = GT[5]
                Q3 = v3(Q.ap)
                CP(Q3[:L, :, :L], b3[:L, :, :L], [bank], [Q], e="act")
                STT(Y3[:L, :, :L], Q3[:L, :, :L], -1.0, ident[:L, :L].unsqueeze(1).to_broadcast([L, 4, L]), ALU.mult, ALU.add,
                    [Q, cst], [Y])
                yield
                nlev = 6 if L == 128 else 3
                for lev in range(nlev + 1):
                    last = lev == nlev
                    if not last:
                        bq = PB(); bp = PB()
                        bq3 = v3(bq.ap); bp3 = v3(bp.ap)
                    if lev > 0:
                        by = PB()
                        by3 = v3(by.ap)
                    for h in range(4):
                        if not last:
                            MM(bq3[:L, h, :L], P3[:L, h, :L], Q3[:L, h, :L], True, True, [P, Q], [bq], inc=False)
                        if lev > 0:
                            MM(by3[:L, h, :L], P3[:L, h, :L], Y3[:L, h, :L], True, True, [P, Y], [by],
                               inc=(last and h == 3))
                        if not last:
                            MM(bp3[:L, h, :L], Q3[:L, h, :L], P3[:L, h, :L], True, True, [P, Q], [bp], inc=(h == 3))
                    yield
                    if lev > 0:
                        TT(Y3[:L, :, :L], Y3[:L, :, :L], by3[:L, :, :L], ALU.add, [Y, by], [Y])
                    if not last:
                        Pn, Qn = (GT[3], GT[4]) if lev % 2 == 0 else (GT[2], GT[5])
                        CP(v3(Qn.ap)[:L, :, :L], bq3[:L, :, :L], [bq], [Qn], e="act")
                        CP(v3(Pn.ap)[:L, :, :L], bp3[:L, :, :L], [bp], [Pn], e="dve")
                        P, Q = Pn, Qn
                        P3, Q3 = v3(P.ap), v3(Q.ap)
                    yield
            bank = PB()
            b3 = v3(bank.ap)
            for h in range(4):
                MM(b3[:, h, :L], Kbg[:L, h * 128:(h + 1) * 128], Y3[:L, h, :L], True, True, [Kbg, Y], [bank], inc=(h == 3))
            nWT = GT[6]
            nW3 = v3(nWT.ap)
            A(nW3[:, :, :L], b3[:, :, :L], AF.Copy, [bank], [nWT], scale=-1.0)
            yield
            Sg3 = v3(Sgdn.ap)
            vnb = PB()
            vn3 = v3(vnb.ap)
            for h in range(4):
                MM(vn3[:L, h, :], Y3[:L, h, :L], Vb[:L, h * 128:(h + 1) * 128], True, smp, [Y, Vb], [vnb],
                   inc=(smp and h == 3))
                if not smp:
                    MM(vn3[:L, h, :], nW3[:, h, :L], Sg3[:, h, :], False, True, [nWT, Sgdn], [vnb], inc=(h == 3))
            vnew = GT[7]
            if not smp:
                CP(vnew[:L, :], vnb[:L, :], [vnb], [vnew], e="act")
                yield
            else:
                wacc = GT[0]; qacc = GT[1]
                CP(wacc[:L, :], vnb[:L, :], [vnb], [wacc], e="act")
                MS(qacc[:L, :], 0.0, [qacc])
                for s in range(NS):
                    St = GT[2]
                    S.dma("sp", v3(St.ap), s_gdn[l, s].rearrange("h d v -> d h v"), w=[St], stream="st")
                    for srcT, s3_, acc in ((qgT, v3(qgT.ap), qacc), (nWT, nW3, wacc)):
                        qm = GT[3]
                        TT(v3(qm[:, 0:64], 4), s3_[:, :, :NS], esel[:, s, :].unsqueeze(1).to_broadcast([128, 4, NS]),
                           ALU.mult, [srcT, esel], [qm])
                        tb_ = PB()
                        tb3 = v3(tb_.ap)
                        for h in range(4):
                            MM(tb3[:NS, h, :], v3(qm[:, 0:64], 4)[:, h, :], v3(St.ap)[:, h, :], True, True, [qm, St], [tb_],
                               inc=(h == 3))
                        TT(acc[:L, :], acc[:L, :], tb_[:NS, :], ALU.add, [acc, tb_], [acc])
                        yield
                CP(vnew[:L, :], wacc[:L, :], [wacc], [vnew])
            ob = PB()
            ob3 = v3(ob.ap)
            for h in range(4):
                MM(ob3[:L, h, :], at3[:L, h, :L], vnew[:L, h * 128:(h + 1) * 128], True, smp, [attT, vnew], [ob],
                   inc=(smp and h == 3))
                if not smp:
                    MM(ob3[:L, h, :], v3(qgT.ap)[:, h, :L], Sg3[:, h, :], False, True, [qgT, Sgdn], [ob], inc=(h == 3))
            yield
            if smp:
                TT(qacc[:L, :], qacc[:L, :], ob[:L, :], ALU.add, [qacc, ob], [qacc])
                osrc, okey = v3(qacc.ap), qacc
                for s in range(NS):
                    St = GT[2]
                    S.dma("sp", v3(St.ap), s_gdn[l, s].rearrange("h d v -> d h v"), w=[St], stream="st")
                    vm = GT[3]
                    TS(vm[:NS, :], vnew[:NS, :], ident[:NS, s:s + 1], None, ALU.mult, None, [vnew, cst], [vm])
                    sb = PB()
                    sb3 = v3(sb.ap)
                    for h in range(4):
                        MM(sb3[:, h, :], kd[:NS, h * 128:(h + 1) * 128], vm[:NS, h * 128:(h + 1) * 128], True, True,
                           [kd, vm], [sb], inc=(h == 3))
                    So = GT[4]
                    for h in range(4):
                        STT(v3(So.ap)[:, h, :], v3(St.ap)[:, h, :], ebs[:, h, s:s + 1], sb3[:, h, :], ALU.mult, ALU.add,
                            [St, sb, ebs], [So])
                    S.dma("pool", o_gdn_s[l, s].rearrange("h d v -> d h v"), v3(So.ap), r=[So], stream="o")
                    yield
            else:
                osrc, okey = ob3, ob
                sb = PB()
                sb3 = v3(sb.ap)
                for h in range(4):
                    MM(sb3[:, h, :], kd[:L, h * 128:(h + 1) * 128], vnew[:L, h * 128:(h + 1) * 128], True, True, [kd, vnew], [sb],
                       inc=(h == 3))
                yield
                for h in range(4):
                    STT(Sg3[:, h, :], Sg3[:, h, :], eglast[:, h:h + 1], sb3[:, h, :], ALU.mult, ALU.add,
                        [Sgdn, sb, eglast], [Sgdn])
                if b == NBLK - 1:
                    S.dma("pool", o_gdn_p[l].rearrange("h d v -> d h v"), Sg3, r=[Sgdn], stream="o")
                yield
            osq = GT[8]
            A(v3(osq.ap)[:L, :, :], osrc[:L, :, :], AF.Square, [okey], [osq])
            S.op("dve", lambda: nc.vector.reduce_sum(out=sm3[:L, :], in_=v3(osq.ap)[:L, :, :], axis=AX.X), [osq], [sm3])
            RSQ(sm3[:L, :], sm3[:L, :], [sm3], [sm3], bias=EPS, scale=1.0 / 128.0)
            yield
            on = GT[5]
            for h in range(4):
                TS(on[:L, h * 128:(h + 1) * 128], osrc[:L, h, :], sm3[:L, h:h + 1], None, ALU.mult, None, [okey, sm3], [on])
            yield
            bank = PB()
            b3 = v3(bank.ap)
            for h in range(4):
                TR(b3[:, h, :L], on[:L, h * 128:(h + 1) * 128], ident[:L, :L], [on, cst], [bank], inc=(h == 3))
            A(z3[:, :, :L], z3[:, :, :L], AF.Silu, [ztg], [ztg])
            STT(mix[:, 8:12, :L], b3[:, :, :L], pft[:, 88:89], z3[:, :, :L], ALU.mult, ALU.mult, [bank, pft, ztg], [mixg])

        if MERGE == "sim":
            branches = []
            for gen in (gen_gdn, gen_ret, gen_rg):
                S.rec = []
                for _ in gen():
                    pass
                ops, S.rec = S.rec, None
                units, cur = [], []
                for o in ops:
                    cur.append(o)
                    if o[5]:
                        units.append(cur)
                        cur = []
                assert not cur
                branches.append(units)
            clock = {}
            wr = {}
            rd = {}
            ptr = [0] * len(branches)
            HOP = 0.3
            while True:
                best = None
                for bi, units in enumerate(branches):
                    if ptr[bi] >= len(units):
                        continue
                    u = units[ptr[bi]]
                    e = u[0][1]
                    t = clock.get(e, 0.0)
                    for o in u:
                        for k in o[3]:
                            if k in wr:
                                t = max(t, wr[k][0] + (HOP if wr[k][1] != e else 0.0))
                        for k in o[4]:
                            if k in wr:
                                t = max(t, wr[k][0] + (HOP if wr[k][1] != e else 0.0))
                            if k in rd:
                                t = max(t, rd[k][0] + (HOP if rd[k][1] != e else 0.0))
                    if best is None or t < best[0] - 1e-9:
                        best = (t, bi)
                if best is None:
                    break
                t, bi = best
                u = branches[bi][ptr[bi]]
                ptr[bi] += 1
                e = u[0][1]
                for o in u:
                    kind, eng, fn, r_, w_, inc, cost = o
                    if kind == "dma":
                        out_, in_, kw = fn
                        S.dma(eng, out_, in_, r=r_, w=w_, **kw)
                        done = t + 2.5
                        t += cost
                        who = "dma"
                    else:
                        S.op(eng, fn, r_, w_, inc=inc)
                        t += cost
                        done = t
                        who = eng
                    for k in r_:
                        if k not in rd or rd[k][0] < done:
                            rd[k] = (done, who)
                    for k in w_:
                        wr[k] = (done, who)
                        rd.pop(k, None)
                clock[e] = t
            gens = []
        else:
            gens = [(gen_rg(), 1), (gen_ret(), 1), (gen_gdn(), GDN_W)]
        if MERGE == "seq":
            for g, _ in gens:
                for _ in g:
                    pass
            gens = []
        while gens:
            for item in list(gens):
                g, wgt = item
                for _ in range(wgt):
                    try:
                        next(g)
                    except StopIteration:
                        gens.remove(item)
                        break

        z = [RT[0], RT[1]]
        for n in range(2):
            bank = ps[n]
            for kc in range(12):
                MM(bank[:L, :], mix[:, kc, :L], Wout[:, kc, n * 512:(n + 1) * 512], kc == 0, kc == 11,
                   [mixr, mixe, mixg, Wout], [bank], inc=(kc == 11))
            STT(z[n][:L, :], xsrc[:L, n * 512:(n + 1) * 512], float(ALPHA), bank[:L, :], ALU.mult, ALU.add, [xsrc, bank],
                [z[n]])
            S.op("dve", lambda n=n: nc.vector.bn_stats(out=bst2[:L, n, :], in_=z[n][:L, :]), [z[n]], [bst2])
        S.op("dve", lambda: nc.vector.bn_aggr(out=bmv2[:L, :], in_=bst2[:L, 0:2, :]), [bst2], [bmv2])
        RSQ(sm2[:L, 0:1], bmv2[:L, 1:2], [bmv2], [sm2])
        for n in range(2):
            sl = slice(n * 512, (n + 1) * 512)
            TS(z[n][:L, :], z[n][:L, :], bmv2[:L, 0:1], sm2[:L, 0:1], ALU.subtract, ALU.mult, [z[n], bmv2, sm2], [z[n]])
            TT(z[n][:L, :], z[n][:L, :], rowt[:L, sl], ALU.mult, [z[n], rowt], [z[n]])
            TT(z[n][:L, :], z[n][:L, :], rowt[:L, 1024 + n * 512:1024 + (n + 1) * 512], ALU.add, [z[n], rowt], [z[n]])
            if smp:
                if l == NL - 1:
                    S.dma("pool", y_s[:, sl], z[n][:NS, :], r=[z[n]], stream="o")
                else:
                    S.dma("pool", xsscr[:, sl], z[n][:NS, :], r=[z[n]], w=[("xsscr", 0)], stream="xo")
            else:
                if l == NL - 1:
                    if b > 0:
                        S.dma("pool", y_p[t0 - 16:t0 - 16 + L, sl], z[n][:L, :], r=[z[n]], stream="o")
                else:
                    S.dma("pool", xscr[t0:t0 + L, sl], z[n][:L, :], r=[z[n]], w=[("xscr", b)], stream="xo")


    for l in range(NL):
        layer_setup(l)
        for b in range(NBLK):
            block(l, "p", b)
        if not SKIP_SAMPLE:
            block(l, "s", 0)
    S.finish("sp")
    print("ops", S.nops, "waits", S.nwaits, "sems", S.nsem + len(S.dstream))
    return nc


def _consts():
    cst = np.zeros((128, NCST), np.float32)
    i = np.arange(128)
    cst[:, 0:128] = np.eye(128)
    cst[:, 128:256] = (i[None, :] >= i[:, None])
    cst[:, 256:384] = (i[None, :] < i[:, None])
    cst[:, 384:512] = 1.0
    sc = 128.0 ** -0.5
    for h in range(4):
        g = np.float64(GAM[h])
        cst[:, 512 + h] = g ** (i + 1.0)
        cst[:, 516 + h] = g ** (-(i + 1.0)) * sc
        cst[:, 520 + h] = g ** (127.0 - i) * sc
        cst[:16, 524 + h] = g ** (15.0 - i[:16]) * sc
        cst[:, 528 + h] = g
        cst[:, 532 + h] = sc / g
        cst[:, 536 + h] = sc
    half = 64
    inv = (np.float32(10000.0) ** (-np.arange(half, dtype=np.float32) / np.float32(half))).astype(np.float32)
    rope = np.zeros((18, 128, 128), np.float32)
    for b in range(18):
        if b == 0:
            pos = np.arange(16, dtype=np.float32)
        elif b < 17:
            pos = 16 + 128 * (b - 1) + np.arange(128, dtype=np.float32)
        else:
            pos = np.full(16, 16384.0, np.float32)
        ang = (pos[:, None].astype(np.float32) * inv[None, :]).astype(np.float32)
        rope[b, :len(pos), 0:64] = np.cos(ang.astype(np.float64))
        rope[b, :len(pos), 64:128] = np.sin(ang.astype(np.float64))
    esel = np.eye(16, dtype=np.float32).reshape(1, 256)
    return cst, rope, esel


_NC_CACHE = {}


def kernel(x_prompt, x_sample, state_rglru_h, state_rglru_conv, state_ret, state_gdn_conv, state_gdn,
           meta_tokens, w_in, rg_conv_w, rg_conv_b, rg_w_a, rg_b_a, rg_w_x, rg_b_x, rg_lambda,
           ret_gn_w, ret_gn_b, gdn_conv_w, gdn_a_log, gdn_dt_bias, gdn_norm_w, w_out, ln_w, ln_b):
    f = lambda a: np.ascontiguousarray(np.asarray(a, dtype=np.float32))
    x_prompt, x_sample, meta_tokens = f(x_prompt), f(x_sample), f(meta_tokens)
    w_in, w_out = f(w_in), f(w_out)
    pf = np.zeros((NL, 128, NPF), np.float32)

    def fm(v, nch):
        return f(v).reshape(NL, nch, 128).transpose(0, 2, 1)

    pf[:, :, 0:16] = f(rg_conv_w).reshape(NL, 4, 4, 128).transpose(0, 3, 2, 1).reshape(NL, 128, 16)
    pf[:, :, 16:20] = fm(rg_conv_b, 4)
    pf[:, :, 20:24] = fm(rg_b_a, 4)
    pf[:, :, 24:28] = fm(rg_b_x, 4)
    pf[:, :, 28:32] = fm(rg_lambda, 4)
    pf[:, :, 32:36] = fm(ret_gn_w, 4)
    pf[:, :, 36:40] = fm(ret_gn_b, 4)
    pf[:, :, 40:88] = f(gdn_conv_w).reshape(NL, 4, 12, 128).transpose(0, 3, 2, 1).reshape(NL, 128, 48)
    pf[:, :, 88] = f(gdn_norm_w)
    rgw = np.zeros((NL, 128, 2, 4, 128), np.float32)
    for which, wsrc in ((0, f(rg_w_a)), (1, f(rg_w_x))):
        for n in range(8):
            c, o = n // 2, (n % 2) * 64
            rgw[:, o:o + 64, which, c, o:o + 64] = wsrc[:, n]
    rgw = rgw.reshape(NL, 128, 1024)
    rows = np.concatenate([f(ln_w), f(ln_b), f(gdn_a_log), f(gdn_dt_bias)], axis=1).reshape(NL, 1, 2056)
    cst, rope, esel = _consts()
    if "nc" not in _NC_CACHE:
        _NC_CACHE["nc"] = build_nc()
    nc = _NC_CACHE["nc"]
    in_maps = []
    for c in range(8):
        sl = slice(NS * c, NS * (c + 1))
        m = {
            "xp": np.ascontiguousarray(np.concatenate([meta_tokens, x_prompt[c]], axis=0)),
            "xs": np.ascontiguousarray(x_sample[sl, 0, :]),
            "s_h": np.ascontiguousarray(f(state_rglru_h)[:, sl].reshape(NL, NS, 4, 128).transpose(0, 3, 2, 1)),
            "s_rgc": np.ascontiguousarray(f(state_rglru_conv)[:, sl].reshape(NL, NS, 3, 4, 128).transpose(0, 4, 3, 2, 1)),
            "s_gc": np.ascontiguousarray(f(state_gdn_conv)[:, sl].reshape(NL, NS, 3, 12, 128).transpose(0, 4, 3, 2, 1)),
            "s_ret": np.ascontiguousarray(f(state_ret)[:, sl]),
            "s_gdn": np.ascontiguousarray(f(state_gdn)[:, sl]),
            "w_in": w_in, "w_out": w_out, "pf": pf, "rgw": rgw, "rows": rows,
            "cst": cst, "ropet": rope, "esel": esel,
        }
        in_maps.append(m)
    res = run_bass_kernel_spmd(nc, in_maps, core_ids=list(range(8)))
    R = res.results
    g = lambda k: [np.asarray(R[c][k], dtype=np.float32) for c in range(8)]
    y_prompt = np.stack(g("y_p"), 0)
    y_sample = np.concatenate(g("y_s"), 0)[:, None, :]
    hp = np.stack([a.transpose(0, 2, 1).reshape(NL, 512) for a in g("o_h_p")], 1)
    rgcp = np.stack([a.transpose(0, 3, 2, 1).reshape(NL, 3, 512) for a in g("o_rgc_p")], 1)
    retp = np.stack(g("o_ret_p"), 1)
    gcp = np.stack([a.transpose(0, 3, 2, 1).reshape(NL, 3, 1536) for a in g("o_gc_p")], 1)
    gdnp = np.stack(g("o_gdn_p"), 1)
    hs = np.concatenate([a.transpose(0, 3, 2, 1).reshape(NL, NS, 512) for a in g("o_h_s")], 1)
    rgcs = np.concatenate([a.transpose(0, 4, 3, 2, 1).reshape(NL, NS, 3, 512) for a in g("o_rgc_s")], 1)
    rets = np.concatenate(g("o_ret_s"), 1)
    gcs = np.concatenate([a.transpose(0, 4, 3, 2, 1).reshape(NL, NS, 3, 1536) for a in g("o_gc_s")], 1)
    gdns = np.concatenate(g("o_gdn_s"), 1)
    c = np.ascontiguousarray
    return (c(y_prompt), c(y_sample), c(hp), c(rgcp), c(retp), c(gcp), c(gdnp), c(hs), c(rgcs), c(rets), c(gcs), c(gdns))
```

```python
import numpy as np
import concourse.bass as bass
import concourse.mybir as mybir
from concourse.bass_utils import run_bass_kernel_spmd

F32 = mybir.dt.float32
BF16 = mybir.dt.bfloat16
ALU = mybir.AluOpType
AF = mybir.ActivationFunctionType
AX = mybir.AxisListType

EPOCH = 12000
DEFCOST = {"pe": 0.2, "act": 0.4, "dve": 0.3, "pool": 0.05, "sp": 0.05}
GDN_STOP = 100000
ENABLE = [True, True, True]
NET = 6
SKIP_SAMPLE = False
PER_TILE_SEMS = True
MERGE = "sim"
SAME_SYNC = True

NL = 4
DM = 1024
DIN = 5128
NTOK = 2064
NBLK = 17
NS = 16
ALPHA = 8.0 ** 0.25
EPS = 1e-6
GAM = [1.0 - 2.0 ** (-5.0 - h) for h in range(4)]
NCST = 540
NPF = 89
COL = dict(rgx=0, rgz=512, rq=1024, rk=1536, rv=2048, rz=2560, gq=3072, gk=3584, gv=4096, gz=4608, gab=5120)


class Tile:
    def __init__(self, nc, name, shape, dtype, psum=False):
        if psum:
            self.h = nc.alloc_psum_tensor("T_" + name, list(shape), dtype)
        else:
            self.h = nc.alloc_sbuf_tensor("T_" + name, list(shape), dtype)
        self.ap = self.h.ap()
        self.name = name

    def __getitem__(self, k):
        return self.ap[k]


class Sched:
    def __init__(self, nc):
        self.nc = nc
        self.eng = {"pe": nc.tensor, "act": nc.scalar, "dve": nc.vector, "pool": nc.gpsimd, "sp": nc.sync}
        self.sem = {}
        self.cnt = {}
        self.pend = {}
        self.nsem = 0
        for e in self.eng:
            self._new_sem(e)
            self.pend[e] = False
        self.lastw = {}
        self.readers = {}
        self.waited = {e: {} for e in self.eng}
        self.dstream = {}
        self.nwaits = 0
        self.nops = 0
        self.rec = None

    def _new_sem(self, e):
        self.sem[e] = self.nc.alloc_semaphore(f"s_{e}_{self.nsem}")
        self.nsem += 1
        self.cnt[e] = 0

    def _deps(self, r, w):
        evs = []
        for k in r:
            if k in self.lastw:
                evs.append(self.lastw[k] + (True,))
        for k in w:
            if k in self.lastw:
                evs.append(self.lastw[k] + (False,))
            evs.extend(v + (False,) for v in self.readers.get(k, {}).values())
        return evs

    def _do_waits(self, e, evs):
        need = {}
        for sem, val, src, raw in evs:
            if src == e and not (SAME_SYNC or raw):
                continue
            if src.startswith("dma:"):
                val = self.dstream[src[4:]][1]
            if val > need.get(sem, (0, None))[0]:
                need[sem] = (val, src)
        for sem, (val, src) in need.items():
            if self.waited[e].get(sem, 0) >= val:
                continue
            if src == e and sem is self.sem[e] and val > self.cnt[e]:
                continue
            self.eng[e].wait_ge(sem, val)
            self.waited[e][sem] = val
            self.nwaits += 1

    def _register(self, ev, r, w):
        sem = ev[0]
        for k in r:
            self.readers.setdefault(k, {})[sem] = ev
        for k in w:
            self.lastw[k] = ev
            self.readers[k] = {}

    def op(self, e, fn, r=(), w=(), inc=True, cost=None):
        if self.rec is not None:
            self.rec.append(("op", e, fn, tuple(r), tuple(w), inc, cost if cost else DEFCOST[e]))
            return None
        self._do_waits(e, self._deps(r, w))
        if self.cnt[e] >= EPOCH and not self.pend[e]:
            self._new_sem(e)
        ins = fn()
        self.nops += 1
        if inc:
            self.cnt[e] += 1
            ins.then_inc(self.sem[e], 1)
            ev = (self.sem[e], self.cnt[e], e)
            self.pend[e] = False
        else:
            ev = (self.sem[e], self.cnt[e] + 1, e)
            self.pend[e] = True
        self._register(ev, r, w)
        return ins

    def dma(self, q, out, in_, r=(), w=(), stream="d", **kw):
        if self.rec is not None:
            self.rec.append(("dma", q, (out, in_, kw), tuple(r), tuple(w), True, 0.05))
            return None
        tl = [k for k in w if isinstance(k, Tile)]
        if not PER_TILE_SEMS:
            pass
        elif tl:
            stream = "ld_" + tl[0].name
        else:
            stream = "st_" + [k for k in r if isinstance(k, Tile)][0].name
        self._do_waits(q, self._deps(r, w))
        if stream not in self.dstream:
            self.dstream[stream] = [self.nc.alloc_semaphore(f"d_{stream}"), 0]
        st = self.dstream[stream]
        ins = self.eng[q].dma_start(out=out, in_=in_, **kw)
        st[1] += 16
        ins.then_inc(st[0], 16)
        ev = (st[0], st[1], "dma:" + stream)
        self._register(ev, r, w)
        return ins

    def finish(self, e="sp"):
        for name, (sem, tot) in self.dstream.items():
            if tot > 0:
                self.eng[e].wait_ge(sem, tot)


def v3(ap, c=4):
    return ap.rearrange("p (c n) -> p c n", c=c)


def build_nc():
    nc = bass.Bass("TRN2", target_bir_lowering=False)

    def din(name, shape):
        return nc.dram_tensor(name, list(shape), F32, kind="ExternalInput").ap()

    def dout(name, shape):
        return nc.dram_tensor(name, list(shape), F32, kind="ExternalOutput").ap()

    xp_d = din("xp", [NTOK, DM])
    xs_d = din("xs", [NS, DM])
    s_h = din("s_h", [NL, 128, 4, NS])
    s_rgc = din("s_rgc", [NL, 128, 4, 3, NS])
    s_gc = din("s_gc", [NL, 128, 12, 3, NS])
    s_ret = din("s_ret", [NL, NS, 4, 128, 128])
    s_gdn = din("s_gdn", [NL, NS, 4, 128, 128])
    w_in = din("w_in", [NL, DM, DIN])
    w_out = din("w_out", [NL, 1536, DM])
    pf_d = din("pf", [NL, 128, NPF])
    rgw_d = din("rgw", [NL, 128, 2 * 4 * 128])
    rows_d = din("rows", [NL, 1, 2056])
    cst_d = din("cst", [128, NCST])
    rope_d = din("ropet", [18, 128, 128])
    esel_d = din("esel", [1, 256])

    y_p = dout("y_p", [2048, DM])
    y_s = dout("y_s", [NS, DM])
    o_h_p = dout("o_h_p", [NL, 128, 4])
    o_rgc_p = dout("o_rgc_p", [NL, 128, 4, 3])
    o_ret_p = dout("o_ret_p", [NL, 4, 128, 128])
    o_gc_p = dout("o_gc_p", [NL, 128, 12, 3])
    o_gdn_p = dout("o_gdn_p", [NL, 4, 128, 128])
    o_h_s = dout("o_h_s", [NL, 128, 4, NS])
    o_rgc_s = dout("o_rgc_s", [NL, 128, 4, 3, NS])
    o_ret_s = dout("o_ret_s", [NL, NS, 4, 128, 128])
    o_gc_s = dout("o_gc_s", [NL, 128, 12, 3, NS])
    o_gdn_s = dout("o_gdn_s", [NL, NS, 4, 128, 128])
    xscr = nc.dram_tensor("xscr", [NTOK, DM], F32, kind="Internal").ap()
    xsscr = nc.dram_tensor("xsscr", [NS, DM], F32, kind="Internal").ap()

    S = Sched(nc)

    def TL(name, shape, dt=F32):
        return Tile(nc, name, shape, dt)

    Win = TL("Win", [128, 8, DIN], BF16)
    Wout = TL("Wout", [128, 12, DM], BF16)
    cst = TL("cst", [128, NCST])
    identb = TL("identb", [128, 128], BF16)
    esel = TL("esel", [128, 16, 16])
    pft = TL("pft", [128, NPF])
    rgwt = TL("rgwt", [128, 2, 4, 128], BF16)
    rowt = TL("rowt", [128, 2056])
    nc8sp = TL("nc8sp", [128, 4])
    negA = TL("negA", [128, 4])
    ropeb = TL("ropeb", [128, 128])
    X = TL("X", [128, 4, 131])
    GX = TL("GX", [128, 12, 131])
    h0s = TL("h0s", [128, 4, NS])
    Sret = TL("Sret", [128, 512])
    Sretb = TL("Sretb", [128, 512], BF16)
    Sgdn = TL("Sgdn", [128, 512])
    hprev = TL("hprev", [128, 4])
    mix = TL("mix", [128, 12, 128], BF16)
    xt = TL("xt", [128, DM])
    xT = TL("xT", [128, 8, 128], BF16)
    ztr = TL("ztr", [128, 512])
    zte = TL("zte", [128, 512])
    ztg = TL("ztg", [128, 512])
    xcb = TL("xcb", [128, 512], BF16)
    mixr, mixe, mixg = "mixr", "mixe", "mixg"
    Vb = TL("Vb", [128, 512])
    Kbg = TL("Kbg", [128, 512])
    kd = TL("kd", [128, 512])
    qgT = TL("qgT", [128, 512])
    attT = TL("attT", [128, 512])
    Y = TL("Y", [128, 512])
    gabt = TL("gabt", [128, 8])
    gt = TL("gt", [128, 4])
    betat = TL("betat", [128, 4])
    gct = TL("gct", [128, 4])
    egt = TL("egt", [128, 4])
    eglt = TL("eglt", [128, 4])
    eglast = TL("eglast", [128, 4])
    sm1 = TL("sm1", [128, 4])
    sm2 = TL("sm2", [128, 4])
    bst = TL("bst", [128, 4, 6])
    bmv = TL("bmv", [128, 4, 2])
    ebs = TL("ebs", [128, 4, NS])
    vb = TL("vb", [128, 512], BF16)
    k2b = TL("k2b", [128, 512], BF16)
    sm3 = TL("sm3", [128, 4])
    bst2 = TL("bst2", [128, 2, 6])
    bmv2 = TL("bmv2", [128, 2])
    RT = [TL(f"rt{i}", [128, 512]) for i in range(4)]
    ET = [TL(f"et{i}", [128, 512]) for i in range(NET)]
    GT = [TL(f"gt{i}", [128, 512]) for i in range(9)]
    B = [TL(f"bb{i}", [128, 1024], BF16) for i in range(3)]
    ps = [Tile(nc, f"ps{i}", [128, 512], F32, psum=True) for i in range(8)]
    print("sbuf bytes remaining", nc.sbuf_bytes_remaining)
    GDN_W = 2

    def fs(ap):
        n = 1
        for d in ap.shape[1:]:
            n *= d
        return n

    def A(out, in_, func, r, w, bias=0.0, scale=1.0):
        S.op("act", lambda: nc.scalar.activation(out=out, in_=in_, func=func, bias=bias, scale=scale), r, w,
             cost=0.22 + fs(out) / 1000.0)

    def TT(out, a, b, op, r, w):
        S.op("dve", lambda: nc.vector.tensor_tensor(out=out, in0=a, in1=b, op=op), r, w, cost=0.2 + fs(out) / 1000.0)

    def TS(out, a, s1, s2, op0, op1, r, w):
        if s2 is None:
            S.op("dve", lambda: nc.vector.tensor_scalar(out=out, in0=a, scalar1=s1, scalar2=None, op0=op0), r, w,
                 cost=0.2 + fs(out) / 1000.0)
        else:
            S.op("dve", lambda: nc.vector.tensor_scalar(out=out, in0=a, scalar1=s1, scalar2=s2, op0=op0, op1=op1), r, w,
                 cost=0.2 + fs(out) / 1000.0)

    def STT(out, a, s, b, op0, op1, r, w):
        S.op("dve", lambda: nc.vector.scalar_tensor_tensor(out=out, in0=a, scalar=s, in1=b, op0=op0, op1=op1), r, w,
             cost=0.2 + fs(out) / 1000.0)

    def CP(out, in_, r, w, e="dve"):
        if e == "dve":
            S.op("dve", lambda: nc.vector.tensor_copy(out=out, in_=in_), r, w, cost=0.2 + fs(out) / 1000.0)
        else:
            S.op("act", lambda: nc.scalar.activation(out=out, in_=in_, func=AF.Copy), r, w, cost=0.22 + fs(out) / 1000.0)

    def MS(out, val, w):
        S.op("dve", lambda: nc.vector.memset(out, val), (), w)

    def MM(out, lhsT, rhs, st, sp, r, w, inc=True):
        S.op("pe", lambda: nc.tensor.matmul(out, lhsT=lhsT, rhs=rhs, start=st, stop=sp), r, w, inc=inc,
             cost=(0.11 + fs(out) / 1200.0) * (2.0 if lhsT.dtype == F32 else 1.0))

    def TR(out, in_, ident, r, w, inc=True):
        S.op("pe", lambda: nc.tensor.transpose(out=out, in_=in_, identity=ident), r, w, inc=inc,
             cost=(0.11 + fs(out) / 1200.0) * (2.0 if in_.dtype == F32 else 1.0))

    def RSQ(out, in_, r, w, bias=EPS, scale=1.0):
        A(out, in_, AF.Sqrt, r, w, bias=bias, scale=scale)
        S.op("dve", lambda: nc.vector.reciprocal(out=out, in_=out), w, w)

    ident = cst[:, 0:128]
    maskT = cst[:, 128:256]
    strictL = cst[:, 256:384]
    ones = cst[:, 384:512]

    S.dma("sp", cst[:], cst_d, w=[cst], stream="c")
    S.dma("sp", esel[:].rearrange("p a b -> p (a b)"), esel_d.partition_broadcast(128), w=[esel], stream="c")
    CP(identb[:], ident, [cst], [identb], e="act")

    def bc(ap2, n, L):
        return ap2.unsqueeze(2).to_broadcast([L, 4, n])

    def load_win(l):
        for kc in range(8):
            S.dma("pool", Win[:, kc, :], w_in[l, kc * 128:(kc + 1) * 128, :], w=[Win], stream="w")

    def layer_setup(l):
        if l == 0:
            load_win(0)
        for kc in range(12):
            S.dma("pool", Wout[:, kc, :], w_out[l, kc * 128:(kc + 1) * 128, :], w=[Wout], stream="w")
        S.dma("pool", rgwt[:].rearrange("p a c n -> p (a c n)"), rgw_d[l], w=[rgwt], stream="w")
        S.dma("sp", pft[:], pf_d[l], w=[pft], stream="c")
        S.dma("sp", rowt[:], rows_d[l].partition_broadcast(128), w=[rowt], stream="c")
        A(nc8sp[:], pft[:, 28:32], AF.Exp, [pft], [nc8sp], scale=-1.0)
        A(nc8sp[:], nc8sp[:], AF.Ln, [nc8sp], [nc8sp], bias=1.0)
        TS(nc8sp[:], nc8sp[:], -8.0, None, ALU.mult, None, [nc8sp], [nc8sp])
        A(negA[:], rowt[:, 2048:2052], AF.Exp, [rowt], [negA])
        TS(negA[:], negA[:], -1.0, None, ALU.mult, None, [negA], [negA])
        MS(Sret[:], 0.0, [Sret])
        MS(Sretb[:], 0.0, [Sretb])
        MS(Sgdn[:], 0.0, [Sgdn])
        MS(hprev[:], 0.0, [hprev])
        MS(X[:, :, 0:3], 0.0, [X])
        MS(GX[:, :, 0:3], 0.0, [GX])

    def block(l, mode, b):
        smp = mode == "s"
        if smp:
            L = NS
            t0 = 0
            src = xs_d if l == 0 else xsscr
            S.dma("sp", xt[:L, :], src, r=[("xsscr", 0)], w=[xt], stream="x")
        else:
            L = 16 if b == 0 else 128
            t0 = 0 if b == 0 else 16 + 128 * (b - 1)
            src = xp_d if l == 0 else xscr
            S.dma("sp", xt[:L, :], src[t0:t0 + L, :], r=[("xscr", b)], w=[xt], stream="x")
        xsrc = xt
        rb = 17 if smp else b
        S.dma("sp", ropeb[:L, :], rope_d[rb, 0:L, :], w=[ropeb], stream="x")
        mT = ident if smp else maskT
        ci = 528 if smp else 512
        qdec = cst[:L, ci:ci + 4]
        kdecp = cst[:L, ci + 4:ci + 8]
        if smp:
            k2dec = cst[:L, 536:540]
        elif L == 128:
            k2dec = cst[:L, 520:524]
        else:
            k2dec = cst[:L, 524:528]
        Xs = X.ap.rearrange("p c n -> p (c n)")[:, 0:4 * 4 * NS].rearrange("p (c j s) -> p c j s", c=4, j=4)
        GXs = GX.ap.rearrange("p c n -> p (c n)")[:, 0:12 * 4 * NS].rearrange("p (c j s) -> p c j s", c=12, j=4)

        xb = B[0]
        CP(xb[:L, :], xsrc[:L, :], [xsrc], [xb], e="act")
        pt = ps[0]
        ptb = pt.ap.bitcast(BF16)
        for kc in range(8):
            TR(ptb[:, kc * 128:kc * 128 + L], xb[:L, kc * 128:(kc + 1) * 128], identb[:L, :L], [xb, identb], [pt],
               inc=(kc == 7))
        CP(xT[:, :, :L], v3(ptb, 8)[:, :, :L], [pt], [xT])

        def fm_group(bank, c0, dst_ap, dst_key, e="act"):
            b3 = v3(bank.ap)
            for c in range(4):
                for kc in range(8):
                    MM(b3[:, c, :L], Win[:, kc, c0 + c * 128:c0 + (c + 1) * 128], xT[:, kc, :L], kc == 0, kc == 7,
                       [Win, xT], [bank], inc=(c == 3 and kc == 7))
            CP(dst_ap, b3[:, :, :L], [bank], [dst_key], e=e)

        def tm_group(bank, c0, n, dst_ap, dst_key, e="dve"):
            for kc in range(8):
                MM(bank[:L, :n], xT[:, kc, :L], Win[:, kc, c0:c0 + n], kc == 0, kc == 7, [Win, xT], [bank],
                   inc=(kc == 7))
            CP(dst_ap, bank[:L, :n], [bank], [dst_key], e=e)

        def conv_chunk(src_tap, wcol0, c, o, dst_key, src_key, bias_col=None):
            w0 = pft[:, wcol0 + c * 4:wcol0 + c * 4 + 1]
            if bias_col is not None:
                TS(o, src_tap(c, 0), w0, pft[:, bias_col + c:bias_col + c + 1], ALU.mult, ALU.add,
                   [src_key, pft], [dst_key])
            else:
                TS(o, src_tap(c, 0), w0, None, ALU.mult, None, [src_key, pft], [dst_key])
            for j in range(1, 4):
                STT(o, src_tap(c, j), pft[:, wcol0 + c * 4 + j:wcol0 + c * 4 + j + 1], o, ALU.mult, ALU.add,
                    [src_key, pft, dst_key], [dst_key])

        def gen_rg():
            pa, pb = ps[0], ps[1]
            if smp:
                S.dma("sp", Xs[:, :, 0:3, :], s_rgc[l], w=[X], stream="st")
                S.dma("sp", h0s[:], s_h[l], w=[h0s], stream="st")
                fm_group(pa, COL["rgx"], Xs[:, :, 3, :], X)
                tap = lambda c, j: Xs[:, c, j, :]
            else:
                fm_group(pa, COL["rgx"], X[:, :, 3:3 + L], X)
                tap = lambda c, j: X[:, c, j:j + L]
            yield
            z3 = v3(ztr.ap)
            fm_group(pb, COL["rgz"], z3[:, :, :L], ztr, e="dve")
            yield "pre"
            xc = RT[0]
            xc3 = v3(xc.ap)
            for c in range(4):
                conv_chunk(tap, 0, c, xc3[:, c, :L], xc, X, bias_col=16)
                yield
            xcb3 = v3(xcb.ap)
            CP(xcb3[:, :, :L], xc3[:, :, :L], [xc], [xcb], e="act")
            rt = RT[1]; it = RT[2]; at = RT[3]
            r3 = v3(rt.ap); i3 = v3(it.ap); a3 = v3(at.ap)
            for which, dst3, dkey, bcol, bank in ((0, r3, rt, 20, pa), (1, i3, it, 24, pb)):
                b3 = v3(bank.ap)
                for c in range(4):
                    MM(b3[:, c, :L], rgwt[:, which, c, :], xcb3[:, c, :L], True, True, [rgwt, xcb], [bank], inc=(c == 3))
                yield
                for c in range(4):
                    A(dst3[:, c, :L], b3[:, c, :L], AF.Sigmoid, [bank, pft], [dkey], bias=pft[:, bcol + c:bcol + c + 1])
                yield
            for c in range(4):
                A(a3[:, c, :L], r3[:, c, :L], AF.Exp, [rt, nc8sp], [at], scale=nc8sp[:, c:c + 1])
            yield
            mt = RT[1]
            m3 = v3(mt.ap)
            TT(m3[:, :, :L], a3[:, :, :L], a3[:, :, :L], ALU.mult, [at], [mt])
            A(m3[:, :, :L], m3[:, :, :L], AF.Sqrt, [mt], [mt], bias=1.0, scale=-1.0)
            yield
            TT(i3[:, :, :L], i3[:, :, :L], xc3[:, :, :L], ALU.mult, [it, xc], [it])
            yield
            TT(i3[:, :, :L], i3[:, :, :L], m3[:, :, :L], ALU.mult, [it, mt], [it])
            yield
            ht = RT[0]
            h3 = v3(ht.ap)
            if smp:
                TT(h3[:, :, :L], a3[:, :, :L], h0s[:], ALU.mult, [at, h0s], [ht])
                TT(h3[:, :, :L], h3[:, :, :L], i3[:, :, :L], ALU.add, [ht, it], [ht])
                S.dma("pool", o_h_s[l], h3[:, :, :L], r=[ht], stream="o")
                S.dma("pool", o_rgc_s[l], Xs[:, :, 1:4, :], r=[X], stream="o")
            else:
                for c in range(4):
                    S.op("dve", lambda c=c: nc.vector.tensor_tensor_scan(
                        out=h3[:, c, :L], data0=a3[:, c, :L], data1=i3[:, c, :L], initial=hprev[:, c:c + 1],
                        op0=ALU.mult, op1=ALU.add), [at, it, hprev], [ht])
                    yield
                CP(hprev[:].unsqueeze(2), h3[:, :, L - 1:L], [ht], [hprev])
                if b == NBLK - 1:
                    S.dma("pool", o_h_p[l], hprev[:], r=[hprev], stream="o")
                    S.dma("pool", o_rgc_p[l], X[:, :, L:L + 3], r=[X], stream="o")
                CP(X[:, :, 0:3], X[:, :, L:L + 3], [X], [X])
            yield
            A(z3[:, :, :L], z3[:, :, :L], AF.Silu, [ztr], [ztr])
            TT(mix[:, 0:4, :L], h3[:, :, :L], z3[:, :, :L], ALU.mult, [ht, ztr], [mixr])

        def gen_ret():
            bk = [ps[2], ps[3], ps[4]]
            rq = ET[0]; rk = ET[1]
            tm_group(bk[0], COL["rq"], 512, rq[:L, :], rq)
            yield
            tm_group(bk[1], COL["rk"], 512, rk[:L, :], rk)
            yield
            tm_group(bk[2], COL["rv"], 512, vb[:L, 0:512], vb)
            yield
            z3 = v3(zte.ap)
            fm_group(bk[0], COL["rz"], z3[:, :, :L], zte, e="dve")
            yield "pre"
            cosb = ropeb[:L, 0:64].unsqueeze(1).to_broadcast([L, 4, 64])
            sinb = ropeb[:L, 64:128].unsqueeze(1).to_broadcast([L, 4, 64])

            def rope(src, dst, tmp):
                s3 = v3(src[:L, :]); d3 = v3(dst[:L, :]); t3 = v3(tmp[:L, :])
                t1 = s3[:, :, 0:64]; t2 = s3[:, :, 64:128]
                TT(d3[:, :, 0:64], t1, cosb, ALU.mult, [src, ropeb], [dst])
                TT(t3[:, :, 0:64], t2, sinb, ALU.mult, [src, ropeb], [tmp])
                yield
                TT(d3[:, :, 0:64], d3[:, :, 0:64], t3[:, :, 0:64], ALU.subtract, [dst, tmp], [dst])
                TT(d3[:, :, 64:128], t1, sinb, ALU.mult, [src, ropeb], [dst])
                yield
                TT(t3[:, :, 64:128], t2, cosb, ALU.mult, [src, ropeb], [tmp])
                TT(d3[:, :, 64:128], d3[:, :, 64:128], t3[:, :, 64:128], ALU.add, [dst, tmp], [dst])
                yield

            rqr = ET[2]; rkr = ET[4]
            yield from rope(rq, rqr, ET[3])
            yield from rope(rk, rkr, ET[3])
            qkb = B[0]
            TT(v3(qkb[:L, 0:512]), v3(rqr[:L, :]), bc(qdec, 128, L), ALU.mult, [rqr, cst], [qkb])
            TT(v3(qkb[:L, 512:1024]), v3(rkr[:L, :]), bc(kdecp, 128, L), ALU.mult, [rkr, cst], [qkb])
            yield
            if smp:
                k2 = ET[5]
                TT(v3(k2[:L, :]), v3(rkr[:L, :]), bc(k2dec, 128, L), ALU.mult, [rkr, cst], [k2])
            else:
                k2 = k2b
                TT(v3(k2[:L, 0:512]), v3(rkr[:L, :]), bc(k2dec, 128, L), ALU.mult, [rkr, cst], [k2])
            yield
            pt = bk[1]
            ptb = pt.ap.bitcast(BF16)
            for j in range(8):
                TR(ptb[:, j * 128:j * 128 + L], qkb[:L, j * 128:(j + 1) * 128], identb[:L, :L], [qkb, identb], [pt],
                   inc=(j == 7))
            qkT = B[1]
            qkT3 = v3(qkT.ap, 8)
            CP(qkT3[:, :, :L], v3(ptb, 8)[:, :, :L], [pt], [qkT], e="act")
            yield
            bank = bk[2]
            b3 = v3(bank.ap)
            for h in range(4):
                MM(b3[:L, h, :L], qkT3[:, 4 + h, :L], qkT3[:, h, :L], True, True, [qkT], [bank], inc=(h == 3))
            scb = B[2]
            sc3 = v3(scb[:, 0:512])
            TT(sc3[:L, :, :L], b3[:L, :, :L], mT[:L, :L].unsqueeze(1).to_broadcast([L, 4, L]), ALU.mult, [bank, cst], [scb])
            yield
            ob = bk[0]
            ob3 = v3(ob.ap)
            for h in range(4):
                MM(ob3[:L, h, :], sc3[:L, h, :L], vb[:L, h * 128:(h + 1) * 128], True, smp, [scb, vb], [ob],
                   inc=(smp and h == 3))
                if not smp:
                    MM(ob3[:L, h, :], qkT3[:, h, :L], v3(Sretb.ap)[:, h, :], False, True, [qkT, Sretb], [ob], inc=(h == 3))
            yield
            if not smp:
                sb = bk[1]
                sb3 = v3(sb.ap)
                for h in range(4):
                    MM(sb3[:, h, :], k2[:L, h * 128:(h + 1) * 128], vb[:L, h * 128:(h + 1) * 128], True, True, [k2, vb], [sb],
                       inc=(h == 3))
                yield
                for h in range(4):
                    STT(v3(Sret.ap)[:, h, :], v3(Sret.ap)[:, h, :], float(GAM[h] ** L), sb3[:, h, :], ALU.mult, ALU.add,
                        [Sret, sb], [Sret])
                yield
                CP(Sretb[:], Sret[:], [Sret], [Sretb], e="act")
                if b == NBLK - 1:
                    S.dma("pool", o_ret_p[l].rearrange("h d v -> d h v"), v3(Sret.ap), r=[Sret], stream="o")
                osrc, okey = ob3, ob
            else:
                oacc = ET[1]
                CP(oacc[:L, :], ob[:L, :], [ob], [oacc])
                xTf = xT.ap.rearrange("p a b -> p (a b)").bitcast(F32)
                for s in range(NS):
                    St = (ET[2], Sret)[s % 2]
                    S.dma("sp", v3(St.ap), s_ret[l, s].rearrange("h d v -> d h v"), w=[St], stream="st")
                    qm = ET[3]
                    TT(v3(qm[:, 0:64], 4), qkT3[:, 0:4, :NS], esel[:, s, :].unsqueeze(1).to_broadcast([128, 4, NS]),
                       ALU.mult, [qkT, esel], [qm])
                    tb_ = bk[1]
                    tb3 = v3(tb_.ap)
                    for h in range(4):
                        MM(tb3[:NS, h, :], v3(qm[:, 0:64], 4)[:, h, :], v3(St.ap)[:, h, :], True, True, [qm, St], [tb_],
                           inc=(h == 3))
                    TT(oacc[:L, :], oacc[:L, :], tb_[:NS, :], ALU.add, [oacc, tb_], [oacc])
                    yield
                    vm = ET[4]
                    TT(vm[:NS, :], vb[:NS, 0:512], ident[:NS, s:s + 1].to_broadcast([NS, 512]), ALU.mult, [vb, cst], [vm])
                    sb = bk[2]
                    sb3 = v3(sb.ap)
                    for h in range(4):
                        MM(sb3[:, h, :], k2[:NS, h * 128:(h + 1) * 128], vm[:NS, h * 128:(h + 1) * 128], True, True,
                           [k2, vm], [sb], inc=(h == 3))
                    So, So3 = ((ET[0], v3(ET[0].ap)), (xT, v3(xTf)))[s % 2]
                    for h in range(4):
                        STT(So3[:, h, :], v3(St.ap)[:, h, :], float(GAM[h]), sb3[:, h, :], ALU.mult, ALU.add,
                            [St, sb], [So])
                    S.dma("pool", o_ret_s[l, s].rearrange("h d v -> d h v"), So3, r=[So], stream="o")
                    yield
                osrc, okey = v3(oacc.ap), oacc
            for h in range(4):
                S.op("dve", lambda h=h: nc.vector.bn_stats(out=bst[:L, h, :], in_=osrc[:L, h, :]), [okey], [bst])
            yield
            for h in range(4):
                S.op("dve", lambda h=h: nc.vector.bn_aggr(out=bmv[:L, h, :], in_=bst[:L, h, :]), [bst], [bmv])
            RSQ(sm1[:L, :], bmv[:L, :, 1], [bmv], [sm1])
            yield
            onb = B[2]
            for h in range(4):
                TS(onb[:L, 512 + h * 128:512 + (h + 1) * 128], osrc[:L, h, :], bmv[:L, h, 0:1], sm1[:L, h:h + 1],
                   ALU.subtract, ALU.mult, [okey, bmv, sm1], [onb])
            yield
            pt = bk[1]
            ptb = pt.ap.bitcast(BF16)
            for h in range(4):
                TR(ptb[:, h * 128:h * 128 + L], onb[:L, 512 + h * 128:512 + (h + 1) * 128], identb[:L, :L], [onb, identb],
                   [pt], inc=(h == 3))
            yt = ET[0]
            y3 = v3(yt.ap)
            for h in range(4):
                A(y3[:, h, :L], ptb[:, h * 128:h * 128 + L], AF.Identity, [pt, pft], [yt], bias=pft[:, 36 + h:37 + h],
                  scale=pft[:, 32 + h:33 + h])
            yield
            A(z3[:, :, :L], z3[:, :, :L], AF.Silu, [zte], [zte])
            TT(mix[:, 4:8, :L], y3[:, :, :L], z3[:, :, :L], ALU.mult, [yt, zte], [mixe])

        def gen_gdn():
            bk = [ps[5], ps[6], ps[7]]
            nb = [0]

            def PB():
                nb[0] = (nb[0] + 1) % 3
                return bk[nb[0]]

            if smp:
                S.dma("sp", GXs[:, :, 0:3, :], s_gc[l], w=[GX], stream="st")
                for g in range(3):
                    fm_group(PB(), COL["gq"] + 512 * g, GXs[:, 4 * g:4 * g + 4, 3, :], GX)
                    yield
                gtap = lambda c, j: GXs[:, c, j, :]
            else:
                for g in range(3):
                    fm_group(PB(), COL["gq"] + 512 * g, GX[:, 4 * g:4 * g + 4, 3:3 + L], GX)
                    yield
                gtap = lambda c, j: GX[:, c, j:j + L]
            tm_group(PB(), COL["gab"], 8, gabt[:L, :], gabt)
            z3 = v3(ztg.ap)
            fm_group(PB(), COL["gz"], z3[:, :, :L], ztg, e="dve")
            yield "pre"
            TT(gt[:L, :], gabt[:L, 0:4], rowt[:L, 2052:2056], ALU.add, [gabt, rowt], [gt])
            A(gt[:L, :], gt[:L, :], AF.Exp, [gt], [gt])
            A(gt[:L, :], gt[:L, :], AF.Ln, [gt], [gt], bias=1.0)
            TT(gt[:L, :], gt[:L, :], negA[:L, :], ALU.mult, [gt, negA], [gt])
            A(betat[:L, :], gabt[:L, 4:8], AF.Sigmoid, [gabt], [betat])
            yield
            bank = PB()
            MM(bank[:L, 0:4], mT[:L, :L], gt[:L, :], True, True, [cst, gt], [bank])
            CP(gct[:L, :], bank[:L, 0:4], [bank], [gct])
            A(egt[:L, :], gct[:L, :], AF.Exp, [gct], [egt])
            yield
            Rt = GT[5]
            R3 = v3(Rt.ap)
            for h in range(4):
                TS(R3[:L, h, :L], mT[:L, :L], gt[:L, h:h + 1], None, ALU.mult, None, [cst, gt], [Rt])
            yield
            gcB = PB()
            g3 = v3(gcB.ap)
            for h in range(4):
                MM(g3[:, h, :L], ones[:L, :], R3[:L, h, :L], True, True, [cst, Rt], [gcB], inc=(h == 3))
            EB = GT[6]
            EB3 = v3(EB.ap)
            A(EB3[:, :, :L], g3[:, :, :L], AF.Exp, [gcB], [EB])
            yield
            dT = GT[7]
            dT3 = v3(dT.ap)
            for h in range(4):
                TS(dT3[:L, h, :L], g3[:L, h, :L], gct[:L, h:h + 1], 0.0, ALU.subtract, ALU.min, [gcB, gct, EB], [dT])
            yield
            if not smp:
                dl = GT[8]
                dl3 = v3(dl.ap)
                for h in range(4):
                    TS(dl3[:L, h, :L], g3[:L, h, :L], gct[:L, h:h + 1], 0.0, ALU.subtract, ALU.max, [gcB, gct], [dl])
                yield
                CP(eglast[:].unsqueeze(2), EB3[:, :, L - 1:L], [EB], [eglast])
                TT(eglt[:L, :].unsqueeze(2), g3[:L, :, L - 1:L], gct[:L, :].unsqueeze(2), ALU.subtract, [gcB, gct], [eglt])
                A(eglt[:L, :], eglt[:L, :], AF.Exp, [eglt], [eglt])
                A(dl3[:L, :, :L], dl3[:L, :, :L], AF.Exp, [dl], [dl], scale=-1.0)
                TT(dl3[:L, :, :L], dl3[:L, :, :L], strictL[:L, :L].unsqueeze(1).to_broadcast([L, 4, L]), ALU.mult,
                   [dl, cst], [dl])
                yield
            else:
                CP(ebs[:], EB3[:, :, :NS], [EB], [ebs])
            A(dT3[:L, :, :L], dT3[:L, :, :L], AF.Exp, [dT], [dT])
            TT(dT3[:L, :, :L], dT3[:L, :, :L], mT[:L, :L].unsqueeze(1).to_broadcast([L, 4, L]), ALU.mult, [dT, cst], [dT])
            yield
            cq = GT[0]; ck = GT[1]; cv = GT[2]
            cqk = [cq, ck, cv]
            for g in range(3):
                cg3 = v3(cqk[g].ap)
                for c in range(4):
                    conv_chunk(lambda c_, j, g=g: gtap(4 * g + c_, j), 40 + 16 * g, c, cg3[:, c, :L], cqk[g], GX)
                    yield
                A(cg3[:, :, :L], cg3[:, :, :L], AF.Silu, [cqk[g]], [cqk[g]])
            if smp:
                S.dma("pool", o_gc_s[l], GXs[:, :, 1:4, :], r=[GX], stream="o")
            else:
                if b == NBLK - 1:
                    S.dma("pool", o_gc_p[l], GX[:, :, L:L + 3], r=[GX], stream="o")
                CP(GX[:, :, 0:3], GX[:, :, L:L + 3], [GX], [GX])
            yield
            for g in range(2):
                cg3 = v3(cqk[g].ap)
                sq = GT[3]
                sq3 = v3(sq.ap)
                TT(sq3[:, :, :L], cg3[:, :, :L], cg3[:, :, :L], ALU.mult, [cqk[g]], [sq])
                bank = PB()
                b3 = v3(bank.ap)
                for h in range(4):
                    MM(b3[:, h, :L], ones, sq3[:, h, :L], True, True, [cst, sq], [bank], inc=(h == 3))
                yield
                rn = GT[4]
                rn3 = v3(rn.ap)
                RSQ(rn3[:, :, :L], b3[:, :, :L], [bank], [rn])
                if g == 0:
                    STT(cg3[:, :, :L], cg3[:, :, :L], float(128 ** -0.5), rn3[:, :, :L], ALU.mult, ALU.mult, [cq, rn], [cq])
                else:
                    TT(cg3[:, :, :L], cg3[:, :, :L], rn3[:, :, :L], ALU.mult, [ck, rn], [ck])
                yield
            q3 = v3(cq.ap); k3 = v3(ck.ap); cv3 = v3(cv.ap)
            TT(v3(qgT.ap)[:, :, :L], q3[:, :, :L], EB3[:, :, :L], ALU.mult, [cq, EB], [qgT])
            bank = PB()
            b3 = v3(bank.ap)
            for h in range(4):
                MM(b3[:L, h, :L], k3[:, h, :L], q3[:, h, :L], True, True, [ck, cq], [bank], inc=(h == 3))
            at3 = v3(attT.ap)
            TT(at3[:L, :, :L], b3[:L, :, :L], dT3[:L, :, :L], ALU.mult, [bank, dT], [attT])
            yield
            kTM = GT[3]; vTM = GT[4]
            for srcT, s3_, dstT in ((ck, k3, kTM), (cv, cv3, vTM)):
                bank = PB()
                for h in range(4):
                    TR(bank[:L, h * 128:(h + 1) * 128], s3_[:, h, :L], ident, [srcT, cst], [bank], inc=(h == 3))
                CP(dstT[:L, :], bank[:L, :], [bank], [dstT], e="act")
                yield
            if not smp:
                TT(v3(kd[:L, :]), v3(kTM[:L, :]), bc(eglt[:L, :], 128, L), ALU.mult, [kTM, eglt], [kd])
            else:
                CP(kd[:L, :], kTM[:L, :], [kTM], [kd])
            TT(v3(Vb[:L, :]), v3(vTM[:L, :]), bc(betat[:L, :], 128, L), ALU.mult, [vTM, betat], [Vb])
            yield
            TT(sm2[:L, :], betat[:L, :], egt[:L, :], ALU.mult, [betat, egt], [sm2])
            TT(v3(Kbg[:L, :]), v3(kTM[:L, :]), bc(sm2[:L, :], 128, L), ALU.mult, [kTM, sm2], [Kbg])
            yield
            Y3 = v3(Y.ap)
            if smp:
                CP(Y3[:L, :, :L], ident[:L, :L].unsqueeze(1).to_broadcast([L, 4, L]), [cst], [Y])
            else:
                bank = PB()
                b3 = v3(bank.ap)
                for h in range(4):
                    MM(b3[:L, h, :L], k3[:, h, :L], k3[:, h, :L], True, True, [ck], [bank], inc=(h == 3))
                P = GT[2]
                P3 = v3(P.ap)
                for h in range(4):
                    STT(P3[:L, h, :L], b3[:L, h, :L], betat[:L, h:h + 1], dl3[:L, h, :L], ALU.mult, ALU.mult,
                        [bank, betat, dl], [P])
                yield
                bank = PB()
                b3 = v3(bank.ap)
                for h in range(4):
                    TR(b3[:L, h, :L], P3[:L, h, :L], ident[:L, :L], [P, cst], [bank], inc=(h == 3))
                Q = GT[5]
                Q3 = v3(Q.ap)
                CP(Q3[:L, :, :L], b3[:L, :, :L], [bank], [Q], e="act")
                STT(Y3[:L, :, :L], Q3[:L, :, :L], -1.0, ident[:L, :L].unsqueeze(1).to_broadcast([L, 4, L]), ALU.mult, ALU.add,
                    [Q, cst], [Y])
                yield
                nlev = 6 if L == 128 else 3
                for lev in range(nlev):
                    bq = PB(); bp = PB()
                    bq3 = v3(bq.ap); bp3 = v3(bp.ap)
                    for h in range(4):
                        MM(bq3[:L, h, :L], P3[:L, h, :L], Q3[:L, h, :L], True, True, [P, Q], [bq], inc=(h == 3))
                    for h in range(4):
                        MM(bp3[:L, h, :L], Q3[:L, h, :L], P3[:L, h, :L], True, True, [P, Q], [bp], inc=(h == 3))
                    yield
                    Pn, Qn = (GT[3], GT[4]) if lev % 2 == 0 else (GT[2], GT[5])
                    CP(v3(Qn.ap)[:L, :, :L], bq3[:L, :, :L], [bq], [Qn], e="act")
                    CP(v3(Pn.ap)[:L, :, :L], bp3[:L, :, :L], [bp], [Pn], e="dve")
                    yield
                    P, Q = Pn, Qn
                    P3, Q3 = v3(P.ap), v3(Q.ap)
                    by = PB()
                    by3 = v3(by.ap)
                    for h in range(4):
                        MM(by3[:L, h, :L], P3[:L, h, :L], Y3[:L, h, :L], True, True, [P, Y], [by], inc=(h == 3))
                    TT(Y3[:L, :, :L], Y3[:L, :, :L], by3[:L, :, :L], ALU.add, [Y, by], [Y])
                    yield
            bank = PB()
            b3 = v3(bank.ap)
            for h in range(4):
                MM(b3[:, h, :L], Kbg[:L, h * 128:(h + 1) * 128], Y3[:L, h, :L], True, True, [Kbg, Y], [bank], inc=(h == 3))
            nWT = GT[6]
            nW3 = v3(nWT.ap)
            A(nW3[:, :, :L], b3[:, :, :L], AF.Copy, [bank], [nWT], scale=-1.0)
            yield
            Sg3 = v3(Sgdn.ap)
            vnb = PB()
            vn3 = v3(vnb.ap)
            for h in range(4):
                MM(vn3[:L, h, :], Y3[:L, h, :L], Vb[:L, h * 128:(h + 1) * 128], True, smp, [Y, Vb], [vnb],
                   inc=(smp and h == 3))
                if not smp:
                    MM(vn3[:L, h, :], nW3[:, h, :L], Sg3[:, h, :], False, True, [nWT, Sgdn], [vnb], inc=(h == 3))
            vnew = GT[7]
            if not smp:
                CP(vnew[:L, :], vnb[:L, :], [vnb], [vnew], e="act")
                yield
            else:
                wacc = GT[0]; qacc = GT[1]
                CP(wacc[:L, :], vnb[:L, :], [vnb], [wacc], e="act")
                MS(qacc[:L, :], 0.0, [qacc])
                for s in range(NS):
                    St = (GT[2], Sgdn)[s % 2]
                    S.dma("sp", v3(St.ap), s_gdn[l, s].rearrange("h d v -> d h v"), w=[St], stream="st")
                    for srcT, s3_, acc in ((qgT, v3(qgT.ap), qacc), (nWT, nW3, wacc)):
                        qm = GT[3]
                        TT(v3(qm[:, 0:64], 4), s3_[:, :, :NS], esel[:, s, :].unsqueeze(1).to_broadcast([128, 4, NS]),
                           ALU.mult, [srcT, esel], [qm])
                        tb_ = PB()
                        tb3 = v3(tb_.ap)
                        for h in range(4):
                            MM(tb3[:NS, h, :], v3(qm[:, 0:64], 4)[:, h, :], v3(St.ap)[:, h, :], True, True, [qm, St], [tb_],
                               inc=(h == 3))
                        TT(acc[:L, :], acc[:L, :], tb_[:NS, :], ALU.add, [acc, tb_], [acc])
                        yield
                CP(vnew[:L, :], wacc[:L, :], [wacc], [vnew])
            ob = PB()
            ob3 = v3(ob.ap)
            for h in range(4):
                MM(ob3[:L, h, :], at3[:L, h, :L], vnew[:L, h * 128:(h + 1) * 128], True, smp, [attT, vnew], [ob],
                   inc=(smp and h == 3))
                if not smp:
                    MM(ob3[:L, h, :], v3(qgT.ap)[:, h, :L], Sg3[:, h, :], False, True, [qgT, Sgdn], [ob], inc=(h == 3))
            yield
            if smp:
                TT(qacc[:L, :], qacc[:L, :], ob[:L, :], ALU.add, [qacc, ob], [qacc])
                osrc, okey = v3(qacc.ap), qacc
                for s in range(NS):
                    St = (GT[2], Sgdn)[s % 2]
                    S.dma("sp", v3(St.ap), s_gdn[l, s].rearrange("h d v -> d h v"), w=[St], stream="st")
                    vm = GT[3]
                    TS(vm[:NS, :], vnew[:NS, :], ident[:NS, s:s + 1], None, ALU.mult, None, [vnew, cst], [vm])
                    sb = PB()
                    sb3 = v3(sb.ap)
                    for h in range(4):
                        MM(sb3[:, h, :], kd[:NS, h * 128:(h + 1) * 128], vm[:NS, h * 128:(h + 1) * 128], True, True,
                           [kd, vm], [sb], inc=(h == 3))
                    So = (GT[4], GT[8])[s % 2]
                    for h in range(4):
                        STT(v3(So.ap)[:, h, :], v3(St.ap)[:, h, :], ebs[:, h, s:s + 1], sb3[:, h, :], ALU.mult, ALU.add,
                            [St, sb, ebs], [So])
                    S.dma("pool", o_gdn_s[l, s].rearrange("h d v -> d h v"), v3(So.ap), r=[So], stream="o")
                    yield
            else:
                osrc, okey = ob3, ob
                sb = PB()
                sb3 = v3(sb.ap)
                for h in range(4):
                    MM(sb3[:, h, :], kd[:L, h * 128:(h + 1) * 128], vnew[:L, h * 128:(h + 1) * 128], True, True, [kd, vnew], [sb],
                       inc=(h == 3))
                yield
                for h in range(4):
                    STT(Sg3[:, h, :], Sg3[:, h, :], eglast[:, h:h + 1], sb3[:, h, :], ALU.mult, ALU.add,
                        [Sgdn, sb, eglast], [Sgdn])
                if b == NBLK - 1:
                    S.dma("pool", o_gdn_p[l].rearrange("h d v -> d h v"), Sg3, r=[Sgdn], stream="o")
                yield
            osq = GT[8]
            A(v3(osq.ap)[:L, :, :], osrc[:L, :, :], AF.Square, [okey], [osq])
            S.op("dve", lambda: nc.vector.reduce_sum(out=sm3[:L, :], in_=v3(osq.ap)[:L, :, :], axis=AX.X), [osq], [sm3])
            RSQ(sm3[:L, :], sm3[:L, :], [sm3], [sm3], bias=EPS, scale=1.0 / 128.0)
            yield
            on = GT[5]
            for h in range(4):
                TS(on[:L, h * 128:(h + 1) * 128], osrc[:L, h, :], sm3[:L, h:h + 1], None, ALU.mult, None, [okey, sm3], [on])
            yield
            bank = PB()
            b3 = v3(bank.ap)
            for h in range(4):
                TR(b3[:, h, :L], on[:L, h * 128:(h + 1) * 128], ident[:L, :L], [on, cst], [bank], inc=(h == 3))
            A(z3[:, :, :L], z3[:, :, :L], AF.Silu, [ztg], [ztg])
            STT(mix[:, 8:12, :L], b3[:, :, :L], pft[:, 88:89], z3[:, :, :L], ALU.mult, ALU.mult, [bank, pft, ztg], [mixg])

        if MERGE == "sim":
            branches = []
            live = []
            for gen in (gen_gdn, gen_ret, gen_rg):
                g = gen()
                for mark in g:
                    if mark == "pre":
                        break
                live.append(g)
            if smp and l < NL - 1:
                load_win(l + 1)
            for g in live:
                S.rec = []
                for _ in g:
                    pass
                ops, S.rec = S.rec, None
                units, cur = [], []
                for o in ops:
                    cur.append(o)
                    if o[5]:
                        units.append(cur)
                        cur = []
                assert not cur
                branches.append(units)
            clock = {}
            wr = {}
            rd = {}
            ptr = [0] * len(branches)
            HOP = 0.3
            while True:
                best = None
                for bi, units in enumerate(branches):
                    if ptr[bi] >= len(units):
                        continue
                    u = units[ptr[bi]]
                    e = u[0][1]
                    t = clock.get(e, 0.0)
                    for o in u:
                        for k in o[3]:
                            if k in wr:
                                t = max(t, wr[k][0] + (HOP if wr[k][1] != e else 0.0))
                        for k in o[4]:
                            if k in wr:
                                t = max(t, wr[k][0] + (HOP if wr[k][1] != e else 0.0))
                            if k in rd:
                                t = max(t, rd[k][0] + (HOP if rd[k][1] != e else 0.0))
                    if best is None or t < best[0] - 1e-9:
                        best = (t, bi)
                if best is None:
                    break
                t, bi = best
                u = branches[bi][ptr[bi]]
                ptr[bi] += 1
                e = u[0][1]
                for o in u:
                    kind, eng, fn, r_, w_, inc, cost = o
                    if kind == "dma":
                        out_, in_, kw = fn
                        S.dma(eng, out_, in_, r=r_, w=w_, **kw)
                        done = t + 2.5
                        t += cost
                        who = "dma"
                    else:
                        S.op(eng, fn, r_, w_, inc=inc)
                        t += cost
                        done = t
                        who = eng
                    for k in r_:
                        if k not in rd or rd[k][0] < done:
                            rd[k] = (done, who)
                    for k in w_:
                        wr[k] = (done, who)
                        rd.pop(k, None)
                clock[e] = t
            gens = []
        else:
            gens = [(gen_rg(), 1), (gen_ret(), 1), (gen_gdn(), GDN_W)]
        if MERGE == "seq":
            for g, _ in gens:
                for _ in g:
                    pass
            gens = []
        while gens:
            for item in list(gens):
                g, wgt = item
                for _ in range(wgt):
                    try:
                        next(g)
                    except StopIteration:
                        gens.remove(item)
                        break

        z = [RT[0], RT[1]]
        for n in range(2):
            bank = ps[n]
            for kc in range(12):
                MM(bank[:L, :], mix[:, kc, :L], Wout[:, kc, n * 512:(n + 1) * 512], kc == 0, kc == 11,
                   [mixr, mixe, mixg, Wout], [bank], inc=(kc == 11))
            STT(z[n][:L, :], xsrc[:L, n * 512:(n + 1) * 512], float(ALPHA), bank[:L, :], ALU.mult, ALU.add, [xsrc, bank],
                [z[n]])
            S.op("dve", lambda n=n: nc.vector.bn_stats(out=bst2[:L, n, :], in_=z[n][:L, :]), [z[n]], [bst2])
        S.op("dve", lambda: nc.vector.bn_aggr(out=bmv2[:L, :], in_=bst2[:L, 0:2, :]), [bst2], [bmv2])
        RSQ(sm2[:L, 0:1], bmv2[:L, 1:2], [bmv2], [sm2])
        for n in range(2):
            sl = slice(n * 512, (n + 1) * 512)
            TS(z[n][:L, :], z[n][:L, :], bmv2[:L, 0:1], sm2[:L, 0:1], ALU.subtract, ALU.mult, [z[n], bmv2, sm2], [z[n]])
            TT(z[n][:L, :], z[n][:L, :], rowt[:L, sl], ALU.mult, [z[n], rowt], [z[n]])
            TT(z[n][:L, :], z[n][:L, :], rowt[:L, 1024 + n * 512:1024 + (n + 1) * 512], ALU.add, [z[n], rowt], [z[n]])
            if smp:
                if l == NL - 1:
                    S.dma("pool", y_s[:, sl], z[n][:NS, :], r=[z[n]], stream="o")
                else:
                    S.dma("pool", xsscr[:, sl], z[n][:NS, :], r=[z[n]], w=[("xsscr", 0)], stream="xo")
            else:
                if l == NL - 1:
                    if b > 0:
                        S.dma("pool", y_p[t0 - 16:t0 - 16 + L, sl], z[n][:L, :], r=[z[n]], stream="o")
                else:
                    S.dma("pool", xscr[t0:t0 + L, sl], z[n][:L, :], r=[z[n]], w=[("xscr", b)], stream="xo")


    for l in range(NL):
        layer_setup(l)
        for b in range(NBLK):
            block(l, "p", b)
        if not SKIP_SAMPLE:
            block(l, "s", 0)
    S.finish("sp")
    print("ops", S.nops, "waits", S.nwaits, "sems", S.nsem + len(S.dstream))
    return nc


def _consts():
    cst = np.zeros((128, NCST), np.float32)
    i = np.arange(128)
    cst[:, 0:128] = np.eye(128)
    cst[:, 128:256] = (i[None, :] >= i[:, None])
    cst[:, 256:384] = (i[None, :] < i[:, None])
    cst[:, 384:512] = 1.0
    sc = 128.0 ** -0.5
    for h in range(4):
        g = np.float64(GAM[h])
        cst[:, 512 + h] = g ** (i + 1.0)
        cst[:, 516 + h] = g ** (-(i + 1.0)) * sc
        cst[:, 520 + h] = g ** (127.0 - i) * sc
        cst[:16, 524 + h] = g ** (15.0 - i[:16]) * sc
        cst[:, 528 + h] = g
        cst[:, 532 + h] = sc / g
        cst[:, 536 + h] = sc
    half = 64
    inv = (np.float32(10000.0) ** (-np.arange(half, dtype=np.float32) / np.float32(half))).astype(np.float32)
    rope = np.zeros((18, 128, 128), np.float32)
    for b in range(18):
        if b == 0:
            pos = np.arange(16, dtype=np.float32)
        elif b < 17:
            pos = 16 + 128 * (b - 1) + np.arange(128, dtype=np.float32)
        else:
            pos = np.full(16, 16384.0, np.float32)
        ang = (pos[:, None].astype(np.float32) * inv[None, :]).astype(np.float32)
        rope[b, :len(pos), 0:64] = np.cos(ang.astype(np.float64))
        rope[b, :len(pos), 64:128] = np.sin(ang.astype(np.float64))
    esel = np.eye(16, dtype=np.float32).reshape(1, 256)
    return cst, rope, esel


_NC_CACHE = {}


def kernel(x_prompt, x_sample, state_rglru_h, state_rglru_conv, state_ret, state_gdn_conv, state_gdn,
           meta_tokens, w_in, rg_conv_w, rg_conv_b, rg_w_a, rg_b_a, rg_w_x, rg_b_x, rg_lambda,
           ret_gn_w, ret_gn_b, gdn_conv_w, gdn_a_log, gdn_dt_bias, gdn_norm_w, w_out, ln_w, ln_b):
    f = lambda a: np.ascontiguousarray(np.asarray(a, dtype=np.float32))
    x_prompt, x_sample, meta_tokens = f(x_prompt), f(x_sample), f(meta_tokens)
    w_in, w_out = f(w_in), f(w_out)
    pf = np.zeros((NL, 128, NPF), np.float32)

    def fm(v, nch):
        return f(v).reshape(NL, nch, 128).transpose(0, 2, 1)

    pf[:, :, 0:16] = f(rg_conv_w).reshape(NL, 4, 4, 128).transpose(0, 3, 2, 1).reshape(NL, 128, 16)
    pf[:, :, 16:20] = fm(rg_conv_b, 4)
    pf[:, :, 20:24] = fm(rg_b_a, 4)
    pf[:, :, 24:28] = fm(rg_b_x, 4)
    pf[:, :, 28:32] = fm(rg_lambda, 4)
    pf[:, :, 32:36] = fm(ret_gn_w, 4)
    pf[:, :, 36:40] = fm(ret_gn_b, 4)
    pf[:, :, 40:88] = f(gdn_conv_w).reshape(NL, 4, 12, 128).transpose(0, 3, 2, 1).reshape(NL, 128, 48)
    pf[:, :, 88] = f(gdn_norm_w)
    rgw = np.zeros((NL, 128, 2, 4, 128), np.float32)
    for which, wsrc in ((0, f(rg_w_a)), (1, f(rg_w_x))):
        for n in range(8):
            c, o = n // 2, (n % 2) * 64
            rgw[:, o:o + 64, which, c, o:o + 64] = wsrc[:, n]
    rgw = rgw.reshape(NL, 128, 1024)
    rows = np.concatenate([f(ln_w), f(ln_b), f(gdn_a_log), f(gdn_dt_bias)], axis=1).reshape(NL, 1, 2056)
    cst, rope, esel = _consts()
    if "nc" not in _NC_CACHE:
        _NC_CACHE["nc"] = build_nc()
    nc = _NC_CACHE["nc"]
    in_maps = []
    for c in range(8):
        sl = slice(NS * c, NS * (c + 1))
        m = {
            "xp": np.ascontiguousarray(np.concatenate([meta_tokens, x_prompt[c]], axis=0)),
            "xs": np.ascontiguousarray(x_sample[sl, 0, :]),
            "s_h": np.ascontiguousarray(f(state_rglru_h)[:, sl].reshape(NL, NS, 4, 128).transpose(0, 3, 2, 1)),
            "s_rgc": np.ascontiguousarray(f(state_rglru_conv)[:, sl].reshape(NL, NS, 3, 4, 128).transpose(0, 4, 3, 2, 1)),
            "s_gc": np.ascontiguousarray(f(state_gdn_conv)[:, sl].reshape(NL, NS, 3, 12, 128).transpose(0, 4, 3, 2, 1)),
            "s_ret": np.ascontiguousarray(f(state_ret)[:, sl]),
            "s_gdn": np.ascontiguousarray(f(state_gdn)[:, sl]),
            "w_in": w_in, "w_out": w_out, "pf": pf, "rgw": rgw, "rows": rows,
            "cst": cst, "ropet": rope, "esel": esel,
        }
        in_maps.append(m)
    res = run_bass_kernel_spmd(nc, in_maps, core_ids=list(range(8)))
    R = res.results
    g = lambda k: [np.asarray(R[c][k], dtype=np.float32) for c in range(8)]
    y_prompt = np.stack(g("y_p"), 0)
    y_sample = np.concatenate(g("y_s"), 0)[:, None, :]
    hp = np.stack([a.transpose(0, 2, 1).reshape(NL, 512) for a in g("o_h_p")], 1)
    rgcp = np.stack([a.transpose(0, 3, 2, 1).reshape(NL, 3, 512) for a in g("o_rgc_p")], 1)
    retp = np.stack(g("o_ret_p"), 1)
    gcp = np.stack([a.transpose(0, 3, 2, 1).reshape(NL, 3, 1536) for a in g("o_gc_p")], 1)
    gdnp = np.stack(g("o_gdn_p"), 1)
    hs = np.concatenate([a.transpose(0, 3, 2, 1).reshape(NL, NS, 512) for a in g("o_h_s")], 1)
    rgcs = np.concatenate([a.transpose(0, 4, 3, 2, 1).reshape(NL, NS, 3, 512) for a in g("o_rgc_s")], 1)
    rets = np.concatenate(g("o_ret_s"), 1)
    gcs = np.concatenate([a.transpose(0, 4, 3, 2, 1).reshape(NL, NS, 3, 1536) for a in g("o_gc_s")], 1)
    gdns = np.concatenate(g("o_gdn_s"), 1)
    c = np.ascontiguousarray
    return (c(y_prompt), c(y_sample), c(hp), c(rgcp), c(retp), c(gcp), c(gdnp), c(hs), c(rgcs), c(rets), c(gcs), c(gdns))
```

```python
import numpy as np
import concourse.bass as bass
import concourse.mybir as mybir
from concourse.bass_utils import run_bass_kernel_spmd

F32 = mybir.dt.float32
BF16 = mybir.dt.bfloat16
ALU = mybir.AluOpType
AF = mybir.ActivationFunctionType
AX = mybir.AxisListType

EPOCH = 12000
DEFCOST = {"pe": 0.2, "act": 0.4, "dve": 0.3, "pool": 0.05, "sp": 0.05}
GDN_STOP = 100000
ENABLE = [True, True, True]
NET = 6
SKIP_SAMPLE = False
PER_TILE_SEMS = True
MERGE = "sim"
SAME_SYNC = True

NL = 4
DM = 1024
DIN = 5128
NTOK = 2064
NBLK = 17
NS = 16
ALPHA = 8.0 ** 0.25
EPS = 1e-6
GAM = [1.0 - 2.0 ** (-5.0 - h) for h in range(4)]
NCST = 540
NPF = 89
COL = dict(rgx=0, rgz=512, rq=1024, rk=1536, rv=2048, rz=2560, gq=3072, gk=3584, gv=4096, gz=4608, gab=5120)


class Tile:
    def __init__(self, nc, name, shape, dtype, psum=False):
        if psum:
            self.h = nc.alloc_psum_tensor("T_" + name, list(shape), dtype)
        else:
            self.h = nc.alloc_sbuf_tensor("T_" + name, list(shape), dtype)
        self.ap = self.h.ap()
        self.name = name

    def __getitem__(self, k):
        return self.ap[k]


class Sched:
    def __init__(self, nc):
        self.nc = nc
        self.eng = {"pe": nc.tensor, "act": nc.scalar, "dve": nc.vector, "pool": nc.gpsimd, "sp": nc.sync}
        self.sem = {}
        self.cnt = {}
        self.pend = {}
        self.nsem = 0
        for e in self.eng:
            self._new_sem(e)
            self.pend[e] = False
        self.lastw = {}
        self.readers = {}
        self.waited = {e: {} for e in self.eng}
        self.dstream = {}
        self.nwaits = 0
        self.nops = 0
        self.rec = None

    def _new_sem(self, e):
        self.sem[e] = self.nc.alloc_semaphore(f"s_{e}_{self.nsem}")
        self.nsem += 1
        self.cnt[e] = 0

    def _deps(self, r, w):
        evs = []
        for k in r:
            if k in self.lastw:
                evs.append(self.lastw[k] + (True,))
        for k in w:
            if k in self.lastw:
                evs.append(self.lastw[k] + (False,))
            evs.extend(v + (False,) for v in self.readers.get(k, {}).values())
        return evs

    def _do_waits(self, e, evs):
        need = {}
        for sem, val, src, raw in evs:
            if src == e and not (SAME_SYNC or raw):
                continue
            if src.startswith("dma:"):
                val = self.dstream[src[4:]][1]
            if val > need.get(sem, (0, None))[0]:
                need[sem] = (val, src)
        for sem, (val, src) in need.items():
            if self.waited[e].get(sem, 0) >= val:
                continue
            if src == e and sem is self.sem[e] and val > self.cnt[e]:
                continue
            self.eng[e].wait_ge(sem, val)
            self.waited[e][sem] = val
            self.nwaits += 1

    def _register(self, ev, r, w):
        sem = ev[0]
        for k in r:
            self.readers.setdefault(k, {})[sem] = ev
        for k in w:
            self.lastw[k] = ev
            self.readers[k] = {}

    def op(self, e, fn, r=(), w=(), inc=True, cost=None):
        if self.rec is not None:
            self.rec.append(("op", e, fn, tuple(r), tuple(w), inc, cost if cost else DEFCOST[e]))
            return None
        self._do_waits(e, self._deps(r, w))
        if self.cnt[e] >= EPOCH and not self.pend[e]:
            self._new_sem(e)
        ins = fn()
        self.nops += 1
        if inc:
            self.cnt[e] += 1
            ins.then_inc(self.sem[e], 1)
            ev = (self.sem[e], self.cnt[e], e)
            self.pend[e] = False
        else:
            ev = (self.sem[e], self.cnt[e] + 1, e)
            self.pend[e] = True
        self._register(ev, r, w)
        return ins

    def dma(self, q, out, in_, r=(), w=(), stream="d", sname=None, **kw):
        if self.rec is not None:
            self.rec.append(("dma", q, (out, in_, dict(kw, sname=sname)), tuple(r), tuple(w), True, 0.05))
            return None
        tl = [k for k in w if isinstance(k, Tile)]
        if sname is not None:
            stream = sname
        elif not PER_TILE_SEMS:
            pass
        elif tl:
            stream = "ld_" + tl[0].name
        else:
            stream = "st_" + [k for k in r if isinstance(k, Tile)][0].name
        self._do_waits(q, self._deps(r, w))
        if stream not in self.dstream:
            self.dstream[stream] = [self.nc.alloc_semaphore(f"d_{stream}"), 0]
        st = self.dstream[stream]
        ins = self.eng[q].dma_start(out=out, in_=in_, **kw)
        st[1] += 16
        ins.then_inc(st[0], 16)
        ev = (st[0], st[1], "dma:" + stream)
        self._register(ev, r, w)
        return ins

    def finish(self, e="sp"):
        for name, (sem, tot) in self.dstream.items():
            if tot > 0:
                self.eng[e].wait_ge(sem, tot)


def v3(ap, c=4):
    return ap.rearrange("p (c n) -> p c n", c=c)


def build_nc():
    nc = bass.Bass("TRN2", target_bir_lowering=False)

    def din(name, shape):
        return nc.dram_tensor(name, list(shape), F32, kind="ExternalInput").ap()

    def dout(name, shape):
        return nc.dram_tensor(name, list(shape), F32, kind="ExternalOutput").ap()

    xp_d = din("xp", [NTOK, DM])
    xs_d = din("xs", [NS, DM])
    s_h = din("s_h", [NL, 128, 4, NS])
    s_rgc = din("s_rgc", [NL, 128, 4, 3, NS])
    s_gc = din("s_gc", [NL, 128, 12, 3, NS])
    s_ret = din("s_ret", [NL, NS, 4, 128, 128])
    s_gdn = din("s_gdn", [NL, NS, 4, 128, 128])
    w_in = din("w_in", [NL, DM, DIN])
    w_out = din("w_out", [NL, 1536, DM])
    pf_d = din("pf", [NL, 128, NPF])
    rgw_d = din("rgw", [NL, 128, 2 * 4 * 128])
    rows_d = din("rows", [NL, 1, 2056])
    cst_d = din("cst", [128, NCST])
    rope_d = din("ropet", [18, 128, 128])
    esel_d = din("esel", [1, 256])

    y_p = dout("y_p", [2048, DM])
    y_s = dout("y_s", [NS, DM])
    o_h_p = dout("o_h_p", [NL, 128, 4])
    o_rgc_p = dout("o_rgc_p", [NL, 128, 4, 3])
    o_ret_p = dout("o_ret_p", [NL, 4, 128, 128])
    o_gc_p = dout("o_gc_p", [NL, 128, 12, 3])
    o_gdn_p = dout("o_gdn_p", [NL, 4, 128, 128])
    o_h_s = dout("o_h_s", [NL, 128, 4, NS])
    o_rgc_s = dout("o_rgc_s", [NL, 128, 4, 3, NS])
    o_ret_s = dout("o_ret_s", [NL, NS, 4, 128, 128])
    o_gc_s = dout("o_gc_s", [NL, 128, 12, 3, NS])
    o_gdn_s = dout("o_gdn_s", [NL, NS, 4, 128, 128])
    xscr = nc.dram_tensor("xscr", [NTOK, DM], F32, kind="Internal").ap()
    xsscr = nc.dram_tensor("xsscr", [NS, DM], F32, kind="Internal").ap()

    S = Sched(nc)

    def TL(name, shape, dt=F32):
        return Tile(nc, name, shape, dt)

    Win = TL("Win", [128, 8, DIN], BF16)
    Wout = TL("Wout", [128, 12, DM], BF16)
    cst = TL("cst", [128, NCST])
    identb = TL("identb", [128, 128], BF16)
    esel = TL("esel", [128, 16, 16])
    pft = TL("pft", [128, NPF])
    rgwt = TL("rgwt", [128, 2, 4, 128], BF16)
    rowt = TL("rowt", [128, 2056])
    nc8sp = TL("nc8sp", [128, 4])
    negA = TL("negA", [128, 4])
    ropeb = TL("ropeb", [128, 128])
    X = TL("X", [128, 4, 131])
    GX = TL("GX", [128, 12, 131])
    h0s = TL("h0s", [128, 4, NS])
    Sret = TL("Sret", [128, 512])
    Sretb = TL("Sretb", [128, 512], BF16)
    Sgdn = TL("Sgdn", [128, 512])
    hprev = TL("hprev", [128, 4])
    mix = TL("mix", [128, 12, 128], BF16)
    xt = TL("xt", [128, DM])
    xT = TL("xT", [128, 8, 128], BF16)
    ztr = TL("ztr", [128, 512])
    zte = TL("zte", [128, 512])
    ztg = TL("ztg", [128, 512])
    xcb = TL("xcb", [128, 512], BF16)
    mixr, mixe, mixg = "mixr", "mixe", "mixg"
    Vb = TL("Vb", [128, 512])
    Kbg = TL("Kbg", [128, 512])
    kd = TL("kd", [128, 512])
    qgT = TL("qgT", [128, 512])
    attT = TL("attT", [128, 512])
    Y = TL("Y", [128, 512])
    gabt = TL("gabt", [128, 8])
    gt = TL("gt", [128, 4])
    betat = TL("betat", [128, 4])
    gct = TL("gct", [128, 4])
    egt = TL("egt", [128, 4])
    eglt = TL("eglt", [128, 4])
    eglast = TL("eglast", [128, 4])
    sm1 = TL("sm1", [128, 4])
    sm2 = TL("sm2", [128, 4])
    bst = TL("bst", [128, 4, 6])
    bmv = TL("bmv", [128, 4, 2])
    ebs = TL("ebs", [128, 4, NS])
    vb = TL("vb", [128, 512], BF16)
    k2b = TL("k2b", [128, 512], BF16)
    sm3 = TL("sm3", [128, 4])
    bst2 = TL("bst2", [128, 2, 6])
    bmv2 = TL("bmv2", [128, 2])
    RT = [TL(f"rt{i}", [128, 512]) for i in range(4)]
    ET = [TL(f"et{i}", [128, 512]) for i in range(NET)]
    GT = [TL(f"gt{i}", [128, 512]) for i in range(9)]
    B = [TL(f"bb{i}", [128, 1024], BF16) for i in range(3)]
    ps = [Tile(nc, f"ps{i}", [128, 512], F32, psum=True) for i in range(8)]
    print("sbuf bytes remaining", nc.sbuf_bytes_remaining)
    GDN_W = 2

    def fs(ap):
        n = 1
        for d in ap.shape[1:]:
            n *= d
        return n

    def A(out, in_, func, r, w, bias=0.0, scale=1.0):
        S.op("act", lambda: nc.scalar.activation(out=out, in_=in_, func=func, bias=bias, scale=scale), r, w,
             cost=0.22 + fs(out) / 1000.0)

    def TT(out, a, b, op, r, w):
        S.op("dve", lambda: nc.vector.tensor_tensor(out=out, in0=a, in1=b, op=op), r, w, cost=0.2 + fs(out) / 1000.0)

    def TS(out, a, s1, s2, op0, op1, r, w):
        if s2 is None:
            S.op("dve", lambda: nc.vector.tensor_scalar(out=out, in0=a, scalar1=s1, scalar2=None, op0=op0), r, w,
                 cost=0.2 + fs(out) / 1000.0)
        else:
            S.op("dve", lambda: nc.vector.tensor_scalar(out=out, in0=a, scalar1=s1, scalar2=s2, op0=op0, op1=op1), r, w,
                 cost=0.2 + fs(out) / 1000.0)

    def STT(out, a, s, b, op0, op1, r, w):
        S.op("dve", lambda: nc.vector.scalar_tensor_tensor(out=out, in0=a, scalar=s, in1=b, op0=op0, op1=op1), r, w,
             cost=0.2 + fs(out) / 1000.0)

    def CP(out, in_, r, w, e="dve"):
        if e == "dve":
            S.op("dve", lambda: nc.vector.tensor_copy(out=out, in_=in_), r, w, cost=0.2 + fs(out) / 1000.0)
        else:
            S.op("act", lambda: nc.scalar.activation(out=out, in_=in_, func=AF.Copy), r, w, cost=0.22 + fs(out) / 1000.0)

    def MS(out, val, w):
        S.op("dve", lambda: nc.vector.memset(out, val), (), w)

    def MM(out, lhsT, rhs, st, sp, r, w, inc=True):
        S.op("pe", lambda: nc.tensor.matmul(out, lhsT=lhsT, rhs=rhs, start=st, stop=sp), r, w, inc=inc,
             cost=(0.11 + fs(out) / 1200.0) * (2.0 if lhsT.dtype == F32 else 1.0))

    def TR(out, in_, ident, r, w, inc=True):
        S.op("pe", lambda: nc.tensor.transpose(out=out, in_=in_, identity=ident), r, w, inc=inc,
             cost=(0.11 + fs(out) / 1200.0) * (2.0 if in_.dtype == F32 else 1.0))

    def RSQ(out, in_, r, w, bias=EPS, scale=1.0):
        A(out, in_, AF.Sqrt, r, w, bias=bias, scale=scale)
        S.op("dve", lambda: nc.vector.reciprocal(out=out, in_=out), w, w)

    ident = cst[:, 0:128]
    maskT = cst[:, 128:256]
    strictL = cst[:, 256:384]
    ones = cst[:, 384:512]

    S.dma("sp", cst[:], cst_d, w=[cst], stream="c")
    S.dma("sp", esel[:].rearrange("p a b -> p (a b)"), esel_d.partition_broadcast(128), w=[esel], stream="c")
    CP(identb[:], ident, [cst], [identb], e="act")

    def bc(ap2, n, L):
        return ap2.unsqueeze(2).to_broadcast([L, 4, n])

    def load_win(l):
        for kc in range(8):
            S.dma("pool", Win[:, kc, :], w_in[l, kc * 128:(kc + 1) * 128, :], w=[Win], stream="w")

    def layer_setup(l):
        if l == 0:
            load_win(0)
        for kc in range(12):
            S.dma("pool", Wout[:, kc, :], w_out[l, kc * 128:(kc + 1) * 128, :], w=[Wout], stream="w")
        S.dma("pool", rgwt[:].rearrange("p a c n -> p (a c n)"), rgw_d[l], w=[rgwt], stream="w")
        S.dma("sp", pft[:], pf_d[l], w=[pft], stream="c")
        S.dma("sp", rowt[:], rows_d[l].partition_broadcast(128), w=[rowt], stream="c")
        A(nc8sp[:], pft[:, 28:32], AF.Exp, [pft], [nc8sp], scale=-1.0)
        A(nc8sp[:], nc8sp[:], AF.Ln, [nc8sp], [nc8sp], bias=1.0)
        TS(nc8sp[:], nc8sp[:], -8.0, None, ALU.mult, None, [nc8sp], [nc8sp])
        A(negA[:], rowt[:, 2048:2052], AF.Exp, [rowt], [negA])
        TS(negA[:], negA[:], -1.0, None, ALU.mult, None, [negA], [negA])
        MS(Sret[:], 0.0, [Sret])
        MS(Sretb[:], 0.0, [Sretb])
        MS(Sgdn[:], 0.0, [Sgdn])
        MS(hprev[:], 0.0, [hprev])
        MS(X[:, :, 0:3], 0.0, [X])
        MS(GX[:, :, 0:3], 0.0, [GX])

    def block(l, mode, b):
        smp = mode == "s"
        if smp:
            L = NS
            t0 = 0
            src = xs_d if l == 0 else xsscr
            S.dma("sp", xt[:L, :], src, r=[("xsscr", 0)], w=[xt], stream="x")
        else:
            L = 16 if b == 0 else 128
            t0 = 0 if b == 0 else 16 + 128 * (b - 1)
            src = xp_d if l == 0 else xscr
            S.dma("sp", xt[:L, :], src[t0:t0 + L, :], r=[("xscr", b)], w=[xt], stream="x")
        xsrc = xt
        rb = 17 if smp else b
        S.dma("sp", ropeb[:L, :], rope_d[rb, 0:L, :], w=[ropeb], stream="x")
        mT = ident if smp else maskT
        ci = 528 if smp else 512
        qdec = cst[:L, ci:ci + 4]
        kdecp = cst[:L, ci + 4:ci + 8]
        if smp:
            k2dec = cst[:L, 536:540]
        elif L == 128:
            k2dec = cst[:L, 520:524]
        else:
            k2dec = cst[:L, 524:528]
        Xs = X.ap.rearrange("p c n -> p (c n)")[:, 0:4 * 4 * NS].rearrange("p (c j s) -> p c j s", c=4, j=4)
        GXs = GX.ap.rearrange("p c n -> p (c n)")[:, 0:12 * 4 * NS].rearrange("p (c j s) -> p c j s", c=12, j=4)

        xb = B[0]
        CP(xb[:L, :], xsrc[:L, :], [xsrc], [xb], e="act")
        pt = ps[0]
        ptb = pt.ap.bitcast(BF16)
        for kc in range(8):
            TR(ptb[:, kc * 128:kc * 128 + L], xb[:L, kc * 128:(kc + 1) * 128], identb[:L, :L], [xb, identb], [pt],
               inc=(kc == 7))
        CP(xT[:, :, :L], v3(ptb, 8)[:, :, :L], [pt], [xT])

        def fm_group(bank, c0, dst_ap, dst_key, e="act"):
            b3 = v3(bank.ap)
            for c in range(4):
                for kc in range(8):
                    MM(b3[:, c, :L], Win[:, kc, c0 + c * 128:c0 + (c + 1) * 128], xT[:, kc, :L], kc == 0, kc == 7,
                       [Win, xT], [bank], inc=(c == 3 and kc == 7))
            CP(dst_ap, b3[:, :, :L], [bank], [dst_key], e=e)

        def tm_group(bank, c0, n, dst_ap, dst_key, e="dve"):
            for kc in range(8):
                MM(bank[:L, :n], xT[:, kc, :L], Win[:, kc, c0:c0 + n], kc == 0, kc == 7, [Win, xT], [bank],
                   inc=(kc == 7))
            CP(dst_ap, bank[:L, :n], [bank], [dst_key], e=e)

        def conv_chunk(src_tap, wcol0, c, o, dst_key, src_key, bias_col=None):
            w0 = pft[:, wcol0 + c * 4:wcol0 + c * 4 + 1]
            if bias_col is not None:
                TS(o, src_tap(c, 0), w0, pft[:, bias_col + c:bias_col + c + 1], ALU.mult, ALU.add,
                   [src_key, pft], [dst_key])
            else:
                TS(o, src_tap(c, 0), w0, None, ALU.mult, None, [src_key, pft], [dst_key])
            for j in range(1, 4):
                STT(o, src_tap(c, j), pft[:, wcol0 + c * 4 + j:wcol0 + c * 4 + j + 1], o, ALU.mult, ALU.add,
                    [src_key, pft, dst_key], [dst_key])

        def gen_rg():
            pa, pb = ps[0], ps[1]
            if smp:
                S.dma("sp", Xs[:, :, 0:3, :], s_rgc[l], w=[X], stream="st")
                S.dma("sp", h0s[:], s_h[l], w=[h0s], stream="st")
                fm_group(pa, COL["rgx"], Xs[:, :, 3, :], X)
                tap = lambda c, j: Xs[:, c, j, :]
            else:
                fm_group(pa, COL["rgx"], X[:, :, 3:3 + L], X)
                tap = lambda c, j: X[:, c, j:j + L]
            yield
            z3 = v3(ztr.ap)
            fm_group(pb, COL["rgz"], z3[:, :, :L], ztr, e="dve")
            yield "pre"
            xc = RT[0]
            xc3 = v3(xc.ap)
            for c in range(4):
                conv_chunk(tap, 0, c, xc3[:, c, :L], xc, X, bias_col=16)
                yield
            xcb3 = v3(xcb.ap)
            CP(xcb3[:, :, :L], xc3[:, :, :L], [xc], [xcb], e="act")
            rt = RT[1]; it = RT[2]; at = RT[3]
            r3 = v3(rt.ap); i3 = v3(it.ap); a3 = v3(at.ap)
            for which, dst3, dkey, bcol, bank in ((0, r3, rt, 20, pa), (1, i3, it, 24, pb)):
                b3 = v3(bank.ap)
                for c in range(4):
                    MM(b3[:, c, :L], rgwt[:, which, c, :], xcb3[:, c, :L], True, True, [rgwt, xcb], [bank], inc=(c == 3))
                yield
                for c in range(4):
                    A(dst3[:, c, :L], b3[:, c, :L], AF.Sigmoid, [bank, pft], [dkey], bias=pft[:, bcol + c:bcol + c + 1])
                yield
            for c in range(4):
                A(a3[:, c, :L], r3[:, c, :L], AF.Exp, [rt, nc8sp], [at], scale=nc8sp[:, c:c + 1])
            yield
            mt = RT[1]
            m3 = v3(mt.ap)
            TT(m3[:, :, :L], a3[:, :, :L], a3[:, :, :L], ALU.mult, [at], [mt])
            A(m3[:, :, :L], m3[:, :, :L], AF.Sqrt, [mt], [mt], bias=1.0, scale=-1.0)
            yield
            TT(i3[:, :, :L], i3[:, :, :L], xc3[:, :, :L], ALU.mult, [it, xc], [it])
            yield
            TT(i3[:, :, :L], i3[:, :, :L], m3[:, :, :L], ALU.mult, [it, mt], [it])
            yield
            ht = RT[0]
            h3 = v3(ht.ap)
            if smp:
                TT(h3[:, :, :L], a3[:, :, :L], h0s[:], ALU.mult, [at, h0s], [ht])
                TT(h3[:, :, :L], h3[:, :, :L], i3[:, :, :L], ALU.add, [ht, it], [ht])
                S.dma("pool", o_h_s[l], h3[:, :, :L], r=[ht], stream="o")
                S.dma("pool", o_rgc_s[l], Xs[:, :, 1:4, :], r=[X], stream="o")
            else:
                for c in range(4):
                    S.op("dve", lambda c=c: nc.vector.tensor_tensor_scan(
                        out=h3[:, c, :L], data0=a3[:, c, :L], data1=i3[:, c, :L], initial=hprev[:, c:c + 1],
                        op0=ALU.mult, op1=ALU.add), [at, it, hprev], [ht])
                    yield
                CP(hprev[:].unsqueeze(2), h3[:, :, L - 1:L], [ht], [hprev])
                if b == NBLK - 1:
                    S.dma("pool", o_h_p[l], hprev[:], r=[hprev], stream="o")
                    S.dma("pool", o_rgc_p[l], X[:, :, L:L + 3], r=[X], stream="o")
                CP(X[:, :, 0:3], X[:, :, L:L + 3], [X], [X])
            yield
            A(z3[:, :, :L], z3[:, :, :L], AF.Silu, [ztr], [ztr])
            TT(mix[:, 0:4, :L], h3[:, :, :L], z3[:, :, :L], ALU.mult, [ht, ztr], [mixr])

        def gen_ret():
            bk = [ps[2], ps[3], ps[4]]
            rq = ET[0]; rk = ET[1]
            tm_group(bk[0], COL["rq"], 512, rq[:L, :], rq)
            yield
            tm_group(bk[1], COL["rk"], 512, rk[:L, :], rk)
            yield
            tm_group(bk[2], COL["rv"], 512, vb[:L, 0:512], vb)
            yield
            z3 = v3(zte.ap)
            fm_group(bk[0], COL["rz"], z3[:, :, :L], zte, e="dve")
            yield "pre"
            cosb = ropeb[:L, 0:64].unsqueeze(1).to_broadcast([L, 4, 64])
            sinb = ropeb[:L, 64:128].unsqueeze(1).to_broadcast([L, 4, 64])

            def rope(src, dst, tmp):
                s3 = v3(src[:L, :]); d3 = v3(dst[:L, :]); t3 = v3(tmp[:L, :])
                t1 = s3[:, :, 0:64]; t2 = s3[:, :, 64:128]
                TT(d3[:, :, 0:64], t1, cosb, ALU.mult, [src, ropeb], [dst])
                TT(t3[:, :, 0:64], t2, sinb, ALU.mult, [src, ropeb], [tmp])
                yield
                TT(d3[:, :, 0:64], d3[:, :, 0:64], t3[:, :, 0:64], ALU.subtract, [dst, tmp], [dst])
                TT(d3[:, :, 64:128], t1, sinb, ALU.mult, [src, ropeb], [dst])
                yield
                TT(t3[:, :, 64:128], t2, cosb, ALU.mult, [src, ropeb], [tmp])
                TT(d3[:, :, 64:128], d3[:, :, 64:128], t3[:, :, 64:128], ALU.add, [dst, tmp], [dst])
                yield

            rqr = ET[2]; rkr = ET[4]
            yield from rope(rq, rqr, ET[3])
            yield from rope(rk, rkr, ET[3])
            qkb = B[0]
            TT(v3(qkb[:L, 0:512]), v3(rqr[:L, :]), bc(qdec, 128, L), ALU.mult, [rqr, cst], [qkb])
            TT(v3(qkb[:L, 512:1024]), v3(rkr[:L, :]), bc(kdecp, 128, L), ALU.mult, [rkr, cst], [qkb])
            yield
            if smp:
                k2 = ET[5]
                TT(v3(k2[:L, :]), v3(rkr[:L, :]), bc(k2dec, 128, L), ALU.mult, [rkr, cst], [k2])
            else:
                k2 = k2b
                TT(v3(k2[:L, 0:512]), v3(rkr[:L, :]), bc(k2dec, 128, L), ALU.mult, [rkr, cst], [k2])
            yield
            pt = bk[1]
            ptb = pt.ap.bitcast(BF16)
            for j in range(8):
                TR(ptb[:, j * 128:j * 128 + L], qkb[:L, j * 128:(j + 1) * 128], identb[:L, :L], [qkb, identb], [pt],
                   inc=(j == 7))
            qkT = B[1]
            qkT3 = v3(qkT.ap, 8)
            CP(qkT3[:, :, :L], v3(ptb, 8)[:, :, :L], [pt], [qkT], e="act")
            yield
            bank = bk[2]
            b3 = v3(bank.ap)
            for h in range(4):
                MM(b3[:L, h, :L], qkT3[:, 4 + h, :L], qkT3[:, h, :L], True, True, [qkT], [bank], inc=(h == 3))
            scb = B[2]
            sc3 = v3(scb[:, 0:512])
            TT(sc3[:L, :, :L], b3[:L, :, :L], mT[:L, :L].unsqueeze(1).to_broadcast([L, 4, L]), ALU.mult, [bank, cst], [scb])
            yield
            ob = bk[0]
            ob3 = v3(ob.ap)
            for h in range(4):
                MM(ob3[:L, h, :], sc3[:L, h, :L], vb[:L, h * 128:(h + 1) * 128], True, smp, [scb, vb], [ob],
                   inc=(smp and h == 3))
                if not smp:
                    MM(ob3[:L, h, :], qkT3[:, h, :L], v3(Sretb.ap)[:, h, :], False, True, [qkT, Sretb], [ob], inc=(h == 3))
            yield
            if not smp:
                sb = bk[1]
                sb3 = v3(sb.ap)
                for h in range(4):
                    MM(sb3[:, h, :], k2[:L, h * 128:(h + 1) * 128], vb[:L, h * 128:(h + 1) * 128], True, True, [k2, vb], [sb],
                       inc=(h == 3))
                yield
                for h in range(4):
                    STT(v3(Sret.ap)[:, h, :], v3(Sret.ap)[:, h, :], float(GAM[h] ** L), sb3[:, h, :], ALU.mult, ALU.add,
                        [Sret, sb], [Sret])
                yield
                CP(Sretb[:], Sret[:], [Sret], [Sretb], e="act")
                if b == NBLK - 1:
                    S.dma("pool", o_ret_p[l].rearrange("h d v -> d h v"), v3(Sret.ap), r=[Sret], stream="o")
                osrc, okey = ob3, ob
            else:
                oacc = ET[1]
                CP(oacc[:L, :], ob[:L, :], [ob], [oacc])
                xTf = xT.ap.rearrange("p a b -> p (a b)").bitcast(F32)
                for s in range(NS):
                    St = (ET[2], Sret)[s % 2]
                    S.dma("sp", v3(St.ap), s_ret[l, s].rearrange("h d v -> d h v"), w=[St], stream="st")
                    qm = ET[3]
                    TT(v3(qm[:, 0:64], 4), qkT3[:, 0:4, :NS], esel[:, s, :].unsqueeze(1).to_broadcast([128, 4, NS]),
                       ALU.mult, [qkT, esel], [qm])
                    tb_ = bk[1]
                    tb3 = v3(tb_.ap)
                    for h in range(4):
                        MM(tb3[:NS, h, :], v3(qm[:, 0:64], 4)[:, h, :], v3(St.ap)[:, h, :], True, True, [qm, St], [tb_],
                           inc=(h == 3))
                    TT(oacc[:L, :], oacc[:L, :], tb_[:NS, :], ALU.add, [oacc, tb_], [oacc])
                    yield
                    vm = ET[4]
                    TT(vm[:NS, :], vb[:NS, 0:512], ident[:NS, s:s + 1].to_broadcast([NS, 512]), ALU.mult, [vb, cst], [vm])
                    sb = bk[2]
                    sb3 = v3(sb.ap)
                    for h in range(4):
                        MM(sb3[:, h, :], k2[:NS, h * 128:(h + 1) * 128], vm[:NS, h * 128:(h + 1) * 128], True, True,
                           [k2, vm], [sb], inc=(h == 3))
                    So, So3 = ((ET[0], v3(ET[0].ap)), (xT, v3(xTf)))[s % 2]
                    for h in range(4):
                        STT(So3[:, h, :], v3(St.ap)[:, h, :], float(GAM[h]), sb3[:, h, :], ALU.mult, ALU.add,
                            [St, sb], [So])
                    S.dma("pool", o_ret_s[l, s].rearrange("h d v -> d h v"), So3, r=[So], stream="o")
                    yield
                osrc, okey = v3(oacc.ap), oacc
            for h in range(4):
                S.op("dve", lambda h=h: nc.vector.bn_stats(out=bst[:L, h, :], in_=osrc[:L, h, :]), [okey], [bst])
            yield
            for h in range(4):
                S.op("dve", lambda h=h: nc.vector.bn_aggr(out=bmv[:L, h, :], in_=bst[:L, h, :]), [bst], [bmv])
            RSQ(sm1[:L, :], bmv[:L, :, 1], [bmv], [sm1])
            yield
            onb = B[2]
            for h in range(4):
                TS(onb[:L, 512 + h * 128:512 + (h + 1) * 128], osrc[:L, h, :], bmv[:L, h, 0:1], sm1[:L, h:h + 1],
                   ALU.subtract, ALU.mult, [okey, bmv, sm1], [onb])
            yield
            pt = bk[1]
            ptb = pt.ap.bitcast(BF16)
            for h in range(4):
                TR(ptb[:, h * 128:h * 128 + L], onb[:L, 512 + h * 128:512 + (h + 1) * 128], identb[:L, :L], [onb, identb],
                   [pt], inc=(h == 3))
            yt = ET[0]
            y3 = v3(yt.ap)
            for h in range(4):
                A(y3[:, h, :L], ptb[:, h * 128:h * 128 + L], AF.Identity, [pt, pft], [yt], bias=pft[:, 36 + h:37 + h],
                  scale=pft[:, 32 + h:33 + h])
            yield
            A(z3[:, :, :L], z3[:, :, :L], AF.Silu, [zte], [zte])
            TT(mix[:, 4:8, :L], y3[:, :, :L], z3[:, :, :L], ALU.mult, [yt, zte], [mixe])

        def gen_gdn():
            bk = [ps[5], ps[6], ps[7]]
            nb = [0]

            def PB():
                nb[0] = (nb[0] + 1) % 3
                return bk[nb[0]]

            if smp:
                S.dma("sp", GXs[:, :, 0:3, :], s_gc[l], w=[GX], stream="st")
                for g in range(3):
                    fm_group(PB(), COL["gq"] + 512 * g, GXs[:, 4 * g:4 * g + 4, 3, :], GX)
                    yield
                gtap = lambda c, j: GXs[:, c, j, :]
            else:
                for g in range(3):
                    fm_group(PB(), COL["gq"] + 512 * g, GX[:, 4 * g:4 * g + 4, 3:3 + L], GX)
                    yield
                gtap = lambda c, j: GX[:, c, j:j + L]
            tm_group(PB(), COL["gab"], 8, gabt[:L, :], gabt)
            z3 = v3(ztg.ap)
            fm_group(PB(), COL["gz"], z3[:, :, :L], ztg, e="dve")
            yield "pre"
            TT(gt[:L, :], gabt[:L, 0:4], rowt[:L, 2052:2056], ALU.add, [gabt, rowt], [gt])
            A(gt[:L, :], gt[:L, :], AF.Exp, [gt], [gt])
            A(gt[:L, :], gt[:L, :], AF.Ln, [gt], [gt], bias=1.0)
            TT(gt[:L, :], gt[:L, :], negA[:L, :], ALU.mult, [gt, negA], [gt])
            A(betat[:L, :], gabt[:L, 4:8], AF.Sigmoid, [gabt], [betat])
            yield
            bank = PB()
            MM(bank[:L, 0:4], mT[:L, :L], gt[:L, :], True, True, [cst, gt], [bank])
            CP(gct[:L, :], bank[:L, 0:4], [bank], [gct])
            A(egt[:L, :], gct[:L, :], AF.Exp, [gct], [egt])
            yield
            Rt = GT[5]
            R3 = v3(Rt.ap)
            for h in range(4):
                TS(R3[:L, h, :L], mT[:L, :L], gt[:L, h:h + 1], None, ALU.mult, None, [cst, gt], [Rt])
            yield
            gcB = PB()
            g3 = v3(gcB.ap)
            for h in range(4):
                MM(g3[:, h, :L], ones[:L, :], R3[:L, h, :L], True, True, [cst, Rt], [gcB], inc=(h == 3))
            EB = GT[6]
            EB3 = v3(EB.ap)
            A(EB3[:, :, :L], g3[:, :, :L], AF.Exp, [gcB], [EB])
            yield
            dT = GT[7]
            dT3 = v3(dT.ap)
            for h in range(4):
                TS(dT3[:L, h, :L], g3[:L, h, :L], gct[:L, h:h + 1], 0.0, ALU.subtract, ALU.min, [gcB, gct, EB], [dT])
            yield
            if not smp:
                dl = GT[8]
                dl3 = v3(dl.ap)
                for h in range(4):
                    TS(dl3[:L, h, :L], g3[:L, h, :L], gct[:L, h:h + 1], 0.0, ALU.subtract, ALU.max, [gcB, gct], [dl])
                yield
                CP(eglast[:].unsqueeze(2), EB3[:, :, L - 1:L], [EB], [eglast])
                TT(eglt[:L, :].unsqueeze(2), g3[:L, :, L - 1:L], gct[:L, :].unsqueeze(2), ALU.subtract, [gcB, gct], [eglt])
                A(eglt[:L, :], eglt[:L, :], AF.Exp, [eglt], [eglt])
                A(dl3[:L, :, :L], dl3[:L, :, :L], AF.Exp, [dl], [dl], scale=-1.0)
                TT(dl3[:L, :, :L], dl3[:L, :, :L], strictL[:L, :L].unsqueeze(1).to_broadcast([L, 4, L]), ALU.mult,
                   [dl, cst], [dl])
                yield
            else:
                CP(ebs[:], EB3[:, :, :NS], [EB], [ebs])
            A(dT3[:L, :, :L], dT3[:L, :, :L], AF.Exp, [dT], [dT])
            TT(dT3[:L, :, :L], dT3[:L, :, :L], mT[:L, :L].unsqueeze(1).to_broadcast([L, 4, L]), ALU.mult, [dT, cst], [dT])
            yield
            cq = GT[0]; ck = GT[1]; cv = GT[2]
            cqk = [cq, ck, cv]
            for g in range(3):
                cg3 = v3(cqk[g].ap)
                for c in range(4):
                    conv_chunk(lambda c_, j, g=g: gtap(4 * g + c_, j), 40 + 16 * g, c, cg3[:, c, :L], cqk[g], GX)
                    yield
                A(cg3[:, :, :L], cg3[:, :, :L], AF.Silu, [cqk[g]], [cqk[g]])
            if smp:
                S.dma("pool", o_gc_s[l], GXs[:, :, 1:4, :], r=[GX], stream="o")
            else:
                if b == NBLK - 1:
                    S.dma("pool", o_gc_p[l], GX[:, :, L:L + 3], r=[GX], stream="o")
                CP(GX[:, :, 0:3], GX[:, :, L:L + 3], [GX], [GX])
            yield
            for g in range(2):
                cg3 = v3(cqk[g].ap)
                sq = GT[3]
                sq3 = v3(sq.ap)
                TT(sq3[:, :, :L], cg3[:, :, :L], cg3[:, :, :L], ALU.mult, [cqk[g]], [sq])
                bank = PB()
                b3 = v3(bank.ap)
                for h in range(4):
                    MM(b3[:, h, :L], ones, sq3[:, h, :L], True, True, [cst, sq], [bank], inc=(h == 3))
                yield
                rn = GT[4]
                rn3 = v3(rn.ap)
                RSQ(rn3[:, :, :L], b3[:, :, :L], [bank], [rn])
                if g == 0:
                    STT(cg3[:, :, :L], cg3[:, :, :L], float(128 ** -0.5), rn3[:, :, :L], ALU.mult, ALU.mult, [cq, rn], [cq])
                else:
                    TT(cg3[:, :, :L], cg3[:, :, :L], rn3[:, :, :L], ALU.mult, [ck, rn], [ck])
                yield
            q3 = v3(cq.ap); k3 = v3(ck.ap); cv3 = v3(cv.ap)
            TT(v3(qgT.ap)[:, :, :L], q3[:, :, :L], EB3[:, :, :L], ALU.mult, [cq, EB], [qgT])
            bank = PB()
            b3 = v3(bank.ap)
            for h in range(4):
                MM(b3[:L, h, :L], k3[:, h, :L], q3[:, h, :L], True, True, [ck, cq], [bank], inc=(h == 3))
            at3 = v3(attT.ap)
            TT(at3[:L, :, :L], b3[:L, :, :L], dT3[:L, :, :L], ALU.mult, [bank, dT], [attT])
            yield
            kTM = GT[3]; vTM = GT[4]
            for srcT, s3_, dstT in ((ck, k3, kTM), (cv, cv3, vTM)):
                bank = PB()
                for h in range(4):
                    TR(bank[:L, h * 128:(h + 1) * 128], s3_[:, h, :L], ident, [srcT, cst], [bank], inc=(h == 3))
                CP(dstT[:L, :], bank[:L, :], [bank], [dstT], e="act")
                yield
            if not smp:
                TT(v3(kd[:L, :]), v3(kTM[:L, :]), bc(eglt[:L, :], 128, L), ALU.mult, [kTM, eglt], [kd])
            else:
                CP(kd[:L, :], kTM[:L, :], [kTM], [kd])
            TT(v3(Vb[:L, :]), v3(vTM[:L, :]), bc(betat[:L, :], 128, L), ALU.mult, [vTM, betat], [Vb])
            yield
            TT(sm2[:L, :], betat[:L, :], egt[:L, :], ALU.mult, [betat, egt], [sm2])
            TT(v3(Kbg[:L, :]), v3(kTM[:L, :]), bc(sm2[:L, :], 128, L), ALU.mult, [kTM, sm2], [Kbg])
            yield
            Y3 = v3(Y.ap)
            if smp:
                CP(Y3[:L, :, :L], ident[:L, :L].unsqueeze(1).to_broadcast([L, 4, L]), [cst], [Y])
            else:
                bank = PB()
                b3 = v3(bank.ap)
                for h in range(4):
                    MM(b3[:L, h, :L], k3[:, h, :L], k3[:, h, :L], True, True, [ck], [bank], inc=(h == 3))
                P = GT[2]
                P3 = v3(P.ap)
                for h in range(4):
                    STT(P3[:L, h, :L], b3[:L, h, :L], betat[:L, h:h + 1], dl3[:L, h, :L], ALU.mult, ALU.mult,
                        [bank, betat, dl], [P])
                yield
                bank = PB()
                b3 = v3(bank.ap)
                for h in range(4):
                    TR(b3[:L, h, :L], P3[:L, h, :L], ident[:L, :L], [P, cst], [bank], inc=(h == 3))
                Q = GT[5]
                Q3 = v3(Q.ap)
                CP(Q3[:L, :, :L], b3[:L, :, :L], [bank], [Q], e="act")
                STT(Y3[:L, :, :L], Q3[:L, :, :L], -1.0, ident[:L, :L].unsqueeze(1).to_broadcast([L, 4, L]), ALU.mult, ALU.add,
                    [Q, cst], [Y])
                yield
                nlev = 6 if L == 128 else 3
                for lev in range(nlev):
                    bq = PB(); bp = PB()
                    bq3 = v3(bq.ap); bp3 = v3(bp.ap)
                    for h in range(4):
                        MM(bq3[:L, h, :L], P3[:L, h, :L], Q3[:L, h, :L], True, True, [P, Q], [bq], inc=(h == 3))
                    for h in range(4):
                        MM(bp3[:L, h, :L], Q3[:L, h, :L], P3[:L, h, :L], True, True, [P, Q], [bp], inc=(h == 3))
                    yield
                    Pn, Qn = (GT[3], GT[4]) if lev % 2 == 0 else (GT[2], GT[5])
                    CP(v3(Qn.ap)[:L, :, :L], bq3[:L, :, :L], [bq], [Qn], e="act")
                    CP(v3(Pn.ap)[:L, :, :L], bp3[:L, :, :L], [bp], [Pn], e="dve")
                    yield
                    P, Q = Pn, Qn
                    P3, Q3 = v3(P.ap), v3(Q.ap)
                    by = PB()
                    by3 = v3(by.ap)
                    for h in range(4):
                        MM(by3[:L, h, :L], P3[:L, h, :L], Y3[:L, h, :L], True, True, [P, Y], [by], inc=(h == 3))
                    TT(Y3[:L, :, :L], Y3[:L, :, :L], by3[:L, :, :L], ALU.add, [Y, by], [Y])
                    yield
            bank = PB()
            b3 = v3(bank.ap)
            for h in range(4):
                MM(b3[:, h, :L], Kbg[:L, h * 128:(h + 1) * 128], Y3[:L, h, :L], True, True, [Kbg, Y], [bank], inc=(h == 3))
            nWT = GT[6]
            nW3 = v3(nWT.ap)
            A(nW3[:, :, :L], b3[:, :, :L], AF.Copy, [bank], [nWT], scale=-1.0)
            yield
            Sg3 = v3(Sgdn.ap)
            vnb = PB()
            vn3 = v3(vnb.ap)
            for h in range(4):
                MM(vn3[:L, h, :], Y3[:L, h, :L], Vb[:L, h * 128:(h + 1) * 128], True, smp, [Y, Vb], [vnb],
                   inc=(smp and h == 3))
                if not smp:
                    MM(vn3[:L, h, :], nW3[:, h, :L], Sg3[:, h, :], False, True, [nWT, Sgdn], [vnb], inc=(h == 3))
            vnew = GT[7]
            if not smp:
                CP(vnew[:L, :], vnb[:L, :], [vnb], [vnew], e="act")
                yield
            else:
                wacc = GT[0]; qacc = GT[1]
                CP(wacc[:L, :], vnb[:L, :], [vnb], [wacc], e="act")
                MS(qacc[:L, :], 0.0, [qacc])
                for s in range(NS):
                    St = (GT[2], Sgdn)[s % 2]
                    S.dma("sp", v3(St.ap), s_gdn[l, s].rearrange("h d v -> d h v"), w=[St], stream="st")
                    for srcT, s3_, acc in ((qgT, v3(qgT.ap), qacc), (nWT, nW3, wacc)):
                        qm = GT[3]
                        TT(v3(qm[:, 0:64], 4), s3_[:, :, :NS], esel[:, s, :].unsqueeze(1).to_broadcast([128, 4, NS]),
                           ALU.mult, [srcT, esel], [qm])
                        tb_ = PB()
                        tb3 = v3(tb_.ap)
                        for h in range(4):
                            MM(tb3[:NS, h, :], v3(qm[:, 0:64], 4)[:, h, :], v3(St.ap)[:, h, :], True, True, [qm, St], [tb_],
                               inc=(h == 3))
                        TT(acc[:L, :], acc[:L, :], tb_[:NS, :], ALU.add, [acc, tb_], [acc])
                        yield
                CP(vnew[:L, :], wacc[:L, :], [wacc], [vnew])
            ob = PB()
            ob3 = v3(ob.ap)
            for h in range(4):
                MM(ob3[:L, h, :], at3[:L, h, :L], vnew[:L, h * 128:(h + 1) * 128], True, smp, [attT, vnew], [ob],
                   inc=(smp and h == 3))
                if not smp:
                    MM(ob3[:L, h, :], v3(qgT.ap)[:, h, :L], Sg3[:, h, :], False, True, [qgT, Sgdn], [ob], inc=(h == 3))
            yield
            if smp:
                TT(qacc[:L, :], qacc[:L, :], ob[:L, :], ALU.add, [qacc, ob], [qacc])
                osrc, okey = v3(qacc.ap), qacc
                for s in range(NS):
                    St = (GT[2], Sgdn)[s % 2]
                    S.dma("sp", v3(St.ap), s_gdn[l, s].rearrange("h d v -> d h v"), w=[St], stream="st")
                    vm = GT[3]
                    TS(vm[:NS, :], vnew[:NS, :], ident[:NS, s:s + 1], None, ALU.mult, None, [vnew, cst], [vm])
                    sb = PB()
                    sb3 = v3(sb.ap)
                    for h in range(4):
                        MM(sb3[:, h, :], kd[:NS, h * 128:(h + 1) * 128], vm[:NS, h * 128:(h + 1) * 128], True, True,
                           [kd, vm], [sb], inc=(h == 3))
                    So = (GT[4], GT[8])[s % 2]
                    for h in range(4):
                        STT(v3(So.ap)[:, h, :], v3(St.ap)[:, h, :], ebs[:, h, s:s + 1], sb3[:, h, :], ALU.mult, ALU.add,
                            [St, sb, ebs], [So])
                    S.dma("pool", o_gdn_s[l, s].rearrange("h d v -> d h v"), v3(So.ap), r=[So], stream="o")
                    yield
            else:
                osrc, okey = ob3, ob
                sb = PB()
                sb3 = v3(sb.ap)
                for h in range(4):
                    MM(sb3[:, h, :], kd[:L, h * 128:(h + 1) * 128], vnew[:L, h * 128:(h + 1) * 128], True, True, [kd, vnew], [sb],
                       inc=(h == 3))
                yield
                for h in range(4):
                    STT(Sg3[:, h, :], Sg3[:, h, :], eglast[:, h:h + 1], sb3[:, h, :], ALU.mult, ALU.add,
                        [Sgdn, sb, eglast], [Sgdn])
                if b == NBLK - 1:
                    S.dma("pool", o_gdn_p[l].rearrange("h d v -> d h v"), Sg3, r=[Sgdn], stream="o")
                yield
            osq = GT[8]
            A(v3(osq.ap)[:L, :, :], osrc[:L, :, :], AF.Square, [okey], [osq])
            S.op("dve", lambda: nc.vector.reduce_sum(out=sm3[:L, :], in_=v3(osq.ap)[:L, :, :], axis=AX.X), [osq], [sm3])
            RSQ(sm3[:L, :], sm3[:L, :], [sm3], [sm3], bias=EPS, scale=1.0 / 128.0)
            yield
            on = GT[5]
            for h in range(4):
                TS(on[:L, h * 128:(h + 1) * 128], osrc[:L, h, :], sm3[:L, h:h + 1], None, ALU.mult, None, [okey, sm3], [on])
            yield
            bank = PB()
            b3 = v3(bank.ap)
            for h in range(4):
                TR(b3[:, h, :L], on[:L, h * 128:(h + 1) * 128], ident[:L, :L], [on, cst], [bank], inc=(h == 3))
            A(z3[:, :, :L], z3[:, :, :L], AF.Silu, [ztg], [ztg])
            STT(mix[:, 8:12, :L], b3[:, :, :L], pft[:, 88:89], z3[:, :, :L], ALU.mult, ALU.mult, [bank, pft, ztg], [mixg])

        if MERGE == "sim":
            branches = []
            live = []
            for gen in (gen_gdn, gen_ret, gen_rg):
                g = gen()
                for mark in g:
                    if mark == "pre":
                        break
                live.append(g)
            if smp and l < NL - 1:
                load_win(l + 1)
            for g in live:
                S.rec = []
                for _ in g:
                    pass
                ops, S.rec = S.rec, None
                units, cur = [], []
                for o in ops:
                    cur.append(o)
                    if o[5]:
                        units.append(cur)
                        cur = []
                assert not cur
                branches.append(units)
            clock = {}
            wr = {}
            rd = {}
            ptr = [0] * len(branches)
            HOP = 0.3
            while True:
                best = None
                for bi, units in enumerate(branches):
                    if ptr[bi] >= len(units):
                        continue
                    u = units[ptr[bi]]
                    e = u[0][1]
                    t = clock.get(e, 0.0)
                    for o in u:
                        for k in o[3]:
                            if k in wr:
                                t = max(t, wr[k][0] + (HOP if wr[k][1] != e else 0.0))
                        for k in o[4]:
                            if k in wr:
                                t = max(t, wr[k][0] + (HOP if wr[k][1] != e else 0.0))
                            if k in rd:
                                t = max(t, rd[k][0] + (HOP if rd[k][1] != e else 0.0))
                    if best is None or t < best[0] - 1e-9:
                        best = (t, bi)
                if best is None:
                    break
                t, bi = best
                u = branches[bi][ptr[bi]]
                ptr[bi] += 1
                e = u[0][1]
                for o in u:
                    kind, eng, fn, r_, w_, inc, cost = o
                    if kind == "dma":
                        out_, in_, kw = fn
                        S.dma(eng, out_, in_, r=r_, w=w_, **kw)
                        done = t + 2.5
                        t += cost
                        who = "dma"
                    else:
                        S.op(eng, fn, r_, w_, inc=inc)
                        t += cost
                        done = t
                        who = eng
                    for k in r_:
                        if k not in rd or rd[k][0] < done:
                            rd[k] = (done, who)
                    for k in w_:
                        wr[k] = (done, who)
                        rd.pop(k, None)
                clock[e] = t
            gens = []
        else:
            gens = [(gen_rg(), 1), (gen_ret(), 1), (gen_gdn(), GDN_W)]
        if MERGE == "seq":
            for g, _ in gens:
                for _ in g:
                    pass
            gens = []
        while gens:
            for item in list(gens):
                g, wgt = item
                for _ in range(wgt):
                    try:
                        next(g)
                    except StopIteration:
                        gens.remove(item)
                        break

        z = [RT[0], RT[1]]
        for n in range(2):
            bank = ps[n]
            for kc in range(12):
                MM(bank[:L, :], mix[:, kc, :L], Wout[:, kc, n * 512:(n + 1) * 512], kc == 0, kc == 11,
                   [mixr, mixe, mixg, Wout], [bank], inc=(kc == 11))
            STT(z[n][:L, :], xsrc[:L, n * 512:(n + 1) * 512], float(ALPHA), bank[:L, :], ALU.mult, ALU.add, [xsrc, bank],
                [z[n]])
            S.op("dve", lambda n=n: nc.vector.bn_stats(out=bst2[:L, n, :], in_=z[n][:L, :]), [z[n]], [bst2])
        S.op("dve", lambda: nc.vector.bn_aggr(out=bmv2[:L, :], in_=bst2[:L, 0:2, :]), [bst2], [bmv2])
        RSQ(sm2[:L, 0:1], bmv2[:L, 1:2], [bmv2], [sm2])
        for n in range(2):
            sl = slice(n * 512, (n + 1) * 512)
            TS(z[n][:L, :], z[n][:L, :], bmv2[:L, 0:1], sm2[:L, 0:1], ALU.subtract, ALU.mult, [z[n], bmv2, sm2], [z[n]])
            TT(z[n][:L, :], z[n][:L, :], rowt[:L, sl], ALU.mult, [z[n], rowt], [z[n]])
            TT(z[n][:L, :], z[n][:L, :], rowt[:L, 1024 + n * 512:1024 + (n + 1) * 512], ALU.add, [z[n], rowt], [z[n]])
            if smp:
                if l == NL - 1:
                    S.dma("pool", y_s[:, sl], z[n][:NS, :], r=[z[n]], stream="o")
                else:
                    S.dma("pool", xsscr[:, sl], z[n][:NS, :], r=[z[n]], w=[("xsscr", 0)], sname=f"xs{n}_{l % 2}")
            else:
                if l == NL - 1:
                    if b > 0:
                        S.dma("pool", y_p[t0 - 16:t0 - 16 + L, sl], z[n][:L, :], r=[z[n]], stream="o")
                else:
                    S.dma("pool", xscr[t0:t0 + L, sl], z[n][:L, :], r=[z[n]], w=[("xscr", b)], sname=f"xo{n}_{l % 2}")


    for l in range(NL):
        layer_setup(l)
        for b in range(NBLK):
            block(l, "p", b)
        if not SKIP_SAMPLE:
            block(l, "s", 0)
    S.finish("sp")
    print("ops", S.nops, "waits", S.nwaits, "sems", S.nsem + len(S.dstream))
    return nc


def _consts():
    cst = np.zeros((128, NCST), np.float32)
    i = np.arange(128)
    cst[:, 0:128] = np.eye(128)
    cst[:, 128:256] = (i[None, :] >= i[:, None])
    cst[:, 256:384] = (i[None, :] < i[:, None])
    cst[:, 384:512] = 1.0
    sc = 128.0 ** -0.5
    for h in range(4):
        g = np.float64(GAM[h])
        cst[:, 512 + h] = g ** (i + 1.0)
        cst[:, 516 + h] = g ** (-(i + 1.0)) * sc
        cst[:, 520 + h] = g ** (127.0 - i) * sc
        cst[:16, 524 + h] = g ** (15.0 - i[:16]) * sc
        cst[:, 528 + h] = g
        cst[:, 532 + h] = sc / g
        cst[:, 536 + h] = sc
    half = 64
    inv = (np.float32(10000.0) ** (-np.arange(half, dtype=np.float32) / np.float32(half))).astype(np.float32)
    rope = np.zeros((18, 128, 128), np.float32)
    for b in range(18):
        if b == 0:
            pos = np.arange(16, dtype=np.float32)
        elif b < 17:
            pos = 16 + 128 * (b - 1) + np.arange(128, dtype=np.float32)
        else:
            pos = np.full(16, 16384.0, np.float32)
        ang = (pos[:, None].astype(np.float32) * inv[None, :]).astype(np.float32)
        rope[b, :len(pos), 0:64] = np.cos(ang.astype(np.float64))
        rope[b, :len(pos), 64:128] = np.sin(ang.astype(np.float64))
    esel = np.eye(16, dtype=np.float32).reshape(1, 256)
    return cst, rope, esel


_NC_CACHE = {}


def kernel(x_prompt, x_sample, state_rglru_h, state_rglru_conv, state_ret, state_gdn_conv, state_gdn,
           meta_tokens, w_in, rg_conv_w, rg_conv_b, rg_w_a, rg_b_a, rg_w_x, rg_b_x, rg_lambda,
           ret_gn_w, ret_gn_b, gdn_conv_w, gdn_a_log, gdn_dt_bias, gdn_norm_w, w_out, ln_w, ln_b):
    f = lambda a: np.ascontiguousarray(np.asarray(a, dtype=np.float32))
    x_prompt, x_sample, meta_tokens = f(x_prompt), f(x_sample), f(meta_tokens)
    w_in, w_out = f(w_in), f(w_out)
    pf = np.zeros((NL, 128, NPF), np.float32)

    def fm(v, nch):
        return f(v).reshape(NL, nch, 128).transpose(0, 2, 1)

    pf[:, :, 0:16] = f(rg_conv_w).reshape(NL, 4, 4, 128).transpose(0, 3, 2, 1).reshape(NL, 128, 16)
    pf[:, :, 16:20] = fm(rg_conv_b, 4)
    pf[:, :, 20:24] = fm(rg_b_a, 4)
    pf[:, :, 24:28] = fm(rg_b_x, 4)
    pf[:, :, 28:32] = fm(rg_lambda, 4)
    pf[:, :, 32:36] = fm(ret_gn_w, 4)
    pf[:, :, 36:40] = fm(ret_gn_b, 4)
    pf[:, :, 40:88] = f(gdn_conv_w).reshape(NL, 4, 12, 128).transpose(0, 3, 2, 1).reshape(NL, 128, 48)
    pf[:, :, 88] = f(gdn_norm_w)
    rgw = np.zeros((NL, 128, 2, 4, 128), np.float32)
    for which, wsrc in ((0, f(rg_w_a)), (1, f(rg_w_x))):
        for n in range(8):
            c, o = n // 2, (n % 2) * 64
            rgw[:, o:o + 64, which, c, o:o + 64] = wsrc[:, n]
    rgw = rgw.reshape(NL, 128, 1024)
    rows = np.concatenate([f(ln_w), f(ln_b), f(gdn_a_log), f(gdn_dt_bias)], axis=1).reshape(NL, 1, 2056)
    cst, rope, esel = _consts()
    if "nc" not in _NC_CACHE:
        _NC_CACHE["nc"] = build_nc()
    nc = _NC_CACHE["nc"]
    in_maps = []
    for c in range(8):
        sl = slice(NS * c, NS * (c + 1))
        m = {
            "xp": np.ascontiguousarray(np.concatenate([meta_tokens, x_prompt[c]], axis=0)),
            "xs": np.ascontiguousarray(x_sample[sl, 0, :]),
            "s_h": np.ascontiguousarray(f(state_rglru_h)[:, sl].reshape(NL, NS, 4, 128).transpose(0, 3, 2, 1)),
            "s_rgc": np.ascontiguousarray(f(state_rglru_conv)[:, sl].reshape(NL, NS, 3, 4, 128).transpose(0, 4, 3, 2, 1)),
            "s_gc": np.ascontiguousarray(f(state_gdn_conv)[:, sl].reshape(NL, NS, 3, 12, 128).transpose(0, 4, 3, 2, 1)),
            "s_ret": np.ascontiguousarray(f(state_ret)[:, sl]),
            "s_gdn": np.ascontiguousarray(f(state_gdn)[:, sl]),
            "w_in": w_in, "w_out": w_out, "pf": pf, "rgw": rgw, "rows": rows,
            "cst": cst, "ropet": rope, "esel": esel,
        }
        in_maps.append(m)
    res = run_bass_kernel_spmd(nc, in_maps, core_ids=list(range(8)))
    R = res.results
    g = lambda k: [np.asarray(R[c][k], dtype=np.float32) for c in range(8)]
    y_prompt = np.stack(g("y_p"), 0)
    y_sample = np.concatenate(g("y_s"), 0)[:, None, :]
    hp = np.stack([a.transpose(0, 2, 1).reshape(NL, 512) for a in g("o_h_p")], 1)
    rgcp = np.stack([a.transpose(0, 3, 2, 1).reshape(NL, 3, 512) for a in g("o_rgc_p")], 1)
    retp = np.stack(g("o_ret_p"), 1)
    gcp = np.stack([a.transpose(0, 3, 2, 1).reshape(NL, 3, 1536) for a in g("o_gc_p")], 1)
    gdnp = np.stack(g("o_gdn_p"), 1)
    hs = np.concatenate([a.transpose(0, 3, 2, 1).reshape(NL, NS, 512) for a in g("o_h_s")], 1)
    rgcs = np.concatenate([a.transpose(0, 4, 3, 2, 1).reshape(NL, NS, 3, 512) for a in g("o_rgc_s")], 1)
    rets = np.concatenate(g("o_ret_s"), 1)
    gcs = np.concatenate([a.transpose(0, 4, 3, 2, 1).reshape(NL, NS, 3, 1536) for a in g("o_gc_s")], 1)
    gdns = np.concatenate(g("o_gdn_s"), 1)
    c = np.ascontiguousarray
    return (c(y_prompt), c(y_sample), c(hp), c(rgcp), c(retp), c(gcp), c(gdnp), c(hs), c(rgcs), c(rets), c(gcs), c(gdns))
```

```python
import numpy as np
import concourse.bass as bass
import concourse.mybir as mybir
from concourse.bass_utils import run_bass_kernel_spmd

F32 = mybir.dt.float32
BF16 = mybir.dt.bfloat16
ALU = mybir.AluOpType
AF = mybir.ActivationFunctionType
AX = mybir.AxisListType

EPOCH = 12000
DEFCOST = {"pe": 0.2, "act": 0.4, "dve": 0.3, "pool": 0.05, "sp": 0.05}
GDN_STOP = 100000
ENABLE = [True, True, True]
NET = 6
SKIP_SAMPLE = False
PER_TILE_SEMS = True
MERGE = "sim"
SAME_SYNC = True

NL = 4
DM = 1024
DIN = 5128
NTOK = 2064
NBLK = 17
NS = 16
ALPHA = 8.0 ** 0.25
EPS = 1e-6
GAM = [1.0 - 2.0 ** (-5.0 - h) for h in range(4)]
NCST = 540
NPF = 89
COL = dict(rgx=0, rgz=512, rq=1024, rk=1536, rv=2048, rz=2560, gq=3072, gk=3584, gv=4096, gz=4608, gab=5120)


class Tile:
    def __init__(self, nc, name, shape, dtype, psum=False):
        if psum:
            self.h = nc.alloc_psum_tensor("T_" + name, list(shape), dtype)
        else:
            self.h = nc.alloc_sbuf_tensor("T_" + name, list(shape), dtype)
        self.ap = self.h.ap()
        self.name = name

    def __getitem__(self, k):
        return self.ap[k]


class Sched:
    def __init__(self, nc):
        self.nc = nc
        self.eng = {"pe": nc.tensor, "act": nc.scalar, "dve": nc.vector, "pool": nc.gpsimd, "sp": nc.sync}
        self.sem = {}
        self.cnt = {}
        self.pend = {}
        self.nsem = 0
        for e in self.eng:
            self._new_sem(e)
            self.pend[e] = False
        self.lastw = {}
        self.readers = {}
        self.waited = {e: {} for e in self.eng}
        self.dstream = {}
        self.nwaits = 0
        self.nops = 0
        self.rec = None

    def _new_sem(self, e):
        self.sem[e] = self.nc.alloc_semaphore(f"s_{e}_{self.nsem}")
        self.nsem += 1
        self.cnt[e] = 0

    def _deps(self, r, w):
        evs = []
        for k in r:
            if k in self.lastw:
                evs.append(self.lastw[k] + (True,))
        for k in w:
            if k in self.lastw:
                evs.append(self.lastw[k] + (False,))
            evs.extend(v + (False,) for v in self.readers.get(k, {}).values())
        return evs

    def _do_waits(self, e, evs):
        need = {}
        for sem, val, src, raw in evs:
            if src == e and not (SAME_SYNC or raw):
                continue
            if src.startswith("dma:"):
                val = self.dstream[src[4:]][1]
            if val > need.get(sem, (0, None))[0]:
                need[sem] = (val, src)
        for sem, (val, src) in need.items():
            if self.waited[e].get(sem, 0) >= val:
                continue
            if src == e and sem is self.sem[e] and val > self.cnt[e]:
                continue
            self.eng[e].wait_ge(sem, val)
            self.waited[e][sem] = val
            self.nwaits += 1

    def _register(self, ev, r, w):
        sem = ev[0]
        for k in r:
            self.readers.setdefault(k, {})[sem] = ev
        for k in w:
            self.lastw[k] = ev
            self.readers[k] = {}

    def op(self, e, fn, r=(), w=(), inc=True, cost=None):
        if self.rec is not None:
            self.rec.append(("op", e, fn, tuple(r), tuple(w), inc, cost if cost else DEFCOST[e]))
            return None
        self._do_waits(e, self._deps(r, w))
        if self.cnt[e] >= EPOCH and not self.pend[e]:
            self._new_sem(e)
        ins = fn()
        self.nops += 1
        if inc:
            self.cnt[e] += 1
            ins.then_inc(self.sem[e], 1)
            ev = (self.sem[e], self.cnt[e], e)
            self.pend[e] = False
        else:
            ev = (self.sem[e], self.cnt[e] + 1, e)
            self.pend[e] = True
        self._register(ev, r, w)
        return ins

    def dma(self, q, out, in_, r=(), w=(), stream="d", sname=None, **kw):
        if self.rec is not None:
            self.rec.append(("dma", q, (out, in_, dict(kw, sname=sname)), tuple(r), tuple(w), True, 0.05))
            return None
        tl = [k for k in w if isinstance(k, Tile)]
        if sname is not None:
            stream = sname
        elif not PER_TILE_SEMS:
            pass
        elif tl:
            stream = "ld_" + tl[0].name
        else:
            stream = "st_" + [k for k in r if isinstance(k, Tile)][0].name
        self._do_waits(q, self._deps(r, w))
        if stream not in self.dstream:
            self.dstream[stream] = [self.nc.alloc_semaphore(f"d_{stream}"), 0]
        st = self.dstream[stream]
        ins = self.eng[q].dma_start(out=out, in_=in_, **kw)
        st[1] += 16
        ins.then_inc(st[0], 16)
        ev = (st[0], st[1], "dma:" + stream)
        self._register(ev, r, w)
        return ins

    def finish(self, e="sp"):
        for name, (sem, tot) in self.dstream.items():
            if tot > 0:
                self.eng[e].wait_ge(sem, tot)


def v3(ap, c=4):
    return ap.rearrange("p (c n) -> p c n", c=c)


def build_nc():
    nc = bass.Bass("TRN2", target_bir_lowering=False)

    def din(name, shape):
        return nc.dram_tensor(name, list(shape), F32, kind="ExternalInput").ap()

    def dout(name, shape):
        return nc.dram_tensor(name, list(shape), F32, kind="ExternalOutput").ap()

    xp_d = din("xp", [NTOK, DM])
    xs_d = din("xs", [NS, DM])
    s_h = din("s_h", [NL, 128, 4, NS])
    s_rgc = din("s_rgc", [NL, 128, 4, 3, NS])
    s_gc = din("s_gc", [NL, 128, 12, 3, NS])
    s_ret = din("s_ret", [NL, NS, 4, 128, 128])
    s_gdn = din("s_gdn", [NL, NS, 4, 128, 128])
    w_in = din("w_in", [NL, DM, DIN])
    w_out = din("w_out", [NL, 1536, DM])
    pf_d = din("pf", [NL, 128, NPF])
    rgw_d = din("rgw", [NL, 128, 2 * 4 * 128])
    rows_d = din("rows", [NL, 1, 2056])
    cst_d = din("cst", [128, NCST])
    rope_d = din("ropet", [18, 128, 128])
    esel_d = din("esel", [1, 256])

    y_p = dout("y_p", [2048, DM])
    y_s = dout("y_s", [NS, DM])
    o_h_p = dout("o_h_p", [NL, 128, 4])
    o_rgc_p = dout("o_rgc_p", [NL, 128, 4, 3])
    o_ret_p = dout("o_ret_p", [NL, 4, 128, 128])
    o_gc_p = dout("o_gc_p", [NL, 128, 12, 3])
    o_gdn_p = dout("o_gdn_p", [NL, 4, 128, 128])
    o_h_s = dout("o_h_s", [NL, 128, 4, NS])
    o_rgc_s = dout("o_rgc_s", [NL, 128, 4, 3, NS])
    o_ret_s = dout("o_ret_s", [NL, NS, 4, 128, 128])
    o_gc_s = dout("o_gc_s", [NL, 128, 12, 3, NS])
    o_gdn_s = dout("o_gdn_s", [NL, NS, 4, 128, 128])
    xscr = nc.dram_tensor("xscr", [NTOK, DM], F32, kind="Internal").ap()
    xsscr = nc.dram_tensor("xsscr", [NS, DM], F32, kind="Internal").ap()

    S = Sched(nc)

    def TL(name, shape, dt=F32):
        return Tile(nc, name, shape, dt)

    Win = TL("Win", [128, 8, DIN], BF16)
    Wout = TL("Wout", [128, 12, DM], BF16)
    cst = TL("cst", [128, NCST])
    identb = TL("identb", [128, 128], BF16)
    esel = TL("esel", [128, 16, 16])
    pft = TL("pft", [128, NPF])
    rgwt = TL("rgwt", [128, 2, 4, 128], BF16)
    rowt = TL("rowt", [128, 2056])
    nc8sp = TL("nc8sp", [128, 4])
    negA = TL("negA", [128, 4])
    ropeb = TL("ropeb", [128, 128])
    X = TL("X", [128, 4, 131])
    GX = TL("GX", [128, 12, 131])
    h0s = TL("h0s", [128, 4, NS])
    Sret = TL("Sret", [128, 512])
    Sretb = TL("Sretb", [128, 512], BF16)
    Sgdn = TL("Sgdn", [128, 512])
    hprev = TL("hprev", [128, 4])
    mix = TL("mix", [128, 12, 128], BF16)
    xt = TL("xt", [128, DM])
    xT = TL("xT", [128, 8, 128], BF16)
    ztr = TL("ztr", [128, 512])
    zte = TL("zte", [128, 512])
    ztg = TL("ztg", [128, 512])
    xcb = TL("xcb", [128, 512], BF16)
    mixr, mixe, mixg = "mixr", "mixe", "mixg"
    Vb = TL("Vb", [128, 512])
    Kbg = TL("Kbg", [128, 512])
    kd = TL("kd", [128, 512])
    qgT = TL("qgT", [128, 512])
    attT = TL("attT", [128, 512])
    Y = TL("Y", [128, 512])
    gabt = TL("gabt", [128, 8])
    gt = TL("gt", [128, 4])
    betat = TL("betat", [128, 4])
    gct = TL("gct", [128, 4])
    egt = TL("egt", [128, 4])
    eglt = TL("eglt", [128, 4])
    eglast = TL("eglast", [128, 4])
    sm1 = TL("sm1", [128, 4])
    sm2 = TL("sm2", [128, 4])
    bst = TL("bst", [128, 4, 6])
    bmv = TL("bmv", [128, 4, 2])
    ebs = TL("ebs", [128, 4, NS])
    vb = TL("vb", [128, 512], BF16)
    k2b = TL("k2b", [128, 512], BF16)
    sm3 = TL("sm3", [128, 4])
    ngct = TL("ngct", [128, 4])
    bst2 = TL("bst2", [128, 2, 6])
    bmv2 = TL("bmv2", [128, 2])
    RT = [TL(f"rt{i}", [128, 512]) for i in range(4)]
    ET = [TL(f"et{i}", [128, 512]) for i in range(NET)]
    GT = [TL(f"gt{i}", [128, 512]) for i in range(9)]
    B = [TL(f"bb{i}", [128, 1024], BF16) for i in range(3)]
    ps = [Tile(nc, f"ps{i}", [128, 512], F32, psum=True) for i in range(8)]
    print("sbuf bytes remaining", nc.sbuf_bytes_remaining)
    GDN_W = 2

    def fs(ap):
        n = 1
        for d in ap.shape[1:]:
            n *= d
        return n

    def A(out, in_, func, r, w, bias=0.0, scale=1.0):
        S.op("act", lambda: nc.scalar.activation(out=out, in_=in_, func=func, bias=bias, scale=scale), r, w,
             cost=0.22 + fs(out) / 1000.0)

    def TT(out, a, b, op, r, w):
        S.op("dve", lambda: nc.vector.tensor_tensor(out=out, in0=a, in1=b, op=op), r, w, cost=0.2 + fs(out) / 1000.0)

    def TS(out, a, s1, s2, op0, op1, r, w):
        if s2 is None:
            S.op("dve", lambda: nc.vector.tensor_scalar(out=out, in0=a, scalar1=s1, scalar2=None, op0=op0), r, w,
                 cost=0.2 + fs(out) / 1000.0)
        else:
            S.op("dve", lambda: nc.vector.tensor_scalar(out=out, in0=a, scalar1=s1, scalar2=s2, op0=op0, op1=op1), r, w,
                 cost=0.2 + fs(out) / 1000.0)

    def STT(out, a, s, b, op0, op1, r, w):
        S.op("dve", lambda: nc.vector.scalar_tensor_tensor(out=out, in0=a, scalar=s, in1=b, op0=op0, op1=op1), r, w,
             cost=0.2 + fs(out) / 1000.0)

    def CP(out, in_, r, w, e="dve"):
        if e == "dve":
            S.op("dve", lambda: nc.vector.tensor_copy(out=out, in_=in_), r, w, cost=0.2 + fs(out) / 1000.0)
        else:
            S.op("act", lambda: nc.scalar.activation(out=out, in_=in_, func=AF.Copy), r, w, cost=0.22 + fs(out) / 1000.0)

    def MS(out, val, w):
        S.op("dve", lambda: nc.vector.memset(out, val), (), w)

    def MM(out, lhsT, rhs, st, sp, r, w, inc=True):
        S.op("pe", lambda: nc.tensor.matmul(out, lhsT=lhsT, rhs=rhs, start=st, stop=sp), r, w, inc=inc,
             cost=(0.11 + fs(out) / 1200.0) * (2.0 if lhsT.dtype == F32 else 1.0))

    def TR(out, in_, ident, r, w, inc=True):
        S.op("pe", lambda: nc.tensor.transpose(out=out, in_=in_, identity=ident), r, w, inc=inc,
             cost=(0.11 + fs(out) / 1200.0) * (2.0 if in_.dtype == F32 else 1.0))

    def RSQ(out, in_, r, w, bias=EPS, scale=1.0):
        A(out, in_, AF.Sqrt, r, w, bias=bias, scale=scale)
        S.op("dve", lambda: nc.vector.reciprocal(out=out, in_=out), w, w)

    ident = cst[:, 0:128]
    maskT = cst[:, 128:256]
    strictL = cst[:, 256:384]
    ones = cst[:, 384:512]

    S.dma("sp", cst[:], cst_d, w=[cst], stream="c")
    S.dma("sp", esel[:].rearrange("p a b -> p (a b)"), esel_d.partition_broadcast(128), w=[esel], stream="c")
    CP(identb[:], ident, [cst], [identb], e="act")

    def bc(ap2, n, L):
        return ap2.unsqueeze(2).to_broadcast([L, 4, n])

    def load_win(l):
        for kc in range(8):
            S.dma("pool", Win[:, kc, :], w_in[l, kc * 128:(kc + 1) * 128, :], w=[Win], stream="w")

    def layer_setup(l):
        if l == 0:
            load_win(0)
        for kc in range(12):
            S.dma("pool", Wout[:, kc, :], w_out[l, kc * 128:(kc + 1) * 128, :], w=[Wout], stream="w")
        S.dma("pool", rgwt[:].rearrange("p a c n -> p (a c n)"), rgw_d[l], w=[rgwt], stream="w")
        S.dma("sp", pft[:], pf_d[l], w=[pft], stream="c")
        S.dma("sp", rowt[:], rows_d[l].partition_broadcast(128), w=[rowt], stream="c")
        A(nc8sp[:], pft[:, 28:32], AF.Exp, [pft], [nc8sp], scale=-1.0)
        A(nc8sp[:], nc8sp[:], AF.Ln, [nc8sp], [nc8sp], bias=1.0)
        TS(nc8sp[:], nc8sp[:], -8.0, None, ALU.mult, None, [nc8sp], [nc8sp])
        A(negA[:], rowt[:, 2048:2052], AF.Exp, [rowt], [negA])
        TS(negA[:], negA[:], -1.0, None, ALU.mult, None, [negA], [negA])
        MS(Sret[:], 0.0, [Sret])
        MS(Sretb[:], 0.0, [Sretb])
        MS(Sgdn[:], 0.0, [Sgdn])
        MS(hprev[:], 0.0, [hprev])
        MS(X[:, :, 0:3], 0.0, [X])
        MS(GX[:, :, 0:3], 0.0, [GX])

    def block(l, mode, b):
        smp = mode == "s"
        if smp:
            L = NS
            t0 = 0
            src = xs_d if l == 0 else xsscr
            S.dma("sp", xt[:L, :], src, r=[("xsscr", 0)], w=[xt], stream="x")
        else:
            L = 16 if b == 0 else 128
            t0 = 0 if b == 0 else 16 + 128 * (b - 1)
            src = xp_d if l == 0 else xscr
            S.dma("sp", xt[:L, :], src[t0:t0 + L, :], r=[("xscr", b)], w=[xt], stream="x")
        xsrc = xt
        rb = 17 if smp else b
        S.dma("sp", ropeb[:L, :], rope_d[rb, 0:L, :], w=[ropeb], stream="x")
        mT = ident if smp else maskT
        ci = 528 if smp else 512
        qdec = cst[:L, ci:ci + 4]
        kdecp = cst[:L, ci + 4:ci + 8]
        if smp:
            k2dec = cst[:L, 536:540]
        elif L == 128:
            k2dec = cst[:L, 520:524]
        else:
            k2dec = cst[:L, 524:528]
        Xs = X.ap.rearrange("p c n -> p (c n)")[:, 0:4 * 4 * NS].rearrange("p (c j s) -> p c j s", c=4, j=4)
        GXs = GX.ap.rearrange("p c n -> p (c n)")[:, 0:12 * 4 * NS].rearrange("p (c j s) -> p c j s", c=12, j=4)

        xb = B[0]
        CP(xb[:L, :], xsrc[:L, :], [xsrc], [xb], e="act")
        pt = ps[0]
        ptb = pt.ap.bitcast(BF16)
        for kc in range(8):
            TR(ptb[:, kc * 128:kc * 128 + L], xb[:L, kc * 128:(kc + 1) * 128], identb[:L, :L], [xb, identb], [pt],
               inc=(kc == 7))
        CP(xT[:, :, :L], v3(ptb, 8)[:, :, :L], [pt], [xT])

        def fm_group(bank, c0, dst_ap, dst_key, e="act"):
            b3 = v3(bank.ap)
            for c in range(4):
                for kc in range(8):
                    MM(b3[:, c, :L], Win[:, kc, c0 + c * 128:c0 + (c + 1) * 128], xT[:, kc, :L], kc == 0, kc == 7,
                       [Win, xT], [bank], inc=(c == 3 and kc == 7))
            CP(dst_ap, b3[:, :, :L], [bank], [dst_key], e=e)

        def tm_group(bank, c0, n, dst_ap, dst_key, e="dve"):
            for kc in range(8):
                MM(bank[:L, :n], xT[:, kc, :L], Win[:, kc, c0:c0 + n], kc == 0, kc == 7, [Win, xT], [bank],
                   inc=(kc == 7))
            CP(dst_ap, bank[:L, :n], [bank], [dst_key], e=e)

        def conv_chunk(src_tap, wcol0, c, o, dst_key, src_key, bias_col=None):
            w0 = pft[:, wcol0 + c * 4:wcol0 + c * 4 + 1]
            if bias_col is not None:
                TS(o, src_tap(c, 0), w0, pft[:, bias_col + c:bias_col + c + 1], ALU.mult, ALU.add,
                   [src_key, pft], [dst_key])
            else:
                TS(o, src_tap(c, 0), w0, None, ALU.mult, None, [src_key, pft], [dst_key])
            for j in range(1, 4):
                STT(o, src_tap(c, j), pft[:, wcol0 + c * 4 + j:wcol0 + c * 4 + j + 1], o, ALU.mult, ALU.add,
                    [src_key, pft, dst_key], [dst_key])

        def gen_rg():
            pa, pb = ps[0], ps[1]
            if smp:
                S.dma("sp", Xs[:, :, 0:3, :], s_rgc[l], w=[X], stream="st")
                S.dma("sp", h0s[:], s_h[l], w=[h0s], stream="st")
                fm_group(pa, COL["rgx"], Xs[:, :, 3, :], X)
                tap = lambda c, j: Xs[:, c, j, :]
            else:
                fm_group(pa, COL["rgx"], X[:, :, 3:3 + L], X)
                tap = lambda c, j: X[:, c, j:j + L]
            yield
            z3 = v3(ztr.ap)
            fm_group(pb, COL["rgz"], z3[:, :, :L], ztr, e="dve")
            yield "pre"
            xc = RT[0]
            xc3 = v3(xc.ap)
            for c in range(4):
                conv_chunk(tap, 0, c, xc3[:, c, :L], xc, X, bias_col=16)
                yield
            xcb3 = v3(xcb.ap)
            CP(xcb3[:, :, :L], xc3[:, :, :L], [xc], [xcb], e="act")
            rt = RT[1]; it = RT[2]; at = RT[3]
            r3 = v3(rt.ap); i3 = v3(it.ap); a3 = v3(at.ap)
            for which, dst3, dkey, bcol, bank in ((0, r3, rt, 20, pa), (1, i3, it, 24, pb)):
                b3 = v3(bank.ap)
                for c in range(4):
                    MM(b3[:, c, :L], rgwt[:, which, c, :], xcb3[:, c, :L], True, True, [rgwt, xcb], [bank], inc=(c == 3))
                yield
                for c in range(4):
                    A(dst3[:, c, :L], b3[:, c, :L], AF.Sigmoid, [bank, pft], [dkey], bias=pft[:, bcol + c:bcol + c + 1])
                yield
            for c in range(4):
                A(a3[:, c, :L], r3[:, c, :L], AF.Exp, [rt, nc8sp], [at], scale=nc8sp[:, c:c + 1])
            yield
            mt = RT[1]
            m3 = v3(mt.ap)
            TT(m3[:, :, :L], a3[:, :, :L], a3[:, :, :L], ALU.mult, [at], [mt])
            A(m3[:, :, :L], m3[:, :, :L], AF.Sqrt, [mt], [mt], bias=1.0, scale=-1.0)
            yield
            TT(i3[:, :, :L], i3[:, :, :L], xc3[:, :, :L], ALU.mult, [it, xc], [it])
            yield
            TT(i3[:, :, :L], i3[:, :, :L], m3[:, :, :L], ALU.mult, [it, mt], [it])
            yield
            ht = RT[0]
            h3 = v3(ht.ap)
            if smp:
                TT(h3[:, :, :L], a3[:, :, :L], h0s[:], ALU.mult, [at, h0s], [ht])
                TT(h3[:, :, :L], h3[:, :, :L], i3[:, :, :L], ALU.add, [ht, it], [ht])
                S.dma("pool", o_h_s[l], h3[:, :, :L], r=[ht], stream="o")
                S.dma("pool", o_rgc_s[l], Xs[:, :, 1:4, :], r=[X], stream="o")
            else:
                for c in range(4):
                    S.op("dve", lambda c=c: nc.vector.tensor_tensor_scan(
                        out=h3[:, c, :L], data0=a3[:, c, :L], data1=i3[:, c, :L], initial=hprev[:, c:c + 1],
                        op0=ALU.mult, op1=ALU.add), [at, it, hprev], [ht])
                    yield
                CP(hprev[:].unsqueeze(2), h3[:, :, L - 1:L], [ht], [hprev])
                if b == NBLK - 1:
                    S.dma("pool", o_h_p[l], hprev[:], r=[hprev], stream="o")
                    S.dma("pool", o_rgc_p[l], X[:, :, L:L + 3], r=[X], stream="o")
                CP(X[:, :, 0:3], X[:, :, L:L + 3], [X], [X])
            yield
            A(z3[:, :, :L], z3[:, :, :L], AF.Silu, [ztr], [ztr])
            TT(mix[:, 0:4, :L], h3[:, :, :L], z3[:, :, :L], ALU.mult, [ht, ztr], [mixr])

        def gen_ret():
            bk = [ps[2], ps[3], ps[4]]
            rq = ET[0]; rk = ET[1]
            tm_group(bk[0], COL["rq"], 512, rq[:L, :], rq)
            yield
            tm_group(bk[1], COL["rk"], 512, rk[:L, :], rk)
            yield
            tm_group(bk[2], COL["rv"], 512, vb[:L, 0:512], vb)
            yield
            z3 = v3(zte.ap)
            fm_group(bk[0], COL["rz"], z3[:, :, :L], zte, e="dve")
            yield "pre"
            cosb = ropeb[:L, 0:64].unsqueeze(1).to_broadcast([L, 4, 64])
            sinb = ropeb[:L, 64:128].unsqueeze(1).to_broadcast([L, 4, 64])

            def rope(src, dst, tmp):
                s3 = v3(src[:L, :]); d3 = v3(dst[:L, :]); t3 = v3(tmp[:L, :])
                t1 = s3[:, :, 0:64]; t2 = s3[:, :, 64:128]
                TT(d3[:, :, 0:64], t1, cosb, ALU.mult, [src, ropeb], [dst])
                TT(t3[:, :, 0:64], t2, sinb, ALU.mult, [src, ropeb], [tmp])
                yield
                TT(d3[:, :, 0:64], d3[:, :, 0:64], t3[:, :, 0:64], ALU.subtract, [dst, tmp], [dst])
                TT(d3[:, :, 64:128], t1, sinb, ALU.mult, [src, ropeb], [dst])
                yield
                TT(t3[:, :, 64:128], t2, cosb, ALU.mult, [src, ropeb], [tmp])
                TT(d3[:, :, 64:128], d3[:, :, 64:128], t3[:, :, 64:128], ALU.add, [dst, tmp], [dst])
                yield

            rqr = ET[2]; rkr = ET[4]
            yield from rope(rq, rqr, ET[3])
            yield from rope(rk, rkr, ET[3])
            qkb = B[0]
            TT(v3(qkb[:L, 0:512]), v3(rqr[:L, :]), bc(qdec, 128, L), ALU.mult, [rqr, cst], [qkb])
            TT(v3(qkb[:L, 512:1024]), v3(rkr[:L, :]), bc(kdecp, 128, L), ALU.mult, [rkr, cst], [qkb])
            yield
            if smp:
                k2 = ET[5]
                TT(v3(k2[:L, :]), v3(rkr[:L, :]), bc(k2dec, 128, L), ALU.mult, [rkr, cst], [k2])
            else:
                k2 = k2b
                TT(v3(k2[:L, 0:512]), v3(rkr[:L, :]), bc(k2dec, 128, L), ALU.mult, [rkr, cst], [k2])
            yield
            pt = bk[1]
            ptb = pt.ap.bitcast(BF16)
            for j in range(8):
                TR(ptb[:, j * 128:j * 128 + L], qkb[:L, j * 128:(j + 1) * 128], identb[:L, :L], [qkb, identb], [pt],
                   inc=(j == 7))
            qkT = B[1]
            qkT3 = v3(qkT.ap, 8)
            CP(qkT3[:, :, :L], v3(ptb, 8)[:, :, :L], [pt], [qkT], e="act")
            yield
            bank = bk[2]
            b3 = v3(bank.ap)
            for h in range(4):
                MM(b3[:L, h, :L], qkT3[:, 4 + h, :L], qkT3[:, h, :L], True, True, [qkT], [bank], inc=(h == 3))
            scb = B[2]
            sc3 = v3(scb[:, 0:512])
            TT(sc3[:L, :, :L], b3[:L, :, :L], mT[:L, :L].unsqueeze(1).to_broadcast([L, 4, L]), ALU.mult, [bank, cst], [scb])
            yield
            ob = bk[0]
            ob3 = v3(ob.ap)
            for h in range(4):
                MM(ob3[:L, h, :], sc3[:L, h, :L], vb[:L, h * 128:(h + 1) * 128], True, smp, [scb, vb], [ob],
                   inc=(smp and h == 3))
                if not smp:
                    MM(ob3[:L, h, :], qkT3[:, h, :L], v3(Sretb.ap)[:, h, :], False, True, [qkT, Sretb], [ob], inc=(h == 3))
            yield
            if not smp:
                sb = bk[1]
                sb3 = v3(sb.ap)
                for h in range(4):
                    MM(sb3[:, h, :], k2[:L, h * 128:(h + 1) * 128], vb[:L, h * 128:(h + 1) * 128], True, True, [k2, vb], [sb],
                       inc=(h == 3))
                yield
                for h in range(4):
                    STT(v3(Sret.ap)[:, h, :], v3(Sret.ap)[:, h, :], float(GAM[h] ** L), sb3[:, h, :], ALU.mult, ALU.add,
                        [Sret, sb], [Sret])
                yield
                CP(Sretb[:], Sret[:], [Sret], [Sretb], e="act")
                if b == NBLK - 1:
                    S.dma("pool", o_ret_p[l].rearrange("h d v -> d h v"), v3(Sret.ap), r=[Sret], stream="o")
                osrc, okey = ob3, ob
            else:
                oacc = ET[1]
                CP(oacc[:L, :], ob[:L, :], [ob], [oacc])
                xTf = xT.ap.rearrange("p a b -> p (a b)").bitcast(F32)
                for s in range(NS):
                    St = (ET[2], Sret)[s % 2]
                    S.dma("sp", v3(St.ap), s_ret[l, s].rearrange("h d v -> d h v"), w=[St], stream="st")
                    qm = ET[3]
                    TT(v3(qm[:, 0:64], 4), qkT3[:, 0:4, :NS], esel[:, s, :].unsqueeze(1).to_broadcast([128, 4, NS]),
                       ALU.mult, [qkT, esel], [qm])
                    tb_ = bk[1]
                    tb3 = v3(tb_.ap)
                    for h in range(4):
                        MM(tb3[:NS, h, :], v3(qm[:, 0:64], 4)[:, h, :], v3(St.ap)[:, h, :], True, True, [qm, St], [tb_],
                           inc=(h == 3))
                    TT(oacc[:L, :], oacc[:L, :], tb_[:NS, :], ALU.add, [oacc, tb_], [oacc])
                    yield
                    vm = ET[4]
                    TT(vm[:NS, :], vb[:NS, 0:512], ident[:NS, s:s + 1].to_broadcast([NS, 512]), ALU.mult, [vb, cst], [vm])
                    sb = bk[2]
                    sb3 = v3(sb.ap)
                    for h in range(4):
                        MM(sb3[:, h, :], k2[:NS, h * 128:(h + 1) * 128], vm[:NS, h * 128:(h + 1) * 128], True, True,
                           [k2, vm], [sb], inc=(h == 3))
                    So, So3 = ((ET[0], v3(ET[0].ap)), (xT, v3(xTf)))[s % 2]
                    for h in range(4):
                        STT(So3[:, h, :], v3(St.ap)[:, h, :], float(GAM[h]), sb3[:, h, :], ALU.mult, ALU.add,
                            [St, sb], [So])
                    S.dma("pool", o_ret_s[l, s].rearrange("h d v -> d h v"), So3, r=[So], stream="o")
                    yield
                osrc, okey = v3(oacc.ap), oacc
            for h in range(4):
                S.op("dve", lambda h=h: nc.vector.bn_stats(out=bst[:L, h, :], in_=osrc[:L, h, :]), [okey], [bst])
            yield
            for h in range(4):
                S.op("dve", lambda h=h: nc.vector.bn_aggr(out=bmv[:L, h, :], in_=bst[:L, h, :]), [bst], [bmv])
            RSQ(sm1[:L, :], bmv[:L, :, 1], [bmv], [sm1])
            yield
            onb = B[2]
            for h in range(4):
                TS(onb[:L, 512 + h * 128:512 + (h + 1) * 128], osrc[:L, h, :], bmv[:L, h, 0:1], sm1[:L, h:h + 1],
                   ALU.subtract, ALU.mult, [okey, bmv, sm1], [onb])
            yield
            pt = bk[1]
            ptb = pt.ap.bitcast(BF16)
            for h in range(4):
                TR(ptb[:, h * 128:h * 128 + L], onb[:L, 512 + h * 128:512 + (h + 1) * 128], identb[:L, :L], [onb, identb],
                   [pt], inc=(h == 3))
            yt = ET[0]
            y3 = v3(yt.ap)
            for h in range(4):
                A(y3[:, h, :L], ptb[:, h * 128:h * 128 + L], AF.Identity, [pt, pft], [yt], bias=pft[:, 36 + h:37 + h],
                  scale=pft[:, 32 + h:33 + h])
            yield
            A(z3[:, :, :L], z3[:, :, :L], AF.Silu, [zte], [zte])
            TT(mix[:, 4:8, :L], y3[:, :, :L], z3[:, :, :L], ALU.mult, [yt, zte], [mixe])

        def gen_gdn():
            bk = [ps[5], ps[6], ps[7]]
            nb = [0]

            def PB():
                nb[0] = (nb[0] + 1) % 3
                return bk[nb[0]]

            if smp:
                S.dma("sp", GXs[:, :, 0:3, :], s_gc[l], w=[GX], stream="st")
                for g in range(3):
                    fm_group(PB(), COL["gq"] + 512 * g, GXs[:, 4 * g:4 * g + 4, 3, :], GX)
                    yield
                gtap = lambda c, j: GXs[:, c, j, :]
            else:
                for g in range(3):
                    fm_group(PB(), COL["gq"] + 512 * g, GX[:, 4 * g:4 * g + 4, 3:3 + L], GX)
                    yield
                gtap = lambda c, j: GX[:, c, j:j + L]
            tm_group(PB(), COL["gab"], 8, gabt[:L, :], gabt)
            z3 = v3(ztg.ap)
            fm_group(PB(), COL["gz"], z3[:, :, :L], ztg, e="dve")
            yield "pre"
            TT(gt[:L, :], gabt[:L, 0:4], rowt[:L, 2052:2056], ALU.add, [gabt, rowt], [gt])
            A(gt[:L, :], gt[:L, :], AF.Exp, [gt], [gt])
            A(gt[:L, :], gt[:L, :], AF.Ln, [gt], [gt], bias=1.0)
            TT(gt[:L, :], gt[:L, :], negA[:L, :], ALU.mult, [gt, negA], [gt])
            A(betat[:L, :], gabt[:L, 4:8], AF.Sigmoid, [gabt], [betat])
            yield
            bank = PB()
            MM(bank[:L, 0:4], mT[:L, :L], gt[:L, :], True, True, [cst, gt], [bank])
            CP(gct[:L, :], bank[:L, 0:4], [bank], [gct])
            A(egt[:L, :], gct[:L, :], AF.Exp, [gct], [egt])
            yield
            Rt = GT[5]
            R3 = v3(Rt.ap)
            for h in range(4):
                A(R3[:L, h, :L], mT[:L, :L], AF.Copy, [cst, gt], [Rt], scale=gt[:L, h:h + 1])
            TS(ngct[:L, :], gct[:L, :], -1.0, None, ALU.mult, None, [gct], [ngct])
            yield
            gcB = PB()
            g3 = v3(gcB.ap)
            for h in range(4):
                MM(g3[:, h, :L], ones[:L, :], R3[:L, h, :L], True, True, [cst, Rt], [gcB], inc=(h == 3))
            EB = GT[6]
            EB3 = v3(EB.ap)
            A(EB3[:, :, :L], g3[:, :, :L], AF.Exp, [gcB], [EB])
            yield
            dT = GT[7]
            dT3 = v3(dT.ap)
            for h in range(4):
                A(dT3[:L, h, :L], g3[:L, h, :L], AF.Relu, [gcB, gct], [dT], bias=gct[:L, h:h + 1], scale=-1.0)
            yield
            if not smp:
                dl = GT[8]
                dl3 = v3(dl.ap)
                for h in range(4):
                    A(dl3[:L, h, :L], g3[:L, h, :L], AF.Relu, [gcB, ngct], [dl], bias=ngct[:L, h:h + 1], scale=1.0)
                yield
                CP(eglast[:].unsqueeze(2), EB3[:, :, L - 1:L], [EB], [eglast])
                for h in range(4):
                    A(eglt[:L, h:h + 1], g3[:L, h, L - 1:L], AF.Exp, [gcB, ngct], [eglt], bias=ngct[:L, h:h + 1])
                A(dl3[:L, :, :L], dl3[:L, :, :L], AF.Exp, [dl], [dl], scale=-1.0)
                TT(dl3[:L, :, :L], dl3[:L, :, :L], strictL[:L, :L].unsqueeze(1).to_broadcast([L, 4, L]), ALU.mult,
                   [dl, cst], [dl])
                yield
            else:
                CP(ebs[:], EB3[:, :, :NS], [EB], [ebs])
            A(dT3[:L, :, :L], dT3[:L, :, :L], AF.Exp, [dT], [dT], scale=-1.0)
            TT(dT3[:L, :, :L], dT3[:L, :, :L], mT[:L, :L].unsqueeze(1).to_broadcast([L, 4, L]), ALU.mult, [dT, cst], [dT])
            yield
            cq = GT[0]; ck = GT[1]; cv = GT[2]
            cqk = [cq, ck, cv]
            for g in range(3):
                cg3 = v3(cqk[g].ap)
                for c in range(4):
                    conv_chunk(lambda c_, j, g=g: gtap(4 * g + c_, j), 40 + 16 * g, c, cg3[:, c, :L], cqk[g], GX)
                    yield
                A(cg3[:, :, :L], cg3[:, :, :L], AF.Silu, [cqk[g]], [cqk[g]])
            if smp:
                S.dma("pool", o_gc_s[l], GXs[:, :, 1:4, :], r=[GX], stream="o")
            else:
                if b == NBLK - 1:
                    S.dma("pool", o_gc_p[l], GX[:, :, L:L + 3], r=[GX], stream="o")
                CP(GX[:, :, 0:3], GX[:, :, L:L + 3], [GX], [GX])
            yield
            for g in range(2):
                cg3 = v3(cqk[g].ap)
                sq = GT[3]
                sq3 = v3(sq.ap)
                A(sq3[:, :, :L], cg3[:, :, :L], AF.Square, [cqk[g]], [sq])
                bank = PB()
                b3 = v3(bank.ap)
                for h in range(4):
                    MM(b3[:, h, :L], ones, sq3[:, h, :L], True, True, [cst, sq], [bank], inc=(h == 3))
                yield
                rn = GT[4]
                rn3 = v3(rn.ap)
                RSQ(rn3[:, :, :L], b3[:, :, :L], [bank], [rn])
                if g == 0:
                    STT(cg3[:, :, :L], cg3[:, :, :L], float(128 ** -0.5), rn3[:, :, :L], ALU.mult, ALU.mult, [cq, rn], [cq])
                else:
                    TT(cg3[:, :, :L], cg3[:, :, :L], rn3[:, :, :L], ALU.mult, [ck, rn], [ck])
                yield
            q3 = v3(cq.ap); k3 = v3(ck.ap); cv3 = v3(cv.ap)
            TT(v3(qgT.ap)[:, :, :L], q3[:, :, :L], EB3[:, :, :L], ALU.mult, [cq, EB], [qgT])
            bank = PB()
            b3 = v3(bank.ap)
            for h in range(4):
                MM(b3[:L, h, :L], k3[:, h, :L], q3[:, h, :L], True, True, [ck, cq], [bank], inc=(h == 3))
            at3 = v3(attT.ap)
            TT(at3[:L, :, :L], b3[:L, :, :L], dT3[:L, :, :L], ALU.mult, [bank, dT], [attT])
            yield
            kTM = GT[3]; vTM = GT[4]
            for srcT, s3_, dstT in ((ck, k3, kTM), (cv, cv3, vTM)):
                bank = PB()
                for h in range(4):
                    TR(bank[:L, h * 128:(h + 1) * 128], s3_[:, h, :L], ident, [srcT, cst], [bank], inc=(h == 3))
                CP(dstT[:L, :], bank[:L, :], [bank], [dstT], e="act")
                yield
            if not smp:
                TT(v3(kd[:L, :]), v3(kTM[:L, :]), bc(eglt[:L, :], 128, L), ALU.mult, [kTM, eglt], [kd])
            else:
                CP(kd[:L, :], kTM[:L, :], [kTM], [kd])
            TT(v3(Vb[:L, :]), v3(vTM[:L, :]), bc(betat[:L, :], 128, L), ALU.mult, [vTM, betat], [Vb])
            yield
            TT(sm2[:L, :], betat[:L, :], egt[:L, :], ALU.mult, [betat, egt], [sm2])
            TT(v3(Kbg[:L, :]), v3(kTM[:L, :]), bc(sm2[:L, :], 128, L), ALU.mult, [kTM, sm2], [Kbg])
            yield
            Y3 = v3(Y.ap)
            if smp:
                CP(Y3[:L, :, :L], ident[:L, :L].unsqueeze(1).to_broadcast([L, 4, L]), [cst], [Y])
            else:
                bank = PB()
                b3 = v3(bank.ap)
                for h in range(4):
                    MM(b3[:L, h, :L], k3[:, h, :L], k3[:, h, :L], True, True, [ck], [bank], inc=(h == 3))
                P = GT[2]
                P3 = v3(P.ap)
                for h in range(4):
                    STT(P3[:L, h, :L], b3[:L, h, :L], betat[:L, h:h + 1], dl3[:L, h, :L], ALU.mult, ALU.mult,
                        [bank, betat, dl], [P])
                yield
                bank = PB()
                b3 = v3(bank.ap)
                for h in range(4):
                    TR(b3[:L, h, :L], P3[:L, h, :L], ident[:L, :L], [P, cst], [bank], inc=(h == 3))
                Q = GT[5]
                Q3 = v3(Q.ap)
                CP(Q3[:L, :, :L], b3[:L, :, :L], [bank], [Q], e="act")
                STT(Y3[:L, :, :L], Q3[:L, :, :L], -1.0, ident[:L, :L].unsqueeze(1).to_broadcast([L, 4, L]), ALU.mult, ALU.add,
                    [Q, cst], [Y])
                yield
                nlev = 6 if L == 128 else 3
                for lev in range(nlev):
                    bq = PB(); bp = PB()
                    bq3 = v3(bq.ap); bp3 = v3(bp.ap)
                    for h in range(4):
                        MM(bq3[:L, h, :L], P3[:L, h, :L], Q3[:L, h, :L], True, True, [P, Q], [bq], inc=(h == 3))
                    for h in range(4):
                        MM(bp3[:L, h, :L], Q3[:L, h, :L], P3[:L, h, :L], True, True, [P, Q], [bp], inc=(h == 3))
                    yield
                    Pn, Qn = (GT[3], GT[4]) if lev % 2 == 0 else (GT[2], GT[5])
                    CP(v3(Qn.ap)[:L, :, :L], bq3[:L, :, :L], [bq], [Qn], e="act")
                    CP(v3(Pn.ap)[:L, :, :L], bp3[:L, :, :L], [bp], [Pn], e="dve")
                    yield
                    P, Q = Pn, Qn
                    P3, Q3 = v3(P.ap), v3(Q.ap)
                    by = PB()
                    by3 = v3(by.ap)
                    for h in range(4):
                        MM(by3[:L, h, :L], P3[:L, h, :L], Y3[:L, h, :L], True, True, [P, Y], [by], inc=(h == 3))
                    TT(Y3[:L, :, :L], Y3[:L, :, :L], by3[:L, :, :L], ALU.add, [Y, by], [Y])
                    yield
            bank = PB()
            b3 = v3(bank.ap)
            for h in range(4):
                MM(b3[:, h, :L], Kbg[:L, h * 128:(h + 1) * 128], Y3[:L, h, :L], True, True, [Kbg, Y], [bank], inc=(h == 3))
            nWT = GT[6]
            nW3 = v3(nWT.ap)
            A(nW3[:, :, :L], b3[:, :, :L], AF.Copy, [bank], [nWT], scale=-1.0)
            yield
            Sg3 = v3(Sgdn.ap)
            vnb = PB()
            vn3 = v3(vnb.ap)
            for h in range(4):
                MM(vn3[:L, h, :], Y3[:L, h, :L], Vb[:L, h * 128:(h + 1) * 128], True, smp, [Y, Vb], [vnb],
                   inc=(smp and h == 3))
                if not smp:
                    MM(vn3[:L, h, :], nW3[:, h, :L], Sg3[:, h, :], False, True, [nWT, Sgdn], [vnb], inc=(h == 3))
            vnew = GT[7]
            if not smp:
                CP(vnew[:L, :], vnb[:L, :], [vnb], [vnew], e="act")
                yield
            else:
                wacc = GT[0]; qacc = GT[1]
                CP(wacc[:L, :], vnb[:L, :], [vnb], [wacc], e="act")
                MS(qacc[:L, :], 0.0, [qacc])
                for s in range(NS):
                    St = (GT[2], Sgdn)[s % 2]
                    S.dma("sp", v3(St.ap), s_gdn[l, s].rearrange("h d v -> d h v"), w=[St], stream="st")
                    for srcT, s3_, acc in ((qgT, v3(qgT.ap), qacc), (nWT, nW3, wacc)):
                        qm = GT[3]
                        TT(v3(qm[:, 0:64], 4), s3_[:, :, :NS], esel[:, s, :].unsqueeze(1).to_broadcast([128, 4, NS]),
                           ALU.mult, [srcT, esel], [qm])
                        tb_ = PB()
                        tb3 = v3(tb_.ap)
                        for h in range(4):
                            MM(tb3[:NS, h, :], v3(qm[:, 0:64], 4)[:, h, :], v3(St.ap)[:, h, :], True, True, [qm, St], [tb_],
                               inc=(h == 3))
                        TT(acc[:L, :], acc[:L, :], tb_[:NS, :], ALU.add, [acc, tb_], [acc])
                        yield
                CP(vnew[:L, :], wacc[:L, :], [wacc], [vnew])
            ob = PB()
            ob3 = v3(ob.ap)
            for h in range(4):
                MM(ob3[:L, h, :], at3[:L, h, :L], vnew[:L, h * 128:(h + 1) * 128], True, smp, [attT, vnew], [ob],
                   inc=(smp and h == 3))
                if not smp:
                    MM(ob3[:L, h, :], v3(qgT.ap)[:, h, :L], Sg3[:, h, :], False, True, [qgT, Sgdn], [ob], inc=(h == 3))
            yield
            if smp:
                TT(qacc[:L, :], qacc[:L, :], ob[:L, :], ALU.add, [qacc, ob], [qacc])
                osrc, okey = v3(qacc.ap), qacc
                for s in range(NS):
                    St = (GT[2], Sgdn)[s % 2]
                    S.dma("sp", v3(St.ap), s_gdn[l, s].rearrange("h d v -> d h v"), w=[St], stream="st")
                    vm = GT[3]
                    TS(vm[:NS, :], vnew[:NS, :], ident[:NS, s:s + 1], None, ALU.mult, None, [vnew, cst], [vm])
                    sb = PB()
                    sb3 = v3(sb.ap)
                    for h in range(4):
                        MM(sb3[:, h, :], kd[:NS, h * 128:(h + 1) * 128], vm[:NS, h * 128:(h + 1) * 128], True, True,
                           [kd, vm], [sb], inc=(h == 3))
                    So = (GT[4], GT[8])[s % 2]
                    for h in range(4):
                        STT(v3(So.ap)[:, h, :], v3(St.ap)[:, h, :], ebs[:, h, s:s + 1], sb3[:, h, :], ALU.mult, ALU.add,
                            [St, sb, ebs], [So])
                    S.dma("pool", o_gdn_s[l, s].rearrange("h d v -> d h v"), v3(So.ap), r=[So], stream="o")
                    yield
            else:
                osrc, okey = ob3, ob
                sb = PB()
                sb3 = v3(sb.ap)
                for h in range(4):
                    MM(sb3[:, h, :], kd[:L, h * 128:(h + 1) * 128], vnew[:L, h * 128:(h + 1) * 128], True, True, [kd, vnew], [sb],
                       inc=(h == 3))
                yield
                for h in range(4):
                    STT(Sg3[:, h, :], Sg3[:, h, :], eglast[:, h:h + 1], sb3[:, h, :], ALU.mult, ALU.add,
                        [Sgdn, sb, eglast], [Sgdn])
                if b == NBLK - 1:
                    S.dma("pool", o_gdn_p[l].rearrange("h d v -> d h v"), Sg3, r=[Sgdn], stream="o")
                yield
            osq = GT[8]
            A(v3(osq.ap)[:L, :, :], osrc[:L, :, :], AF.Square, [okey], [osq])
            S.op("dve", lambda: nc.vector.reduce_sum(out=sm3[:L, :], in_=v3(osq.ap)[:L, :, :], axis=AX.X), [osq], [sm3])
            RSQ(sm3[:L, :], sm3[:L, :], [sm3], [sm3], bias=EPS, scale=1.0 / 128.0)
            yield
            on = GT[5]
            for h in range(4):
                TS(on[:L, h * 128:(h + 1) * 128], osrc[:L, h, :], sm3[:L, h:h + 1], None, ALU.mult, None, [okey, sm3], [on])
            yield
            bank = PB()
            b3 = v3(bank.ap)
            for h in range(4):
                TR(b3[:, h, :L], on[:L, h * 128:(h + 1) * 128], ident[:L, :L], [on, cst], [bank], inc=(h == 3))
            A(z3[:, :, :L], z3[:, :, :L], AF.Silu, [ztg], [ztg])
            STT(mix[:, 8:12, :L], b3[:, :, :L], pft[:, 88:89], z3[:, :, :L], ALU.mult, ALU.mult, [bank, pft, ztg], [mixg])

        if MERGE == "sim":
            branches = []
            live = []
            for gen in (gen_gdn, gen_ret, gen_rg):
                g = gen()
                for mark in g:
                    if mark == "pre":
                        break
                live.append(g)
            if smp and l < NL - 1:
                load_win(l + 1)
            for g in live:
                S.rec = []
                for _ in g:
                    pass
                ops, S.rec = S.rec, None
                units, cur = [], []
                for o in ops:
                    cur.append(o)
                    if o[5]:
                        units.append(cur)
                        cur = []
                assert not cur
                branches.append(units)
            clock = {}
            wr = {}
            rd = {}
            ptr = [0] * len(branches)
            HOP = 0.3
            while True:
                best = None
                for bi, units in enumerate(branches):
                    if ptr[bi] >= len(units):
                        continue
                    u = units[ptr[bi]]
                    e = u[0][1]
                    t = clock.get(e, 0.0)
                    for o in u:
                        for k in o[3]:
                            if k in wr:
                                t = max(t, wr[k][0] + (HOP if wr[k][1] != e else 0.0))
                        for k in o[4]:
                            if k in wr:
                                t = max(t, wr[k][0] + (HOP if wr[k][1] != e else 0.0))
                            if k in rd:
                                t = max(t, rd[k][0] + (HOP if rd[k][1] != e else 0.0))
                    if best is None or t < best[0] - 1e-9:
                        best = (t, bi)
                if best is None:
                    break
                t, bi = best
                u = branches[bi][ptr[bi]]
                ptr[bi] += 1
                e = u[0][1]
                for o in u:
                    kind, eng, fn, r_, w_, inc, cost = o
                    if kind == "dma":
                        out_, in_, kw = fn
                        S.dma(eng, out_, in_, r=r_, w=w_, **kw)
                        done = t + 2.5
                        t += cost
                        who = "dma"
                    else:
                        S.op(eng, fn, r_, w_, inc=inc)
                        t += cost
                        done = t
                        who = eng
                    for k in r_:
                        if k not in rd or rd[k][0] < done:
                            rd[k] = (done, who)
                    for k in w_:
                        wr[k] = (done, who)
                        rd.pop(k, None)
                clock[e] = t
            gens = []
        else:
            gens = [(gen_rg(), 1), (gen_ret(), 1), (gen_gdn(), GDN_W)]
        if MERGE == "seq":
            for g, _ in gens:
                for _ in g:
                    pass
            gens = []
        while gens:
            for item in list(gens):
                g, wgt = item
                for _ in range(wgt):
                    try:
                        next(g)
                    except StopIteration:
                        gens.remove(item)
                        break

        z = [RT[0], RT[1]]
        for n in range(2):
            bank = ps[n]
            for kc in range(12):
                MM(bank[:L, :], mix[:, kc, :L], Wout[:, kc, n * 512:(n + 1) * 512], kc == 0, kc == 11,
                   [mixr, mixe, mixg, Wout], [bank], inc=(kc == 11))
            STT(z[n][:L, :], xsrc[:L, n * 512:(n + 1) * 512], float(ALPHA), bank[:L, :], ALU.mult, ALU.add, [xsrc, bank],
                [z[n]])
            S.op("dve", lambda n=n: nc.vector.bn_stats(out=bst2[:L, n, :], in_=z[n][:L, :]), [z[n]], [bst2])
        S.op("dve", lambda: nc.vector.bn_aggr(out=bmv2[:L, :], in_=bst2[:L, 0:2, :]), [bst2], [bmv2])
        RSQ(sm2[:L, 0:1], bmv2[:L, 1:2], [bmv2], [sm2])
        for n in range(2):
            sl = slice(n * 512, (n + 1) * 512)
            TS(z[n][:L, :], z[n][:L, :], bmv2[:L, 0:1], sm2[:L, 0:1], ALU.subtract, ALU.mult, [z[n], bmv2, sm2], [z[n]])
            TT(z[n][:L, :], z[n][:L, :], rowt[:L, sl], ALU.mult, [z[n], rowt], [z[n]])
            TT(z[n][:L, :], z[n][:L, :], rowt[:L, 1024 + n * 512:1024 + (n + 1) * 512], ALU.add, [z[n], rowt], [z[n]])
            if smp:
                if l == NL - 1:
                    S.dma("pool", y_s[:, sl], z[n][:NS, :], r=[z[n]], stream="o")
                else:
                    S.dma("pool", xsscr[:, sl], z[n][:NS, :], r=[z[n]], w=[("xsscr", 0)], sname=f"xs{n}_{l % 2}")
            else:
                if l == NL - 1:
                    if b > 0:
                        S.dma("pool", y_p[t0 - 16:t0 - 16 + L, sl], z[n][:L, :], r=[z[n]], stream="o")
                else:
                    S.dma("pool", xscr[t0:t0 + L, sl], z[n][:L, :], r=[z[n]], w=[("xscr", b)], sname=f"xo{n}_{l % 2}")


    for l in range(NL):
        layer_setup(l)
        for b in range(NBLK):
            block(l, "p", b)
        if not SKIP_SAMPLE:
            block(l, "s", 0)
    S.finish("sp")
    print("ops", S.nops, "waits", S.nwaits, "sems", S.nsem + len(S.dstream))
    return nc


def _consts():
    cst = np.zeros((128, NCST), np.float32)
    i = np.arange(128)
    cst[:, 0:128] = np.eye(128)
    cst[:, 128:256] = (i[None, :] >= i[:, None])
    cst[:, 256:384] = (i[None, :] < i[:, None])
    cst[:, 384:512] = 1.0
    sc = 128.0 ** -0.5
    for h in range(4):
        g = np.float64(GAM[h])
        cst[:, 512 + h] = g ** (i + 1.0)
        cst[:, 516 + h] = g ** (-(i + 1.0)) * sc
        cst[:, 520 + h] = g ** (127.0 - i) * sc
        cst[:16, 524 + h] = g ** (15.0 - i[:16]) * sc
        cst[:, 528 + h] = g
        cst[:, 532 + h] = sc / g
        cst[:, 536 + h] = sc
    half = 64
    inv = (np.float32(10000.0) ** (-np.arange(half, dtype=np.float32) / np.float32(half))).astype(np.float32)
    rope = np.zeros((18, 128, 128), np.float32)
    for b in range(18):
        if b == 0:
            pos = np.arange(16, dtype=np.float32)
        elif b < 17:
            pos = 16 + 128 * (b - 1) + np.arange(128, dtype=np.float32)
        else:
            pos = np.full(16, 16384.0, np.float32)
        ang = (pos[:, None].astype(np.float32) * inv[None, :]).astype(np.float32)
        rope[b, :len(pos), 0:64] = np.cos(ang.astype(np.float64))
        rope[b, :len(pos), 64:128] = np.sin(ang.astype(np.float64))
    esel = np.eye(16, dtype=np.float32).reshape(1, 256)
    return cst, rope, esel


_NC_CACHE = {}


def kernel(x_prompt, x_sample, state_rglru_h, state_rglru_conv, state_ret, state_gdn_conv, state_gdn,
           meta_tokens, w_in, rg_conv_w, rg_conv_b, rg_w_a, rg_b_a, rg_w_x, rg_b_x, rg_lambda,
           ret_gn_w, ret_gn_b, gdn_conv_w, gdn_a_log, gdn_dt_bias, gdn_norm_w, w_out, ln_w, ln_b):
    f = lambda a: np.ascontiguousarray(np.asarray(a, dtype=np.float32))
    x_prompt, x_sample, meta_tokens = f(x_prompt), f(x_sample), f(meta_tokens)
    w_in, w_out = f(w_in), f(w_out)
    pf = np.zeros((NL, 128, NPF), np.float32)

    def fm(v, nch):
        return f(v).reshape(NL, nch, 128).transpose(0, 2, 1)

    pf[:, :, 0:16] = f(rg_conv_w).reshape(NL, 4, 4, 128).transpose(0, 3, 2, 1).reshape(NL, 128, 16)
    pf[:, :, 16:20] = fm(rg_conv_b, 4)
    pf[:, :, 20:24] = fm(rg_b_a, 4)
    pf[:, :, 24:28] = fm(rg_b_x, 4)
    pf[:, :, 28:32] = fm(rg_lambda, 4)
    pf[:, :, 32:36] = fm(ret_gn_w, 4)
    pf[:, :, 36:40] = fm(ret_gn_b, 4)
    pf[:, :, 40:88] = f(gdn_conv_w).reshape(NL, 4, 12, 128).transpose(0, 3, 2, 1).reshape(NL, 128, 48)
    pf[:, :, 88] = f(gdn_norm_w)
    rgw = np.zeros((NL, 128, 2, 4, 128), np.float32)
    for which, wsrc in ((0, f(rg_w_a)), (1, f(rg_w_x))):
        for n in range(8):
            c, o = n // 2, (n % 2) * 64
            rgw[:, o:o + 64, which, c, o:o + 64] = wsrc[:, n]
    rgw = rgw.reshape(NL, 128, 1024)
    rows = np.concatenate([f(ln_w), f(ln_b), f(gdn_a_log), f(gdn_dt_bias)], axis=1).reshape(NL, 1, 2056)
    cst, rope, esel = _consts()
    if "nc" not in _NC_CACHE:
        _NC_CACHE["nc"] = build_nc()
    nc = _NC_CACHE["nc"]
    in_maps = []
    for c in range(8):
        sl = slice(NS * c, NS * (c + 1))
        m = {
            "xp": np.ascontiguousarray(np.concatenate([meta_tokens, x_prompt[c]], axis=0)),
            "xs": np.ascontiguousarray(x_sample[sl, 0, :]),
            "s_h": np.ascontiguousarray(f(state_rglru_h)[:, sl].reshape(NL, NS, 4, 128).transpose(0, 3, 2, 1)),
            "s_rgc": np.ascontiguousarray(f(state_rglru_conv)[:, sl].reshape(NL, NS, 3, 4, 128).transpose(0, 4, 3, 2, 1)),
            "s_gc": np.ascontiguousarray(f(state_gdn_conv)[:, sl].reshape(NL, NS, 3, 12, 128).transpose(0, 4, 3, 2, 1)),
            "s_ret": np.ascontiguousarray(f(state_ret)[:, sl]),
            "s_gdn": np.ascontiguousarray(f(state_gdn)[:, sl]),
            "w_in": w_in, "w_out": w_out, "pf": pf, "rgw": rgw, "rows": rows,
            "cst": cst, "ropet": rope, "esel": esel,
        }
        in_maps.append(m)
    res = run_bass_kernel_spmd(nc, in_maps, core_ids=list(range(8)))
    R = res.results
    g = lambda k: [np.asarray(R[c][k], dtype=np.float32) for c in range(8)]
    y_prompt = np.stack(g("y_p"), 0)
    y_sample = np.concatenate(g("y_s"), 0)[:, None, :]
    hp = np.stack([a.transpose(0, 2, 1).reshape(NL, 512) for a in g("o_h_p")], 1)
    rgcp = np.stack([a.transpose(0, 3, 2, 1).reshape(NL, 3, 512) for a in g("o_rgc_p")], 1)
    retp = np.stack(g("o_ret_p"), 1)
    gcp = np.stack([a.transpose(0, 3, 2, 1).reshape(NL, 3, 1536) for a in g("o_gc_p")], 1)
    gdnp = np.stack(g("o_gdn_p"), 1)
    hs = np.concatenate([a.transpose(0, 3, 2, 1).reshape(NL, NS, 512) for a in g("o_h_s")], 1)
    rgcs = np.concatenate([a.transpose(0, 4, 3, 2, 1).reshape(NL, NS, 3, 512) for a in g("o_rgc_s")], 1)
    rets = np.concatenate(g("o_ret_s"), 1)
    gcs = np.concatenate([a.transpose(0, 4, 3, 2, 1).reshape(NL, NS, 3, 1536) for a in g("o_gc_s")], 1)
    gdns = np.concatenate(g("o_gdn_s"), 1)
    c = np.ascontiguousarray
    return (c(y_prompt), c(y_sample), c(hp), c(rgcp), c(retp), c(gcp), c(gdnp), c(hs), c(rgcs), c(rets), c(gcs), c(gdns))
```

```python
import numpy as np
import concourse.bass as bass
import concourse.mybir as mybir
from concourse.bass_utils import run_bass_kernel_spmd

F32 = mybir.dt.float32
BF16 = mybir.dt.bfloat16
ALU = mybir.AluOpType
AF = mybir.ActivationFunctionType
AX = mybir.AxisListType

EPOCH = 12000
DEFCOST = {"pe": 0.2, "act": 0.4, "dve": 0.3, "pool": 0.05, "sp": 0.05}
GDN_STOP = 100000
ENABLE = [True, True, True]
NET = 6
SKIP_SAMPLE = False
PER_TILE_SEMS = True
MERGE = "sim"
SAME_SYNC = True

NL = 4
DM = 1024
DIN = 5128
NTOK = 2064
NBLK = 17
NS = 16
ALPHA = 8.0 ** 0.25
EPS = 1e-6
GAM = [1.0 - 2.0 ** (-5.0 - h) for h in range(4)]
NCST = 540
NPF = 89
COL = dict(rgx=0, rgz=512, rq=1024, rk=1536, rv=2048, rz=2560, gq=3072, gk=3584, gv=4096, gz=4608, gab=5120)


class Tile:
    def __init__(self, nc, name, shape, dtype, psum=False):
        if psum:
            self.h = nc.alloc_psum_tensor("T_" + name, list(shape), dtype)
        else:
            self.h = nc.alloc_sbuf_tensor("T_" + name, list(shape), dtype)
        self.ap = self.h.ap()
        self.name = name

    def __getitem__(self, k):
        return self.ap[k]


class Sched:
    def __init__(self, nc):
        self.nc = nc
        self.eng = {"pe": nc.tensor, "act": nc.scalar, "dve": nc.vector, "pool": nc.gpsimd, "sp": nc.sync}
        self.sem = {}
        self.cnt = {}
        self.pend = {}
        self.nsem = 0
        for e in self.eng:
            self._new_sem(e)
            self.pend[e] = False
        self.lastw = {}
        self.readers = {}
        self.waited = {e: {} for e in self.eng}
        self.dstream = {}
        self.nwaits = 0
        self.nops = 0
        self.rec = None

    def _new_sem(self, e):
        self.sem[e] = self.nc.alloc_semaphore(f"s_{e}_{self.nsem}")
        self.nsem += 1
        self.cnt[e] = 0

    def _deps(self, r, w):
        evs = []
        for k in r:
            if k in self.lastw:
                evs.append(self.lastw[k] + (True,))
        for k in w:
            if k in self.lastw:
                evs.append(self.lastw[k] + (False,))
            evs.extend(v + (False,) for v in self.readers.get(k, {}).values())
        return evs

    def _do_waits(self, e, evs):
        need = {}
        for sem, val, src, raw in evs:
            if src == e and not (SAME_SYNC or raw):
                continue
            if src.startswith("dma:"):
                val = self.dstream[src[4:]][1]
            if val > need.get(sem, (0, None))[0]:
                need[sem] = (val, src)
        for sem, (val, src) in need.items():
            if self.waited[e].get(sem, 0) >= val:
                continue
            if src == e and sem is self.sem[e] and val > self.cnt[e]:
                continue
            self.eng[e].wait_ge(sem, val)
            self.waited[e][sem] = val
            self.nwaits += 1

    def _register(self, ev, r, w):
        sem = ev[0]
        for k in r:
            self.readers.setdefault(k, {})[sem] = ev
        for k in w:
            self.lastw[k] = ev
            self.readers[k] = {}

    def op(self, e, fn, r=(), w=(), inc=True, cost=None):
        if self.rec is not None:
            self.rec.append(("op", e, fn, tuple(r), tuple(w), inc, cost if cost else DEFCOST[e]))
            return None
        self._do_waits(e, self._deps(r, w))
        if self.cnt[e] >= EPOCH and not self.pend[e]:
            self._new_sem(e)
        ins = fn()
        self.nops += 1
        if inc:
            self.cnt[e] += 1
            ins.then_inc(self.sem[e], 1)
            ev = (self.sem[e], self.cnt[e], e)
            self.pend[e] = False
        else:
            ev = (self.sem[e], self.cnt[e] + 1, e)
            self.pend[e] = True
        self._register(ev, r, w)
        return ins

    def dma(self, q, out, in_, r=(), w=(), stream="d", sname=None, **kw):
        if self.rec is not None:
            self.rec.append(("dma", q, (out, in_, dict(kw, sname=sname)), tuple(r), tuple(w), True, 0.05))
            return None
        tl = [k for k in w if isinstance(k, Tile)]
        if sname is not None:
            stream = sname
        elif not PER_TILE_SEMS:
            pass
        elif tl:
            stream = "ld_" + tl[0].name
        else:
            stream = "st_" + [k for k in r if isinstance(k, Tile)][0].name
        self._do_waits(q, self._deps(r, w))
        if stream not in self.dstream:
            self.dstream[stream] = [self.nc.alloc_semaphore(f"d_{stream}"), 0]
        st = self.dstream[stream]
        ins = self.eng[q].dma_start(out=out, in_=in_, **kw)
        st[1] += 16
        ins.then_inc(st[0], 16)
        ev = (st[0], st[1], "dma:" + stream)
        self._register(ev, r, w)
        return ins

    def finish(self, e="sp"):
        for name, (sem, tot) in self.dstream.items():
            if tot > 0:
                self.eng[e].wait_ge(sem, tot)


def v3(ap, c=4):
    return ap.rearrange("p (c n) -> p c n", c=c)


def build_nc():
    nc = bass.Bass("TRN2", target_bir_lowering=False)

    def din(name, shape):
        return nc.dram_tensor(name, list(shape), F32, kind="ExternalInput").ap()

    def dout(name, shape):
        return nc.dram_tensor(name, list(shape), F32, kind="ExternalOutput").ap()

    xp_d = din("xp", [NTOK, DM])
    xs_d = din("xs", [NS, DM])
    s_h = din("s_h", [NL, 128, 4, NS])
    s_rgc = din("s_rgc", [NL, 128, 4, 3, NS])
    s_gc = din("s_gc", [NL, 128, 12, 3, NS])
    s_ret = din("s_ret", [NL, NS, 4, 128, 128])
    s_gdn = din("s_gdn", [NL, NS, 4, 128, 128])
    w_in = din("w_in", [NL, DM, DIN])
    w_out = din("w_out", [NL, 1536, DM])
    pf_d = din("pf", [NL, 128, NPF])
    rgw_d = din("rgw", [NL, 128, 2 * 4 * 128])
    rows_d = din("rows", [NL, 1, 2056])
    cst_d = din("cst", [128, NCST])
    rope_d = din("ropet", [18, 128, 128])
    esel_d = din("esel", [1, 256])

    y_p = dout("y_p", [2048, DM])
    y_s = dout("y_s", [NS, DM])
    o_h_p = dout("o_h_p", [NL, 128, 4])
    o_rgc_p = dout("o_rgc_p", [NL, 128, 4, 3])
    o_ret_p = dout("o_ret_p", [NL, 4, 128, 128])
    o_gc_p = dout("o_gc_p", [NL, 128, 12, 3])
    o_gdn_p = dout("o_gdn_p", [NL, 4, 128, 128])
    o_h_s = dout("o_h_s", [NL, 128, 4, NS])
    o_rgc_s = dout("o_rgc_s", [NL, 128, 4, 3, NS])
    o_ret_s = dout("o_ret_s", [NL, NS, 4, 128, 128])
    o_gc_s = dout("o_gc_s", [NL, 128, 12, 3, NS])
    o_gdn_s = dout("o_gdn_s", [NL, NS, 4, 128, 128])
    xscr = nc.dram_tensor("xscr", [NTOK, DM], F32, kind="Internal").ap()
    xsscr = nc.dram_tensor("xsscr", [NS, DM], F32, kind="Internal").ap()

    S = Sched(nc)

    def TL(name, shape, dt=F32):
        return Tile(nc, name, shape, dt)

    Win = TL("Win", [128, 8, DIN], BF16)
    Wout = TL("Wout", [128, 12, DM], BF16)
    cst = TL("cst", [128, NCST])
    identb = TL("identb", [128, 128], BF16)
    esel = TL("esel", [128, 16, 16])
    pft = TL("pft", [128, NPF])
    rgwt = TL("rgwt", [128, 2, 4, 128], BF16)
    rowt = TL("rowt", [128, 2056])
    nc8sp = TL("nc8sp", [128, 4])
    negA = TL("negA", [128, 4])
    ropeb = TL("ropeb", [128, 128])
    X = TL("X", [128, 4, 131])
    GX = TL("GX", [128, 12, 131])
    h0s = TL("h0s", [128, 4, NS])
    Sret = TL("Sret", [128, 512])
    Sretb = TL("Sretb", [128, 512], BF16)
    Sgdn = TL("Sgdn", [128, 512])
    hprev = TL("hprev", [128, 4])
    mix = TL("mix", [128, 12, 128], BF16)
    xt = TL("xt", [128, DM])
    xT = TL("xT", [128, 8, 128], BF16)
    ztr = TL("ztr", [128, 512])
    zte = TL("zte", [128, 512])
    ztg = TL("ztg", [128, 512])
    xcb = TL("xcb", [128, 512], BF16)
    mixr, mixe, mixg = "mixr", "mixe", "mixg"
    Vb = TL("Vb", [128, 512])
    Kbg = TL("Kbg", [128, 512])
    kd = TL("kd", [128, 512])
    qgT = TL("qgT", [128, 512])
    attT = TL("attT", [128, 512])
    Y = TL("Y", [128, 512])
    gabt = TL("gabt", [128, 8])
    gt = TL("gt", [128, 4])
    betat = TL("betat", [128, 4])
    gct = TL("gct", [128, 4])
    egt = TL("egt", [128, 4])
    eglt = TL("eglt", [128, 4])
    eglast = TL("eglast", [128, 4])
    sm1 = TL("sm1", [128, 4])
    sm2 = TL("sm2", [128, 4])
    bst = TL("bst", [128, 4, 6])
    bmv = TL("bmv", [128, 4, 2])
    ebs = TL("ebs", [128, 4, NS])
    vb = TL("vb", [128, 512], BF16)
    k2b = TL("k2b", [128, 512], BF16)
    sm3 = TL("sm3", [128, 4])
    nmr = TL("nmr", [128, 4])
    nmr2 = TL("nmr2", [128, 1])
    ngct = TL("ngct", [128, 4])
    bst2 = TL("bst2", [128, 2, 6])
    bmv2 = TL("bmv2", [128, 2])
    RT = [TL(f"rt{i}", [128, 512]) for i in range(4)]
    ET = [TL(f"et{i}", [128, 512]) for i in range(NET)]
    GT = [TL(f"gt{i}", [128, 512]) for i in range(9)]
    B = [TL(f"bb{i}", [128, 1024], BF16) for i in range(3)]
    ps = [Tile(nc, f"ps{i}", [128, 512], F32, psum=True) for i in range(8)]
    print("sbuf bytes remaining", nc.sbuf_bytes_remaining)
    GDN_W = 2

    def fs(ap):
        n = 1
        for d in ap.shape[1:]:
            n *= d
        return n

    def A(out, in_, func, r, w, bias=0.0, scale=1.0):
        S.op("act", lambda: nc.scalar.activation(out=out, in_=in_, func=func, bias=bias, scale=scale), r, w,
             cost=0.22 + fs(out) / 1000.0)

    def TT(out, a, b, op, r, w):
        S.op("dve", lambda: nc.vector.tensor_tensor(out=out, in0=a, in1=b, op=op), r, w, cost=0.2 + fs(out) / 1000.0)

    def TS(out, a, s1, s2, op0, op1, r, w):
        if s2 is None:
            S.op("dve", lambda: nc.vector.tensor_scalar(out=out, in0=a, scalar1=s1, scalar2=None, op0=op0), r, w,
                 cost=0.2 + fs(out) / 1000.0)
        else:
            S.op("dve", lambda: nc.vector.tensor_scalar(out=out, in0=a, scalar1=s1, scalar2=s2, op0=op0, op1=op1), r, w,
                 cost=0.2 + fs(out) / 1000.0)

    def STT(out, a, s, b, op0, op1, r, w):
        S.op("dve", lambda: nc.vector.scalar_tensor_tensor(out=out, in0=a, scalar=s, in1=b, op0=op0, op1=op1), r, w,
             cost=0.2 + fs(out) / 1000.0)

    def CP(out, in_, r, w, e="dve"):
        if e == "dve":
            S.op("dve", lambda: nc.vector.tensor_copy(out=out, in_=in_), r, w, cost=0.2 + fs(out) / 1000.0)
        else:
            S.op("act", lambda: nc.scalar.activation(out=out, in_=in_, func=AF.Copy), r, w, cost=0.22 + fs(out) / 1000.0)

    def MS(out, val, w):
        S.op("dve", lambda: nc.vector.memset(out, val), (), w)

    def MM(out, lhsT, rhs, st, sp, r, w, inc=True):
        S.op("pe", lambda: nc.tensor.matmul(out, lhsT=lhsT, rhs=rhs, start=st, stop=sp), r, w, inc=inc,
             cost=(0.11 + fs(out) / 1200.0) * (2.0 if lhsT.dtype == F32 else 1.0))

    def TR(out, in_, ident, r, w, inc=True):
        S.op("pe", lambda: nc.tensor.transpose(out=out, in_=in_, identity=ident), r, w, inc=inc,
             cost=(0.11 + fs(out) / 1200.0) * (2.0 if in_.dtype == F32 else 1.0))

    def RSQ(out, in_, r, w, bias=EPS, scale=1.0):
        A(out, in_, AF.Sqrt, r, w, bias=bias, scale=scale)
        S.op("dve", lambda: nc.vector.reciprocal(out=out, in_=out), w, w)

    ident = cst[:, 0:128]
    maskT = cst[:, 128:256]
    strictL = cst[:, 256:384]
    ones = cst[:, 384:512]

    S.dma("sp", cst[:], cst_d, w=[cst], stream="c")
    S.dma("sp", esel[:].rearrange("p a b -> p (a b)"), esel_d.partition_broadcast(128), w=[esel], stream="c")
    CP(identb[:], ident, [cst], [identb], e="act")

    def bc(ap2, n, L):
        return ap2.unsqueeze(2).to_broadcast([L, 4, n])

    def load_win(l):
        for kc in range(8):
            S.dma("pool", Win[:, kc, :], w_in[l, kc * 128:(kc + 1) * 128, :], w=[Win], stream="w")

    def layer_setup(l):
        if l == 0:
            load_win(0)
        for kc in range(12):
            S.dma("pool", Wout[:, kc, :], w_out[l, kc * 128:(kc + 1) * 128, :], w=[Wout], stream="w")
        S.dma("pool", rgwt[:].rearrange("p a c n -> p (a c n)"), rgw_d[l], w=[rgwt], stream="w")
        S.dma("sp", pft[:], pf_d[l], w=[pft], stream="c")
        S.dma("sp", rowt[:], rows_d[l].partition_broadcast(128), w=[rowt], stream="c")
        A(nc8sp[:], pft[:, 28:32], AF.Exp, [pft], [nc8sp], scale=-1.0)
        A(nc8sp[:], nc8sp[:], AF.Ln, [nc8sp], [nc8sp], bias=1.0)
        TS(nc8sp[:], nc8sp[:], -8.0, None, ALU.mult, None, [nc8sp], [nc8sp])
        A(negA[:], rowt[:, 2048:2052], AF.Exp, [rowt], [negA])
        TS(negA[:], negA[:], -1.0, None, ALU.mult, None, [negA], [negA])
        MS(Sret[:], 0.0, [Sret])
        MS(Sretb[:], 0.0, [Sretb])
        MS(Sgdn[:], 0.0, [Sgdn])
        MS(hprev[:], 0.0, [hprev])
        MS(X[:, :, 0:3], 0.0, [X])
        MS(GX[:, :, 0:3], 0.0, [GX])

    def block(l, mode, b):
        smp = mode == "s"
        if smp:
            L = NS
            t0 = 0
            src = xs_d if l == 0 else xsscr
            S.dma("sp", xt[:L, :], src, r=[("xsscr", 0)], w=[xt], stream="x")
        else:
            L = 16 if b == 0 else 128
            t0 = 0 if b == 0 else 16 + 128 * (b - 1)
            src = xp_d if l == 0 else xscr
            S.dma("sp", xt[:L, :], src[t0:t0 + L, :], r=[("xscr", b)], w=[xt], stream="x")
        xsrc = xt
        rb = 17 if smp else b
        S.dma("sp", ropeb[:L, :], rope_d[rb, 0:L, :], w=[ropeb], stream="x")
        mT = ident if smp else maskT
        ci = 528 if smp else 512
        qdec = cst[:L, ci:ci + 4]
        kdecp = cst[:L, ci + 4:ci + 8]
        if smp:
            k2dec = cst[:L, 536:540]
        elif L == 128:
            k2dec = cst[:L, 520:524]
        else:
            k2dec = cst[:L, 524:528]
        Xs = X.ap.rearrange("p c n -> p (c n)")[:, 0:4 * 4 * NS].rearrange("p (c j s) -> p c j s", c=4, j=4)
        GXs = GX.ap.rearrange("p c n -> p (c n)")[:, 0:12 * 4 * NS].rearrange("p (c j s) -> p c j s", c=12, j=4)

        xb = B[0]
        CP(xb[:L, :], xsrc[:L, :], [xsrc], [xb], e="act")
        pt = ps[0]
        ptb = pt.ap.bitcast(BF16)
        for kc in range(8):
            TR(ptb[:, kc * 128:kc * 128 + L], xb[:L, kc * 128:(kc + 1) * 128], identb[:L, :L], [xb, identb], [pt],
               inc=(kc == 7))
        CP(xT[:, :, :L], v3(ptb, 8)[:, :, :L], [pt], [xT])

        def fm_group(bank, c0, dst_ap, dst_key, e="act"):
            b3 = v3(bank.ap)
            for c in range(4):
                for kc in range(8):
                    MM(b3[:, c, :L], Win[:, kc, c0 + c * 128:c0 + (c + 1) * 128], xT[:, kc, :L], kc == 0, kc == 7,
                       [Win, xT], [bank], inc=(c == 3 and kc == 7))
            CP(dst_ap, b3[:, :, :L], [bank], [dst_key], e=e)

        def tm_group(bank, c0, n, dst_ap, dst_key, e="dve"):
            for kc in range(8):
                MM(bank[:L, :n], xT[:, kc, :L], Win[:, kc, c0:c0 + n], kc == 0, kc == 7, [Win, xT], [bank],
                   inc=(kc == 7))
            CP(dst_ap, bank[:L, :n], [bank], [dst_key], e=e)

        def conv_chunk(src_tap, wcol0, c, o, dst_key, src_key, bias_col=None):
            w0 = pft[:, wcol0 + c * 4:wcol0 + c * 4 + 1]
            if bias_col is not None:
                TS(o, src_tap(c, 0), w0, pft[:, bias_col + c:bias_col + c + 1], ALU.mult, ALU.add,
                   [src_key, pft], [dst_key])
            else:
                TS(o, src_tap(c, 0), w0, None, ALU.mult, None, [src_key, pft], [dst_key])
            for j in range(1, 4):
                STT(o, src_tap(c, j), pft[:, wcol0 + c * 4 + j:wcol0 + c * 4 + j + 1], o, ALU.mult, ALU.add,
                    [src_key, pft, dst_key], [dst_key])

        def gen_rg():
            pa, pb = ps[0], ps[1]
            if smp:
                S.dma("sp", Xs[:, :, 0:3, :], s_rgc[l], w=[X], stream="st")
                S.dma("sp", h0s[:], s_h[l], w=[h0s], stream="st")
                fm_group(pa, COL["rgx"], Xs[:, :, 3, :], X)
                tap = lambda c, j: Xs[:, c, j, :]
            else:
                fm_group(pa, COL["rgx"], X[:, :, 3:3 + L], X)
                tap = lambda c, j: X[:, c, j:j + L]
            yield
            z3 = v3(ztr.ap)
            fm_group(pb, COL["rgz"], z3[:, :, :L], ztr, e="dve")
            yield "pre"
            xc = RT[0]
            xc3 = v3(xc.ap)
            for c in range(4):
                conv_chunk(tap, 0, c, xc3[:, c, :L], xc, X, bias_col=16)
                yield
            xcb3 = v3(xcb.ap)
            CP(xcb3[:, :, :L], xc3[:, :, :L], [xc], [xcb], e="act")
            rt = RT[1]; it = RT[2]; at = RT[3]
            r3 = v3(rt.ap); i3 = v3(it.ap); a3 = v3(at.ap)
            for which, dst3, dkey, bcol, bank in ((0, r3, rt, 20, pa), (1, i3, it, 24, pb)):
                b3 = v3(bank.ap)
                for c in range(4):
                    MM(b3[:, c, :L], rgwt[:, which, c, :], xcb3[:, c, :L], True, True, [rgwt, xcb], [bank], inc=(c == 3))
                yield
                for c in range(4):
                    A(dst3[:, c, :L], b3[:, c, :L], AF.Sigmoid, [bank, pft], [dkey], bias=pft[:, bcol + c:bcol + c + 1])
                yield
            for c in range(4):
                A(a3[:, c, :L], r3[:, c, :L], AF.Exp, [rt, nc8sp], [at], scale=nc8sp[:, c:c + 1])
            yield
            mt = RT[1]
            m3 = v3(mt.ap)
            A(m3[:, :, :L], a3[:, :, :L], AF.Square, [at], [mt])
            A(m3[:, :, :L], m3[:, :, :L], AF.Sqrt, [mt], [mt], bias=1.0, scale=-1.0)
            yield
            TT(i3[:, :, :L], i3[:, :, :L], xc3[:, :, :L], ALU.mult, [it, xc], [it])
            yield
            TT(i3[:, :, :L], i3[:, :, :L], m3[:, :, :L], ALU.mult, [it, mt], [it])
            yield
            ht = RT[0]
            h3 = v3(ht.ap)
            if smp:
                TT(h3[:, :, :L], a3[:, :, :L], h0s[:], ALU.mult, [at, h0s], [ht])
                TT(h3[:, :, :L], h3[:, :, :L], i3[:, :, :L], ALU.add, [ht, it], [ht])
                S.dma("pool", o_h_s[l], h3[:, :, :L], r=[ht], stream="o")
                S.dma("pool", o_rgc_s[l], Xs[:, :, 1:4, :], r=[X], stream="o")
            else:
                for c in range(4):
                    S.op("dve", lambda c=c: nc.vector.tensor_tensor_scan(
                        out=h3[:, c, :L], data0=a3[:, c, :L], data1=i3[:, c, :L], initial=hprev[:, c:c + 1],
                        op0=ALU.mult, op1=ALU.add), [at, it, hprev], [ht])
                    yield
                CP(hprev[:].unsqueeze(2), h3[:, :, L - 1:L], [ht], [hprev])
                if b == NBLK - 1:
                    S.dma("pool", o_h_p[l], hprev[:], r=[hprev], stream="o")
                    S.dma("pool", o_rgc_p[l], X[:, :, L:L + 3], r=[X], stream="o")
                CP(X[:, :, 0:3], X[:, :, L:L + 3], [X], [X])
            yield
            A(z3[:, :, :L], z3[:, :, :L], AF.Silu, [ztr], [ztr])
            TT(mix[:, 0:4, :L], h3[:, :, :L], z3[:, :, :L], ALU.mult, [ht, ztr], [mixr])

        def gen_ret():
            bk = [ps[2], ps[3], ps[4]]
            rq = ET[0]; rk = ET[1]
            tm_group(bk[0], COL["rq"], 512, rq[:L, :], rq)
            yield
            tm_group(bk[1], COL["rk"], 512, rk[:L, :], rk)
            yield
            tm_group(bk[2], COL["rv"], 512, vb[:L, 0:512], vb)
            yield
            z3 = v3(zte.ap)
            fm_group(bk[0], COL["rz"], z3[:, :, :L], zte, e="dve")
            yield "pre"
            cosb = ropeb[:L, 0:64].unsqueeze(1).to_broadcast([L, 4, 64])
            sinb = ropeb[:L, 64:128].unsqueeze(1).to_broadcast([L, 4, 64])

            def rope(src, dst, tmp):
                s3 = v3(src[:L, :]); d3 = v3(dst[:L, :]); t3 = v3(tmp[:L, :])
                t1 = s3[:, :, 0:64]; t2 = s3[:, :, 64:128]
                TT(d3[:, :, 0:64], t1, cosb, ALU.mult, [src, ropeb], [dst])
                TT(t3[:, :, 0:64], t2, sinb, ALU.mult, [src, ropeb], [tmp])
                yield
                TT(d3[:, :, 0:64], d3[:, :, 0:64], t3[:, :, 0:64], ALU.subtract, [dst, tmp], [dst])
                TT(d3[:, :, 64:128], t1, sinb, ALU.mult, [src, ropeb], [dst])
                yield
                TT(t3[:, :, 64:128], t2, cosb, ALU.mult, [src, ropeb], [tmp])
                TT(d3[:, :, 64:128], d3[:, :, 64:128], t3[:, :, 64:128], ALU.add, [dst, tmp], [dst])
                yield

            rqr = ET[2]; rkr = ET[4]
            yield from rope(rq, rqr, ET[3])
            yield from rope(rk, rkr, ET[3])
            qkb = B[0]
            TT(v3(qkb[:L, 0:512]), v3(rqr[:L, :]), bc(qdec, 128, L), ALU.mult, [rqr, cst], [qkb])
            TT(v3(qkb[:L, 512:1024]), v3(rkr[:L, :]), bc(kdecp, 128, L), ALU.mult, [rkr, cst], [qkb])
            yield
            if smp:
                k2 = ET[5]
                TT(v3(k2[:L, :]), v3(rkr[:L, :]), bc(k2dec, 128, L), ALU.mult, [rkr, cst], [k2])
            else:
                k2 = k2b
                TT(v3(k2[:L, 0:512]), v3(rkr[:L, :]), bc(k2dec, 128, L), ALU.mult, [rkr, cst], [k2])
            yield
            pt = bk[1]
            ptb = pt.ap.bitcast(BF16)
            for j in range(8):
                TR(ptb[:, j * 128:j * 128 + L], qkb[:L, j * 128:(j + 1) * 128], identb[:L, :L], [qkb, identb], [pt],
                   inc=(j == 7))
            qkT = B[1]
            qkT3 = v3(qkT.ap, 8)
            CP(qkT3[:, :, :L], v3(ptb, 8)[:, :, :L], [pt], [qkT], e="act")
            yield
            bank = bk[2]
            b3 = v3(bank.ap)
            for h in range(4):
                MM(b3[:L, h, :L], qkT3[:, 4 + h, :L], qkT3[:, h, :L], True, True, [qkT], [bank], inc=(h == 3))
            scb = B[2]
            sc3 = v3(scb[:, 0:512])
            TT(sc3[:L, :, :L], b3[:L, :, :L], mT[:L, :L].unsqueeze(1).to_broadcast([L, 4, L]), ALU.mult, [bank, cst], [scb])
            yield
            ob = bk[0]
            ob3 = v3(ob.ap)
            for h in range(4):
                MM(ob3[:L, h, :], sc3[:L, h, :L], vb[:L, h * 128:(h + 1) * 128], True, smp, [scb, vb], [ob],
                   inc=(smp and h == 3))
                if not smp:
                    MM(ob3[:L, h, :], qkT3[:, h, :L], v3(Sretb.ap)[:, h, :], False, True, [qkT, Sretb], [ob], inc=(h == 3))
            yield
            if not smp:
                sb = bk[1]
                sb3 = v3(sb.ap)
                for h in range(4):
                    MM(sb3[:, h, :], k2[:L, h * 128:(h + 1) * 128], vb[:L, h * 128:(h + 1) * 128], True, True, [k2, vb], [sb],
                       inc=(h == 3))
                yield
                for h in range(4):
                    STT(v3(Sret.ap)[:, h, :], v3(Sret.ap)[:, h, :], float(GAM[h] ** L), sb3[:, h, :], ALU.mult, ALU.add,
                        [Sret, sb], [Sret])
                yield
                CP(Sretb[:], Sret[:], [Sret], [Sretb], e="act")
                if b == NBLK - 1:
                    S.dma("pool", o_ret_p[l].rearrange("h d v -> d h v"), v3(Sret.ap), r=[Sret], stream="o")
                osrc, okey = ob3, ob
            else:
                oacc = ET[1]
                CP(oacc[:L, :], ob[:L, :], [ob], [oacc])
                xTf = xT.ap.rearrange("p a b -> p (a b)").bitcast(F32)
                for s in range(NS):
                    St = (ET[2], Sret)[s % 2]
                    S.dma("sp", v3(St.ap), s_ret[l, s].rearrange("h d v -> d h v"), w=[St], stream="st")
                    qm = ET[3]
                    TT(v3(qm[:, 0:64], 4), qkT3[:, 0:4, :NS], esel[:, s, :].unsqueeze(1).to_broadcast([128, 4, NS]),
                       ALU.mult, [qkT, esel], [qm])
                    tb_ = bk[1]
                    tb3 = v3(tb_.ap)
                    for h in range(4):
                        MM(tb3[:NS, h, :], v3(qm[:, 0:64], 4)[:, h, :], v3(St.ap)[:, h, :], True, True, [qm, St], [tb_],
                           inc=(h == 3))
                    TT(oacc[:L, :], oacc[:L, :], tb_[:NS, :], ALU.add, [oacc, tb_], [oacc])
                    yield
                    vm = ET[4]
                    TT(vm[:NS, :], vb[:NS, 0:512], ident[:NS, s:s + 1].to_broadcast([NS, 512]), ALU.mult, [vb, cst], [vm])
                    sb = bk[2]
                    sb3 = v3(sb.ap)
                    for h in range(4):
                        MM(sb3[:, h, :], k2[:NS, h * 128:(h + 1) * 128], vm[:NS, h * 128:(h + 1) * 128], True, True,
                           [k2, vm], [sb], inc=(h == 3))
                    So, So3 = ((ET[0], v3(ET[0].ap)), (xT, v3(xTf)))[s % 2]
                    for h in range(4):
                        STT(So3[:, h, :], v3(St.ap)[:, h, :], float(GAM[h]), sb3[:, h, :], ALU.mult, ALU.add,
                            [St, sb], [So])
                    S.dma("pool", o_ret_s[l, s].rearrange("h d v -> d h v"), So3, r=[So], stream="o")
                    yield
                osrc, okey = v3(oacc.ap), oacc
            for h in range(4):
                S.op("dve", lambda h=h: nc.vector.bn_stats(out=bst[:L, h, :], in_=osrc[:L, h, :]), [okey], [bst])
            yield
            for h in range(4):
                S.op("dve", lambda h=h: nc.vector.bn_aggr(out=bmv[:L, h, :], in_=bst[:L, h, :]), [bst], [bmv])
            RSQ(sm1[:L, :], bmv[:L, :, 1], [bmv], [sm1])
            yield
            onb = B[2]
            STT(nmr[:L, :], bmv[:L, :, 0], -1.0, sm1[:L, :], ALU.mult, ALU.mult, [bmv, sm1], [nmr])
            for h in range(4):
                A(onb[:L, 512 + h * 128:512 + (h + 1) * 128], osrc[:L, h, :], AF.Identity, [okey, nmr, sm1], [onb],
                  bias=nmr[:L, h:h + 1], scale=sm1[:L, h:h + 1])
            yield
            pt = bk[1]
            ptb = pt.ap.bitcast(BF16)
            for h in range(4):
                TR(ptb[:, h * 128:h * 128 + L], onb[:L, 512 + h * 128:512 + (h + 1) * 128], identb[:L, :L], [onb, identb],
                   [pt], inc=(h == 3))
            yt = ET[0]
            y3 = v3(yt.ap)
            for h in range(4):
                A(y3[:, h, :L], ptb[:, h * 128:h * 128 + L], AF.Identity, [pt, pft], [yt], bias=pft[:, 36 + h:37 + h],
                  scale=pft[:, 32 + h:33 + h])
            yield
            A(z3[:, :, :L], z3[:, :, :L], AF.Silu, [zte], [zte])
            TT(mix[:, 4:8, :L], y3[:, :, :L], z3[:, :, :L], ALU.mult, [yt, zte], [mixe])

        def gen_gdn():
            bk = [ps[5], ps[6], ps[7]]
            nb = [0]

            def PB():
                nb[0] = (nb[0] + 1) % 3
                return bk[nb[0]]

            if smp:
                S.dma("sp", GXs[:, :, 0:3, :], s_gc[l], w=[GX], stream="st")
                for g in range(3):
                    fm_group(PB(), COL["gq"] + 512 * g, GXs[:, 4 * g:4 * g + 4, 3, :], GX)
                    yield
                gtap = lambda c, j: GXs[:, c, j, :]
            else:
                for g in range(3):
                    fm_group(PB(), COL["gq"] + 512 * g, GX[:, 4 * g:4 * g + 4, 3:3 + L], GX)
                    yield
                gtap = lambda c, j: GX[:, c, j:j + L]
            tm_group(PB(), COL["gab"], 8, gabt[:L, :], gabt)
            z3 = v3(ztg.ap)
            fm_group(PB(), COL["gz"], z3[:, :, :L], ztg, e="dve")
            yield "pre"
            TT(gt[:L, :], gabt[:L, 0:4], rowt[:L, 2052:2056], ALU.add, [gabt, rowt], [gt])
            A(gt[:L, :], gt[:L, :], AF.Exp, [gt], [gt])
            A(gt[:L, :], gt[:L, :], AF.Ln, [gt], [gt], bias=1.0)
            TT(gt[:L, :], gt[:L, :], negA[:L, :], ALU.mult, [gt, negA], [gt])
            A(betat[:L, :], gabt[:L, 4:8], AF.Sigmoid, [gabt], [betat])
            yield
            bank = PB()
            MM(bank[:L, 0:4], mT[:L, :L], gt[:L, :], True, True, [cst, gt], [bank])
            CP(gct[:L, :], bank[:L, 0:4], [bank], [gct])
            A(egt[:L, :], gct[:L, :], AF.Exp, [gct], [egt])
            yield
            Rt = GT[5]
            R3 = v3(Rt.ap)
            for h in range(4):
                A(R3[:L, h, :L], mT[:L, :L], AF.Copy, [cst, gt], [Rt], scale=gt[:L, h:h + 1])
            TS(ngct[:L, :], gct[:L, :], -1.0, None, ALU.mult, None, [gct], [ngct])
            yield
            gcB = PB()
            g3 = v3(gcB.ap)
            for h in range(4):
                MM(g3[:, h, :L], ones[:L, :], R3[:L, h, :L], True, True, [cst, Rt], [gcB], inc=(h == 3))
            EB = GT[6]
            EB3 = v3(EB.ap)
            A(EB3[:, :, :L], g3[:, :, :L], AF.Exp, [gcB], [EB])
            yield
            dT = GT[7]
            dT3 = v3(dT.ap)
            for h in range(4):
                A(dT3[:L, h, :L], g3[:L, h, :L], AF.Relu, [gcB, gct], [dT], bias=gct[:L, h:h + 1], scale=-1.0)
            yield
            if not smp:
                dl = GT[8]
                dl3 = v3(dl.ap)
                for h in range(4):
                    A(dl3[:L, h, :L], g3[:L, h, :L], AF.Relu, [gcB, ngct], [dl], bias=ngct[:L, h:h + 1], scale=1.0)
                yield
                CP(eglast[:].unsqueeze(2), EB3[:, :, L - 1:L], [EB], [eglast])
                for h in range(4):
                    A(eglt[:L, h:h + 1], g3[:L, h, L - 1:L], AF.Exp, [gcB, ngct], [eglt], bias=ngct[:L, h:h + 1])
                A(dl3[:L, :, :L], dl3[:L, :, :L], AF.Exp, [dl], [dl], scale=-1.0)
                TT(dl3[:L, :, :L], dl3[:L, :, :L], strictL[:L, :L].unsqueeze(1).to_broadcast([L, 4, L]), ALU.mult,
                   [dl, cst], [dl])
                for h in range(4):
                    A(dl3[:L, h, :L], dl3[:L, h, :L], AF.Copy, [dl, betat], [dl], scale=betat[:L, h:h + 1])
                yield
            else:
                CP(ebs[:], EB3[:, :, :NS], [EB], [ebs])
            A(dT3[:L, :, :L], dT3[:L, :, :L], AF.Exp, [dT], [dT], scale=-1.0)
            TT(dT3[:L, :, :L], dT3[:L, :, :L], mT[:L, :L].unsqueeze(1).to_broadcast([L, 4, L]), ALU.mult, [dT, cst], [dT])
            yield
            cq = GT[0]; ck = GT[1]; cv = GT[2]
            cqk = [cq, ck, cv]
            for g in range(3):
                cg3 = v3(cqk[g].ap)
                for c in range(4):
                    conv_chunk(lambda c_, j, g=g: gtap(4 * g + c_, j), 40 + 16 * g, c, cg3[:, c, :L], cqk[g], GX)
                    yield
                A(cg3[:, :, :L], cg3[:, :, :L], AF.Silu, [cqk[g]], [cqk[g]])
            if smp:
                S.dma("pool", o_gc_s[l], GXs[:, :, 1:4, :], r=[GX], stream="o")
            else:
                if b == NBLK - 1:
                    S.dma("pool", o_gc_p[l], GX[:, :, L:L + 3], r=[GX], stream="o")
                CP(GX[:, :, 0:3], GX[:, :, L:L + 3], [GX], [GX])
            yield
            for g in range(2):
                cg3 = v3(cqk[g].ap)
                sq = GT[3]
                sq3 = v3(sq.ap)
                A(sq3[:, :, :L], cg3[:, :, :L], AF.Square, [cqk[g]], [sq])
                bank = PB()
                b3 = v3(bank.ap)
                for h in range(4):
                    MM(b3[:, h, :L], ones, sq3[:, h, :L], True, True, [cst, sq], [bank], inc=(h == 3))
                yield
                rn = GT[4]
                rn3 = v3(rn.ap)
                RSQ(rn3[:, :, :L], b3[:, :, :L], [bank], [rn])
                if g == 0:
                    STT(cg3[:, :, :L], cg3[:, :, :L], float(128 ** -0.5), rn3[:, :, :L], ALU.mult, ALU.mult, [cq, rn], [cq])
                else:
                    TT(cg3[:, :, :L], cg3[:, :, :L], rn3[:, :, :L], ALU.mult, [ck, rn], [ck])
                yield
            q3 = v3(cq.ap); k3 = v3(ck.ap); cv3 = v3(cv.ap)
            TT(v3(qgT.ap)[:, :, :L], q3[:, :, :L], EB3[:, :, :L], ALU.mult, [cq, EB], [qgT])
            bank = PB()
            b3 = v3(bank.ap)
            for h in range(4):
                MM(b3[:L, h, :L], k3[:, h, :L], q3[:, h, :L], True, True, [ck, cq], [bank], inc=(h == 3))
            at3 = v3(attT.ap)
            TT(at3[:L, :, :L], b3[:L, :, :L], dT3[:L, :, :L], ALU.mult, [bank, dT], [attT])
            yield
            kTM = GT[3]; vTM = GT[4]
            for srcT, s3_, dstT in ((ck, k3, kTM), (cv, cv3, vTM)):
                bank = PB()
                for h in range(4):
                    TR(bank[:L, h * 128:(h + 1) * 128], s3_[:, h, :L], ident, [srcT, cst], [bank], inc=(h == 3))
                CP(dstT[:L, :], bank[:L, :], [bank], [dstT], e="act")
                yield
            if not smp:
                TT(v3(kd[:L, :]), v3(kTM[:L, :]), bc(eglt[:L, :], 128, L), ALU.mult, [kTM, eglt], [kd])
            else:
                CP(kd[:L, :], kTM[:L, :], [kTM], [kd])
            TT(v3(Vb[:L, :]), v3(vTM[:L, :]), bc(betat[:L, :], 128, L), ALU.mult, [vTM, betat], [Vb])
            yield
            TT(sm2[:L, :], betat[:L, :], egt[:L, :], ALU.mult, [betat, egt], [sm2])
            TT(v3(Kbg[:L, :]), v3(kTM[:L, :]), bc(sm2[:L, :], 128, L), ALU.mult, [kTM, sm2], [Kbg])
            yield
            Y3 = v3(Y.ap)
            if smp:
                CP(Y3[:L, :, :L], ident[:L, :L].unsqueeze(1).to_broadcast([L, 4, L]), [cst], [Y])
            else:
                bank = PB()
                b3 = v3(bank.ap)
                for h in range(4):
                    MM(b3[:L, h, :L], k3[:, h, :L], k3[:, h, :L], True, True, [ck], [bank], inc=(h == 3))
                P = GT[2]
                P3 = v3(P.ap)
                TT(P3[:L, :, :L], b3[:L, :, :L], dl3[:L, :, :L], ALU.mult, [bank, dl], [P])
                yield
                bank = PB()
                b3 = v3(bank.ap)
                for h in range(4):
                    TR(b3[:L, h, :L], P3[:L, h, :L], ident[:L, :L], [P, cst], [bank], inc=(h == 3))
                Q = GT[5]
                Q3 = v3(Q.ap)
                CP(Q3[:L, :, :L], b3[:L, :, :L], [bank], [Q], e="act")
                STT(Y3[:L, :, :L], Q3[:L, :, :L], -1.0, ident[:L, :L].unsqueeze(1).to_broadcast([L, 4, L]), ALU.mult, ALU.add,
                    [Q, cst], [Y])
                yield
                nlev = 6 if L == 128 else 3
                for lev in range(nlev):
                    bq = PB(); bp = PB()
                    bq3 = v3(bq.ap); bp3 = v3(bp.ap)
                    for h in range(4):
                        MM(bq3[:L, h, :L], P3[:L, h, :L], Q3[:L, h, :L], True, True, [P, Q], [bq], inc=(h == 3))
                    for h in range(4):
                        MM(bp3[:L, h, :L], Q3[:L, h, :L], P3[:L, h, :L], True, True, [P, Q], [bp], inc=(h == 3))
                    yield
                    Pn, Qn = (GT[3], GT[4]) if lev % 2 == 0 else (GT[2], GT[5])
                    CP(v3(Qn.ap)[:L, :, :L], bq3[:L, :, :L], [bq], [Qn], e="act")
                    CP(v3(Pn.ap)[:L, :, :L], bp3[:L, :, :L], [bp], [Pn], e="dve")
                    yield
                    P, Q = Pn, Qn
                    P3, Q3 = v3(P.ap), v3(Q.ap)
                    by = PB()
                    by3 = v3(by.ap)
                    for h in range(4):
                        MM(by3[:L, h, :L], P3[:L, h, :L], Y3[:L, h, :L], True, True, [P, Y], [by], inc=(h == 3))
                    TT(Y3[:L, :, :L], Y3[:L, :, :L], by3[:L, :, :L], ALU.add, [Y, by], [Y])
                    yield
            bank = PB()
            b3 = v3(bank.ap)
            for h in range(4):
                MM(b3[:, h, :L], Kbg[:L, h * 128:(h + 1) * 128], Y3[:L, h, :L], True, True, [Kbg, Y], [bank], inc=(h == 3))
            nWT = GT[6]
            nW3 = v3(nWT.ap)
            A(nW3[:, :, :L], b3[:, :, :L], AF.Copy, [bank], [nWT], scale=-1.0)
            yield
            Sg3 = v3(Sgdn.ap)
            vnb = PB()
            vn3 = v3(vnb.ap)
            for h in range(4):
                MM(vn3[:L, h, :], Y3[:L, h, :L], Vb[:L, h * 128:(h + 1) * 128], True, smp, [Y, Vb], [vnb],
                   inc=(smp and h == 3))
                if not smp:
                    MM(vn3[:L, h, :], nW3[:, h, :L], Sg3[:, h, :], False, True, [nWT, Sgdn], [vnb], inc=(h == 3))
            vnew = GT[7]
            if not smp:
                CP(vnew[:L, :], vnb[:L, :], [vnb], [vnew], e="act")
                yield
            else:
                wacc = GT[0]; qacc = GT[1]
                CP(wacc[:L, :], vnb[:L, :], [vnb], [wacc], e="act")
                MS(qacc[:L, :], 0.0, [qacc])
                for s in range(NS):
                    St = (GT[2], Sgdn)[s % 2]
                    S.dma("sp", v3(St.ap), s_gdn[l, s].rearrange("h d v -> d h v"), w=[St], stream="st")
                    for srcT, s3_, acc in ((qgT, v3(qgT.ap), qacc), (nWT, nW3, wacc)):
                        qm = GT[3]
                        TT(v3(qm[:, 0:64], 4), s3_[:, :, :NS], esel[:, s, :].unsqueeze(1).to_broadcast([128, 4, NS]),
                           ALU.mult, [srcT, esel], [qm])
                        tb_ = PB()
                        tb3 = v3(tb_.ap)
                        for h in range(4):
                            MM(tb3[:NS, h, :], v3(qm[:, 0:64], 4)[:, h, :], v3(St.ap)[:, h, :], True, True, [qm, St], [tb_],
                               inc=(h == 3))
                        TT(acc[:L, :], acc[:L, :], tb_[:NS, :], ALU.add, [acc, tb_], [acc])
                        yield
                CP(vnew[:L, :], wacc[:L, :], [wacc], [vnew])
            ob = PB()
            ob3 = v3(ob.ap)
            for h in range(4):
                MM(ob3[:L, h, :], at3[:L, h, :L], vnew[:L, h * 128:(h + 1) * 128], True, smp, [attT, vnew], [ob],
                   inc=(smp and h == 3))
                if not smp:
                    MM(ob3[:L, h, :], v3(qgT.ap)[:, h, :L], Sg3[:, h, :], False, True, [qgT, Sgdn], [ob], inc=(h == 3))
            yield
            if smp:
                TT(qacc[:L, :], qacc[:L, :], ob[:L, :], ALU.add, [qacc, ob], [qacc])
                osrc, okey = v3(qacc.ap), qacc
                for s in range(NS):
                    St = (GT[2], Sgdn)[s % 2]
                    S.dma("sp", v3(St.ap), s_gdn[l, s].rearrange("h d v -> d h v"), w=[St], stream="st")
                    vm = GT[3]
                    TS(vm[:NS, :], vnew[:NS, :], ident[:NS, s:s + 1], None, ALU.mult, None, [vnew, cst], [vm])
                    sb = PB()
                    sb3 = v3(sb.ap)
                    for h in range(4):
                        MM(sb3[:, h, :], kd[:NS, h * 128:(h + 1) * 128], vm[:NS, h * 128:(h + 1) * 128], True, True,
                           [kd, vm], [sb], inc=(h == 3))
                    So = (GT[4], GT[8])[s % 2]
                    for h in range(4):
                        STT(v3(So.ap)[:, h, :], v3(St.ap)[:, h, :], ebs[:, h, s:s + 1], sb3[:, h, :], ALU.mult, ALU.add,
                            [St, sb, ebs], [So])
                    S.dma("pool", o_gdn_s[l, s].rearrange("h d v -> d h v"), v3(So.ap), r=[So], stream="o")
                    yield
            else:
                osrc, okey = ob3, ob
                sb = PB()
                sb3 = v3(sb.ap)
                for h in range(4):
                    MM(sb3[:, h, :], kd[:L, h * 128:(h + 1) * 128], vnew[:L, h * 128:(h + 1) * 128], True, True, [kd, vnew], [sb],
                       inc=(h == 3))
                yield
                for h in range(4):
                    STT(Sg3[:, h, :], Sg3[:, h, :], eglast[:, h:h + 1], sb3[:, h, :], ALU.mult, ALU.add,
                        [Sgdn, sb, eglast], [Sgdn])
                if b == NBLK - 1:
                    S.dma("pool", o_gdn_p[l].rearrange("h d v -> d h v"), Sg3, r=[Sgdn], stream="o")
                yield
            osq = GT[8]
            A(v3(osq.ap)[:L, :, :], osrc[:L, :, :], AF.Square, [okey], [osq])
            S.op("dve", lambda: nc.vector.reduce_sum(out=sm3[:L, :], in_=v3(osq.ap)[:L, :, :], axis=AX.X), [osq], [sm3])
            RSQ(sm3[:L, :], sm3[:L, :], [sm3], [sm3], bias=EPS, scale=1.0 / 128.0)
            yield
            on = GT[5]
            for h in range(4):
                A(on[:L, h * 128:(h + 1) * 128], osrc[:L, h, :], AF.Copy, [okey, sm3], [on], scale=sm3[:L, h:h + 1])
            yield
            bank = PB()
            b3 = v3(bank.ap)
            for h in range(4):
                TR(b3[:, h, :L], on[:L, h * 128:(h + 1) * 128], ident[:L, :L], [on, cst], [bank], inc=(h == 3))
            A(z3[:, :, :L], z3[:, :, :L], AF.Silu, [ztg], [ztg])
            STT(mix[:, 8:12, :L], b3[:, :, :L], pft[:, 88:89], z3[:, :, :L], ALU.mult, ALU.mult, [bank, pft, ztg], [mixg])

        if MERGE == "sim":
            branches = []
            live = []
            for gen in (gen_gdn, gen_ret, gen_rg):
                g = gen()
                for mark in g:
                    if mark == "pre":
                        break
                live.append(g)
            if smp and l < NL - 1:
                load_win(l + 1)
            for g in live:
                S.rec = []
                for _ in g:
                    pass
                ops, S.rec = S.rec, None
                units, cur = [], []
                for o in ops:
                    cur.append(o)
                    if o[5]:
                        units.append(cur)
                        cur = []
                assert not cur
                branches.append(units)
            clock = {}
            wr = {}
            rd = {}
            ptr = [0] * len(branches)
            HOP = 0.3
            while True:
                best = None
                for bi, units in enumerate(branches):
                    if ptr[bi] >= len(units):
                        continue
                    u = units[ptr[bi]]
                    e = u[0][1]
                    t = clock.get(e, 0.0)
                    for o in u:
                        for k in o[3]:
                            if k in wr:
                                t = max(t, wr[k][0] + (HOP if wr[k][1] != e else 0.0))
                        for k in o[4]:
                            if k in wr:
                                t = max(t, wr[k][0] + (HOP if wr[k][1] != e else 0.0))
                            if k in rd:
                                t = max(t, rd[k][0] + (HOP if rd[k][1] != e else 0.0))
                    if best is None or t < best[0] - 1e-9:
                        best = (t, bi)
                if best is None:
                    break
                t, bi = best
                u = branches[bi][ptr[bi]]
                ptr[bi] += 1
                e = u[0][1]
                for o in u:
                    kind, eng, fn, r_, w_, inc, cost = o
                    if kind == "dma":
                        out_, in_, kw = fn
                        S.dma(eng, out_, in_, r=r_, w=w_, **kw)
                        done = t + 2.5
                        t += cost
                        who = "dma"
                    else:
                        S.op(eng, fn, r_, w_, inc=inc)
                        t += cost
                        done = t
                        who = eng
                    for k in r_:
                        if k not in rd or rd[k][0] < done:
                            rd[k] = (done, who)
                    for k in w_:
                        wr[k] = (done, who)
                        rd.pop(k, None)
                clock[e] = t
            gens = []
        else:
            gens = [(gen_rg(), 1), (gen_ret(), 1), (gen_gdn(), GDN_W)]
        if MERGE == "seq":
            for g, _ in gens:
                for _ in g:
                    pass
            gens = []
        while gens:
            for item in list(gens):
                g, wgt = item
                for _ in range(wgt):
                    try:
                        next(g)
                    except StopIteration:
                        gens.remove(item)
                        break

        z = [RT[0], RT[1]]
        for n in range(2):
            bank = ps[n]
            for kc in range(12):
                MM(bank[:L, :], mix[:, kc, :L], Wout[:, kc, n * 512:(n + 1) * 512], kc == 0, kc == 11,
                   [mixr, mixe, mixg, Wout], [bank], inc=(kc == 11))
            STT(z[n][:L, :], xsrc[:L, n * 512:(n + 1) * 512], float(ALPHA), bank[:L, :], ALU.mult, ALU.add, [xsrc, bank],
                [z[n]])
            S.op("dve", lambda n=n: nc.vector.bn_stats(out=bst2[:L, n, :], in_=z[n][:L, :]), [z[n]], [bst2])
        S.op("dve", lambda: nc.vector.bn_aggr(out=bmv2[:L, :], in_=bst2[:L, 0:2, :]), [bst2], [bmv2])
        RSQ(sm2[:L, 0:1], bmv2[:L, 1:2], [bmv2], [sm2])
        for n in range(2):
            sl = slice(n * 512, (n + 1) * 512)
            if n == 0:
                STT(nmr2[:L, :], bmv2[:L, 0:1], -1.0, sm2[:L, 0:1], ALU.mult, ALU.mult, [bmv2, sm2], [nmr2])
            A(z[n][:L, :], z[n][:L, :], AF.Identity, [z[n], nmr2, sm2], [z[n]], bias=nmr2[:L, 0:1], scale=sm2[:L, 0:1])
            TT(z[n][:L, :], z[n][:L, :], rowt[:L, sl], ALU.mult, [z[n], rowt], [z[n]])
            TT(z[n][:L, :], z[n][:L, :], rowt[:L, 1024 + n * 512:1024 + (n + 1) * 512], ALU.add, [z[n], rowt], [z[n]])
            if smp:
                if l == NL - 1:
                    S.dma("pool", y_s[:, sl], z[n][:NS, :], r=[z[n]], stream="o")
                else:
                    S.dma("pool", xsscr[:, sl], z[n][:NS, :], r=[z[n]], w=[("xsscr", 0)], sname=f"xs{n}_{l % 2}")
            else:
                if l == NL - 1:
                    if b > 0:
                        S.dma("pool", y_p[t0 - 16:t0 - 16 + L, sl], z[n][:L, :], r=[z[n]], stream="o")
                else:
                    S.dma("pool", xscr[t0:t0 + L, sl], z[n][:L, :], r=[z[n]], w=[("xscr", b)], sname=f"xo{n}_{l % 2}")


    for l in range(NL):
        layer_setup(l)
        for b in range(NBLK):
            block(l, "p", b)
        if not SKIP_SAMPLE:
            block(l, "s", 0)
    S.finish("sp")
    print("ops", S.nops, "waits", S.nwaits, "sems", S.nsem + len(S.dstream))
    return nc


def _consts():
    cst = np.zeros((128, NCST), np.float32)
    i = np.arange(128)
    cst[:, 0:128] = np.eye(128)
    cst[:, 128:256] = (i[None, :] >= i[:, None])
    cst[:, 256:384] = (i[None, :] < i[:, None])
    cst[:, 384:512] = 1.0
    sc = 128.0 ** -0.5
    for h in range(4):
        g = np.float64(GAM[h])
        cst[:, 512 + h] = g ** (i + 1.0)
        cst[:, 516 + h] = g ** (-(i + 1.0)) * sc
        cst[:, 520 + h] = g ** (127.0 - i) * sc
        cst[:16, 524 + h] = g ** (15.0 - i[:16]) * sc
        cst[:, 528 + h] = g
        cst[:, 532 + h] = sc / g
        cst[:, 536 + h] = sc
    half = 64
    inv = (np.float32(10000.0) ** (-np.arange(half, dtype=np.float32) / np.float32(half))).astype(np.float32)
    rope = np.zeros((18, 128, 128), np.float32)
    for b in range(18):
        if b == 0:
            pos = np.arange(16, dtype=np.float32)
        elif b < 17:
            pos = 16 + 128 * (b - 1) + np.arange(128, dtype=np.float32)
        else:
            pos = np.full(16, 16384.0, np.float32)
        ang = (pos[:, None].astype(np.float32) * inv[None, :]).astype(np.float32)
        rope[b, :len(pos), 0:64] = np.cos(ang.astype(np.float64))
        rope[b, :len(pos), 64:128] = np.sin(ang.astype(np.float64))
    esel = np.eye(16, dtype=np.float32).reshape(1, 256)
    return cst, rope, esel


_NC_CACHE = {}


def kernel(x_prompt, x_sample, state_rglru_h, state_rglru_conv, state_ret, state_gdn_conv, state_gdn,
           meta_tokens, w_in, rg_conv_w, rg_conv_b, rg_w_a, rg_b_a, rg_w_x, rg_b_x, rg_lambda,
           ret_gn_w, ret_gn_b, gdn_conv_w, gdn_a_log, gdn_dt_bias, gdn_norm_w, w_out, ln_w, ln_b):
    f = lambda a: np.ascontiguousarray(np.asarray(a, dtype=np.float32))
    x_prompt, x_sample, meta_tokens = f(x_prompt), f(x_sample), f(meta_tokens)
    w_in, w_out = f(w_in), f(w_out)
    pf = np.zeros((NL, 128, NPF), np.float32)

    def fm(v, nch):
        return f(v).reshape(NL, nch, 128).transpose(0, 2, 1)

    pf[:, :, 0:16] = f(rg_conv_w).reshape(NL, 4, 4, 128).transpose(0, 3, 2, 1).reshape(NL, 128, 16)
    pf[:, :, 16:20] = fm(rg_conv_b, 4)
    pf[:, :, 20:24] = fm(rg_b_a, 4)
    pf[:, :, 24:28] = fm(rg_b_x, 4)
    pf[:, :, 28:32] = fm(rg_lambda, 4)
    pf[:, :, 32:36] = fm(ret_gn_w, 4)
    pf[:, :, 36:40] = fm(ret_gn_b, 4)
    pf[:, :, 40:88] = f(gdn_conv_w).reshape(NL, 4, 12, 128).transpose(0, 3, 2, 1).reshape(NL, 128, 48)
    pf[:, :, 88] = f(gdn_norm_w)
    rgw = np.zeros((NL, 128, 2, 4, 128), np.float32)
    for which, wsrc in ((0, f(rg_w_a)), (1, f(rg_w_x))):
        for n in range(8):
            c, o = n // 2, (n % 2) * 64
            rgw[:, o:o + 64, which, c, o:o + 64] = wsrc[:, n]
    rgw = rgw.reshape(NL, 128, 1024)
    rows = np.concatenate([f(ln_w), f(ln_b), f(gdn_a_log), f(gdn_dt_bias)], axis=1).reshape(NL, 1, 2056)
    cst, rope, esel = _consts()
    if "nc" not in _NC_CACHE:
        _NC_CACHE["nc"] = build_nc()
    nc = _NC_CACHE["nc"]
    in_maps = []
    for c in range(8):
        sl = slice(NS * c, NS * (c + 1))
        m = {
            "xp": np.ascontiguousarray(np.concatenate([meta_tokens, x_prompt[c]], axis=0)),
            "xs": np.ascontiguousarray(x_sample[sl, 0, :]),
            "s_h": np.ascontiguousarray(f(state_rglru_h)[:, sl].reshape(NL, NS, 4, 128).transpose(0, 3, 2, 1)),
            "s_rgc": np.ascontiguousarray(f(state_rglru_conv)[:, sl].reshape(NL, NS, 3, 4, 128).transpose(0, 4, 3, 2, 1)),
            "s_gc": np.ascontiguousarray(f(state_gdn_conv)[:, sl].reshape(NL, NS, 3, 12, 128).transpose(0, 4, 3, 2, 1)),
            "s_ret": np.ascontiguousarray(f(state_ret)[:, sl]),
            "s_gdn": np.ascontiguousarray(f(state_gdn)[:, sl]),
            "w_in": w_in, "w_out": w_out, "pf": pf, "rgw": rgw, "rows": rows,
            "cst": cst, "ropet": rope, "esel": esel,
        }
        in_maps.append(m)
    res = run_bass_kernel_spmd(nc, in_maps, core_ids=list(range(8)))
    R = res.results
    g = lambda k: [np.asarray(R[c][k], dtype=np.float32) for c in range(8)]
    y_prompt = np.stack(g("y_p"), 0)
    y_sample = np.concatenate(g("y_s"), 0)[:, None, :]
    hp = np.stack([a.transpose(0, 2, 1).reshape(NL, 512) for a in g("o_h_p")], 1)
    rgcp = np.stack([a.transpose(0, 3, 2, 1).reshape(NL, 3, 512) for a in g("o_rgc_p")], 1)
    retp = np.stack(g("o_ret_p"), 1)
    gcp = np.stack([a.transpose(0, 3, 2, 1).reshape(NL, 3, 1536) for a in g("o_gc_p")], 1)
    gdnp = np.stack(g("o_gdn_p"), 1)
    hs = np.concatenate([a.transpose(0, 3, 2, 1).reshape(NL, NS, 512) for a in g("o_h_s")], 1)
    rgcs = np.concatenate([a.transpose(0, 4, 3, 2, 1).reshape(NL, NS, 3, 512) for a in g("o_rgc_s")], 1)
    rets = np.concatenate(g("o_ret_s"), 1)
    gcs = np.concatenate([a.transpose(0, 4, 3, 2, 1).reshape(NL, NS, 3, 1536) for a in g("o_gc_s")], 1)
    gdns = np.concatenate(g("o_gdn_s"), 1)
    c = np.ascontiguousarray
    return (c(y_prompt), c(y_sample), c(hp), c(rgcp), c(retp), c(gcp), c(gdnp), c(hs), c(rgcs), c(rets), c(gcs), c(gdns))
```

```python
import numpy as np
import concourse.bass as bass
import concourse.mybir as mybir
from concourse.bass_utils import run_bass_kernel_spmd

F32 = mybir.dt.float32
BF16 = mybir.dt.bfloat16
ALU = mybir.AluOpType
AF = mybir.ActivationFunctionType
AX = mybir.AxisListType

EPOCH = 12000
RE = "dve"
DEFCOST = {"pe": 0.2, "act": 0.4, "dve": 0.3, "pool": 0.05, "sp": 0.05}
GDN_STOP = 100000
ENABLE = [True, True, True]
NET = 6
SKIP_SAMPLE = False
PER_TILE_SEMS = True
MERGE = "sim"
SAME_SYNC = True

NL = 4
DM = 1024
DIN = 5128
NTOK = 2064
NBLK = 17
NS = 16
ALPHA = 8.0 ** 0.25
EPS = 1e-6
GAM = [1.0 - 2.0 ** (-5.0 - h) for h in range(4)]
NCST = 540
NPF = 89
COL = dict(rgx=0, rgz=512, rq=1024, rk=1536, rv=2048, rz=2560, gq=3072, gk=3584, gv=4096, gz=4608, gab=5120)


class Tile:
    def __init__(self, nc, name, shape, dtype, psum=False):
        if psum:
            self.h = nc.alloc_psum_tensor("T_" + name, list(shape), dtype)
        else:
            self.h = nc.alloc_sbuf_tensor("T_" + name, list(shape), dtype)
        self.ap = self.h.ap()
        self.name = name

    def __getitem__(self, k):
        return self.ap[k]


class Sched:
    def __init__(self, nc):
        self.nc = nc
        self.eng = {"pe": nc.tensor, "act": nc.scalar, "dve": nc.vector, "pool": nc.gpsimd, "sp": nc.sync}
        self.sem = {}
        self.cnt = {}
        self.pend = {}
        self.nsem = 0
        for e in self.eng:
            self._new_sem(e)
            self.pend[e] = False
        self.lastw = {}
        self.readers = {}
        self.waited = {e: {} for e in self.eng}
        self.dstream = {}
        self.nwaits = 0
        self.nops = 0
        self.rec = None

    def _new_sem(self, e):
        self.sem[e] = self.nc.alloc_semaphore(f"s_{e}_{self.nsem}")
        self.nsem += 1
        self.cnt[e] = 0

    def _deps(self, r, w):
        evs = []
        for k in r:
            if k in self.lastw:
                evs.append(self.lastw[k] + (True,))
        for k in w:
            if k in self.lastw:
                evs.append(self.lastw[k] + (False,))
            evs.extend(v + (False,) for v in self.readers.get(k, {}).values())
        return evs

    def _do_waits(self, e, evs):
        need = {}
        for sem, val, src, raw in evs:
            if src == e and not (SAME_SYNC or raw):
                continue
            if src.startswith("dma:"):
                val = self.dstream[src[4:]][1]
            if val > need.get(sem, (0, None))[0]:
                need[sem] = (val, src)
        for sem, (val, src) in need.items():
            if self.waited[e].get(sem, 0) >= val:
                continue
            if src == e and sem is self.sem[e] and val > self.cnt[e]:
                continue
            self.eng[e].wait_ge(sem, val)
            self.waited[e][sem] = val
            self.nwaits += 1

    def _register(self, ev, r, w):
        sem = ev[0]
        for k in r:
            self.readers.setdefault(k, {})[sem] = ev
        for k in w:
            self.lastw[k] = ev
            self.readers[k] = {}

    def op(self, e, fn, r=(), w=(), inc=True, cost=None):
        if self.rec is not None:
            self.rec.append(("op", e, fn, tuple(r), tuple(w), inc, cost if cost else DEFCOST[e]))
            return None
        self._do_waits(e, self._deps(r, w))
        if self.cnt[e] >= EPOCH and not self.pend[e]:
            self._new_sem(e)
        ins = fn()
        self.nops += 1
        if inc:
            self.cnt[e] += 1
            ins.then_inc(self.sem[e], 1)
            ev = (self.sem[e], self.cnt[e], e)
            self.pend[e] = False
        else:
            ev = (self.sem[e], self.cnt[e] + 1, e)
            self.pend[e] = True
        self._register(ev, r, w)
        return ins

    def dma(self, q, out, in_, r=(), w=(), stream="d", sname=None, **kw):
        if self.rec is not None:
            self.rec.append(("dma", q, (out, in_, dict(kw, sname=sname)), tuple(r), tuple(w), True, 0.05))
            return None
        tl = [k for k in w if isinstance(k, Tile)]
        if sname is not None:
            stream = sname
        elif not PER_TILE_SEMS:
            pass
        elif tl:
            stream = "ld_" + tl[0].name
        else:
            stream = "st_" + [k for k in r if isinstance(k, Tile)][0].name
        self._do_waits(q, self._deps(r, w))
        if stream not in self.dstream:
            self.dstream[stream] = [self.nc.alloc_semaphore(f"d_{stream}"), 0]
        st = self.dstream[stream]
        ins = self.eng[q].dma_start(out=out, in_=in_, **kw)
        st[1] += 16
        ins.then_inc(st[0], 16)
        ev = (st[0], st[1], "dma:" + stream)
        self._register(ev, r, w)
        return ins

    def finish(self, e="sp"):
        for name, (sem, tot) in self.dstream.items():
            if tot > 0:
                self.eng[e].wait_ge(sem, tot)


def v3(ap, c=4):
    return ap.rearrange("p (c n) -> p c n", c=c)


def build_nc():
    nc = bass.Bass("TRN2", target_bir_lowering=False)

    def din(name, shape):
        return nc.dram_tensor(name, list(shape), F32, kind="ExternalInput").ap()

    def dout(name, shape):
        return nc.dram_tensor(name, list(shape), F32, kind="ExternalOutput").ap()

    xp_d = din("xp", [NTOK, DM])
    xs_d = din("xs", [NS, DM])
    s_h = din("s_h", [NL, 128, 4, NS])
    s_rgc = din("s_rgc", [NL, 128, 4, 3, NS])
    s_gc = din("s_gc", [NL, 128, 12, 3, NS])
    s_ret = din("s_ret", [NL, NS, 4, 128, 128])
    s_gdn = din("s_gdn", [NL, NS, 4, 128, 128])
    w_in = din("w_in", [NL, DM, DIN])
    w_out = din("w_out", [NL, 1536, DM])
    pf_d = din("pf", [NL, 128, NPF])
    rgw_d = din("rgw", [NL, 128, 2 * 4 * 128])
    rows_d = din("rows", [NL, 1, 2056])
    cst_d = din("cst", [128, NCST])
    rope_d = din("ropet", [18, 128, 128])
    esel_d = din("esel", [1, 256])

    y_p = dout("y_p", [2048, DM])
    y_s = dout("y_s", [NS, DM])
    o_h_p = dout("o_h_p", [NL, 128, 4])
    o_rgc_p = dout("o_rgc_p", [NL, 128, 4, 3])
    o_ret_p = dout("o_ret_p", [NL, 4, 128, 128])
    o_gc_p = dout("o_gc_p", [NL, 128, 12, 3])
    o_gdn_p = dout("o_gdn_p", [NL, 4, 128, 128])
    o_h_s = dout("o_h_s", [NL, 128, 4, NS])
    o_rgc_s = dout("o_rgc_s", [NL, 128, 4, 3, NS])
    o_ret_s = dout("o_ret_s", [NL, NS, 4, 128, 128])
    o_gc_s = dout("o_gc_s", [NL, 128, 12, 3, NS])
    o_gdn_s = dout("o_gdn_s", [NL, NS, 4, 128, 128])
    xscr = nc.dram_tensor("xscr", [NTOK, DM], F32, kind="Internal").ap()
    xsscr = nc.dram_tensor("xsscr", [NS, DM], F32, kind="Internal").ap()

    S = Sched(nc)

    def TL(name, shape, dt=F32):
        return Tile(nc, name, shape, dt)

    Win = TL("Win", [128, 8, DIN], BF16)
    Wout = TL("Wout", [128, 12, DM], BF16)
    cst = TL("cst", [128, NCST])
    identb = TL("identb", [128, 128], BF16)
    esel = TL("esel", [128, 16, 16])
    pft = TL("pft", [128, NPF])
    rgwt = TL("rgwt", [128, 2, 4, 128], BF16)
    rowt = TL("rowt", [128, 2056])
    nc8sp = TL("nc8sp", [128, 4])
    negA = TL("negA", [128, 4])
    ropeb = TL("ropeb", [128, 128])
    X = TL("X", [128, 4, 131])
    GX = TL("GX", [128, 12, 131])
    h0s = TL("h0s", [128, 4, NS])
    Sret = TL("Sret", [128, 512])
    Sretb = TL("Sretb", [128, 512], BF16)
    Sgdn = TL("Sgdn", [128, 512])
    hprev = TL("hprev", [128, 4])
    mix = TL("mix", [128, 12, 128], BF16)
    xt = TL("xt", [128, DM])
    xT = TL("xT", [128, 8, 128], BF16)
    ztr = TL("ztr", [128, 512])
    zte = TL("zte", [128, 512])
    ztg = TL("ztg", [128, 512])
    xcb = TL("xcb", [128, 512], BF16)
    mixr, mixe, mixg = "mixr", "mixe", "mixg"
    Vb = TL("Vb", [128, 512])
    Kbg = TL("Kbg", [128, 512])
    kd = TL("kd", [128, 512])
    qgT = TL("qgT", [128, 512])
    attT = TL("attT", [128, 512])
    Y = TL("Y", [128, 512])
    gabt = TL("gabt", [128, 8])
    gt = TL("gt", [128, 4])
    betat = TL("betat", [128, 4])
    gct = TL("gct", [128, 4])
    egt = TL("egt", [128, 4])
    eglt = TL("eglt", [128, 4])
    eglast = TL("eglast", [128, 4])
    sm1 = TL("sm1", [128, 4])
    sm2 = TL("sm2", [128, 4])
    bst = TL("bst", [128, 4, 6])
    bmv = TL("bmv", [128, 4, 2])
    ebs = TL("ebs", [128, 4, NS])
    vb = TL("vb", [128, 512], BF16)
    k2b = TL("k2b", [128, 512], BF16)
    sm3 = TL("sm3", [128, 4])
    nmr = TL("nmr", [128, 4])
    nmr2 = TL("nmr2", [128, 1])
    ngct = TL("ngct", [128, 4])
    bst2 = TL("bst2", [128, 2, 6])
    bmv2 = TL("bmv2", [128, 2])
    RT = [TL(f"rt{i}", [128, 512]) for i in range(4)]
    ET = [TL(f"et{i}", [128, 512]) for i in range(NET)]
    GT = [TL(f"gt{i}", [128, 512]) for i in range(9)]
    B = [TL(f"bb{i}", [128, 1024], BF16) for i in range(3)]
    ps = [Tile(nc, f"ps{i}", [128, 512], F32, psum=True) for i in range(8)]
    print("sbuf bytes remaining", nc.sbuf_bytes_remaining)
    GDN_W = 2

    def fs(ap):
        n = 1
        for d in ap.shape[1:]:
            n *= d
        return n

    def A(out, in_, func, r, w, bias=0.0, scale=1.0):
        S.op("act", lambda: nc.scalar.activation(out=out, in_=in_, func=func, bias=bias, scale=scale), r, w,
             cost=0.22 + fs(out) / 1000.0)

    def TT(out, a, b, op, r, w, e="dve"):
        if e == "pool":
            S.op("pool", lambda: nc.gpsimd.tensor_tensor(out=out, in0=a, in1=b, op=op), r, w, cost=0.3 + fs(out) / 500.0)
            return
        S.op("dve", lambda: nc.vector.tensor_tensor(out=out, in0=a, in1=b, op=op), r, w, cost=0.2 + fs(out) / 1000.0)

    def TS(out, a, s1, s2, op0, op1, r, w):
        if s2 is None:
            S.op("dve", lambda: nc.vector.tensor_scalar(out=out, in0=a, scalar1=s1, scalar2=None, op0=op0), r, w,
                 cost=0.2 + fs(out) / 1000.0)
        else:
            S.op("dve", lambda: nc.vector.tensor_scalar(out=out, in0=a, scalar1=s1, scalar2=s2, op0=op0, op1=op1), r, w,
                 cost=0.2 + fs(out) / 1000.0)

    def STT(out, a, s, b, op0, op1, r, w):
        S.op("dve", lambda: nc.vector.scalar_tensor_tensor(out=out, in0=a, scalar=s, in1=b, op0=op0, op1=op1), r, w,
             cost=0.2 + fs(out) / 1000.0)

    def CP(out, in_, r, w, e="dve"):
        if e == "dve":
            S.op("dve", lambda: nc.vector.tensor_copy(out=out, in_=in_), r, w, cost=0.2 + fs(out) / 1000.0)
        else:
            S.op("act", lambda: nc.scalar.activation(out=out, in_=in_, func=AF.Copy), r, w, cost=0.22 + fs(out) / 1000.0)

    def MS(out, val, w):
        S.op("dve", lambda: nc.vector.memset(out, val), (), w)

    def MM(out, lhsT, rhs, st, sp, r, w, inc=True):
        S.op("pe", lambda: nc.tensor.matmul(out, lhsT=lhsT, rhs=rhs, start=st, stop=sp), r, w, inc=inc,
             cost=(0.11 + fs(out) / 1200.0) * (2.0 if lhsT.dtype == F32 else 1.0))

    def TR(out, in_, ident, r, w, inc=True):
        S.op("pe", lambda: nc.tensor.transpose(out=out, in_=in_, identity=ident), r, w, inc=inc,
             cost=(0.11 + fs(out) / 1200.0) * (2.0 if in_.dtype == F32 else 1.0))

    def RSQ(out, in_, r, w, bias=EPS, scale=1.0):
        A(out, in_, AF.Sqrt, r, w, bias=bias, scale=scale)
        S.op("dve", lambda: nc.vector.reciprocal(out=out, in_=out), w, w)

    ident = cst[:, 0:128]
    maskT = cst[:, 128:256]
    strictL = cst[:, 256:384]
    ones = cst[:, 384:512]

    S.dma("sp", cst[:], cst_d, w=[cst], stream="c")
    S.dma("sp", esel[:].rearrange("p a b -> p (a b)"), esel_d.partition_broadcast(128), w=[esel], stream="c")
    CP(identb[:], ident, [cst], [identb], e="act")

    def bc(ap2, n, L):
        return ap2.unsqueeze(2).to_broadcast([L, 4, n])

    def load_win(l):
        for kc in range(8):
            S.dma("pool", Win[:, kc, :], w_in[l, kc * 128:(kc + 1) * 128, :], w=[Win], stream="w")

    def layer_setup(l):
        if l == 0:
            load_win(0)
        for kc in range(12):
            S.dma("pool", Wout[:, kc, :], w_out[l, kc * 128:(kc + 1) * 128, :], w=[Wout], stream="w")
        S.dma("pool", rgwt[:].rearrange("p a c n -> p (a c n)"), rgw_d[l], w=[rgwt], stream="w")
        S.dma("sp", pft[:], pf_d[l], w=[pft], stream="c")
        S.dma("sp", rowt[:], rows_d[l].partition_broadcast(128), w=[rowt], stream="c")
        A(nc8sp[:], pft[:, 28:32], AF.Exp, [pft], [nc8sp], scale=-1.0)
        A(nc8sp[:], nc8sp[:], AF.Ln, [nc8sp], [nc8sp], bias=1.0)
        TS(nc8sp[:], nc8sp[:], -8.0, None, ALU.mult, None, [nc8sp], [nc8sp])
        A(negA[:], rowt[:, 2048:2052], AF.Exp, [rowt], [negA])
        TS(negA[:], negA[:], -1.0, None, ALU.mult, None, [negA], [negA])
        MS(Sret[:], 0.0, [Sret])
        MS(Sretb[:], 0.0, [Sretb])
        MS(Sgdn[:], 0.0, [Sgdn])
        MS(hprev[:], 0.0, [hprev])
        MS(X[:, :, 0:3], 0.0, [X])
        MS(GX[:, :, 0:3], 0.0, [GX])

    def xsource(l, mode, b):
        if mode == "s":
            return NS, 0, (xs_d if l == 0 else xsscr), ("xsscr", 0)
        L = 16 if b == 0 else 128
        t0 = 0 if b == 0 else 16 + 128 * (b - 1)
        return L, t0, (xp_d if l == 0 else xscr)[t0:t0 + L, :], ("xscr", b)

    head_done = set()

    def head(l, mode, b):
        L, t0, src, dkey = xsource(l, mode, b)
        S.dma("sp", xt[:L, :], src, r=[dkey], w=[xt], stream="x")
        xb = B[0]
        CP(xb[:L, :], xt[:L, :], [xt], [xb], e="act")
        pt = ps[0]
        ptb = pt.ap.bitcast(BF16)
        for kc in range(8):
            TR(ptb[:, kc * 128:kc * 128 + L], xb[:L, kc * 128:(kc + 1) * 128], identb[:L, :L], [xb, identb], [pt],
               inc=(kc == 7))
        CP(xT[:, :, :L], v3(ptb, 8)[:, :, :L], [pt], [xT], e="act")
        head_done.add((l, mode, b))

    def block(l, mode, b):
        smp = mode == "s"
        L, t0, xsrc_d, xdkey = xsource(l, mode, b)
        if (l, mode, b) not in head_done:
            head(l, mode, b)
        rb = 17 if smp else b
        S.dma("sp", ropeb[:L, :], rope_d[rb, 0:L, :], w=[ropeb], stream="x")
        mT = ident if smp else maskT
        ci = 528 if smp else 512
        qdec = cst[:L, ci:ci + 4]
        kdecp = cst[:L, ci + 4:ci + 8]
        if smp:
            k2dec = cst[:L, 536:540]
        elif L == 128:
            k2dec = cst[:L, 520:524]
        else:
            k2dec = cst[:L, 524:528]
        Xs = X.ap.rearrange("p c n -> p (c n)")[:, 0:4 * 4 * NS].rearrange("p (c j s) -> p c j s", c=4, j=4)
        GXs = GX.ap.rearrange("p c n -> p (c n)")[:, 0:12 * 4 * NS].rearrange("p (c j s) -> p c j s", c=12, j=4)

        def fm_group(bank, c0, dst_ap, dst_key, e="act"):
            b3 = v3(bank.ap)
            for c in range(4):
                for kc in range(8):
                    MM(b3[:, c, :L], Win[:, kc, c0 + c * 128:c0 + (c + 1) * 128], xT[:, kc, :L], kc == 0, kc == 7,
                       [Win, xT], [bank], inc=(c == 3 and kc == 7))
            CP(dst_ap, b3[:, :, :L], [bank], [dst_key], e=e)

        def tm_group(bank, c0, n, dst_ap, dst_key, e="dve"):
            for kc in range(8):
                MM(bank[:L, :n], xT[:, kc, :L], Win[:, kc, c0:c0 + n], kc == 0, kc == 7, [Win, xT], [bank],
                   inc=(kc == 7))
            CP(dst_ap, bank[:L, :n], [bank], [dst_key], e=e)

        def conv_chunk(src_tap, wcol0, c, o, dst_key, src_key, bias_col=None):
            w0 = pft[:, wcol0 + c * 4:wcol0 + c * 4 + 1]
            if bias_col is not None:
                TS(o, src_tap(c, 0), w0, pft[:, bias_col + c:bias_col + c + 1], ALU.mult, ALU.add,
                   [src_key, pft], [dst_key])
            else:
                TS(o, src_tap(c, 0), w0, None, ALU.mult, None, [src_key, pft], [dst_key])
            for j in range(1, 4):
                STT(o, src_tap(c, j), pft[:, wcol0 + c * 4 + j:wcol0 + c * 4 + j + 1], o, ALU.mult, ALU.add,
                    [src_key, pft, dst_key], [dst_key])

        def gen_rg():
            pa, pb = ps[0], ps[1]
            if smp:
                S.dma("sp", Xs[:, :, 0:3, :], s_rgc[l], w=[X], stream="st")
                S.dma("sp", h0s[:], s_h[l], w=[h0s], stream="st")
                fm_group(pa, COL["rgx"], Xs[:, :, 3, :], X)
                tap = lambda c, j: Xs[:, c, j, :]
            else:
                fm_group(pa, COL["rgx"], X[:, :, 3:3 + L], X)
                tap = lambda c, j: X[:, c, j:j + L]
            yield
            z3 = v3(ztr.ap)
            fm_group(pb, COL["rgz"], z3[:, :, :L], ztr, e="dve")
            yield "pre"
            xc = RT[0]
            xc3 = v3(xc.ap)
            for c in range(4):
                conv_chunk(tap, 0, c, xc3[:, c, :L], xc, X, bias_col=16)
                yield
            xcb3 = v3(xcb.ap)
            CP(xcb3[:, :, :L], xc3[:, :, :L], [xc], [xcb], e="act")
            rt = RT[1]; it = RT[2]; at = RT[3]
            r3 = v3(rt.ap); i3 = v3(it.ap); a3 = v3(at.ap)
            for which, dst3, dkey, bcol, bank in ((0, r3, rt, 20, pa), (1, i3, it, 24, pb)):
                b3 = v3(bank.ap)
                for c in range(4):
                    MM(b3[:, c, :L], rgwt[:, which, c, :], xcb3[:, c, :L], True, True, [rgwt, xcb], [bank], inc=(c == 3))
                yield
                for c in range(4):
                    A(dst3[:, c, :L], b3[:, c, :L], AF.Sigmoid, [bank, pft], [dkey], bias=pft[:, bcol + c:bcol + c + 1])
                yield
            for c in range(4):
                A(a3[:, c, :L], r3[:, c, :L], AF.Exp, [rt, nc8sp], [at], scale=nc8sp[:, c:c + 1])
            yield
            mt = RT[1]
            m3 = v3(mt.ap)
            A(m3[:, :, :L], a3[:, :, :L], AF.Square, [at], [mt])
            A(m3[:, :, :L], m3[:, :, :L], AF.Sqrt, [mt], [mt], bias=1.0, scale=-1.0)
            yield
            TT(i3[:, :, :L], i3[:, :, :L], xc3[:, :, :L], ALU.mult, [it, xc], [it])
            yield
            TT(i3[:, :, :L], i3[:, :, :L], m3[:, :, :L], ALU.mult, [it, mt], [it])
            yield
            ht = RT[0]
            h3 = v3(ht.ap)
            if smp:
                TT(h3[:, :, :L], a3[:, :, :L], h0s[:], ALU.mult, [at, h0s], [ht])
                TT(h3[:, :, :L], h3[:, :, :L], i3[:, :, :L], ALU.add, [ht, it], [ht])
                S.dma("pool", o_h_s[l], h3[:, :, :L], r=[ht], stream="o")
                S.dma("pool", o_rgc_s[l], Xs[:, :, 1:4, :], r=[X], stream="o")
            else:
                for c in range(4):
                    S.op("dve", lambda c=c: nc.vector.tensor_tensor_scan(
                        out=h3[:, c, :L], data0=a3[:, c, :L], data1=i3[:, c, :L], initial=hprev[:, c:c + 1],
                        op0=ALU.mult, op1=ALU.add), [at, it, hprev], [ht])
                    yield
                CP(hprev[:].unsqueeze(2), h3[:, :, L - 1:L], [ht], [hprev])
                if b == NBLK - 1:
                    S.dma("pool", o_h_p[l], hprev[:], r=[hprev], stream="o")
                    S.dma("pool", o_rgc_p[l], X[:, :, L:L + 3], r=[X], stream="o")
                CP(X[:, :, 0:3], X[:, :, L:L + 3], [X], [X])
            yield
            A(z3[:, :, :L], z3[:, :, :L], AF.Silu, [ztr], [ztr])
            TT(mix[:, 0:4, :L], h3[:, :, :L], z3[:, :, :L], ALU.mult, [ht, ztr], [mixr])

        def gen_ret():
            bk = [ps[2], ps[3], ps[4]]
            rq = ET[0]; rk = ET[1]
            tm_group(bk[0], COL["rq"], 512, rq[:L, :], rq)
            yield
            tm_group(bk[1], COL["rk"], 512, rk[:L, :], rk)
            yield
            tm_group(bk[2], COL["rv"], 512, vb[:L, 0:512], vb)
            yield
            z3 = v3(zte.ap)
            fm_group(bk[0], COL["rz"], z3[:, :, :L], zte, e="dve")
            yield "pre"
            cosb = ropeb[:L, 0:64].unsqueeze(1).to_broadcast([L, 4, 64])
            sinb = ropeb[:L, 64:128].unsqueeze(1).to_broadcast([L, 4, 64])

            def rope(src, dst, tmp):
                s3 = v3(src[:L, :]); d3 = v3(dst[:L, :]); t3 = v3(tmp[:L, :])
                t1 = s3[:, :, 0:64]; t2 = s3[:, :, 64:128]
                TT(d3[:, :, 0:64], t1, cosb, ALU.mult, [src, ropeb], [dst], e=RE)
                TT(t3[:, :, 0:64], t2, sinb, ALU.mult, [src, ropeb], [tmp], e=RE)
                yield
                TT(d3[:, :, 0:64], d3[:, :, 0:64], t3[:, :, 0:64], ALU.subtract, [dst, tmp], [dst], e=RE)
                TT(d3[:, :, 64:128], t1, sinb, ALU.mult, [src, ropeb], [dst], e=RE)
                yield
                TT(t3[:, :, 64:128], t2, cosb, ALU.mult, [src, ropeb], [tmp], e=RE)
                TT(d3[:, :, 64:128], d3[:, :, 64:128], t3[:, :, 64:128], ALU.add, [dst, tmp], [dst], e=RE)
                yield

            rqr = ET[2]; rkr = ET[4]
            yield from rope(rq, rqr, ET[3])
            yield from rope(rk, rkr, ET[3])
            qkb = B[0]
            TT(v3(qkb[:L, 0:512]), v3(rqr[:L, :]), bc(qdec, 128, L), ALU.mult, [rqr, cst], [qkb], e=RE)
            TT(v3(qkb[:L, 512:1024]), v3(rkr[:L, :]), bc(kdecp, 128, L), ALU.mult, [rkr, cst], [qkb], e=RE)
            yield
            if smp:
                k2 = ET[5]
                TT(v3(k2[:L, :]), v3(rkr[:L, :]), bc(k2dec, 128, L), ALU.mult, [rkr, cst], [k2], e=RE)
            else:
                k2 = k2b
                TT(v3(k2[:L, 0:512]), v3(rkr[:L, :]), bc(k2dec, 128, L), ALU.mult, [rkr, cst], [k2], e=RE)
            yield
            pt = bk[1]
            ptb = pt.ap.bitcast(BF16)
            for j in range(8):
                TR(ptb[:, j * 128:j * 128 + L], qkb[:L, j * 128:(j + 1) * 128], identb[:L, :L], [qkb, identb], [pt],
                   inc=(j == 7))
            qkT = B[1]
            qkT3 = v3(qkT.ap, 8)
            CP(qkT3[:, :, :L], v3(ptb, 8)[:, :, :L], [pt], [qkT], e="act")
            yield
            bank = bk[2]
            b3 = v3(bank.ap)
            for h in range(4):
                MM(b3[:L, h, :L], qkT3[:, 4 + h, :L], qkT3[:, h, :L], True, True, [qkT], [bank], inc=(h == 3))
            scb = B[2]
            sc3 = v3(scb[:, 0:512])
            TT(sc3[:L, :, :L], b3[:L, :, :L], mT[:L, :L].unsqueeze(1).to_broadcast([L, 4, L]), ALU.mult, [bank, cst], [scb])
            yield
            ob = bk[0]
            ob3 = v3(ob.ap)
            for h in range(4):
                MM(ob3[:L, h, :], sc3[:L, h, :L], vb[:L, h * 128:(h + 1) * 128], True, smp, [scb, vb], [ob],
                   inc=(smp and h == 3))
                if not smp:
                    MM(ob3[:L, h, :], qkT3[:, h, :L], v3(Sretb.ap)[:, h, :], False, True, [qkT, Sretb], [ob], inc=(h == 3))
            yield
            if not smp:
                sb = bk[1]
                sb3 = v3(sb.ap)
                for h in range(4):
                    MM(sb3[:, h, :], k2[:L, h * 128:(h + 1) * 128], vb[:L, h * 128:(h + 1) * 128], True, True, [k2, vb], [sb],
                       inc=(h == 3))
                yield
                for h in range(4):
                    STT(v3(Sret.ap)[:, h, :], v3(Sret.ap)[:, h, :], float(GAM[h] ** L), sb3[:, h, :], ALU.mult, ALU.add,
                        [Sret, sb], [Sret])
                yield
                CP(Sretb[:], Sret[:], [Sret], [Sretb], e="act")
                if b == NBLK - 1:
                    S.dma("pool", o_ret_p[l].rearrange("h d v -> d h v"), v3(Sret.ap), r=[Sret], stream="o")
                osrc, okey = ob3, ob
            else:
                oacc = ET[1]
                CP(oacc[:L, :], ob[:L, :], [ob], [oacc])
                xTf = xT.ap.rearrange("p a b -> p (a b)").bitcast(F32)
                for s in range(NS):
                    St = (ET[2], Sret)[s % 2]
                    S.dma("sp", v3(St.ap), s_ret[l, s].rearrange("h d v -> d h v"), w=[St], stream="st")
                    qm = ET[3]
                    TT(v3(qm[:, 0:64], 4), qkT3[:, 0:4, :NS], esel[:, s, :].unsqueeze(1).to_broadcast([128, 4, NS]),
                       ALU.mult, [qkT, esel], [qm])
                    tb_ = bk[1]
                    tb3 = v3(tb_.ap)
                    for h in range(4):
                        MM(tb3[:NS, h, :], v3(qm[:, 0:64], 4)[:, h, :], v3(St.ap)[:, h, :], True, True, [qm, St], [tb_],
                           inc=(h == 3))
                    TT(oacc[:L, :], oacc[:L, :], tb_[:NS, :], ALU.add, [oacc, tb_], [oacc])
                    yield
                    vm = ET[4]
                    TT(vm[:NS, :], vb[:NS, 0:512], ident[:NS, s:s + 1].to_broadcast([NS, 512]), ALU.mult, [vb, cst], [vm])
                    sb = bk[2]
                    sb3 = v3(sb.ap)
                    for h in range(4):
                        MM(sb3[:, h, :], k2[:NS, h * 128:(h + 1) * 128], vm[:NS, h * 128:(h + 1) * 128], True, True,
                           [k2, vm], [sb], inc=(h == 3))
                    So, So3 = ((ET[0], v3(ET[0].ap)), (xT, v3(xTf)))[s % 2]
                    for h in range(4):
                        STT(So3[:, h, :], v3(St.ap)[:, h, :], float(GAM[h]), sb3[:, h, :], ALU.mult, ALU.add,
                            [St, sb], [So])
                    S.dma("pool", o_ret_s[l, s].rearrange("h d v -> d h v"), So3, r=[So], stream="o")
                    yield
                osrc, okey = v3(oacc.ap), oacc
            for h in range(4):
                S.op("dve", lambda h=h: nc.vector.bn_stats(out=bst[:L, h, :], in_=osrc[:L, h, :]), [okey], [bst])
            yield
            for h in range(4):
                S.op("dve", lambda h=h: nc.vector.bn_aggr(out=bmv[:L, h, :], in_=bst[:L, h, :]), [bst], [bmv])
            RSQ(sm1[:L, :], bmv[:L, :, 1], [bmv], [sm1])
            yield
            onb = B[2]
            STT(nmr[:L, :], bmv[:L, :, 0], -1.0, sm1[:L, :], ALU.mult, ALU.mult, [bmv, sm1], [nmr])
            for h in range(4):
                A(onb[:L, 512 + h * 128:512 + (h + 1) * 128], osrc[:L, h, :], AF.Identity, [okey, nmr, sm1], [onb],
                  bias=nmr[:L, h:h + 1], scale=sm1[:L, h:h + 1])
            yield
            pt = bk[1]
            ptb = pt.ap.bitcast(BF16)
            for h in range(4):
                TR(ptb[:, h * 128:h * 128 + L], onb[:L, 512 + h * 128:512 + (h + 1) * 128], identb[:L, :L], [onb, identb],
                   [pt], inc=(h == 3))
            yt = ET[0]
            y3 = v3(yt.ap)
            for h in range(4):
                A(y3[:, h, :L], ptb[:, h * 128:h * 128 + L], AF.Identity, [pt, pft], [yt], bias=pft[:, 36 + h:37 + h],
                  scale=pft[:, 32 + h:33 + h])
            yield
            A(z3[:, :, :L], z3[:, :, :L], AF.Silu, [zte], [zte])
            TT(mix[:, 4:8, :L], y3[:, :, :L], z3[:, :, :L], ALU.mult, [yt, zte], [mixe])

        def gen_gdn():
            bk = [ps[5], ps[6], ps[7]]
            nb = [0]

            def PB():
                nb[0] = (nb[0] + 1) % 3
                return bk[nb[0]]

            if smp:
                S.dma("sp", GXs[:, :, 0:3, :], s_gc[l], w=[GX], stream="st")
                for g in range(3):
                    fm_group(PB(), COL["gq"] + 512 * g, GXs[:, 4 * g:4 * g + 4, 3, :], GX)
                    yield
                gtap = lambda c, j: GXs[:, c, j, :]
            else:
                for g in range(3):
                    fm_group(PB(), COL["gq"] + 512 * g, GX[:, 4 * g:4 * g + 4, 3:3 + L], GX)
                    yield
                gtap = lambda c, j: GX[:, c, j:j + L]
            tm_group(PB(), COL["gab"], 8, gabt[:L, :], gabt)
            z3 = v3(ztg.ap)
            fm_group(PB(), COL["gz"], z3[:, :, :L], ztg, e="dve")
            yield "pre"
            TT(gt[:L, :], gabt[:L, 0:4], rowt[:L, 2052:2056], ALU.add, [gabt, rowt], [gt])
            A(gt[:L, :], gt[:L, :], AF.Exp, [gt], [gt])
            A(gt[:L, :], gt[:L, :], AF.Ln, [gt], [gt], bias=1.0)
            TT(gt[:L, :], gt[:L, :], negA[:L, :], ALU.mult, [gt, negA], [gt])
            A(betat[:L, :], gabt[:L, 4:8], AF.Sigmoid, [gabt], [betat])
            yield
            bank = PB()
            MM(bank[:L, 0:4], mT[:L, :L], gt[:L, :], True, True, [cst, gt], [bank])
            CP(gct[:L, :], bank[:L, 0:4], [bank], [gct])
            A(egt[:L, :], gct[:L, :], AF.Exp, [gct], [egt])
            yield
            Rt = GT[5]
            R3 = v3(Rt.ap)
            for h in range(4):
                A(R3[:L, h, :L], mT[:L, :L], AF.Copy, [cst, gt], [Rt], scale=gt[:L, h:h + 1])
            TS(ngct[:L, :], gct[:L, :], -1.0, None, ALU.mult, None, [gct], [ngct])
            yield
            gcB = PB()
            g3 = v3(gcB.ap)
            for h in range(4):
                MM(g3[:, h, :L], ones[:L, :], R3[:L, h, :L], True, True, [cst, Rt], [gcB], inc=(h == 3))
            EB = GT[6]
            EB3 = v3(EB.ap)
            A(EB3[:, :, :L], g3[:, :, :L], AF.Exp, [gcB], [EB])
            yield
            dT = GT[7]
            dT3 = v3(dT.ap)
            for h in range(4):
                A(dT3[:L, h, :L], g3[:L, h, :L], AF.Relu, [gcB, gct], [dT], bias=gct[:L, h:h + 1], scale=-1.0)
            yield
            if not smp:
                dl = GT[8]
                dl3 = v3(dl.ap)
                for h in range(4):
                    A(dl3[:L, h, :L], g3[:L, h, :L], AF.Relu, [gcB, ngct], [dl], bias=ngct[:L, h:h + 1], scale=1.0)
                yield
                CP(eglast[:].unsqueeze(2), EB3[:, :, L - 1:L], [EB], [eglast])
                for h in range(4):
                    A(eglt[:L, h:h + 1], g3[:L, h, L - 1:L], AF.Exp, [gcB, ngct], [eglt], bias=ngct[:L, h:h + 1])
                A(dl3[:L, :, :L], dl3[:L, :, :L], AF.Exp, [dl], [dl], scale=-1.0)
                TT(dl3[:L, :, :L], dl3[:L, :, :L], strictL[:L, :L].unsqueeze(1).to_broadcast([L, 4, L]), ALU.mult,
                   [dl, cst], [dl])
                for h in range(4):
                    A(dl3[:L, h, :L], dl3[:L, h, :L], AF.Copy, [dl, betat], [dl], scale=betat[:L, h:h + 1])
                yield
            else:
                CP(ebs[:], EB3[:, :, :NS], [EB], [ebs])
            A(dT3[:L, :, :L], dT3[:L, :, :L], AF.Exp, [dT], [dT], scale=-1.0)
            TT(dT3[:L, :, :L], dT3[:L, :, :L], mT[:L, :L].unsqueeze(1).to_broadcast([L, 4, L]), ALU.mult, [dT, cst], [dT])
            yield
            cq = GT[0]; ck = GT[1]; cv = GT[2]
            cqk = [cq, ck, cv]
            for g in range(3):
                cg3 = v3(cqk[g].ap)
                for c in range(4):
                    conv_chunk(lambda c_, j, g=g: gtap(4 * g + c_, j), 40 + 16 * g, c, cg3[:, c, :L], cqk[g], GX)
                    yield
                A(cg3[:, :, :L], cg3[:, :, :L], AF.Silu, [cqk[g]], [cqk[g]])
            if smp:
                S.dma("pool", o_gc_s[l], GXs[:, :, 1:4, :], r=[GX], stream="o")
            else:
                if b == NBLK - 1:
                    S.dma("pool", o_gc_p[l], GX[:, :, L:L + 3], r=[GX], stream="o")
                CP(GX[:, :, 0:3], GX[:, :, L:L + 3], [GX], [GX])
            yield
            for g in range(2):
                cg3 = v3(cqk[g].ap)
                sq = GT[3]
                sq3 = v3(sq.ap)
                A(sq3[:, :, :L], cg3[:, :, :L], AF.Square, [cqk[g]], [sq])
                bank = PB()
                b3 = v3(bank.ap)
                for h in range(4):
                    MM(b3[:, h, :L], ones, sq3[:, h, :L], True, True, [cst, sq], [bank], inc=(h == 3))
                yield
                rn = GT[4]
                rn3 = v3(rn.ap)
                RSQ(rn3[:, :, :L], b3[:, :, :L], [bank], [rn])
                if g == 0:
                    STT(cg3[:, :, :L], cg3[:, :, :L], float(128 ** -0.5), rn3[:, :, :L], ALU.mult, ALU.mult, [cq, rn], [cq])
                else:
                    TT(cg3[:, :, :L], cg3[:, :, :L], rn3[:, :, :L], ALU.mult, [ck, rn], [ck])
                yield
            q3 = v3(cq.ap); k3 = v3(ck.ap); cv3 = v3(cv.ap)
            TT(v3(qgT.ap)[:, :, :L], q3[:, :, :L], EB3[:, :, :L], ALU.mult, [cq, EB], [qgT])
            bank = PB()
            b3 = v3(bank.ap)
            for h in range(4):
                MM(b3[:L, h, :L], k3[:, h, :L], q3[:, h, :L], True, True, [ck, cq], [bank], inc=(h == 3))
            at3 = v3(attT.ap)
            TT(at3[:L, :, :L], b3[:L, :, :L], dT3[:L, :, :L], ALU.mult, [bank, dT], [attT])
            yield
            kTM = GT[3]; vTM = GT[4]
            for srcT, s3_, dstT in ((ck, k3, kTM), (cv, cv3, vTM)):
                bank = PB()
                for h in range(4):
                    TR(bank[:L, h * 128:(h + 1) * 128], s3_[:, h, :L], ident, [srcT, cst], [bank], inc=(h == 3))
                CP(dstT[:L, :], bank[:L, :], [bank], [dstT], e="act")
                yield
            if not smp:
                TT(v3(kd[:L, :]), v3(kTM[:L, :]), bc(eglt[:L, :], 128, L), ALU.mult, [kTM, eglt], [kd])
            else:
                CP(kd[:L, :], kTM[:L, :], [kTM], [kd])
            TT(v3(Vb[:L, :]), v3(vTM[:L, :]), bc(betat[:L, :], 128, L), ALU.mult, [vTM, betat], [Vb])
            yield
            TT(sm2[:L, :], betat[:L, :], egt[:L, :], ALU.mult, [betat, egt], [sm2])
            TT(v3(Kbg[:L, :]), v3(kTM[:L, :]), bc(sm2[:L, :], 128, L), ALU.mult, [kTM, sm2], [Kbg])
            yield
            Y3 = v3(Y.ap)
            if smp:
                CP(Y3[:L, :, :L], ident[:L, :L].unsqueeze(1).to_broadcast([L, 4, L]), [cst], [Y])
            else:
                bank = PB()
                b3 = v3(bank.ap)
                for h in range(4):
                    MM(b3[:L, h, :L], k3[:, h, :L], k3[:, h, :L], True, True, [ck], [bank], inc=(h == 3))
                P = GT[2]
                P3 = v3(P.ap)
                TT(P3[:L, :, :L], b3[:L, :, :L], dl3[:L, :, :L], ALU.mult, [bank, dl], [P])
                yield
                bank = PB()
                b3 = v3(bank.ap)
                for h in range(4):
                    TR(b3[:L, h, :L], P3[:L, h, :L], ident[:L, :L], [P, cst], [bank], inc=(h == 3))
                Q = GT[5]
                Q3 = v3(Q.ap)
                CP(Q3[:L, :, :L], b3[:L, :, :L], [bank], [Q], e="act")
                STT(Y3[:L, :, :L], Q3[:L, :, :L], -1.0, ident[:L, :L].unsqueeze(1).to_broadcast([L, 4, L]), ALU.mult, ALU.add,
                    [Q, cst], [Y])
                yield
                nlev = 6 if L == 128 else 3
                for lev in range(nlev):
                    bq = PB(); bp = PB()
                    bq3 = v3(bq.ap); bp3 = v3(bp.ap)
                    for h in range(4):
                        MM(bq3[:L, h, :L], P3[:L, h, :L], Q3[:L, h, :L], True, True, [P, Q], [bq], inc=(h == 3))
                    for h in range(4):
                        MM(bp3[:L, h, :L], Q3[:L, h, :L], P3[:L, h, :L], True, True, [P, Q], [bp], inc=(h == 3))
                    yield
                    Pn, Qn = (GT[3], GT[4]) if lev % 2 == 0 else (GT[2], GT[5])
                    CP(v3(Qn.ap)[:L, :, :L], bq3[:L, :, :L], [bq], [Qn], e="act")
                    CP(v3(Pn.ap)[:L, :, :L], bp3[:L, :, :L], [bp], [Pn], e="dve")
                    yield
                    P, Q = Pn, Qn
                    P3, Q3 = v3(P.ap), v3(Q.ap)
                    by = PB()
                    by3 = v3(by.ap)
                    for h in range(4):
                        MM(by3[:L, h, :L], P3[:L, h, :L], Y3[:L, h, :L], True, True, [P, Y], [by], inc=(h == 3))
                    TT(Y3[:L, :, :L], Y3[:L, :, :L], by3[:L, :, :L], ALU.add, [Y, by], [Y])
                    yield
            bank = PB()
            b3 = v3(bank.ap)
            for h in range(4):
                MM(b3[:, h, :L], Kbg[:L, h * 128:(h + 1) * 128], Y3[:L, h, :L], True, True, [Kbg, Y], [bank], inc=(h == 3))
            nWT = GT[6]
            nW3 = v3(nWT.ap)
            A(nW3[:, :, :L], b3[:, :, :L], AF.Copy, [bank], [nWT], scale=-1.0)
            yield
            Sg3 = v3(Sgdn.ap)
            vnb = PB()
            vn3 = v3(vnb.ap)
            for h in range(4):
                MM(vn3[:L, h, :], Y3[:L, h, :L], Vb[:L, h * 128:(h + 1) * 128], True, smp, [Y, Vb], [vnb],
                   inc=(smp and h == 3))
                if not smp:
                    MM(vn3[:L, h, :], nW3[:, h, :L], Sg3[:, h, :], False, True, [nWT, Sgdn], [vnb], inc=(h == 3))
            vnew = GT[7]
            if not smp:
                CP(vnew[:L, :], vnb[:L, :], [vnb], [vnew], e="act")
                yield
            else:
                wacc = GT[0]; qacc = GT[1]
                CP(wacc[:L, :], vnb[:L, :], [vnb], [wacc], e="act")
                MS(qacc[:L, :], 0.0, [qacc])
                for s in range(NS):
                    St = (GT[2], Sgdn)[s % 2]
                    S.dma("sp", v3(St.ap), s_gdn[l, s].rearrange("h d v -> d h v"), w=[St], stream="st")
                    for srcT, s3_, acc in ((qgT, v3(qgT.ap), qacc), (nWT, nW3, wacc)):
                        qm = GT[3]
                        TT(v3(qm[:, 0:64], 4), s3_[:, :, :NS], esel[:, s, :].unsqueeze(1).to_broadcast([128, 4, NS]),
                           ALU.mult, [srcT, esel], [qm])
                        tb_ = PB()
                        tb3 = v3(tb_.ap)
                        for h in range(4):
                            MM(tb3[:NS, h, :], v3(qm[:, 0:64], 4)[:, h, :], v3(St.ap)[:, h, :], True, True, [qm, St], [tb_],
                               inc=(h == 3))
                        TT(acc[:L, :], acc[:L, :], tb_[:NS, :], ALU.add, [acc, tb_], [acc])
                        yield
                CP(vnew[:L, :], wacc[:L, :], [wacc], [vnew])
            ob = PB()
            ob3 = v3(ob.ap)
            for h in range(4):
                MM(ob3[:L, h, :], at3[:L, h, :L], vnew[:L, h * 128:(h + 1) * 128], True, smp, [attT, vnew], [ob],
                   inc=(smp and h == 3))
                if not smp:
                    MM(ob3[:L, h, :], v3(qgT.ap)[:, h, :L], Sg3[:, h, :], False, True, [qgT, Sgdn], [ob], inc=(h == 3))
            yield
            if smp:
                TT(qacc[:L, :], qacc[:L, :], ob[:L, :], ALU.add, [qacc, ob], [qacc])
                osrc, okey = v3(qacc.ap), qacc
                for s in range(NS):
                    St = (GT[2], Sgdn)[s % 2]
                    S.dma("sp", v3(St.ap), s_gdn[l, s].rearrange("h d v -> d h v"), w=[St], stream="st")
                    vm = GT[3]
                    TS(vm[:NS, :], vnew[:NS, :], ident[:NS, s:s + 1], None, ALU.mult, None, [vnew, cst], [vm])
                    sb = PB()
                    sb3 = v3(sb.ap)
                    for h in range(4):
                        MM(sb3[:, h, :], kd[:NS, h * 128:(h + 1) * 128], vm[:NS, h * 128:(h + 1) * 128], True, True,
                           [kd, vm], [sb], inc=(h == 3))
                    So = (GT[4], GT[8])[s % 2]
                    for h in range(4):
                        STT(v3(So.ap)[:, h, :], v3(St.ap)[:, h, :], ebs[:, h, s:s + 1], sb3[:, h, :], ALU.mult, ALU.add,
                            [St, sb, ebs], [So])
                    S.dma("pool", o_gdn_s[l, s].rearrange("h d v -> d h v"), v3(So.ap), r=[So], stream="o")
                    yield
            else:
                osrc, okey = ob3, ob
                sb = PB()
                sb3 = v3(sb.ap)
                for h in range(4):
                    MM(sb3[:, h, :], kd[:L, h * 128:(h + 1) * 128], vnew[:L, h * 128:(h + 1) * 128], True, True, [kd, vnew], [sb],
                       inc=(h == 3))
                yield
                for h in range(4):
                    STT(Sg3[:, h, :], Sg3[:, h, :], eglast[:, h:h + 1], sb3[:, h, :], ALU.mult, ALU.add,
                        [Sgdn, sb, eglast], [Sgdn])
                if b == NBLK - 1:
                    S.dma("pool", o_gdn_p[l].rearrange("h d v -> d h v"), Sg3, r=[Sgdn], stream="o")
                yield
            osq = GT[8]
            A(v3(osq.ap)[:L, :, :], osrc[:L, :, :], AF.Square, [okey], [osq])
            S.op("dve", lambda: nc.vector.reduce_sum(out=sm3[:L, :], in_=v3(osq.ap)[:L, :, :], axis=AX.X), [osq], [sm3])
            RSQ(sm3[:L, :], sm3[:L, :], [sm3], [sm3], bias=EPS, scale=1.0 / 128.0)
            yield
            on = GT[5]
            for h in range(4):
                A(on[:L, h * 128:(h + 1) * 128], osrc[:L, h, :], AF.Copy, [okey, sm3], [on], scale=sm3[:L, h:h + 1])
            yield
            bank = PB()
            b3 = v3(bank.ap)
            for h in range(4):
                TR(b3[:, h, :L], on[:L, h * 128:(h + 1) * 128], ident[:L, :L], [on, cst], [bank], inc=(h == 3))
            A(z3[:, :, :L], z3[:, :, :L], AF.Silu, [ztg], [ztg])
            STT(mix[:, 8:12, :L], b3[:, :, :L], pft[:, 88:89], z3[:, :, :L], ALU.mult, ALU.mult, [bank, pft, ztg], [mixg])

        if MERGE == "sim":
            branches = []
            live = []
            for gen in (gen_gdn, gen_ret, gen_rg):
                g = gen()
                for mark in g:
                    if mark == "pre":
                        break
                live.append(g)
            if smp and l < NL - 1:
                load_win(l + 1)
            if not smp:
                if b + 1 < NBLK:
                    head(l, "p", b + 1)
                elif not SKIP_SAMPLE:
                    head(l, "s", 0)
            for g in live:
                S.rec = []
                for _ in g:
                    pass
                ops, S.rec = S.rec, None
                units, cur = [], []
                for o in ops:
                    cur.append(o)
                    if o[5]:
                        units.append(cur)
                        cur = []
                assert not cur
                branches.append(units)
            clock = {}
            wr = {}
            rd = {}
            ptr = [0] * len(branches)
            HOP = 0.3
            while True:
                best = None
                for bi, units in enumerate(branches):
                    if ptr[bi] >= len(units):
                        continue
                    u = units[ptr[bi]]
                    e = u[0][1]
                    t = clock.get(e, 0.0)
                    for o in u:
                        for k in o[3]:
                            if k in wr:
                                t = max(t, wr[k][0] + (HOP if wr[k][1] != e else 0.0))
                        for k in o[4]:
                            if k in wr:
                                t = max(t, wr[k][0] + (HOP if wr[k][1] != e else 0.0))
                            if k in rd:
                                t = max(t, rd[k][0] + (HOP if rd[k][1] != e else 0.0))
                    if best is None or t < best[0] - 1e-9:
                        best = (t, bi)
                if best is None:
                    break
                t, bi = best
                u = branches[bi][ptr[bi]]
                ptr[bi] += 1
                e = u[0][1]
                for o in u:
                    kind, eng, fn, r_, w_, inc, cost = o
                    if kind == "dma":
                        out_, in_, kw = fn
                        S.dma(eng, out_, in_, r=r_, w=w_, **kw)
                        done = t + 2.5
                        t += cost
                        who = "dma"
                    else:
                        S.op(eng, fn, r_, w_, inc=inc)
                        t += cost
                        done = t
                        who = eng
                    for k in r_:
                        if k not in rd or rd[k][0] < done:
                            rd[k] = (done, who)
                    for k in w_:
                        wr[k] = (done, who)
                        rd.pop(k, None)
                clock[e] = t
            gens = []
        else:
            gens = [(gen_rg(), 1), (gen_ret(), 1), (gen_gdn(), GDN_W)]
        if MERGE == "seq":
            for g, _ in gens:
                for _ in g:
                    pass
            gens = []
        while gens:
            for item in list(gens):
                g, wgt = item
                for _ in range(wgt):
                    try:
                        next(g)
                    except StopIteration:
                        gens.remove(item)
                        break

        z = [RT[0], RT[1]]
        for n in range(2):
            bank = ps[n]
            for kc in range(12):
                MM(bank[:L, :], mix[:, kc, :L], Wout[:, kc, n * 512:(n + 1) * 512], kc == 0, kc == 11,
                   [mixr, mixe, mixg, Wout], [bank], inc=(kc == 11))
            S.dma("sp", z[n][:L, :], xsrc_d[:, n * 512:(n + 1) * 512], r=[xdkey], w=[z[n]], stream="x")
            STT(z[n][:L, :], z[n][:L, :], float(ALPHA), bank[:L, :], ALU.mult, ALU.add, [z[n], bank], [z[n]])
            S.op("dve", lambda n=n: nc.vector.bn_stats(out=bst2[:L, n, :], in_=z[n][:L, :]), [z[n]], [bst2])
        S.op("dve", lambda: nc.vector.bn_aggr(out=bmv2[:L, :], in_=bst2[:L, 0:2, :]), [bst2], [bmv2])
        RSQ(sm2[:L, 0:1], bmv2[:L, 1:2], [bmv2], [sm2])
        for n in range(2):
            sl = slice(n * 512, (n + 1) * 512)
            if n == 0:
                STT(nmr2[:L, :], bmv2[:L, 0:1], -1.0, sm2[:L, 0:1], ALU.mult, ALU.mult, [bmv2, sm2], [nmr2])
            A(z[n][:L, :], z[n][:L, :], AF.Identity, [z[n], nmr2, sm2], [z[n]], bias=nmr2[:L, 0:1], scale=sm2[:L, 0:1])
            TT(z[n][:L, :], z[n][:L, :], rowt[:L, sl], ALU.mult, [z[n], rowt], [z[n]])
            TT(z[n][:L, :], z[n][:L, :], rowt[:L, 1024 + n * 512:1024 + (n + 1) * 512], ALU.add, [z[n], rowt], [z[n]])
            if smp:
                if l == NL - 1:
                    S.dma("pool", y_s[:, sl], z[n][:NS, :], r=[z[n]], stream="o")
                else:
                    S.dma("pool", xsscr[:, sl], z[n][:NS, :], r=[z[n]], w=[("xsscr", 0)], sname=f"xs{n}_{l % 2}")
            else:
                if l == NL - 1:
                    if b > 0:
                        S.dma("pool", y_p[t0 - 16:t0 - 16 + L, sl], z[n][:L, :], r=[z[n]], stream="o")
                else:
                    S.dma("pool", xscr[t0:t0 + L, sl], z[n][:L, :], r=[z[n]], w=[("xscr", b)], sname=f"xo{n}_{l % 2}")


    for l in range(NL):
        layer_setup(l)
        for b in range(NBLK):
            block(l, "p", b)
        if not SKIP_SAMPLE:
            block(l, "s", 0)
    S.finish("sp")
    print("ops", S.nops, "waits", S.nwaits, "sems", S.nsem + len(S.dstream))
    return nc


def _consts():
    cst = np.zeros((128, NCST), np.float32)
    i = np.arange(128)
    cst[:, 0:128] = np.eye(128)
    cst[:, 128:256] = (i[None, :] >= i[:, None])
    cst[:, 256:384] = (i[None, :] < i[:, None])
    cst[:, 384:512] = 1.0
    sc = 128.0 ** -0.5
    for h in range(4):
        g = np.float64(GAM[h])
        cst[:, 512 + h] = g ** (i + 1.0)
        cst[:, 516 + h] = g ** (-(i + 1.0)) * sc
        cst[:, 520 + h] = g ** (127.0 - i) * sc
        cst[:16, 524 + h] = g ** (15.0 - i[:16]) * sc
        cst[:, 528 + h] = g
        cst[:, 532 + h] = sc / g
        cst[:, 536 + h] = sc
    half = 64
    inv = (np.float32(10000.0) ** (-np.arange(half, dtype=np.float32) / np.float32(half))).astype(np.float32)
    rope = np.zeros((18, 128, 128), np.float32)
    for b in range(18):
        if b == 0:
            pos = np.arange(16, dtype=np.float32)
        elif b < 17:
            pos = 16 + 128 * (b - 1) + np.arange(128, dtype=np.float32)
        else:
            pos = np.full(16, 16384.0, np.float32)
        ang = (pos[:, None].astype(np.float32) * inv[None, :]).astype(np.float32)
        rope[b, :len(pos), 0:64] = np.cos(ang.astype(np.float64))
        rope[b, :len(pos), 64:128] = np.sin(ang.astype(np.float64))
    esel = np.eye(16, dtype=np.float32).reshape(1, 256)
    return cst, rope, esel


_NC_CACHE = {}


def kernel(x_prompt, x_sample, state_rglru_h, state_rglru_conv, state_ret, state_gdn_conv, state_gdn,
           meta_tokens, w_in, rg_conv_w, rg_conv_b, rg_w_a, rg_b_a, rg_w_x, rg_b_x, rg_lambda,
           ret_gn_w, ret_gn_b, gdn_conv_w, gdn_a_log, gdn_dt_bias, gdn_norm_w, w_out, ln_w, ln_b):
    f = lambda a: np.ascontiguousarray(np.asarray(a, dtype=np.float32))
    x_prompt, x_sample, meta_tokens = f(x_prompt), f(x_sample), f(meta_tokens)
    w_in, w_out = f(w_in), f(w_out)
    pf = np.zeros((NL, 128, NPF), np.float32)

    def fm(v, nch):
        return f(v).reshape(NL, nch, 128).transpose(0, 2, 1)

    pf[:, :, 0:16] = f(rg_conv_w).reshape(NL, 4, 4, 128).transpose(0, 3, 2, 1).reshape(NL, 128, 16)
    pf[:, :, 16:20] = fm(rg_conv_b, 4)
    pf[:, :, 20:24] = fm(rg_b_a, 4)
    pf[:, :, 24:28] = fm(rg_b_x, 4)
    pf[:, :, 28:32] = fm(rg_lambda, 4)
    pf[:, :, 32:36] = fm(ret_gn_w, 4)
    pf[:, :, 36:40] = fm(ret_gn_b, 4)
    pf[:, :, 40:88] = f(gdn_conv_w).reshape(NL, 4, 12, 128).transpose(0, 3, 2, 1).reshape(NL, 128, 48)
    pf[:, :, 88] = f(gdn_norm_w)
    rgw = np.zeros((NL, 128, 2, 4, 128), np.float32)
    for which, wsrc in ((0, f(rg_w_a)), (1, f(rg_w_x))):
        for n in range(8):
            c, o = n // 2, (n % 2) * 64
            rgw[:, o:o + 64, which, c, o:o + 64] = wsrc[:, n]
    rgw = rgw.reshape(NL, 128, 1024)
    rows = np.concatenate([f(ln_w), f(ln_b), f(gdn_a_log), f(gdn_dt_bias)], axis=1).reshape(NL, 1, 2056)
    cst, rope, esel = _consts()
    if "nc" not in _NC_CACHE:
        _NC_CACHE["nc"] = build_nc()
    nc = _NC_CACHE["nc"]
    in_maps = []
    for c in range(8):
        sl = slice(NS * c, NS * (c + 1))
        m = {
            "xp": np.ascontiguousarray(np.concatenate([meta_tokens, x_prompt[c]], axis=0)),
            "xs": np.ascontiguousarray(x_sample[sl, 0, :]),
            "s_h": np.ascontiguousarray(f(state_rglru_h)[:, sl].reshape(NL, NS, 4, 128).transpose(0, 3, 2, 1)),
            "s_rgc": np.ascontiguousarray(f(state_rglru_conv)[:, sl].reshape(NL, NS, 3, 4, 128).transpose(0, 4, 3, 2, 1)),
            "s_gc": np.ascontiguousarray(f(state_gdn_conv)[:, sl].reshape(NL, NS, 3, 12, 128).transpose(0, 4, 3, 2, 1)),
            "s_ret": np.ascontiguousarray(f(state_ret)[:, sl]),
            "s_gdn": np.ascontiguousarray(f(state_gdn)[:, sl]),
            "w_in": w_in, "w_out": w_out, "pf": pf, "rgw": rgw, "rows": rows,
            "cst": cst, "ropet": rope, "esel": esel,
        }
        in_maps.append(m)
    res = run_bass_kernel_spmd(nc, in_maps, core_ids=list(range(8)))
    R = res.results
    g = lambda k: [np.asarray(R[c][k], dtype=np.float32) for c in range(8)]
    y_prompt = np.stack(g("y_p"), 0)
    y_sample = np.concatenate(g("y_s"), 0)[:, None, :]
    hp = np.stack([a.transpose(0, 2, 1).reshape(NL, 512) for a in g("o_h_p")], 1)
    rgcp = np.stack([a.transpose(0, 3, 2, 1).reshape(NL, 3, 512) for a in g("o_rgc_p")], 1)
    retp = np.stack(g("o_ret_p"), 1)
    gcp = np.stack([a.transpose(0, 3, 2, 1).reshape(NL, 3, 1536) for a in g("o_gc_p")], 1)
    gdnp = np.stack(g("o_gdn_p"), 1)
    hs = np.concatenate([a.transpose(0, 3, 2, 1).reshape(NL, NS, 512) for a in g("o_h_s")], 1)
    rgcs = np.concatenate([a.transpose(0, 4, 3, 2, 1).reshape(NL, NS, 3, 512) for a in g("o_rgc_s")], 1)
    rets = np.concatenate(g("o_ret_s"), 1)
    gcs = np.concatenate([a.transpose(0, 4, 3, 2, 1).reshape(NL, NS, 3, 1536) for a in g("o_gc_s")], 1)
    gdns = np.concatenate(g("o_gdn_s"), 1)
    c = np.ascontiguousarray
    return (c(y_prompt), c(y_sample), c(hp), c(rgcp), c(retp), c(gcp), c(gdnp), c(hs), c(rgcs), c(rets), c(gcs), c(gdns))
```

```python
import numpy as np
import concourse.bass as bass
import concourse.mybir as mybir
from concourse.bass_utils import run_bass_kernel_spmd

F32 = mybir.dt.float32
BF16 = mybir.dt.bfloat16
ALU = mybir.AluOpType
AF = mybir.ActivationFunctionType
AX = mybir.AxisListType

EPOCH = 12000
RE = "dve"
DEFCOST = {"pe": 0.2, "act": 0.4, "dve": 0.3, "pool": 0.05, "sp": 0.05}
GDN_STOP = 100000
ENABLE = [True, True, True]
NET = 6
SKIP_SAMPLE = False
PER_TILE_SEMS = True
MERGE = "sim"
SAME_SYNC = True

NL = 4
DM = 1024
DIN = 5128
NTOK = 2064
NBLK = 17
NS = 16
ALPHA = 8.0 ** 0.25
EPS = 1e-6
GAM = [1.0 - 2.0 ** (-5.0 - h) for h in range(4)]
NCST = 540
NPF = 89
COL = dict(rgx=0, rgz=512, rq=1024, rk=1536, rv=2048, rz=2560, gq=3072, gk=3584, gv=4096, gz=4608, gab=5120)


class Tile:
    def __init__(self, nc, name, shape, dtype, psum=False):
        if psum:
            self.h = nc.alloc_psum_tensor("T_" + name, list(shape), dtype)
        else:
            self.h = nc.alloc_sbuf_tensor("T_" + name, list(shape), dtype)
        self.ap = self.h.ap()
        self.name = name

    def __getitem__(self, k):
        return self.ap[k]


class Sched:
    def __init__(self, nc):
        self.nc = nc
        self.eng = {"pe": nc.tensor, "act": nc.scalar, "dve": nc.vector, "pool": nc.gpsimd, "sp": nc.sync}
        self.sem = {}
        self.cnt = {}
        self.pend = {}
        self.nsem = 0
        for e in self.eng:
            self._new_sem(e)
            self.pend[e] = False
        self.lastw = {}
        self.readers = {}
        self.waited = {e: {} for e in self.eng}
        self.dstream = {}
        self.nwaits = 0
        self.nops = 0
        self.rec = None

    def _new_sem(self, e):
        self.sem[e] = self.nc.alloc_semaphore(f"s_{e}_{self.nsem}")
        self.nsem += 1
        self.cnt[e] = 0

    def _deps(self, r, w):
        evs = []
        for k in r:
            if k in self.lastw:
                evs.append(self.lastw[k] + (True,))
        for k in w:
            if k in self.lastw:
                evs.append(self.lastw[k] + (False,))
            evs.extend(v + (False,) for v in self.readers.get(k, {}).values())
        return evs

    def _do_waits(self, e, evs):
        need = {}
        for sem, val, src, raw in evs:
            if src == e and not (SAME_SYNC or raw):
                continue
            if src.startswith("dma:"):
                val = self.dstream[src[4:]][1]
            if val > need.get(sem, (0, None))[0]:
                need[sem] = (val, src)
        for sem, (val, src) in need.items():
            if self.waited[e].get(sem, 0) >= val:
                continue
            if src == e and sem is self.sem[e] and val > self.cnt[e]:
                continue
            self.eng[e].wait_ge(sem, val)
            self.waited[e][sem] = val
            self.nwaits += 1

    def _register(self, ev, r, w):
        sem = ev[0]
        for k in r:
            self.readers.setdefault(k, {})[sem] = ev
        for k in w:
            self.lastw[k] = ev
            self.readers[k] = {}

    def op(self, e, fn, r=(), w=(), inc=True, cost=None):
        if self.rec is not None:
            self.rec.append(("op", e, fn, tuple(r), tuple(w), inc, cost if cost else DEFCOST[e]))
            return None
        self._do_waits(e, self._deps(r, w))
        if self.cnt[e] >= EPOCH and not self.pend[e]:
            self._new_sem(e)
        ins = fn()
        self.nops += 1
        if inc:
            self.cnt[e] += 1
            ins.then_inc(self.sem[e], 1)
            ev = (self.sem[e], self.cnt[e], e)
            self.pend[e] = False
        else:
            ev = (self.sem[e], self.cnt[e] + 1, e)
            self.pend[e] = True
        self._register(ev, r, w)
        return ins

    def dma(self, q, out, in_, r=(), w=(), stream="d", sname=None, **kw):
        if self.rec is not None:
            self.rec.append(("dma", q, (out, in_, dict(kw, sname=sname)), tuple(r), tuple(w), True, 0.05))
            return None
        tl = [k for k in w if isinstance(k, Tile)]
        if sname is not None:
            stream = sname
        elif not PER_TILE_SEMS:
            pass
        elif tl:
            stream = "ld_" + tl[0].name
        else:
            stream = "st_" + [k for k in r if isinstance(k, Tile)][0].name
        self._do_waits(q, self._deps(r, w))
        if stream not in self.dstream:
            self.dstream[stream] = [self.nc.alloc_semaphore(f"d_{stream}"), 0]
        st = self.dstream[stream]
        ins = self.eng[q].dma_start(out=out, in_=in_, **kw)
        st[1] += 16
        ins.then_inc(st[0], 16)
        ev = (st[0], st[1], "dma:" + stream)
        self._register(ev, r, w)
        return ins

    def finish(self, e="sp"):
        for name, (sem, tot) in self.dstream.items():
            if tot > 0:
                self.eng[e].wait_ge(sem, tot)


def v3(ap, c=4):
    return ap.rearrange("p (c n) -> p c n", c=c)


def build_nc():
    nc = bass.Bass("TRN2", target_bir_lowering=False)

    def din(name, shape):
        return nc.dram_tensor(name, list(shape), F32, kind="ExternalInput").ap()

    def dout(name, shape):
        return nc.dram_tensor(name, list(shape), F32, kind="ExternalOutput").ap()

    xp_d = din("xp", [NTOK, DM])
    xs_d = din("xs", [NS, DM])
    s_h = din("s_h", [NL, 128, 4, NS])
    s_rgc = din("s_rgc", [NL, 128, 4, 3, NS])
    s_gc = din("s_gc", [NL, 128, 12, 3, NS])
    s_ret = din("s_ret", [NL, NS, 4, 128, 128])
    s_gdn = din("s_gdn", [NL, NS, 4, 128, 128])
    w_in = din("w_in", [NL, DM, DIN])
    w_out = din("w_out", [NL, 1536, DM])
    pf_d = din("pf", [NL, 128, NPF])
    rgw_d = din("rgw", [NL, 128, 2 * 4 * 128])
    rows_d = din("rows", [NL, 1, 2056])
    cst_d = din("cst", [128, NCST])
    rope_d = din("ropet", [18, 128, 128])
    esel_d = din("esel", [1, 256])

    y_p = dout("y_p", [2048, DM])
    y_s = dout("y_s", [NS, DM])
    o_h_p = dout("o_h_p", [NL, 128, 4])
    o_rgc_p = dout("o_rgc_p", [NL, 128, 4, 3])
    o_ret_p = dout("o_ret_p", [NL, 4, 128, 128])
    o_gc_p = dout("o_gc_p", [NL, 128, 12, 3])
    o_gdn_p = dout("o_gdn_p", [NL, 4, 128, 128])
    o_h_s = dout("o_h_s", [NL, 128, 4, NS])
    o_rgc_s = dout("o_rgc_s", [NL, 128, 4, 3, NS])
    o_ret_s = dout("o_ret_s", [NL, NS, 4, 128, 128])
    o_gc_s = dout("o_gc_s", [NL, 128, 12, 3, NS])
    o_gdn_s = dout("o_gdn_s", [NL, NS, 4, 128, 128])
    xscr = nc.dram_tensor("xscr", [NTOK, DM], F32, kind="Internal").ap()
    xsscr = nc.dram_tensor("xsscr", [NS, DM], F32, kind="Internal").ap()

    S = Sched(nc)

    def TL(name, shape, dt=F32):
        return Tile(nc, name, shape, dt)

    Win = TL("Win", [128, 8, DIN], BF16)
    Wout = TL("Wout", [128, 12, DM], BF16)
    cst = TL("cst", [128, NCST])
    identb = TL("identb", [128, 128], BF16)
    esel = TL("esel", [128, 16, 16])
    pft = TL("pft", [128, NPF])
    rgwt = TL("rgwt", [128, 2, 4, 128], BF16)
    rowt = TL("rowt", [128, 2056])
    nc8sp = TL("nc8sp", [128, 4])
    negA = TL("negA", [128, 4])
    ropeb = TL("ropeb", [128, 128])
    X = TL("X", [128, 4, 131])
    GX = TL("GX", [128, 12, 131])
    h0s = TL("h0s", [128, 4, NS])
    Sret = TL("Sret", [128, 512])
    Sretb = TL("Sretb", [128, 512], BF16)
    Sgdn = TL("Sgdn", [128, 512])
    hprev = TL("hprev", [128, 4])
    mix = TL("mix", [128, 12, 128], BF16)
    xt = TL("xt", [128, DM])
    xT = TL("xT", [128, 8, 128], BF16)
    ztr = TL("ztr", [128, 512])
    zte = TL("zte", [128, 512])
    ztg = TL("ztg", [128, 512])
    xcb = TL("xcb", [128, 512], BF16)
    mixr, mixe, mixg = "mixr", "mixe", "mixg"
    Vb = TL("Vb", [128, 512])
    Kbg = TL("Kbg", [128, 512])
    kd = TL("kd", [128, 512])
    qgT = TL("qgT", [128, 512])
    attT = TL("attT", [128, 512])
    Y = TL("Y", [128, 512])
    gabt = TL("gabt", [128, 8])
    gt = TL("gt", [128, 4])
    betat = TL("betat", [128, 4])
    gct = TL("gct", [128, 4])
    egt = TL("egt", [128, 4])
    eglt = TL("eglt", [128, 4])
    eglast = TL("eglast", [128, 4])
    sm1 = TL("sm1", [128, 4])
    sm2 = TL("sm2", [128, 4])
    bst = TL("bst", [128, 4, 6])
    bmv = TL("bmv", [128, 4, 2])
    ebs = TL("ebs", [128, 4, NS])
    vb = TL("vb", [128, 512], BF16)
    k2b = TL("k2b", [128, 512], BF16)
    sm3 = TL("sm3", [128, 4])
    nmr = TL("nmr", [128, 4])
    nmr2 = TL("nmr2", [128, 1])
    ngct = TL("ngct", [128, 4])
    bst2 = TL("bst2", [128, 2, 6])
    bmv2 = TL("bmv2", [128, 2])
    RT = [TL(f"rt{i}", [128, 512]) for i in range(4)]
    ET = [TL(f"et{i}", [128, 512]) for i in range(NET)]
    GT = [TL(f"gt{i}", [128, 512]) for i in range(9)]
    B = [TL(f"bb{i}", [128, 1024], BF16) for i in range(3)]
    ps = [Tile(nc, f"ps{i}", [128, 512], F32, psum=True) for i in range(8)]
    print("sbuf bytes remaining", nc.sbuf_bytes_remaining)
    GDN_W = 2

    def fs(ap):
        n = 1
        for d in ap.shape[1:]:
            n *= d
        return n

    def A(out, in_, func, r, w, bias=0.0, scale=1.0):
        S.op("act", lambda: nc.scalar.activation(out=out, in_=in_, func=func, bias=bias, scale=scale), r, w,
             cost=0.22 + fs(out) / 1000.0)

    def TT(out, a, b, op, r, w, e="dve"):
        if e == "pool":
            S.op("pool", lambda: nc.gpsimd.tensor_tensor(out=out, in0=a, in1=b, op=op), r, w, cost=0.3 + fs(out) / 500.0)
            return
        S.op("dve", lambda: nc.vector.tensor_tensor(out=out, in0=a, in1=b, op=op), r, w, cost=0.2 + fs(out) / 1000.0)

    def TS(out, a, s1, s2, op0, op1, r, w):
        if s2 is None:
            S.op("dve", lambda: nc.vector.tensor_scalar(out=out, in0=a, scalar1=s1, scalar2=None, op0=op0), r, w,
                 cost=0.2 + fs(out) / 1000.0)
        else:
            S.op("dve", lambda: nc.vector.tensor_scalar(out=out, in0=a, scalar1=s1, scalar2=s2, op0=op0, op1=op1), r, w,
                 cost=0.2 + fs(out) / 1000.0)

    def STT(out, a, s, b, op0, op1, r, w):
        S.op("dve", lambda: nc.vector.scalar_tensor_tensor(out=out, in0=a, scalar=s, in1=b, op0=op0, op1=op1), r, w,
             cost=0.2 + fs(out) / 1000.0)

    def CP(out, in_, r, w, e="dve"):
        if e == "dve":
            S.op("dve", lambda: nc.vector.tensor_copy(out=out, in_=in_), r, w, cost=0.2 + fs(out) / 1000.0)
        else:
            S.op("act", lambda: nc.scalar.activation(out=out, in_=in_, func=AF.Copy), r, w, cost=0.22 + fs(out) / 1000.0)

    def MS(out, val, w):
        S.op("dve", lambda: nc.vector.memset(out, val), (), w)

    def MM(out, lhsT, rhs, st, sp, r, w, inc=True):
        S.op("pe", lambda: nc.tensor.matmul(out, lhsT=lhsT, rhs=rhs, start=st, stop=sp), r, w, inc=inc,
             cost=(0.11 + fs(out) / 1200.0) * (2.0 if lhsT.dtype == F32 else 1.0))

    def TR(out, in_, ident, r, w, inc=True):
        S.op("pe", lambda: nc.tensor.transpose(out=out, in_=in_, identity=ident), r, w, inc=inc,
             cost=(0.11 + fs(out) / 1200.0) * (2.0 if in_.dtype == F32 else 1.0))

    def RSQ(out, in_, r, w, bias=EPS, scale=1.0):
        A(out, in_, AF.Ln, r, w, bias=bias, scale=scale)
        A(out, out, AF.Exp, w, w, scale=-0.5)

    ident = cst[:, 0:128]
    maskT = cst[:, 128:256]
    strictL = cst[:, 256:384]
    ones = cst[:, 384:512]

    S.dma("sp", cst[:], cst_d, w=[cst], stream="c")
    S.dma("sp", esel[:].rearrange("p a b -> p (a b)"), esel_d.partition_broadcast(128), w=[esel], stream="c")
    CP(identb[:], ident, [cst], [identb], e="act")

    def bc(ap2, n, L):
        return ap2.unsqueeze(2).to_broadcast([L, 4, n])

    def load_win(l):
        for kc in range(8):
            S.dma("pool", Win[:, kc, :], w_in[l, kc * 128:(kc + 1) * 128, :], w=[Win], stream="w")

    def layer_setup(l):
        if l == 0:
            load_win(0)
        for kc in range(12):
            S.dma("pool", Wout[:, kc, :], w_out[l, kc * 128:(kc + 1) * 128, :], w=[Wout], stream="w")
        S.dma("pool", rgwt[:].rearrange("p a c n -> p (a c n)"), rgw_d[l], w=[rgwt], stream="w")
        S.dma("sp", pft[:], pf_d[l], w=[pft], stream="c")
        S.dma("sp", rowt[:], rows_d[l].partition_broadcast(128), w=[rowt], stream="c")
        A(nc8sp[:], pft[:, 28:32], AF.Exp, [pft], [nc8sp], scale=-1.0)
        A(nc8sp[:], nc8sp[:], AF.Ln, [nc8sp], [nc8sp], bias=1.0)
        TS(nc8sp[:], nc8sp[:], -8.0, None, ALU.mult, None, [nc8sp], [nc8sp])
        A(negA[:], rowt[:, 2048:2052], AF.Exp, [rowt], [negA])
        TS(negA[:], negA[:], -1.0, None, ALU.mult, None, [negA], [negA])
        MS(Sret[:], 0.0, [Sret])
        MS(Sretb[:], 0.0, [Sretb])
        MS(Sgdn[:], 0.0, [Sgdn])
        MS(hprev[:], 0.0, [hprev])
        MS(X[:, :, 0:3], 0.0, [X])
        MS(GX[:, :, 0:3], 0.0, [GX])

    def xsource(l, mode, b):
        if mode == "s":
            return NS, 0, (xs_d if l == 0 else xsscr), ("xsscr", 0)
        L = 16 if b == 0 else 128
        t0 = 0 if b == 0 else 16 + 128 * (b - 1)
        return L, t0, (xp_d if l == 0 else xscr)[t0:t0 + L, :], ("xscr", b)

    head_done = set()

    def head(l, mode, b):
        L, t0, src, dkey = xsource(l, mode, b)
        S.dma("sp", xt[:L, :], src, r=[dkey], w=[xt], stream="x")
        xb = B[0]
        CP(xb[:L, :], xt[:L, :], [xt], [xb], e="act")
        pt = ps[0]
        ptb = pt.ap.bitcast(BF16)
        for kc in range(8):
            TR(ptb[:, kc * 128:kc * 128 + L], xb[:L, kc * 128:(kc + 1) * 128], identb[:L, :L], [xb, identb], [pt],
               inc=(kc == 7))
        CP(xT[:, :, :L], v3(ptb, 8)[:, :, :L], [pt], [xT], e="act")
        head_done.add((l, mode, b))

    def block(l, mode, b):
        smp = mode == "s"
        L, t0, xsrc_d, xdkey = xsource(l, mode, b)
        if (l, mode, b) not in head_done:
            head(l, mode, b)
        rb = 17 if smp else b
        S.dma("sp", ropeb[:L, :], rope_d[rb, 0:L, :], w=[ropeb], stream="x")
        mT = ident if smp else maskT
        ci = 528 if smp else 512
        qdec = cst[:L, ci:ci + 4]
        kdecp = cst[:L, ci + 4:ci + 8]
        if smp:
            k2dec = cst[:L, 536:540]
        elif L == 128:
            k2dec = cst[:L, 520:524]
        else:
            k2dec = cst[:L, 524:528]
        Xs = X.ap.rearrange("p c n -> p (c n)")[:, 0:4 * 4 * NS].rearrange("p (c j s) -> p c j s", c=4, j=4)
        GXs = GX.ap.rearrange("p c n -> p (c n)")[:, 0:12 * 4 * NS].rearrange("p (c j s) -> p c j s", c=12, j=4)

        def fm_group(bank, c0, dst_ap, dst_key, e="act"):
            b3 = v3(bank.ap)
            for c in range(4):
                for kc in range(8):
                    MM(b3[:, c, :L], Win[:, kc, c0 + c * 128:c0 + (c + 1) * 128], xT[:, kc, :L], kc == 0, kc == 7,
                       [Win, xT], [bank], inc=(c == 3 and kc == 7))
            CP(dst_ap, b3[:, :, :L], [bank], [dst_key], e=e)

        def tm_group(bank, c0, n, dst_ap, dst_key, e="dve"):
            for kc in range(8):
                MM(bank[:L, :n], xT[:, kc, :L], Win[:, kc, c0:c0 + n], kc == 0, kc == 7, [Win, xT], [bank],
                   inc=(kc == 7))
            CP(dst_ap, bank[:L, :n], [bank], [dst_key], e=e)

        def conv_chunk(src_tap, wcol0, c, o, dst_key, src_key, bias_col=None):
            w0 = pft[:, wcol0 + c * 4:wcol0 + c * 4 + 1]
            if bias_col is not None:
                TS(o, src_tap(c, 0), w0, pft[:, bias_col + c:bias_col + c + 1], ALU.mult, ALU.add,
                   [src_key, pft], [dst_key])
            else:
                TS(o, src_tap(c, 0), w0, None, ALU.mult, None, [src_key, pft], [dst_key])
            for j in range(1, 4):
                STT(o, src_tap(c, j), pft[:, wcol0 + c * 4 + j:wcol0 + c * 4 + j + 1], o, ALU.mult, ALU.add,
                    [src_key, pft, dst_key], [dst_key])

        def gen_rg():
            pa, pb = ps[0], ps[1]
            if smp:
                S.dma("sp", Xs[:, :, 0:3, :], s_rgc[l], w=[X], stream="st")
                S.dma("sp", h0s[:], s_h[l], w=[h0s], stream="st")
                fm_group(pa, COL["rgx"], Xs[:, :, 3, :], X)
                tap = lambda c, j: Xs[:, c, j, :]
            else:
                fm_group(pa, COL["rgx"], X[:, :, 3:3 + L], X)
                tap = lambda c, j: X[:, c, j:j + L]
            yield
            z3 = v3(ztr.ap)
            fm_group(pb, COL["rgz"], z3[:, :, :L], ztr, e="dve")
            yield "pre"
            xc = RT[0]
            xc3 = v3(xc.ap)
            for c in range(4):
                conv_chunk(tap, 0, c, xc3[:, c, :L], xc, X, bias_col=16)
                yield
            xcb3 = v3(xcb.ap)
            CP(xcb3[:, :, :L], xc3[:, :, :L], [xc], [xcb], e="act")
            rt = RT[1]; it = RT[2]; at = RT[3]
            r3 = v3(rt.ap); i3 = v3(it.ap); a3 = v3(at.ap)
            for which, dst3, dkey, bcol, bank in ((0, r3, rt, 20, pa), (1, i3, it, 24, pb)):
                b3 = v3(bank.ap)
                for c in range(4):
                    MM(b3[:, c, :L], rgwt[:, which, c, :], xcb3[:, c, :L], True, True, [rgwt, xcb], [bank], inc=(c == 3))
                yield
                for c in range(4):
                    A(dst3[:, c, :L], b3[:, c, :L], AF.Sigmoid, [bank, pft], [dkey], bias=pft[:, bcol + c:bcol + c + 1])
                yield
            for c in range(4):
                A(a3[:, c, :L], r3[:, c, :L], AF.Exp, [rt, nc8sp], [at], scale=nc8sp[:, c:c + 1])
            yield
            mt = RT[1]
            m3 = v3(mt.ap)
            A(m3[:, :, :L], a3[:, :, :L], AF.Square, [at], [mt])
            A(m3[:, :, :L], m3[:, :, :L], AF.Ln, [mt], [mt], bias=1.0, scale=-1.0)
            A(m3[:, :, :L], m3[:, :, :L], AF.Exp, [mt], [mt], scale=0.5)
            yield
            TT(i3[:, :, :L], i3[:, :, :L], xc3[:, :, :L], ALU.mult, [it, xc], [it])
            yield
            TT(i3[:, :, :L], i3[:, :, :L], m3[:, :, :L], ALU.mult, [it, mt], [it])
            yield
            ht = RT[0]
            h3 = v3(ht.ap)
            if smp:
                TT(h3[:, :, :L], a3[:, :, :L], h0s[:], ALU.mult, [at, h0s], [ht])
                TT(h3[:, :, :L], h3[:, :, :L], i3[:, :, :L], ALU.add, [ht, it], [ht])
                S.dma("pool", o_h_s[l], h3[:, :, :L], r=[ht], stream="o")
                S.dma("pool", o_rgc_s[l], Xs[:, :, 1:4, :], r=[X], stream="o")
            else:
                for c in range(4):
                    S.op("dve", lambda c=c: nc.vector.tensor_tensor_scan(
                        out=h3[:, c, :L], data0=a3[:, c, :L], data1=i3[:, c, :L], initial=hprev[:, c:c + 1],
                        op0=ALU.mult, op1=ALU.add), [at, it, hprev], [ht])
                    yield
                CP(hprev[:].unsqueeze(2), h3[:, :, L - 1:L], [ht], [hprev])
                if b == NBLK - 1:
                    S.dma("pool", o_h_p[l], hprev[:], r=[hprev], stream="o")
                    S.dma("pool", o_rgc_p[l], X[:, :, L:L + 3], r=[X], stream="o")
                CP(X[:, :, 0:3], X[:, :, L:L + 3], [X], [X])
            yield
            A(z3[:, :, :L], z3[:, :, :L], AF.Silu, [ztr], [ztr])
            TT(mix[:, 0:4, :L], h3[:, :, :L], z3[:, :, :L], ALU.mult, [ht, ztr], [mixr])

        def gen_ret():
            bk = [ps[2], ps[3], ps[4]]
            rq = ET[0]; rk = ET[1]
            tm_group(bk[0], COL["rq"], 512, rq[:L, :], rq)
            yield
            tm_group(bk[1], COL["rk"], 512, rk[:L, :], rk)
            yield
            tm_group(bk[2], COL["rv"], 512, vb[:L, 0:512], vb)
            yield
            z3 = v3(zte.ap)
            fm_group(bk[0], COL["rz"], z3[:, :, :L], zte, e="dve")
            yield "pre"
            cosb = ropeb[:L, 0:64].unsqueeze(1).to_broadcast([L, 4, 64])
            sinb = ropeb[:L, 64:128].unsqueeze(1).to_broadcast([L, 4, 64])

            def rope(src, dst, tmp):
                s3 = v3(src[:L, :]); d3 = v3(dst[:L, :]); t3 = v3(tmp[:L, :])
                t1 = s3[:, :, 0:64]; t2 = s3[:, :, 64:128]
                TT(d3[:, :, 0:64], t1, cosb, ALU.mult, [src, ropeb], [dst], e=RE)
                TT(t3[:, :, 0:64], t2, sinb, ALU.mult, [src, ropeb], [tmp], e=RE)
                yield
                TT(d3[:, :, 0:64], d3[:, :, 0:64], t3[:, :, 0:64], ALU.subtract, [dst, tmp], [dst], e=RE)
                TT(d3[:, :, 64:128], t1, sinb, ALU.mult, [src, ropeb], [dst], e=RE)
                yield
                TT(t3[:, :, 64:128], t2, cosb, ALU.mult, [src, ropeb], [tmp], e=RE)
                TT(d3[:, :, 64:128], d3[:, :, 64:128], t3[:, :, 64:128], ALU.add, [dst, tmp], [dst], e=RE)
                yield

            rqr = ET[2]; rkr = ET[4]
            yield from rope(rq, rqr, ET[3])
            yield from rope(rk, rkr, ET[3])
            qkb = B[0]
            TT(v3(qkb[:L, 0:512]), v3(rqr[:L, :]), bc(qdec, 128, L), ALU.mult, [rqr, cst], [qkb], e=RE)
            TT(v3(qkb[:L, 512:1024]), v3(rkr[:L, :]), bc(kdecp, 128, L), ALU.mult, [rkr, cst], [qkb], e=RE)
            yield
            if smp:
                k2 = ET[5]
                TT(v3(k2[:L, :]), v3(rkr[:L, :]), bc(k2dec, 128, L), ALU.mult, [rkr, cst], [k2], e=RE)
            else:
                k2 = k2b
                TT(v3(k2[:L, 0:512]), v3(rkr[:L, :]), bc(k2dec, 128, L), ALU.mult, [rkr, cst], [k2], e=RE)
            yield
            pt = bk[1]
            ptb = pt.ap.bitcast(BF16)
            for j in range(8):
                TR(ptb[:, j * 128:j * 128 + L], qkb[:L, j * 128:(j + 1) * 128], identb[:L, :L], [qkb, identb], [pt],
                   inc=(j == 7))
            qkT = B[1]
            qkT3 = v3(qkT.ap, 8)
            CP(qkT3[:, :, :L], v3(ptb, 8)[:, :, :L], [pt], [qkT], e="act")
            yield
            bank = bk[2]
            b3 = v3(bank.ap)
            for h in range(4):
                MM(b3[:L, h, :L], qkT3[:, 4 + h, :L], qkT3[:, h, :L], True, True, [qkT], [bank], inc=(h == 3))
            scb = B[2]
            sc3 = v3(scb[:, 0:512])
            TT(sc3[:L, :, :L], b3[:L, :, :L], mT[:L, :L].unsqueeze(1).to_broadcast([L, 4, L]), ALU.mult, [bank, cst], [scb])
            yield
            ob = bk[0]
            ob3 = v3(ob.ap)
            for h in range(4):
                MM(ob3[:L, h, :], sc3[:L, h, :L], vb[:L, h * 128:(h + 1) * 128], True, smp, [scb, vb], [ob],
                   inc=(smp and h == 3))
                if not smp:
                    MM(ob3[:L, h, :], qkT3[:, h, :L], v3(Sretb.ap)[:, h, :], False, True, [qkT, Sretb], [ob], inc=(h == 3))
            yield
            if not smp:
                sb = bk[1]
                sb3 = v3(sb.ap)
                for h in range(4):
                    MM(sb3[:, h, :], k2[:L, h * 128:(h + 1) * 128], vb[:L, h * 128:(h + 1) * 128], True, True, [k2, vb], [sb],
                       inc=(h == 3))
                yield
                for h in range(4):
                    STT(v3(Sret.ap)[:, h, :], v3(Sret.ap)[:, h, :], float(GAM[h] ** L), sb3[:, h, :], ALU.mult, ALU.add,
                        [Sret, sb], [Sret])
                yield
                CP(Sretb[:], Sret[:], [Sret], [Sretb], e="act")
                if b == NBLK - 1:
                    S.dma("pool", o_ret_p[l].rearrange("h d v -> d h v"), v3(Sret.ap), r=[Sret], stream="o")
                osrc, okey = ob3, ob
            else:
                oacc = ET[1]
                CP(oacc[:L, :], ob[:L, :], [ob], [oacc])
                xTf = xT.ap.rearrange("p a b -> p (a b)").bitcast(F32)
                for s in range(NS):
                    St = (ET[2], Sret)[s % 2]
                    S.dma("sp", v3(St.ap), s_ret[l, s].rearrange("h d v -> d h v"), w=[St], stream="st")
                    qm = ET[3]
                    TT(v3(qm[:, 0:64], 4), qkT3[:, 0:4, :NS], esel[:, s, :].unsqueeze(1).to_broadcast([128, 4, NS]),
                       ALU.mult, [qkT, esel], [qm])
                    tb_ = bk[1]
                    tb3 = v3(tb_.ap)
                    for h in range(4):
                        MM(tb3[:NS, h, :], v3(qm[:, 0:64], 4)[:, h, :], v3(St.ap)[:, h, :], True, True, [qm, St], [tb_],
                           inc=(h == 3))
                    TT(oacc[:L, :], oacc[:L, :], tb_[:NS, :], ALU.add, [oacc, tb_], [oacc])
                    yield
                    vm = ET[4]
                    TT(vm[:NS, :], vb[:NS, 0:512], ident[:NS, s:s + 1].to_broadcast([NS, 512]), ALU.mult, [vb, cst], [vm])
                    sb = bk[2]
                    sb3 = v3(sb.ap)
                    for h in range(4):
                        MM(sb3[:, h, :], k2[:NS, h * 128:(h + 1) * 128], vm[:NS, h * 128:(h + 1) * 128], True, True,
                           [k2, vm], [sb], inc=(h == 3))
                    So, So3 = ((ET[0], v3(ET[0].ap)), (xT, v3(xTf)))[s % 2]
                    for h in range(4):
                        STT(So3[:, h, :], v3(St.ap)[:, h, :], float(GAM[h]), sb3[:, h, :], ALU.mult, ALU.add,
                            [St, sb], [So])
                    S.dma("pool", o_ret_s[l, s].rearrange("h d v -> d h v"), So3, r=[So], stream="o")
                    yield
                osrc, okey = v3(oacc.ap), oacc
            for h in range(4):
                S.op("dve", lambda h=h: nc.vector.bn_stats(out=bst[:L, h, :], in_=osrc[:L, h, :]), [okey], [bst])
            yield
            for h in range(4):
                S.op("dve", lambda h=h: nc.vector.bn_aggr(out=bmv[:L, h, :], in_=bst[:L, h, :]), [bst], [bmv])
            RSQ(sm1[:L, :], bmv[:L, :, 1], [bmv], [sm1])
            yield
            onb = B[2]
            STT(nmr[:L, :], bmv[:L, :, 0], -1.0, sm1[:L, :], ALU.mult, ALU.mult, [bmv, sm1], [nmr])
            for h in range(4):
                A(onb[:L, 512 + h * 128:512 + (h + 1) * 128], osrc[:L, h, :], AF.Identity, [okey, nmr, sm1], [onb],
                  bias=nmr[:L, h:h + 1], scale=sm1[:L, h:h + 1])
            yield
            pt = bk[1]
            ptb = pt.ap.bitcast(BF16)
            for h in range(4):
                TR(ptb[:, h * 128:h * 128 + L], onb[:L, 512 + h * 128:512 + (h + 1) * 128], identb[:L, :L], [onb, identb],
                   [pt], inc=(h == 3))
            yt = ET[0]
            y3 = v3(yt.ap)
            for h in range(4):
                A(y3[:, h, :L], ptb[:, h * 128:h * 128 + L], AF.Identity, [pt, pft], [yt], bias=pft[:, 36 + h:37 + h],
                  scale=pft[:, 32 + h:33 + h])
            yield
            A(z3[:, :, :L], z3[:, :, :L], AF.Silu, [zte], [zte])
            TT(mix[:, 4:8, :L], y3[:, :, :L], z3[:, :, :L], ALU.mult, [yt, zte], [mixe])

        def gen_gdn():
            bk = [ps[5], ps[6], ps[7]]
            nb = [0]

            def PB():
                nb[0] = (nb[0] + 1) % 3
                return bk[nb[0]]

            if smp:
                S.dma("sp", GXs[:, :, 0:3, :], s_gc[l], w=[GX], stream="st")
                for g in range(3):
                    fm_group(PB(), COL["gq"] + 512 * g, GXs[:, 4 * g:4 * g + 4, 3, :], GX)
                    yield
                gtap = lambda c, j: GXs[:, c, j, :]
            else:
                for g in range(3):
                    fm_group(PB(), COL["gq"] + 512 * g, GX[:, 4 * g:4 * g + 4, 3:3 + L], GX)
                    yield
                gtap = lambda c, j: GX[:, c, j:j + L]
            tm_group(PB(), COL["gab"], 8, gabt[:L, :], gabt)
            z3 = v3(ztg.ap)
            fm_group(PB(), COL["gz"], z3[:, :, :L], ztg, e="dve")
            yield "pre"
            TT(gt[:L, :], gabt[:L, 0:4], rowt[:L, 2052:2056], ALU.add, [gabt, rowt], [gt])
            A(gt[:L, :], gt[:L, :], AF.Exp, [gt], [gt])
            A(gt[:L, :], gt[:L, :], AF.Ln, [gt], [gt], bias=1.0)
            TT(gt[:L, :], gt[:L, :], negA[:L, :], ALU.mult, [gt, negA], [gt])
            A(betat[:L, :], gabt[:L, 4:8], AF.Sigmoid, [gabt], [betat])
            yield
            bank = PB()
            MM(bank[:L, 0:4], mT[:L, :L], gt[:L, :], True, True, [cst, gt], [bank])
            CP(gct[:L, :], bank[:L, 0:4], [bank], [gct])
            A(egt[:L, :], gct[:L, :], AF.Exp, [gct], [egt])
            yield
            Rt = GT[5]
            R3 = v3(Rt.ap)
            for h in range(4):
                A(R3[:L, h, :L], mT[:L, :L], AF.Copy, [cst, gt], [Rt], scale=gt[:L, h:h + 1])
            TS(ngct[:L, :], gct[:L, :], -1.0, None, ALU.mult, None, [gct], [ngct])
            yield
            gcB = PB()
            g3 = v3(gcB.ap)
            for h in range(4):
                MM(g3[:, h, :L], ones[:L, :], R3[:L, h, :L], True, True, [cst, Rt], [gcB], inc=(h == 3))
            EB = GT[6]
            EB3 = v3(EB.ap)
            A(EB3[:, :, :L], g3[:, :, :L], AF.Exp, [gcB], [EB])
            yield
            dT = GT[7]
            dT3 = v3(dT.ap)
            for h in range(4):
                A(dT3[:L, h, :L], g3[:L, h, :L], AF.Relu, [gcB, gct], [dT], bias=gct[:L, h:h + 1], scale=-1.0)
            yield
            if not smp:
                dl = GT[8]
                dl3 = v3(dl.ap)
                for h in range(4):
                    A(dl3[:L, h, :L], g3[:L, h, :L], AF.Relu, [gcB, ngct], [dl], bias=ngct[:L, h:h + 1], scale=1.0)
                yield
                CP(eglast[:].unsqueeze(2), EB3[:, :, L - 1:L], [EB], [eglast])
                for h in range(4):
                    A(eglt[:L, h:h + 1], g3[:L, h, L - 1:L], AF.Exp, [gcB, ngct], [eglt], bias=ngct[:L, h:h + 1])
                A(dl3[:L, :, :L], dl3[:L, :, :L], AF.Exp, [dl], [dl], scale=-1.0)
                TT(dl3[:L, :, :L], dl3[:L, :, :L], strictL[:L, :L].unsqueeze(1).to_broadcast([L, 4, L]), ALU.mult,
                   [dl, cst], [dl])
                for h in range(4):
                    A(dl3[:L, h, :L], dl3[:L, h, :L], AF.Copy, [dl, betat], [dl], scale=betat[:L, h:h + 1])
                yield
            else:
                CP(ebs[:], EB3[:, :, :NS], [EB], [ebs])
            A(dT3[:L, :, :L], dT3[:L, :, :L], AF.Exp, [dT], [dT], scale=-1.0)
            TT(dT3[:L, :, :L], dT3[:L, :, :L], mT[:L, :L].unsqueeze(1).to_broadcast([L, 4, L]), ALU.mult, [dT, cst], [dT])
            yield
            cq = GT[0]; ck = GT[1]; cv = GT[2]
            cqk = [cq, ck, cv]
            for g in range(3):
                cg3 = v3(cqk[g].ap)
                for c in range(4):
                    conv_chunk(lambda c_, j, g=g: gtap(4 * g + c_, j), 40 + 16 * g, c, cg3[:, c, :L], cqk[g], GX)
                    yield
                A(cg3[:, :, :L], cg3[:, :, :L], AF.Silu, [cqk[g]], [cqk[g]])
            if smp:
                S.dma("pool", o_gc_s[l], GXs[:, :, 1:4, :], r=[GX], stream="o")
            else:
                if b == NBLK - 1:
                    S.dma("pool", o_gc_p[l], GX[:, :, L:L + 3], r=[GX], stream="o")
                CP(GX[:, :, 0:3], GX[:, :, L:L + 3], [GX], [GX])
            yield
            for g in range(2):
                cg3 = v3(cqk[g].ap)
                sq = GT[3]
                sq3 = v3(sq.ap)
                A(sq3[:, :, :L], cg3[:, :, :L], AF.Square, [cqk[g]], [sq])
                bank = PB()
                b3 = v3(bank.ap)
                for h in range(4):
                    MM(b3[:, h, :L], ones, sq3[:, h, :L], True, True, [cst, sq], [bank], inc=(h == 3))
                yield
                rn = GT[4]
                rn3 = v3(rn.ap)
                RSQ(rn3[:, :, :L], b3[:, :, :L], [bank], [rn])
                if g == 0:
                    STT(cg3[:, :, :L], cg3[:, :, :L], float(128 ** -0.5), rn3[:, :, :L], ALU.mult, ALU.mult, [cq, rn], [cq])
                else:
                    TT(cg3[:, :, :L], cg3[:, :, :L], rn3[:, :, :L], ALU.mult, [ck, rn], [ck])
                yield
            q3 = v3(cq.ap); k3 = v3(ck.ap); cv3 = v3(cv.ap)
            TT(v3(qgT.ap)[:, :, :L], q3[:, :, :L], EB3[:, :, :L], ALU.mult, [cq, EB], [qgT])
            bank = PB()
            b3 = v3(bank.ap)
            for h in range(4):
                MM(b3[:L, h, :L], k3[:, h, :L], q3[:, h, :L], True, True, [ck, cq], [bank], inc=(h == 3))
            at3 = v3(attT.ap)
            TT(at3[:L, :, :L], b3[:L, :, :L], dT3[:L, :, :L], ALU.mult, [bank, dT], [attT])
            yield
            kTM = GT[3]; vTM = GT[4]
            for srcT, s3_, dstT in ((ck, k3, kTM), (cv, cv3, vTM)):
                bank = PB()
                for h in range(4):
                    TR(bank[:L, h * 128:(h + 1) * 128], s3_[:, h, :L], ident, [srcT, cst], [bank], inc=(h == 3))
                CP(dstT[:L, :], bank[:L, :], [bank], [dstT], e="act")
                yield
            if not smp:
                TT(v3(kd[:L, :]), v3(kTM[:L, :]), bc(eglt[:L, :], 128, L), ALU.mult, [kTM, eglt], [kd])
            else:
                CP(kd[:L, :], kTM[:L, :], [kTM], [kd])
            TT(v3(Vb[:L, :]), v3(vTM[:L, :]), bc(betat[:L, :], 128, L), ALU.mult, [vTM, betat], [Vb])
            yield
            TT(sm2[:L, :], betat[:L, :], egt[:L, :], ALU.mult, [betat, egt], [sm2])
            TT(v3(Kbg[:L, :]), v3(kTM[:L, :]), bc(sm2[:L, :], 128, L), ALU.mult, [kTM, sm2], [Kbg])
            yield
            Y3 = v3(Y.ap)
            if smp:
                CP(Y3[:L, :, :L], ident[:L, :L].unsqueeze(1).to_broadcast([L, 4, L]), [cst], [Y])
            else:
                bank = PB()
                b3 = v3(bank.ap)
                for h in range(4):
                    MM(b3[:L, h, :L], k3[:, h, :L], k3[:, h, :L], True, True, [ck], [bank], inc=(h == 3))
                P = GT[2]
                P3 = v3(P.ap)
                TT(P3[:L, :, :L], b3[:L, :, :L], dl3[:L, :, :L], ALU.mult, [bank, dl], [P])
                yield
                bank = PB()
                b3 = v3(bank.ap)
                for h in range(4):
                    TR(b3[:L, h, :L], P3[:L, h, :L], ident[:L, :L], [P, cst], [bank], inc=(h == 3))
                Q = GT[5]
                Q3 = v3(Q.ap)
                CP(Q3[:L, :, :L], b3[:L, :, :L], [bank], [Q], e="act")
                STT(Y3[:L, :, :L], Q3[:L, :, :L], -1.0, ident[:L, :L].unsqueeze(1).to_broadcast([L, 4, L]), ALU.mult, ALU.add,
                    [Q, cst], [Y])
                yield
                nlev = 6 if L == 128 else 3
                for lev in range(nlev):
                    bq = PB(); bp = PB()
                    bq3 = v3(bq.ap); bp3 = v3(bp.ap)
                    for h in range(4):
                        MM(bq3[:L, h, :L], P3[:L, h, :L], Q3[:L, h, :L], True, True, [P, Q], [bq], inc=(h == 3))
                    for h in range(4):
                        MM(bp3[:L, h, :L], Q3[:L, h, :L], P3[:L, h, :L], True, True, [P, Q], [bp], inc=(h == 3))
                    yield
                    Pn, Qn = (GT[3], GT[4]) if lev % 2 == 0 else (GT[2], GT[5])
                    CP(v3(Qn.ap)[:L, :, :L], bq3[:L, :, :L], [bq], [Qn], e="act")
                    CP(v3(Pn.ap)[:L, :, :L], bp3[:L, :, :L], [bp], [Pn], e="dve")
                    yield
                    P, Q = Pn, Qn
                    P3, Q3 = v3(P.ap), v3(Q.ap)
                    by = PB()
                    by3 = v3(by.ap)
                    for h in range(4):
                        MM(by3[:L, h, :L], P3[:L, h, :L], Y3[:L, h, :L], True, True, [P, Y], [by], inc=(h == 3))
                    TT(Y3[:L, :, :L], Y3[:L, :, :L], by3[:L, :, :L], ALU.add, [Y, by], [Y])
                    yield
            bank = PB()
            b3 = v3(bank.ap)
            for h in range(4):
                MM(b3[:, h, :L], Kbg[:L, h * 128:(h + 1) * 128], Y3[:L, h, :L], True, True, [Kbg, Y], [bank], inc=(h == 3))
            nWT = GT[6]
            nW3 = v3(nWT.ap)
            A(nW3[:, :, :L], b3[:, :, :L], AF.Copy, [bank], [nWT], scale=-1.0)
            yield
            Sg3 = v3(Sgdn.ap)
            vnb = PB()
            vn3 = v3(vnb.ap)
            for h in range(4):
                MM(vn3[:L, h, :], Y3[:L, h, :L], Vb[:L, h * 128:(h + 1) * 128], True, smp, [Y, Vb], [vnb],
                   inc=(smp and h == 3))
                if not smp:
                    MM(vn3[:L, h, :], nW3[:, h, :L], Sg3[:, h, :], False, True, [nWT, Sgdn], [vnb], inc=(h == 3))
            vnew = GT[7]
            if not smp:
                CP(vnew[:L, :], vnb[:L, :], [vnb], [vnew], e="act")
                yield
            else:
                wacc = GT[0]; qacc = GT[1]
                CP(wacc[:L, :], vnb[:L, :], [vnb], [wacc], e="act")
                MS(qacc[:L, :], 0.0, [qacc])
                for s in range(NS):
                    St = (GT[2], Sgdn)[s % 2]
                    S.dma("sp", v3(St.ap), s_gdn[l, s].rearrange("h d v -> d h v"), w=[St], stream="st")
                    for srcT, s3_, acc in ((qgT, v3(qgT.ap), qacc), (nWT, nW3, wacc)):
                        qm = GT[3]
                        TT(v3(qm[:, 0:64], 4), s3_[:, :, :NS], esel[:, s, :].unsqueeze(1).to_broadcast([128, 4, NS]),
                           ALU.mult, [srcT, esel], [qm])
                        tb_ = PB()
                        tb3 = v3(tb_.ap)
                        for h in range(4):
                            MM(tb3[:NS, h, :], v3(qm[:, 0:64], 4)[:, h, :], v3(St.ap)[:, h, :], True, True, [qm, St], [tb_],
                               inc=(h == 3))
                        TT(acc[:L, :], acc[:L, :], tb_[:NS, :], ALU.add, [acc, tb_], [acc])
                        yield
                CP(vnew[:L, :], wacc[:L, :], [wacc], [vnew])
            ob = PB()
            ob3 = v3(ob.ap)
            for h in range(4):
                MM(ob3[:L, h, :], at3[:L, h, :L], vnew[:L, h * 128:(h + 1) * 128], True, smp, [attT, vnew], [ob],
                   inc=(smp and h == 3))
                if not smp:
                    MM(ob3[:L, h, :], v3(qgT.ap)[:, h, :L], Sg3[:, h, :], False, True, [qgT, Sgdn], [ob], inc=(h == 3))
            yield
            if smp:
                TT(qacc[:L, :], qacc[:L, :], ob[:L, :], ALU.add, [qacc, ob], [qacc])
                osrc, okey = v3(qacc.ap), qacc
                for s in range(NS):
                    St = (GT[2], Sgdn)[s % 2]
                    S.dma("sp", v3(St.ap), s_gdn[l, s].rearrange("h d v -> d h v"), w=[St], stream="st")
                    vm = GT[3]
                    TS(vm[:NS, :], vnew[:NS, :], ident[:NS, s:s + 1], None, ALU.mult, None, [vnew, cst], [vm])
                    sb = PB()
                    sb3 = v3(sb.ap)
                    for h in range(4):
                        MM(sb3[:, h, :], kd[:NS, h * 128:(h + 1) * 128], vm[:NS, h * 128:(h + 1) * 128], True, True,
                           [kd, vm], [sb], inc=(h == 3))
                    So = (GT[4], GT[8])[s % 2]
                    for h in range(4):
                        STT(v3(So.ap)[:, h, :], v3(St.ap)[:, h, :], ebs[:, h, s:s + 1], sb3[:, h, :], ALU.mult, ALU.add,
                            [St, sb, ebs], [So])
                    S.dma("pool", o_gdn_s[l, s].rearrange("h d v -> d h v"), v3(So.ap), r=[So], stream="o")
                    yield
            else:
                osrc, okey = ob3, ob
                sb = PB()
                sb3 = v3(sb.ap)
                for h in range(4):
                    MM(sb3[:, h, :], kd[:L, h * 128:(h + 1) * 128], vnew[:L, h * 128:(h + 1) * 128], True, True, [kd, vnew], [sb],
                       inc=(h == 3))
                yield
                for h in range(4):
                    STT(Sg3[:, h, :], Sg3[:, h, :], eglast[:, h:h + 1], sb3[:, h, :], ALU.mult, ALU.add,
                        [Sgdn, sb, eglast], [Sgdn])
                if b == NBLK - 1:
                    S.dma("pool", o_gdn_p[l].rearrange("h d v -> d h v"), Sg3, r=[Sgdn], stream="o")
                yield
            osq = GT[8]
            A(v3(osq.ap)[:L, :, :], osrc[:L, :, :], AF.Square, [okey], [osq])
            S.op("dve", lambda: nc.vector.reduce_sum(out=sm3[:L, :], in_=v3(osq.ap)[:L, :, :], axis=AX.X), [osq], [sm3])
            RSQ(sm3[:L, :], sm3[:L, :], [sm3], [sm3], bias=EPS, scale=1.0 / 128.0)
            yield
            on = GT[5]
            for h in range(4):
                A(on[:L, h * 128:(h + 1) * 128], osrc[:L, h, :], AF.Copy, [okey, sm3], [on], scale=sm3[:L, h:h + 1])
            yield
            bank = PB()
            b3 = v3(bank.ap)
            for h in range(4):
                TR(b3[:, h, :L], on[:L, h * 128:(h + 1) * 128], ident[:L, :L], [on, cst], [bank], inc=(h == 3))
            A(z3[:, :, :L], z3[:, :, :L], AF.Silu, [ztg], [ztg])
            STT(mix[:, 8:12, :L], b3[:, :, :L], pft[:, 88:89], z3[:, :, :L], ALU.mult, ALU.mult, [bank, pft, ztg], [mixg])

        if MERGE == "sim":
            branches = []
            live = []
            for gen in (gen_gdn, gen_ret, gen_rg):
                g = gen()
                for mark in g:
                    if mark == "pre":
                        break
                live.append(g)
            if smp and l < NL - 1:
                load_win(l + 1)
            if not smp:
                if b + 1 < NBLK:
                    head(l, "p", b + 1)
                elif not SKIP_SAMPLE:
                    head(l, "s", 0)
            for g in live:
                S.rec = []
                for _ in g:
                    pass
                ops, S.rec = S.rec, None
                units, cur = [], []
                for o in ops:
                    cur.append(o)
                    if o[5]:
                        units.append(cur)
                        cur = []
                assert not cur
                branches.append(units)
            clock = {}
            wr = {}
            rd = {}
            ptr = [0] * len(branches)
            HOP = 0.3
            while True:
                best = None
                for bi, units in enumerate(branches):
                    if ptr[bi] >= len(units):
                        continue
                    u = units[ptr[bi]]
                    e = u[0][1]
                    t = clock.get(e, 0.0)
                    for o in u:
                        for k in o[3]:
                            if k in wr:
                                t = max(t, wr[k][0] + (HOP if wr[k][1] != e else 0.0))
                        for k in o[4]:
                            if k in wr:
                                t = max(t, wr[k][0] + (HOP if wr[k][1] != e else 0.0))
                            if k in rd:
                                t = max(t, rd[k][0] + (HOP if rd[k][1] != e else 0.0))
                    if best is None or t < best[0] - 1e-9:
                        best = (t, bi)
                if best is None:
                    break
                t, bi = best
                u = branches[bi][ptr[bi]]
                ptr[bi] += 1
                e = u[0][1]
                for o in u:
                    kind, eng, fn, r_, w_, inc, cost = o
                    if kind == "dma":
                        out_, in_, kw = fn
                        S.dma(eng, out_, in_, r=r_, w=w_, **kw)
                        done = t + 2.5
                        t += cost
                        who = "dma"
                    else:
                        S.op(eng, fn, r_, w_, inc=inc)
                        t += cost
                        done = t
                        who = eng
                    for k in r_:
                        if k not in rd or rd[k][0] < done:
                            rd[k] = (done, who)
                    for k in w_:
                        wr[k] = (done, who)
                        rd.pop(k, None)
                clock[e] = t
            gens = []
        else:
            gens = [(gen_rg(), 1), (gen_ret(), 1), (gen_gdn(), GDN_W)]
        if MERGE == "seq":
            for g, _ in gens:
                for _ in g:
                    pass
            gens = []
        while gens:
            for item in list(gens):
                g, wgt = item
                for _ in range(wgt):
                    try:
                        next(g)
                    except StopIteration:
                        gens.remove(item)
                        break

        z = [RT[0], RT[1]]
        for n in range(2):
            bank = ps[n]
            for kc in range(12):
                MM(bank[:L, :], mix[:, kc, :L], Wout[:, kc, n * 512:(n + 1) * 512], kc == 0, kc == 11,
                   [mixr, mixe, mixg, Wout], [bank], inc=(kc == 11))
            S.dma("sp", z[n][:L, :], xsrc_d[:, n * 512:(n + 1) * 512], r=[xdkey], w=[z[n]], stream="x")
            STT(z[n][:L, :], z[n][:L, :], float(ALPHA), bank[:L, :], ALU.mult, ALU.add, [z[n], bank], [z[n]])
            S.op("dve", lambda n=n: nc.vector.bn_stats(out=bst2[:L, n, :], in_=z[n][:L, :]), [z[n]], [bst2])
        S.op("dve", lambda: nc.vector.bn_aggr(out=bmv2[:L, :], in_=bst2[:L, 0:2, :]), [bst2], [bmv2])
        RSQ(sm2[:L, 0:1], bmv2[:L, 1:2], [bmv2], [sm2])
        for n in range(2):
            sl = slice(n * 512, (n + 1) * 512)
            if n == 0:
                STT(nmr2[:L, :], bmv2[:L, 0:1], -1.0, sm2[:L, 0:1], ALU.mult, ALU.mult, [bmv2, sm2], [nmr2])
            A(z[n][:L, :], z[n][:L, :], AF.Identity, [z[n], nmr2, sm2], [z[n]], bias=nmr2[:L, 0:1], scale=sm2[:L, 0:1])
            TT(z[n][:L, :], z[n][:L, :], rowt[:L, sl], ALU.mult, [z[n], rowt], [z[n]])
            TT(z[n][:L, :], z[n][:L, :], rowt[:L, 1024 + n * 512:1024 + (n + 1) * 512], ALU.add, [z[n], rowt], [z[n]])
            if smp:
                if l == NL - 1:
                    S.dma("pool", y_s[:, sl], z[n][:NS, :], r=[z[n]], stream="o")
                else:
                    S.dma("pool", xsscr[:, sl], z[n][:NS, :], r=[z[n]], w=[("xsscr", 0)], sname=f"xs{n}_{l % 2}")
            else:
                if l == NL - 1:
                    if b > 0:
                        S.dma("pool", y_p[t0 - 16:t0 - 16 + L, sl], z[n][:L, :], r=[z[n]], stream="o")
                else:
                    S.dma("pool", xscr[t0:t0 + L, sl], z[n][:L, :], r=[z[n]], w=[("xscr", b)], sname=f"xo{n}_{l % 2}")


    for l in range(NL):
        layer_setup(l)
        for b in range(NBLK):
            block(l, "p", b)
        if not SKIP_SAMPLE:
            block(l, "s", 0)
    S.finish("sp")
    print("ops", S.nops, "waits", S.nwaits, "sems", S.nsem + len(S.dstream))
    return nc


def _consts():
    cst = np.zeros((128, NCST), np.float32)
    i = np.arange(128)
    cst[:, 0:128] = np.eye(128)
    cst[:, 128:256] = (i[None, :] >= i[:, None])
    cst[:, 256:384] = (i[None, :] < i[:, None])
    cst[:, 384:512] = 1.0
    sc = 128.0 ** -0.5
    for h in range(4):
        g = np.float64(GAM[h])
        cst[:, 512 + h] = g ** (i + 1.0)
        cst[:, 516 + h] = g ** (-(i + 1.0)) * sc
        cst[:, 520 + h] = g ** (127.0 - i) * sc
        cst[:16, 524 + h] = g ** (15.0 - i[:16]) * sc
        cst[:, 528 + h] = g
        cst[:, 532 + h] = sc / g
        cst[:, 536 + h] = sc
    half = 64
    inv = (np.float32(10000.0) ** (-np.arange(half, dtype=np.float32) / np.float32(half))).astype(np.float32)
    rope = np.zeros((18, 128, 128), np.float32)
    for b in range(18):
        if b == 0:
            pos = np.arange(16, dtype=np.float32)
        elif b < 17:
            pos = 16 + 128 * (b - 1) + np.arange(128, dtype=np.float32)
        else:
            pos = np.full(16, 16384.0, np.float32)
        ang = (pos[:, None].astype(np.float32) * inv[None, :]).astype(np.float32)
        rope[b, :len(pos), 0:64] = np.cos(ang.astype(np.float64))
        rope[b, :len(pos), 64:128] = np.sin(ang.astype(np.float64))
    esel = np.eye(16, dtype=np.float32).reshape(1, 256)
    return cst, rope, esel


_NC_CACHE = {}


def kernel(x_prompt, x_sample, state_rglru_h, state_rglru_conv, state_ret, state_gdn_conv, state_gdn,
           meta_tokens, w_in, rg_conv_w, rg_conv_b, rg_w_a, rg_b_a, rg_w_x, rg_b_x, rg_lambda,
           ret_gn_w, ret_gn_b, gdn_conv_w, gdn_a_log, gdn_dt_bias, gdn_norm_w, w_out, ln_w, ln_b):
    f = lambda a: np.ascontiguousarray(np.asarray(a, dtype=np.float32))
    x_prompt, x_sample, meta_tokens = f(x_prompt), f(x_sample), f(meta_tokens)
    w_in, w_out = f(w_in), f(w_out)
    pf = np.zeros((NL, 128, NPF), np.float32)

    def fm(v, nch):
        return f(v).reshape(NL, nch, 128).transpose(0, 2, 1)

    pf[:, :, 0:16] = f(rg_conv_w).reshape(NL, 4, 4, 128).transpose(0, 3, 2, 1).reshape(NL, 128, 16)
    pf[:, :, 16:20] = fm(rg_conv_b, 4)
    pf[:, :, 20:24] = fm(rg_b_a, 4)
    pf[:, :, 24:28] = fm(rg_b_x, 4)
    pf[:, :, 28:32] = fm(rg_lambda, 4)
    pf[:, :, 32:36] = fm(ret_gn_w, 4)
    pf[:, :, 36:40] = fm(ret_gn_b, 4)
    pf[:, :, 40:88] = f(gdn_conv_w).reshape(NL, 4, 12, 128).transpose(0, 3, 2, 1).reshape(NL, 128, 48)
    pf[:, :, 88] = f(gdn_norm_w)
    rgw = np.zeros((NL, 128, 2, 4, 128), np.float32)
    for which, wsrc in ((0, f(rg_w_a)), (1, f(rg_w_x))):
        for n in range(8):
            c, o = n // 2, (n % 2) * 64
            rgw[:, o:o + 64, which, c, o:o + 64] = wsrc[:, n]
    rgw = rgw.reshape(NL, 128, 1024)
    rows = np.concatenate([f(ln_w), f(ln_b), f(gdn_a_log), f(gdn_dt_bias)], axis=1).reshape(NL, 1, 2056)
    cst, rope, esel = _consts()
    if "nc" not in _NC_CACHE:
        _NC_CACHE["nc"] = build_nc()
    nc = _NC_CACHE["nc"]
    in_maps = []
    for c in range(8):
        sl = slice(NS * c, NS * (c + 1))
        m = {
            "xp": np.ascontiguousarray(np.concatenate([meta_tokens, x_prompt[c]], axis=0)),
            "xs": np.ascontiguousarray(x_sample[sl, 0, :]),
            "s_h": np.ascontiguousarray(f(state_rglru_h)[:, sl].reshape(NL, NS, 4, 128).transpose(0, 3, 2, 1)),
            "s_rgc": np.ascontiguousarray(f(state_rglru_conv)[:, sl].reshape(NL, NS, 3, 4, 128).transpose(0, 4, 3, 2, 1)),
            "s_gc": np.ascontiguousarray(f(state_gdn_conv)[:, sl].reshape(NL, NS, 3, 12, 128).transpose(0, 4, 3, 2, 1)),
            "s_ret": np.ascontiguousarray(f(state_ret)[:, sl]),
            "s_gdn": np.ascontiguousarray(f(state_gdn)[:, sl]),
            "w_in": w_in, "w_out": w_out, "pf": pf, "rgw": rgw, "rows": rows,
            "cst": cst, "ropet": rope, "esel": esel,
        }
        in_maps.append(m)
    res = run_bass_kernel_spmd(nc, in_maps, core_ids=list(range(8)))
    R = res.results
    g = lambda k: [np.asarray(R[c][k], dtype=np.float32) for c in range(8)]
    y_prompt = np.stack(g("y_p"), 0)
    y_sample = np.concatenate(g("y_s"), 0)[:, None, :]
    hp = np.stack([a.transpose(0, 2, 1).reshape(NL, 512) for a in g("o_h_p")], 1)
    rgcp = np.stack([a.transpose(0, 3, 2, 1).reshape(NL, 3, 512) for a in g("o_rgc_p")], 1)
    retp = np.stack(g("o_ret_p"), 1)
    gcp = np.stack([a.transpose(0, 3, 2, 1).reshape(NL, 3, 1536) for a in g("o_gc_p")], 1)
    gdnp = np.stack(g("o_gdn_p"), 1)
    hs = np.concatenate([a.transpose(0, 3, 2, 1).reshape(NL, NS, 512) for a in g("o_h_s")], 1)
    rgcs = np.concatenate([a.transpose(0, 4, 3, 2, 1).reshape(NL, NS, 3, 512) for a in g("o_rgc_s")], 1)
    rets = np.concatenate(g("o_ret_s"), 1)
    gcs = np.concatenate([a.transpose(0, 4, 3, 2, 1).reshape(NL, NS, 3, 1536) for a in g("o_gc_s")], 1)
    gdns = np.concatenate(g("o_gdn_s"), 1)
    c = np.ascontiguousarray
    return (c(y_prompt), c(y_sample), c(hp), c(rgcp), c(retp), c(gcp), c(gdnp), c(hs), c(rgcs), c(rets), c(gcs), c(gdns))
```

```python
import numpy as np
import concourse.bass as bass
import concourse.mybir as mybir
from concourse.bass_utils import run_bass_kernel_spmd

F32 = mybir.dt.float32
BF16 = mybir.dt.bfloat16
ALU = mybir.AluOpType
AF = mybir.ActivationFunctionType
AX = mybir.AxisListType

EPOCH = 12000
RE = "dve"
DEFCOST = {"pe": 0.2, "act": 0.4, "dve": 0.3, "pool": 0.05, "sp": 0.05}
GDN_STOP = 100000
ENABLE = [True, True, True]
NET = 6
SKIP_SAMPLE = False
PER_TILE_SEMS = True
MERGE = "sim"
SAME_SYNC = True

NL = 4
DM = 1024
DIN = 5128
NTOK = 2064
NBLK = 17
NS = 16
ALPHA = 8.0 ** 0.25
EPS = 1e-6
GAM = [1.0 - 2.0 ** (-5.0 - h) for h in range(4)]
NCST = 540
NPF = 89
COL = dict(rgx=0, rgz=512, rq=1024, rk=1536, rv=2048, rz=2560, gq=3072, gk=3584, gv=4096, gz=4608, gab=5120)


class Tile:
    def __init__(self, nc, name, shape, dtype, psum=False):
        if psum:
            self.h = nc.alloc_psum_tensor("T_" + name, list(shape), dtype)
        else:
            self.h = nc.alloc_sbuf_tensor("T_" + name, list(shape), dtype)
        self.ap = self.h.ap()
        self.name = name

    def __getitem__(self, k):
        return self.ap[k]


class Sched:
    def __init__(self, nc):
        self.nc = nc
        self.eng = {"pe": nc.tensor, "act": nc.scalar, "dve": nc.vector, "pool": nc.gpsimd, "sp": nc.sync}
        self.sem = {}
        self.cnt = {}
        self.pend = {}
        self.nsem = 0
        for e in self.eng:
            self._new_sem(e)
            self.pend[e] = False
        self.lastw = {}
        self.readers = {}
        self.waited = {e: {} for e in self.eng}
        self.dstream = {}
        self.nwaits = 0
        self.nops = 0
        self.rec = None

    def _new_sem(self, e):
        self.sem[e] = self.nc.alloc_semaphore(f"s_{e}_{self.nsem}")
        self.nsem += 1
        self.cnt[e] = 0

    def _deps(self, r, w):
        evs = []
        for k in r:
            if k in self.lastw:
                evs.append(self.lastw[k] + (True,))
        for k in w:
            if k in self.lastw:
                evs.append(self.lastw[k] + (False,))
            evs.extend(v + (False,) for v in self.readers.get(k, {}).values())
        return evs

    def _do_waits(self, e, evs):
        need = {}
        for sem, val, src, raw in evs:
            if src == e and not (SAME_SYNC or raw):
                continue
            if src.startswith("dma:"):
                val = self.dstream[src[4:]][1]
            if val > need.get(sem, (0, None))[0]:
                need[sem] = (val, src)
        for sem, (val, src) in need.items():
            if self.waited[e].get(sem, 0) >= val:
                continue
            if src == e and sem is self.sem[e] and val > self.cnt[e]:
                continue
            self.eng[e].wait_ge(sem, val)
            self.waited[e][sem] = val
            self.nwaits += 1

    def _register(self, ev, r, w):
        sem = ev[0]
        for k in r:
            self.readers.setdefault(k, {})[sem] = ev
        for k in w:
            self.lastw[k] = ev
            self.readers[k] = {}

    def op(self, e, fn, r=(), w=(), inc=True, cost=None):
        if self.rec is not None:
            self.rec.append(("op", e, fn, tuple(r), tuple(w), inc, cost if cost else DEFCOST[e]))
            return None
        self._do_waits(e, self._deps(r, w))
        if self.cnt[e] >= EPOCH and not self.pend[e]:
            self._new_sem(e)
        ins = fn()
        self.nops += 1
        if inc:
            self.cnt[e] += 1
            ins.then_inc(self.sem[e], 1)
            ev = (self.sem[e], self.cnt[e], e)
            self.pend[e] = False
        else:
            ev = (self.sem[e], self.cnt[e] + 1, e)
            self.pend[e] = True
        self._register(ev, r, w)
        return ins

    def dma(self, q, out, in_, r=(), w=(), stream="d", sname=None, **kw):
        if self.rec is not None:
            self.rec.append(("dma", q, (out, in_, dict(kw, sname=sname)), tuple(r), tuple(w), True, 0.05))
            return None
        tl = [k for k in w if isinstance(k, Tile)]
        if sname is not None:
            stream = sname
        elif not PER_TILE_SEMS:
            pass
        elif tl:
            stream = "ld_" + tl[0].name
        else:
            stream = "st_" + [k for k in r if isinstance(k, Tile)][0].name
        self._do_waits(q, self._deps(r, w))
        if stream not in self.dstream:
            self.dstream[stream] = [self.nc.alloc_semaphore(f"d_{stream}"), 0]
        st = self.dstream[stream]
        ins = self.eng[q].dma_start(out=out, in_=in_, **kw)
        st[1] += 16
        ins.then_inc(st[0], 16)
        ev = (st[0], st[1], "dma:" + stream)
        self._register(ev, r, w)
        return ins

    def finish(self, e="sp"):
        for name, (sem, tot) in self.dstream.items():
            if tot > 0:
                self.eng[e].wait_ge(sem, tot)


def v3(ap, c=4):
    return ap.rearrange("p (c n) -> p c n", c=c)


def build_nc():
    nc = bass.Bass("TRN2", target_bir_lowering=False)

    def din(name, shape):
        return nc.dram_tensor(name, list(shape), F32, kind="ExternalInput").ap()

    def dout(name, shape):
        return nc.dram_tensor(name, list(shape), F32, kind="ExternalOutput").ap()

    xp_d = din("xp", [NTOK, DM])
    xs_d = din("xs", [NS, DM])
    s_h = din("s_h", [NL, 128, 4, NS])
    s_rgc = din("s_rgc", [NL, 128, 4, 3, NS])
    s_gc = din("s_gc", [NL, 128, 12, 3, NS])
    s_ret = din("s_ret", [NL, NS, 4, 128, 128])
    s_gdn = din("s_gdn", [NL, NS, 4, 128, 128])
    w_in = din("w_in", [NL, DM, DIN])
    w_out = din("w_out", [NL, 1536, DM])
    pf_d = din("pf", [NL, 128, NPF])
    rgw_d = din("rgw", [NL, 128, 2 * 4 * 128])
    rows_d = din("rows", [NL, 1, 2056])
    cst_d = din("cst", [128, NCST])
    rope_d = din("ropet", [18, 128, 128])
    esel_d = din("esel", [1, 256])

    y_p = dout("y_p", [2048, DM])
    y_s = dout("y_s", [NS, DM])
    o_h_p = dout("o_h_p", [NL, 128, 4])
    o_rgc_p = dout("o_rgc_p", [NL, 128, 4, 3])
    o_ret_p = dout("o_ret_p", [NL, 4, 128, 128])
    o_gc_p = dout("o_gc_p", [NL, 128, 12, 3])
    o_gdn_p = dout("o_gdn_p", [NL, 4, 128, 128])
    o_h_s = dout("o_h_s", [NL, 128, 4, NS])
    o_rgc_s = dout("o_rgc_s", [NL, 128, 4, 3, NS])
    o_ret_s = dout("o_ret_s", [NL, NS, 4, 128, 128])
    o_gc_s = dout("o_gc_s", [NL, 128, 12, 3, NS])
    o_gdn_s = dout("o_gdn_s", [NL, NS, 4, 128, 128])
    xscr = nc.dram_tensor("xscr", [NTOK, DM], F32, kind="Internal").ap()
    xsscr = nc.dram_tensor("xsscr", [NS, DM], F32, kind="Internal").ap()

    S = Sched(nc)

    def TL(name, shape, dt=F32):
        return Tile(nc, name, shape, dt)

    Win = TL("Win", [128, 8, DIN], BF16)
    Wout = TL("Wout", [128, 12, DM], BF16)
    cst = TL("cst", [128, NCST])
    identb = TL("identb", [128, 128], BF16)
    esel = TL("esel", [128, 16, 16])
    pft = TL("pft", [128, NPF])
    rgwt = TL("rgwt", [128, 2, 4, 128], BF16)
    rowt = TL("rowt", [128, 2056])
    nc8sp = TL("nc8sp", [128, 4])
    negA = TL("negA", [128, 4])
    ropeb = TL("ropeb", [128, 128])
    X = TL("X", [128, 4, 131])
    GX = TL("GX", [128, 12, 131])
    h0s = TL("h0s", [128, 4, NS])
    Sret = TL("Sret", [128, 512])
    Sretb = TL("Sretb", [128, 512], BF16)
    Sgdn = TL("Sgdn", [128, 512])
    hprev = TL("hprev", [128, 4])
    mix = TL("mix", [128, 12, 128], BF16)
    xt = TL("xt", [128, DM])
    xT = TL("xT", [128, 8, 128], BF16)
    ztr = TL("ztr", [128, 512])
    zte = TL("zte", [128, 512])
    ztg = TL("ztg", [128, 512])
    xcb = TL("xcb", [128, 512], BF16)
    mixr, mixe, mixg = "mixr", "mixe", "mixg"
    Vb = TL("Vb", [128, 512])
    Kbg = TL("Kbg", [128, 512])
    kd = TL("kd", [128, 512])
    qgT = TL("qgT", [128, 512])
    attT = TL("attT", [128, 512])
    Y = TL("Y", [128, 512])
    gabt = TL("gabt", [128, 8])
    gt = TL("gt", [128, 4])
    betat = TL("betat", [128, 4])
    gct = TL("gct", [128, 4])
    egt = TL("egt", [128, 4])
    eglt = TL("eglt", [128, 4])
    eglast = TL("eglast", [128, 4])
    sm1 = TL("sm1", [128, 4])
    sm2 = TL("sm2", [128, 4])
    bst = TL("bst", [128, 4, 6])
    bmv = TL("bmv", [128, 4, 2])
    ebs = TL("ebs", [128, 4, NS])
    vb = TL("vb", [128, 512], BF16)
    k2b = TL("k2b", [128, 512], BF16)
    sm3 = TL("sm3", [128, 4])
    nmr = TL("nmr", [128, 4])
    nmr2 = TL("nmr2", [128, 1])
    ngct = TL("ngct", [128, 4])
    bst2 = TL("bst2", [128, 2, 6])
    bmv2 = TL("bmv2", [128, 2])
    RT = [TL(f"rt{i}", [128, 512]) for i in range(4)]
    ET = [TL(f"et{i}", [128, 512]) for i in range(NET)]
    GT = [TL(f"gt{i}", [128, 512]) for i in range(9)]
    B = [TL(f"bb{i}", [128, 1024], BF16) for i in range(3)]
    ps = [Tile(nc, f"ps{i}", [128, 512], F32, psum=True) for i in range(8)]
    print("sbuf bytes remaining", nc.sbuf_bytes_remaining)
    GDN_W = 2

    def fs(ap):
        n = 1
        for d in ap.shape[1:]:
            n *= d
        return n

    def A(out, in_, func, r, w, bias=0.0, scale=1.0):
        S.op("act", lambda: nc.scalar.activation(out=out, in_=in_, func=func, bias=bias, scale=scale), r, w,
             cost=0.22 + fs(out) / 1000.0)

    def TT(out, a, b, op, r, w, e="dve"):
        if e == "pool":
            S.op("pool", lambda: nc.gpsimd.tensor_tensor(out=out, in0=a, in1=b, op=op), r, w, cost=0.3 + fs(out) / 500.0)
            return
        S.op("dve", lambda: nc.vector.tensor_tensor(out=out, in0=a, in1=b, op=op), r, w, cost=0.2 + fs(out) / 1000.0)

    def TS(out, a, s1, s2, op0, op1, r, w):
        if s2 is None:
            S.op("dve", lambda: nc.vector.tensor_scalar(out=out, in0=a, scalar1=s1, scalar2=None, op0=op0), r, w,
                 cost=0.2 + fs(out) / 1000.0)
        else:
            S.op("dve", lambda: nc.vector.tensor_scalar(out=out, in0=a, scalar1=s1, scalar2=s2, op0=op0, op1=op1), r, w,
                 cost=0.2 + fs(out) / 1000.0)

    def STT(out, a, s, b, op0, op1, r, w):
        S.op("dve", lambda: nc.vector.scalar_tensor_tensor(out=out, in0=a, scalar=s, in1=b, op0=op0, op1=op1), r, w,
             cost=0.2 + fs(out) / 1000.0)

    def CP(out, in_, r, w, e="dve"):
        if e == "dve":
            S.op("dve", lambda: nc.vector.tensor_copy(out=out, in_=in_), r, w, cost=0.2 + fs(out) / 1000.0)
        else:
            S.op("act", lambda: nc.scalar.activation(out=out, in_=in_, func=AF.Copy), r, w, cost=0.22 + fs(out) / 1000.0)

    def MS(out, val, w):
        S.op("dve", lambda: nc.vector.memset(out, val), (), w)

    def MM(out, lhsT, rhs, st, sp, r, w, inc=True):
        S.op("pe", lambda: nc.tensor.matmul(out, lhsT=lhsT, rhs=rhs, start=st, stop=sp), r, w, inc=inc,
             cost=(0.11 + fs(out) / 1200.0) * (2.0 if lhsT.dtype == F32 else 1.0))

    def TR(out, in_, ident, r, w, inc=True):
        S.op("pe", lambda: nc.tensor.transpose(out=out, in_=in_, identity=ident), r, w, inc=inc,
             cost=(0.11 + fs(out) / 1200.0) * (2.0 if in_.dtype == F32 else 1.0))

    def RSQ(out, in_, r, w, bias=EPS, scale=1.0):
        A(out, in_, AF.Ln, r, w, bias=bias, scale=scale)
        A(out, out, AF.Exp, w, w, scale=-0.5)

    ident = cst[:, 0:128]
    maskT = cst[:, 128:256]
    strictL = cst[:, 256:384]
    ones = cst[:, 384:512]

    S.dma("sp", cst[:], cst_d, w=[cst], stream="c")
    S.dma("sp", esel[:].rearrange("p a b -> p (a b)"), esel_d.partition_broadcast(128), w=[esel], stream="c")
    CP(identb[:], ident, [cst], [identb], e="act")

    def bc(ap2, n, L):
        return ap2.unsqueeze(2).to_broadcast([L, 4, n])

    def load_win(l):
        for kc in range(8):
            S.dma("pool", Win[:, kc, :], w_in[l, kc * 128:(kc + 1) * 128, :], w=[Win], stream="w")

    def layer_setup(l):
        if l == 0:
            load_win(0)
        for kc in range(12):
            S.dma("pool", Wout[:, kc, :], w_out[l, kc * 128:(kc + 1) * 128, :], w=[Wout], stream="w")
        S.dma("pool", rgwt[:].rearrange("p a c n -> p (a c n)"), rgw_d[l], w=[rgwt], stream="w")
        S.dma("sp", pft[:], pf_d[l], w=[pft], stream="c")
        S.dma("sp", rowt[:], rows_d[l].partition_broadcast(128), w=[rowt], stream="c")
        A(nc8sp[:], pft[:, 28:32], AF.Exp, [pft], [nc8sp], scale=-1.0)
        A(nc8sp[:], nc8sp[:], AF.Ln, [nc8sp], [nc8sp], bias=1.0)
        TS(nc8sp[:], nc8sp[:], -8.0, None, ALU.mult, None, [nc8sp], [nc8sp])
        A(negA[:], rowt[:, 2048:2052], AF.Exp, [rowt], [negA])
        TS(negA[:], negA[:], -1.0, None, ALU.mult, None, [negA], [negA])
        MS(Sret[:], 0.0, [Sret])
        MS(Sretb[:], 0.0, [Sretb])
        MS(Sgdn[:], 0.0, [Sgdn])
        MS(hprev[:], 0.0, [hprev])
        MS(X[:, :, 0:3], 0.0, [X])
        MS(GX[:, :, 0:3], 0.0, [GX])

    def xsource(l, mode, b):
        if mode == "s":
            return NS, 0, (xs_d if l == 0 else xsscr), ("xsscr", 0)
        L = 16 if b == 0 else 128
        t0 = 0 if b == 0 else 16 + 128 * (b - 1)
        return L, t0, (xp_d if l == 0 else xscr)[t0:t0 + L, :], ("xscr", b)

    head_done = set()

    def head(l, mode, b):
        L, t0, src, dkey = xsource(l, mode, b)
        S.dma("sp", xt[:L, :], src, r=[dkey], w=[xt], stream="x")
        xb = B[0]
        CP(xb[:L, :], xt[:L, :], [xt], [xb], e="act")
        pt = ps[0]
        ptb = pt.ap.bitcast(BF16)
        for kc in range(8):
            TR(ptb[:, kc * 128:kc * 128 + L], xb[:L, kc * 128:(kc + 1) * 128], identb[:L, :L], [xb, identb], [pt],
               inc=(kc == 7))
        CP(xT[:, :, :L], v3(ptb, 8)[:, :, :L], [pt], [xT], e="act")
        head_done.add((l, mode, b))

    def block(l, mode, b):
        smp = mode == "s"
        L, t0, xsrc_d, xdkey = xsource(l, mode, b)
        if (l, mode, b) not in head_done:
            head(l, mode, b)
        rb = 17 if smp else b
        S.dma("sp", ropeb[:L, :], rope_d[rb, 0:L, :], w=[ropeb], stream="x")
        mT = ident if smp else maskT
        ci = 528 if smp else 512
        qdec = cst[:L, ci:ci + 4]
        kdecp = cst[:L, ci + 4:ci + 8]
        if smp:
            k2dec = cst[:L, 536:540]
        elif L == 128:
            k2dec = cst[:L, 520:524]
        else:
            k2dec = cst[:L, 524:528]
        Xs = X.ap.rearrange("p c n -> p (c n)")[:, 0:4 * 4 * NS].rearrange("p (c j s) -> p c j s", c=4, j=4)
        GXs = GX.ap.rearrange("p c n -> p (c n)")[:, 0:12 * 4 * NS].rearrange("p (c j s) -> p c j s", c=12, j=4)

        def fm_group(bank, c0, dst_ap, dst_key, e="act"):
            b3 = v3(bank.ap)
            for c in range(4):
                for kc in range(8):
                    MM(b3[:, c, :L], Win[:, kc, c0 + c * 128:c0 + (c + 1) * 128], xT[:, kc, :L], kc == 0, kc == 7,
                       [Win, xT], [bank], inc=(c == 3 and kc == 7))
            CP(dst_ap, b3[:, :, :L], [bank], [dst_key], e=e)

        def tm_group(bank, c0, n, dst_ap, dst_key, e="dve"):
            for kc in range(8):
                MM(bank[:L, :n], xT[:, kc, :L], Win[:, kc, c0:c0 + n], kc == 0, kc == 7, [Win, xT], [bank],
                   inc=(kc == 7))
            CP(dst_ap, bank[:L, :n], [bank], [dst_key], e=e)

        def conv_tile(src_tap, wcol0, o3, dst_key, src_key, bias_col=None):
            for j in range(4):
                for c in range(4):
                    o = o3[:, c, :L]
                    wj = pft[:, wcol0 + c * 4 + j:wcol0 + c * 4 + j + 1]
                    ck_ = (dst_key, c)
                    if j == 0:
                        if bias_col is not None:
                            TS(o, src_tap(c, 0), wj, pft[:, bias_col + c:bias_col + c + 1], ALU.mult, ALU.add,
                               [src_key, pft], [ck_, dst_key])
                        else:
                            TS(o, src_tap(c, 0), wj, None, ALU.mult, None, [src_key, pft], [ck_, dst_key])
                    else:
                        STT(o, src_tap(c, j), wj, o, ALU.mult, ALU.add, [src_key, pft, ck_], [ck_])
            return [(dst_key, c) for c in range(4)]

        def conv_chunk(src_tap, wcol0, c, o, dst_key, src_key, bias_col=None):
            w0 = pft[:, wcol0 + c * 4:wcol0 + c * 4 + 1]
            if bias_col is not None:
                TS(o, src_tap(c, 0), w0, pft[:, bias_col + c:bias_col + c + 1], ALU.mult, ALU.add,
                   [src_key, pft], [dst_key])
            else:
                TS(o, src_tap(c, 0), w0, None, ALU.mult, None, [src_key, pft], [dst_key])
            for j in range(1, 4):
                STT(o, src_tap(c, j), pft[:, wcol0 + c * 4 + j:wcol0 + c * 4 + j + 1], o, ALU.mult, ALU.add,
                    [src_key, pft, dst_key], [dst_key])

        def gen_rg():
            pa, pb = ps[0], ps[1]
            if smp:
                S.dma("sp", Xs[:, :, 0:3, :], s_rgc[l], w=[X], stream="st")
                S.dma("sp", h0s[:], s_h[l], w=[h0s], stream="st")
                fm_group(pa, COL["rgx"], Xs[:, :, 3, :], X)
                tap = lambda c, j: Xs[:, c, j, :]
            else:
                fm_group(pa, COL["rgx"], X[:, :, 3:3 + L], X)
                tap = lambda c, j: X[:, c, j:j + L]
            yield
            z3 = v3(ztr.ap)
            fm_group(pb, COL["rgz"], z3[:, :, :L], ztr, e="dve")
            yield "pre"
            xc = RT[0]
            xc3 = v3(xc.ap)
            cks = conv_tile(tap, 0, xc3, xc, X, bias_col=16)
            yield
            xcb3 = v3(xcb.ap)
            CP(xcb3[:, :, :L], xc3[:, :, :L], cks, [xcb, xc], e="act")
            rt = RT[1]; it = RT[2]; at = RT[3]
            r3 = v3(rt.ap); i3 = v3(it.ap); a3 = v3(at.ap)
            for which, dst3, dkey, bcol, bank in ((0, r3, rt, 20, pa), (1, i3, it, 24, pb)):
                b3 = v3(bank.ap)
                for c in range(4):
                    MM(b3[:, c, :L], rgwt[:, which, c, :], xcb3[:, c, :L], True, True, [rgwt, xcb], [bank], inc=(c == 3))
                yield
                for c in range(4):
                    A(dst3[:, c, :L], b3[:, c, :L], AF.Sigmoid, [bank, pft], [dkey], bias=pft[:, bcol + c:bcol + c + 1])
                yield
            for c in range(4):
                A(a3[:, c, :L], r3[:, c, :L], AF.Exp, [rt, nc8sp], [at], scale=nc8sp[:, c:c + 1])
            yield
            mt = RT[1]
            m3 = v3(mt.ap)
            A(m3[:, :, :L], a3[:, :, :L], AF.Square, [at], [mt])
            A(m3[:, :, :L], m3[:, :, :L], AF.Ln, [mt], [mt], bias=1.0, scale=-1.0)
            A(m3[:, :, :L], m3[:, :, :L], AF.Exp, [mt], [mt], scale=0.5)
            yield
            TT(i3[:, :, :L], i3[:, :, :L], xc3[:, :, :L], ALU.mult, [it, xc], [it])
            yield
            TT(i3[:, :, :L], i3[:, :, :L], m3[:, :, :L], ALU.mult, [it, mt], [it])
            yield
            ht = RT[0]
            h3 = v3(ht.ap)
            if smp:
                TT(h3[:, :, :L], a3[:, :, :L], h0s[:], ALU.mult, [at, h0s], [ht])
                TT(h3[:, :, :L], h3[:, :, :L], i3[:, :, :L], ALU.add, [ht, it], [ht])
                S.dma("pool", o_h_s[l], h3[:, :, :L], r=[ht], stream="o")
                S.dma("pool", o_rgc_s[l], Xs[:, :, 1:4, :], r=[X], stream="o")
            else:
                for c in range(4):
                    S.op("dve", lambda c=c: nc.vector.tensor_tensor_scan(
                        out=h3[:, c, :L], data0=a3[:, c, :L], data1=i3[:, c, :L], initial=hprev[:, c:c + 1],
                        op0=ALU.mult, op1=ALU.add), [at, it, hprev], [ht])
                    yield
                CP(hprev[:].unsqueeze(2), h3[:, :, L - 1:L], [ht], [hprev])
                if b == NBLK - 1:
                    S.dma("pool", o_h_p[l], hprev[:], r=[hprev], stream="o")
                    S.dma("pool", o_rgc_p[l], X[:, :, L:L + 3], r=[X], stream="o")
                CP(X[:, :, 0:3], X[:, :, L:L + 3], [X], [X])
            yield
            A(z3[:, :, :L], z3[:, :, :L], AF.Silu, [ztr], [ztr])
            TT(mix[:, 0:4, :L], h3[:, :, :L], z3[:, :, :L], ALU.mult, [ht, ztr], [mixr])

        def gen_ret():
            bk = [ps[2], ps[3], ps[4]]
            rq = ET[0]; rk = ET[1]
            tm_group(bk[0], COL["rq"], 512, rq[:L, :], rq)
            yield
            tm_group(bk[1], COL["rk"], 512, rk[:L, :], rk)
            yield
            tm_group(bk[2], COL["rv"], 512, vb[:L, 0:512], vb)
            yield
            z3 = v3(zte.ap)
            fm_group(bk[0], COL["rz"], z3[:, :, :L], zte, e="dve")
            yield "pre"
            cosb = ropeb[:L, 0:64].unsqueeze(1).to_broadcast([L, 4, 64])
            sinb = ropeb[:L, 64:128].unsqueeze(1).to_broadcast([L, 4, 64])

            def rope(src, dst, tmp):
                s3 = v3(src[:L, :]); d3 = v3(dst[:L, :]); t3 = v3(tmp[:L, :])
                t1 = s3[:, :, 0:64]; t2 = s3[:, :, 64:128]
                TT(d3[:, :, 0:64], t1, cosb, ALU.mult, [src, ropeb], [dst], e=RE)
                TT(t3[:, :, 0:64], t2, sinb, ALU.mult, [src, ropeb], [tmp], e=RE)
                yield
                TT(d3[:, :, 0:64], d3[:, :, 0:64], t3[:, :, 0:64], ALU.subtract, [dst, tmp], [dst], e=RE)
                TT(d3[:, :, 64:128], t1, sinb, ALU.mult, [src, ropeb], [dst], e=RE)
                yield
                TT(t3[:, :, 64:128], t2, cosb, ALU.mult, [src, ropeb], [tmp], e=RE)
                TT(d3[:, :, 64:128], d3[:, :, 64:128], t3[:, :, 64:128], ALU.add, [dst, tmp], [dst], e=RE)
                yield

            rqr = ET[2]; rkr = ET[4]
            yield from rope(rq, rqr, ET[3])
            yield from rope(rk, rkr, ET[3])
            qkb = B[0]
            TT(v3(qkb[:L, 0:512]), v3(rqr[:L, :]), bc(qdec, 128, L), ALU.mult, [rqr, cst], [qkb], e=RE)
            TT(v3(qkb[:L, 512:1024]), v3(rkr[:L, :]), bc(kdecp, 128, L), ALU.mult, [rkr, cst], [qkb], e=RE)
            yield
            if smp:
                k2 = ET[5]
                TT(v3(k2[:L, :]), v3(rkr[:L, :]), bc(k2dec, 128, L), ALU.mult, [rkr, cst], [k2], e=RE)
            else:
                k2 = k2b
                TT(v3(k2[:L, 0:512]), v3(rkr[:L, :]), bc(k2dec, 128, L), ALU.mult, [rkr, cst], [k2], e=RE)
            yield
            pt = bk[1]
            ptb = pt.ap.bitcast(BF16)
            for j in range(8):
                TR(ptb[:, j * 128:j * 128 + L], qkb[:L, j * 128:(j + 1) * 128], identb[:L, :L], [qkb, identb], [pt],
                   inc=(j == 7))
            qkT = B[1]
            qkT3 = v3(qkT.ap, 8)
            CP(qkT3[:, :, :L], v3(ptb, 8)[:, :, :L], [pt], [qkT], e="act")
            yield
            bank = bk[2]
            b3 = v3(bank.ap)
            for h in range(4):
                MM(b3[:L, h, :L], qkT3[:, 4 + h, :L], qkT3[:, h, :L], True, True, [qkT], [bank], inc=(h == 3))
            scb = B[2]
            sc3 = v3(scb[:, 0:512])
            TT(sc3[:L, :, :L], b3[:L, :, :L], mT[:L, :L].unsqueeze(1).to_broadcast([L, 4, L]), ALU.mult, [bank, cst], [scb])
            yield
            ob = bk[0]
            ob3 = v3(ob.ap)
            for h in range(4):
                MM(ob3[:L, h, :], sc3[:L, h, :L], vb[:L, h * 128:(h + 1) * 128], True, smp, [scb, vb], [ob],
                   inc=(smp and h == 3))
                if not smp:
                    MM(ob3[:L, h, :], qkT3[:, h, :L], v3(Sretb.ap)[:, h, :], False, True, [qkT, Sretb], [ob], inc=(h == 3))
            yield
            if not smp:
                sb = bk[1]
                sb3 = v3(sb.ap)
                for h in range(4):
                    MM(sb3[:, h, :], k2[:L, h * 128:(h + 1) * 128], vb[:L, h * 128:(h + 1) * 128], True, True, [k2, vb], [sb],
                       inc=(h == 3))
                yield
                for h in range(4):
                    STT(v3(Sret.ap)[:, h, :], v3(Sret.ap)[:, h, :], float(GAM[h] ** L), sb3[:, h, :], ALU.mult, ALU.add,
                        [Sret, sb], [Sret])
                yield
                CP(Sretb[:], Sret[:], [Sret], [Sretb], e="act")
                if b == NBLK - 1:
                    S.dma("pool", o_ret_p[l].rearrange("h d v -> d h v"), v3(Sret.ap), r=[Sret], stream="o")
                osrc, okey = ob3, ob
            else:
                oacc = ET[1]
                CP(oacc[:L, :], ob[:L, :], [ob], [oacc])
                xTf = xT.ap.rearrange("p a b -> p (a b)").bitcast(F32)
                for s in range(NS):
                    St = (ET[2], Sret)[s % 2]
                    S.dma("sp", v3(St.ap), s_ret[l, s].rearrange("h d v -> d h v"), w=[St], stream="st")
                    qm = ET[3]
                    TT(v3(qm[:, 0:64], 4), qkT3[:, 0:4, :NS], esel[:, s, :].unsqueeze(1).to_broadcast([128, 4, NS]),
                       ALU.mult, [qkT, esel], [qm])
                    tb_ = bk[1]
                    tb3 = v3(tb_.ap)
                    for h in range(4):
                        MM(tb3[:NS, h, :], v3(qm[:, 0:64], 4)[:, h, :], v3(St.ap)[:, h, :], True, True, [qm, St], [tb_],
                           inc=(h == 3))
                    TT(oacc[:L, :], oacc[:L, :], tb_[:NS, :], ALU.add, [oacc, tb_], [oacc])
                    yield
                    vm = ET[4]
                    TT(vm[:NS, :], vb[:NS, 0:512], ident[:NS, s:s + 1].to_broadcast([NS, 512]), ALU.mult, [vb, cst], [vm])
                    sb = bk[2]
                    sb3 = v3(sb.ap)
                    for h in range(4):
                        MM(sb3[:, h, :], k2[:NS, h * 128:(h + 1) * 128], vm[:NS, h * 128:(h + 1) * 128], True, True,
                           [k2, vm], [sb], inc=(h == 3))
                    So, So3 = ((ET[0], v3(ET[0].ap)), (xT, v3(xTf)))[s % 2]
                    for h in range(4):
                        STT(So3[:, h, :], v3(St.ap)[:, h, :], float(GAM[h]), sb3[:, h, :], ALU.mult, ALU.add,
                            [St, sb], [So])
                    S.dma("pool", o_ret_s[l, s].rearrange("h d v -> d h v"), So3, r=[So], stream="o")
                    yield
                osrc, okey = v3(oacc.ap), oacc
            for h in range(4):
                S.op("dve", lambda h=h: nc.vector.bn_stats(out=bst[:L, h, :], in_=osrc[:L, h, :]), [okey], [bst])
            yield
            for h in range(4):
                S.op("dve", lambda h=h: nc.vector.bn_aggr(out=bmv[:L, h, :], in_=bst[:L, h, :]), [bst], [bmv])
            RSQ(sm1[:L, :], bmv[:L, :, 1], [bmv], [sm1])
            yield
            onb = B[2]
            STT(nmr[:L, :], bmv[:L, :, 0], -1.0, sm1[:L, :], ALU.mult, ALU.mult, [bmv, sm1], [nmr])
            for h in range(4):
                A(onb[:L, 512 + h * 128:512 + (h + 1) * 128], osrc[:L, h, :], AF.Identity, [okey, nmr, sm1], [onb],
                  bias=nmr[:L, h:h + 1], scale=sm1[:L, h:h + 1])
            yield
            pt = bk[1]
            ptb = pt.ap.bitcast(BF16)
            for h in range(4):
                TR(ptb[:, h * 128:h * 128 + L], onb[:L, 512 + h * 128:512 + (h + 1) * 128], identb[:L, :L], [onb, identb],
                   [pt], inc=(h == 3))
            yt = ET[0]
            y3 = v3(yt.ap)
            for h in range(4):
                A(y3[:, h, :L], ptb[:, h * 128:h * 128 + L], AF.Identity, [pt, pft], [yt], bias=pft[:, 36 + h:37 + h],
                  scale=pft[:, 32 + h:33 + h])
            yield
            A(z3[:, :, :L], z3[:, :, :L], AF.Silu, [zte], [zte])
            TT(mix[:, 4:8, :L], y3[:, :, :L], z3[:, :, :L], ALU.mult, [yt, zte], [mixe])

        def gen_gdn():
            bk = [ps[5], ps[6], ps[7]]
            nb = [0]

            def PB():
                nb[0] = (nb[0] + 1) % 3
                return bk[nb[0]]

            if smp:
                S.dma("sp", GXs[:, :, 0:3, :], s_gc[l], w=[GX], stream="st")
                for g in range(3):
                    fm_group(PB(), COL["gq"] + 512 * g, GXs[:, 4 * g:4 * g + 4, 3, :], GX)
                    yield
                gtap = lambda c, j: GXs[:, c, j, :]
            else:
                for g in range(3):
                    fm_group(PB(), COL["gq"] + 512 * g, GX[:, 4 * g:4 * g + 4, 3:3 + L], GX)
                    yield
                gtap = lambda c, j: GX[:, c, j:j + L]
            tm_group(PB(), COL["gab"], 8, gabt[:L, :], gabt)
            z3 = v3(ztg.ap)
            fm_group(PB(), COL["gz"], z3[:, :, :L], ztg, e="dve")
            yield "pre"
            TT(gt[:L, :], gabt[:L, 0:4], rowt[:L, 2052:2056], ALU.add, [gabt, rowt], [gt])
            A(gt[:L, :], gt[:L, :], AF.Exp, [gt], [gt])
            A(gt[:L, :], gt[:L, :], AF.Ln, [gt], [gt], bias=1.0)
            TT(gt[:L, :], gt[:L, :], negA[:L, :], ALU.mult, [gt, negA], [gt])
            A(betat[:L, :], gabt[:L, 4:8], AF.Sigmoid, [gabt], [betat])
            yield
            bank = PB()
            MM(bank[:L, 0:4], mT[:L, :L], gt[:L, :], True, True, [cst, gt], [bank])
            CP(gct[:L, :], bank[:L, 0:4], [bank], [gct])
            A(egt[:L, :], gct[:L, :], AF.Exp, [gct], [egt])
            yield
            Rt = GT[5]
            R3 = v3(Rt.ap)
            for h in range(4):
                A(R3[:L, h, :L], mT[:L, :L], AF.Copy, [cst, gt], [Rt], scale=gt[:L, h:h + 1])
            TS(ngct[:L, :], gct[:L, :], -1.0, None, ALU.mult, None, [gct], [ngct])
            yield
            gcB = PB()
            g3 = v3(gcB.ap)
            for h in range(4):
                MM(g3[:, h, :L], ones[:L, :], R3[:L, h, :L], True, True, [cst, Rt], [gcB], inc=(h == 3))
            EB = GT[6]
            EB3 = v3(EB.ap)
            A(EB3[:, :, :L], g3[:, :, :L], AF.Exp, [gcB], [EB])
            yield
            dT = GT[7]
            dT3 = v3(dT.ap)
            for h in range(4):
                A(dT3[:L, h, :L], g3[:L, h, :L], AF.Relu, [gcB, gct], [dT], bias=gct[:L, h:h + 1], scale=-1.0)
            yield
            if not smp:
                dl = GT[8]
                dl3 = v3(dl.ap)
                for h in range(4):
                    A(dl3[:L, h, :L], g3[:L, h, :L], AF.Relu, [gcB, ngct], [dl], bias=ngct[:L, h:h + 1], scale=1.0)
                yield
                CP(eglast[:].unsqueeze(2), EB3[:, :, L - 1:L], [EB], [eglast])
                for h in range(4):
                    A(eglt[:L, h:h + 1], g3[:L, h, L - 1:L], AF.Exp, [gcB, ngct], [eglt], bias=ngct[:L, h:h + 1])
                A(dl3[:L, :, :L], dl3[:L, :, :L], AF.Exp, [dl], [dl], scale=-1.0)
                TT(dl3[:L, :, :L], dl3[:L, :, :L], strictL[:L, :L].unsqueeze(1).to_broadcast([L, 4, L]), ALU.mult,
                   [dl, cst], [dl])
                for h in range(4):
                    A(dl3[:L, h, :L], dl3[:L, h, :L], AF.Copy, [dl, betat], [dl], scale=betat[:L, h:h + 1])
                yield
            else:
                CP(ebs[:], EB3[:, :, :NS], [EB], [ebs])
            A(dT3[:L, :, :L], dT3[:L, :, :L], AF.Exp, [dT], [dT], scale=-1.0)
            TT(dT3[:L, :, :L], dT3[:L, :, :L], mT[:L, :L].unsqueeze(1).to_broadcast([L, 4, L]), ALU.mult, [dT, cst], [dT])
            yield
            cq = GT[0]; ck = GT[1]; cv = GT[2]
            cqk = [cq, ck, cv]
            for g in range(3):
                cg3 = v3(cqk[g].ap)
                cks = conv_tile(lambda c_, j, g=g: gtap(4 * g + c_, j), 40 + 16 * g, cg3, cqk[g], GX)
                yield
                A(cg3[:, :, :L], cg3[:, :, :L], AF.Silu, cks, [cqk[g]])
            if smp:
                S.dma("pool", o_gc_s[l], GXs[:, :, 1:4, :], r=[GX], stream="o")
            else:
                if b == NBLK - 1:
                    S.dma("pool", o_gc_p[l], GX[:, :, L:L + 3], r=[GX], stream="o")
                CP(GX[:, :, 0:3], GX[:, :, L:L + 3], [GX], [GX])
            yield
            for g in range(2):
                cg3 = v3(cqk[g].ap)
                sq = GT[3]
                sq3 = v3(sq.ap)
                A(sq3[:, :, :L], cg3[:, :, :L], AF.Square, [cqk[g]], [sq])
                bank = PB()
                b3 = v3(bank.ap)
                for h in range(4):
                    MM(b3[:, h, :L], ones, sq3[:, h, :L], True, True, [cst, sq], [bank], inc=(h == 3))
                yield
                rn = GT[4]
                rn3 = v3(rn.ap)
                RSQ(rn3[:, :, :L], b3[:, :, :L], [bank], [rn])
                if g == 0:
                    STT(cg3[:, :, :L], cg3[:, :, :L], float(128 ** -0.5), rn3[:, :, :L], ALU.mult, ALU.mult, [cq, rn], [cq])
                else:
                    TT(cg3[:, :, :L], cg3[:, :, :L], rn3[:, :, :L], ALU.mult, [ck, rn], [ck])
                yield
            q3 = v3(cq.ap); k3 = v3(ck.ap); cv3 = v3(cv.ap)
            TT(v3(qgT.ap)[:, :, :L], q3[:, :, :L], EB3[:, :, :L], ALU.mult, [cq, EB], [qgT])
            bank = PB()
            b3 = v3(bank.ap)
            for h in range(4):
                MM(b3[:L, h, :L], k3[:, h, :L], q3[:, h, :L], True, True, [ck, cq], [bank], inc=(h == 3))
            at3 = v3(attT.ap)
            TT(at3[:L, :, :L], b3[:L, :, :L], dT3[:L, :, :L], ALU.mult, [bank, dT], [attT])
            yield
            kTM = GT[3]; vTM = GT[4]
            for srcT, s3_, dstT in ((ck, k3, kTM), (cv, cv3, vTM)):
                bank = PB()
                for h in range(4):
                    TR(bank[:L, h * 128:(h + 1) * 128], s3_[:, h, :L], ident, [srcT, cst], [bank], inc=(h == 3))
                CP(dstT[:L, :], bank[:L, :], [bank], [dstT], e="act")
                yield
            if not smp:
                TT(v3(kd[:L, :]), v3(kTM[:L, :]), bc(eglt[:L, :], 128, L), ALU.mult, [kTM, eglt], [kd])
            else:
                CP(kd[:L, :], kTM[:L, :], [kTM], [kd])
            TT(v3(Vb[:L, :]), v3(vTM[:L, :]), bc(betat[:L, :], 128, L), ALU.mult, [vTM, betat], [Vb])
            yield
            TT(sm2[:L, :], betat[:L, :], egt[:L, :], ALU.mult, [betat, egt], [sm2])
            TT(v3(Kbg[:L, :]), v3(kTM[:L, :]), bc(sm2[:L, :], 128, L), ALU.mult, [kTM, sm2], [Kbg])
            yield
            Y3 = v3(Y.ap)
            if smp:
                CP(Y3[:L, :, :L], ident[:L, :L].unsqueeze(1).to_broadcast([L, 4, L]), [cst], [Y])
            else:
                bank = PB()
                b3 = v3(bank.ap)
                for h in range(4):
                    MM(b3[:L, h, :L], k3[:, h, :L], k3[:, h, :L], True, True, [ck], [bank], inc=(h == 3))
                P = GT[2]
                P3 = v3(P.ap)
                TT(P3[:L, :, :L], b3[:L, :, :L], dl3[:L, :, :L], ALU.mult, [bank, dl], [P])
                yield
                bank = PB()
                b3 = v3(bank.ap)
                for h in range(4):
                    TR(b3[:L, h, :L], P3[:L, h, :L], ident[:L, :L], [P, cst], [bank], inc=(h == 3))
                Q = GT[5]
                Q3 = v3(Q.ap)
                CP(Q3[:L, :, :L], b3[:L, :, :L], [bank], [Q], e="act")
                STT(Y3[:L, :, :L], Q3[:L, :, :L], -1.0, ident[:L, :L].unsqueeze(1).to_broadcast([L, 4, L]), ALU.mult, ALU.add,
                    [Q, cst], [Y])
                yield
                nlev = 6 if L == 128 else 3
                for lev in range(nlev):
                    bq = PB(); bp = PB()
                    bq3 = v3(bq.ap); bp3 = v3(bp.ap)
                    for h in range(4):
                        MM(bq3[:L, h, :L], P3[:L, h, :L], Q3[:L, h, :L], True, True, [P, Q], [bq], inc=(h == 3))
                    for h in range(4):
                        MM(bp3[:L, h, :L], Q3[:L, h, :L], P3[:L, h, :L], True, True, [P, Q], [bp], inc=(h == 3))
                    yield
                    Pn, Qn = (GT[3], GT[4]) if lev % 2 == 0 else (GT[2], GT[5])
                    CP(v3(Qn.ap)[:L, :, :L], bq3[:L, :, :L], [bq], [Qn], e="act")
                    CP(v3(Pn.ap)[:L, :, :L], bp3[:L, :, :L], [bp], [Pn], e="dve")
                    yield
                    P, Q = Pn, Qn
                    P3, Q3 = v3(P.ap), v3(Q.ap)
                    by = PB()
                    by3 = v3(by.ap)
                    for h in range(4):
                        MM(by3[:L, h, :L], P3[:L, h, :L], Y3[:L, h, :L], True, True, [P, Y], [by], inc=(h == 3))
                    TT(Y3[:L, :, :L], Y3[:L, :, :L], by3[:L, :, :L], ALU.add, [Y, by], [Y])
                    yield
            bank = PB()
            b3 = v3(bank.ap)
            for h in range(4):
                MM(b3[:, h, :L], Kbg[:L, h * 128:(h + 1) * 128], Y3[:L, h, :L], True, True, [Kbg, Y], [bank], inc=(h == 3))
            nWT = GT[6]
            nW3 = v3(nWT.ap)
            A(nW3[:, :, :L], b3[:, :, :L], AF.Copy, [bank], [nWT], scale=-1.0)
            yield
            Sg3 = v3(Sgdn.ap)
            vnb = PB()
            vn3 = v3(vnb.ap)
            for h in range(4):
                MM(vn3[:L, h, :], Y3[:L, h, :L], Vb[:L, h * 128:(h + 1) * 128], True, smp, [Y, Vb], [vnb],
                   inc=(smp and h == 3))
                if not smp:
                    MM(vn3[:L, h, :], nW3[:, h, :L], Sg3[:, h, :], False, True, [nWT, Sgdn], [vnb], inc=(h == 3))
            vnew = GT[7]
            if not smp:
                CP(vnew[:L, :], vnb[:L, :], [vnb], [vnew], e="act")
                yield
            else:
                wacc = GT[0]; qacc = GT[1]
                CP(wacc[:L, :], vnb[:L, :], [vnb], [wacc], e="act")
                MS(qacc[:L, :], 0.0, [qacc])
                for s in range(NS):
                    St = (GT[2], Sgdn)[s % 2]
                    S.dma("sp", v3(St.ap), s_gdn[l, s].rearrange("h d v -> d h v"), w=[St], stream="st")
                    for srcT, s3_, acc in ((qgT, v3(qgT.ap), qacc), (nWT, nW3, wacc)):
                        qm = GT[3]
                        TT(v3(qm[:, 0:64], 4), s3_[:, :, :NS], esel[:, s, :].unsqueeze(1).to_broadcast([128, 4, NS]),
                           ALU.mult, [srcT, esel], [qm])
                        tb_ = PB()
                        tb3 = v3(tb_.ap)
                        for h in range(4):
                            MM(tb3[:NS, h, :], v3(qm[:, 0:64], 4)[:, h, :], v3(St.ap)[:, h, :], True, True, [qm, St], [tb_],
                               inc=(h == 3))
                        TT(acc[:L, :], acc[:L, :], tb_[:NS, :], ALU.add, [acc, tb_], [acc])
                        yield
                CP(vnew[:L, :], wacc[:L, :], [wacc], [vnew])
            ob = PB()
            ob3 = v3(ob.ap)
            for h in range(4):
                MM(ob3[:L, h, :], at3[:L, h, :L], vnew[:L, h * 128:(h + 1) * 128], True, smp, [attT, vnew], [ob],
                   inc=(smp and h == 3))
                if not smp:
                    MM(ob3[:L, h, :], v3(qgT.ap)[:, h, :L], Sg3[:, h, :], False, True, [qgT, Sgdn], [ob], inc=(h == 3))
            yield
            if smp:
                TT(qacc[:L, :], qacc[:L, :], ob[:L, :], ALU.add, [qacc, ob], [qacc])
                osrc, okey = v3(qacc.ap), qacc
                for s in range(NS):
                    St = (GT[2], Sgdn)[s % 2]
                    S.dma("sp", v3(St.ap), s_gdn[l, s].rearrange("h d v -> d h v"), w=[St], stream="st")
                    vm = GT[3]
                    TS(vm[:NS, :], vnew[:NS, :], ident[:NS, s:s + 1], None, ALU.mult, None, [vnew, cst], [vm])
                    sb = PB()
                    sb3 = v3(sb.ap)
                    for h in range(4):
                        MM(sb3[:, h, :], kd[:NS, h * 128:(h + 1) * 128], vm[:NS, h * 128:(h + 1) * 128], True, True,
                           [kd, vm], [sb], inc=(h == 3))
                    So = (GT[4], GT[8])[s % 2]
                    for h in range(4):
                        STT(v3(So.ap)[:, h, :], v3(St.ap)[:, h, :], ebs[:, h, s:s + 1], sb3[:, h, :], ALU.mult, ALU.add,
                            [St, sb, ebs], [So])
                    S.dma("pool", o_gdn_s[l, s].rearrange("h d v -> d h v"), v3(So.ap), r=[So], stream="o")
                    yield
            else:
                osrc, okey = ob3, ob
                sb = PB()
                sb3 = v3(sb.ap)
                for h in range(4):
                    MM(sb3[:, h, :], kd[:L, h * 128:(h + 1) * 128], vnew[:L, h * 128:(h + 1) * 128], True, True, [kd, vnew], [sb],
                       inc=(h == 3))
                yield
                for h in range(4):
                    STT(Sg3[:, h, :], Sg3[:, h, :], eglast[:, h:h + 1], sb3[:, h, :], ALU.mult, ALU.add,
                        [Sgdn, sb, eglast], [Sgdn])
                if b == NBLK - 1:
                    S.dma("pool", o_gdn_p[l].rearrange("h d v -> d h v"), Sg3, r=[Sgdn], stream="o")
                yield
            osq = GT[8]
            A(v3(osq.ap)[:L, :, :], osrc[:L, :, :], AF.Square, [okey], [osq])
            S.op("dve", lambda: nc.vector.reduce_sum(out=sm3[:L, :], in_=v3(osq.ap)[:L, :, :], axis=AX.X), [osq], [sm3])
            RSQ(sm3[:L, :], sm3[:L, :], [sm3], [sm3], bias=EPS, scale=1.0 / 128.0)
            yield
            on = GT[5]
            for h in range(4):
                A(on[:L, h * 128:(h + 1) * 128], osrc[:L, h, :], AF.Copy, [okey, sm3], [on], scale=sm3[:L, h:h + 1])
            yield
            bank = PB()
            b3 = v3(bank.ap)
            for h in range(4):
                TR(b3[:, h, :L], on[:L, h * 128:(h + 1) * 128], ident[:L, :L], [on, cst], [bank], inc=(h == 3))
            A(z3[:, :, :L], z3[:, :, :L], AF.Silu, [ztg], [ztg])
            STT(mix[:, 8:12, :L], b3[:, :, :L], pft[:, 88:89], z3[:, :, :L], ALU.mult, ALU.mult, [bank, pft, ztg], [mixg])

        if MERGE == "sim":
            branches = []
            live = []
            for gen in (gen_gdn, gen_ret, gen_rg):
                g = gen()
                for mark in g:
                    if mark == "pre":
                        break
                live.append(g)
            if smp and l < NL - 1:
                load_win(l + 1)
            if not smp:
                if b + 1 < NBLK:
                    head(l, "p", b + 1)
                elif not SKIP_SAMPLE:
                    head(l, "s", 0)
            for g in live:
                S.rec = []
                for _ in g:
                    pass
                ops, S.rec = S.rec, None
                units, cur = [], []
                for o in ops:
                    cur.append(o)
                    if o[5]:
                        units.append(cur)
                        cur = []
                assert not cur
                branches.append(units)
            clock = {}
            wr = {}
            rd = {}
            ptr = [0] * len(branches)
            HOP = 0.3
            while True:
                best = None
                for bi, units in enumerate(branches):
                    if ptr[bi] >= len(units):
                        continue
                    u = units[ptr[bi]]
                    e = u[0][1]
                    t = clock.get(e, 0.0)
                    for o in u:
                        for k in o[3]:
                            if k in wr:
                                t = max(t, wr[k][0] + (HOP if wr[k][1] != e else 0.0))
                        for k in o[4]:
                            if k in wr:
                                t = max(t, wr[k][0] + (HOP if wr[k][1] != e else 0.0))
                            if k in rd:
                                t = max(t, rd[k][0] + (HOP if rd[k][1] != e else 0.0))
                    if best is None or t < best[0] - 1e-9:
                        best = (t, bi)
                if best is None:
                    break
                t, bi = best
                u = branches[bi][ptr[bi]]
                ptr[bi] += 1
                e = u[0][1]
                for o in u:
                    kind, eng, fn, r_, w_, inc, cost = o
                    if kind == "dma":
                        out_, in_, kw = fn
                        S.dma(eng, out_, in_, r=r_, w=w_, **kw)
                        done = t + 2.5
                        t += cost
                        who = "dma"
                    else:
                        S.op(eng, fn, r_, w_, inc=inc)
                        t += cost
                        done = t
                        who = eng
                    for k in r_:
                        if k not in rd or rd[k][0] < done:
                            rd[k] = (done, who)
                    for k in w_:
                        wr[k] = (done, who)
                        rd.pop(k, None)
                clock[e] = t
            gens = []
        else:
            gens = [(gen_rg(), 1), (gen_ret(), 1), (gen_gdn(), GDN_W)]
        if MERGE == "seq":
            for g, _ in gens:
                for _ in g:
                    pass
            gens = []
        while gens:
            for item in list(gens):
                g, wgt = item
                for _ in range(wgt):
                    try:
                        next(g)
                    except StopIteration:
                        gens.remove(item)
                        break

        z = [RT[0], RT[1]]
        for n in range(2):
            bank = ps[n]
            for kc in range(12):
                MM(bank[:L, :], mix[:, kc, :L], Wout[:, kc, n * 512:(n + 1) * 512], kc == 0, kc == 11,
                   [mixr, mixe, mixg, Wout], [bank], inc=(kc == 11))
            S.dma("sp", z[n][:L, :], xsrc_d[:, n * 512:(n + 1) * 512], r=[xdkey], w=[z[n]], stream="x")
            STT(z[n][:L, :], z[n][:L, :], float(ALPHA), bank[:L, :], ALU.mult, ALU.add, [z[n], bank], [z[n]])
            S.op("dve", lambda n=n: nc.vector.bn_stats(out=bst2[:L, n, :], in_=z[n][:L, :]), [z[n]], [bst2])
        S.op("dve", lambda: nc.vector.bn_aggr(out=bmv2[:L, :], in_=bst2[:L, 0:2, :]), [bst2], [bmv2])
        RSQ(sm2[:L, 0:1], bmv2[:L, 1:2], [bmv2], [sm2])
        for n in range(2):
            sl = slice(n * 512, (n + 1) * 512)
            if n == 0:
                STT(nmr2[:L, :], bmv2[:L, 0:1], -1.0, sm2[:L, 0:1], ALU.mult, ALU.mult, [bmv2, sm2], [nmr2])
            A(z[n][:L, :], z[n][:L, :], AF.Identity, [z[n], nmr2, sm2], [z[n]], bias=nmr2[:L, 0:1], scale=sm2[:L, 0:1])
            TT(z[n][:L, :], z[n][:L, :], rowt[:L, sl], ALU.mult, [z[n], rowt], [z[n]])
            TT(z[n][:L, :], z[n][:L, :], rowt[:L, 1024 + n * 512:1024 + (n + 1) * 512], ALU.add, [z[n], rowt], [z[n]])
            if smp:
                if l == NL - 1:
                    S.dma("pool", y_s[:, sl], z[n][:NS, :], r=[z[n]], stream="o")
                else:
                    S.dma("pool", xsscr[:, sl], z[n][:NS, :], r=[z[n]], w=[("xsscr", 0)], sname=f"xs{n}_{l % 2}")
            else:
                if l == NL - 1:
                    if b > 0:
                        S.dma("pool", y_p[t0 - 16:t0 - 16 + L, sl], z[n][:L, :], r=[z[n]], stream="o")
                else:
                    S.dma("pool", xscr[t0:t0 + L, sl], z[n][:L, :], r=[z[n]], w=[("xscr", b)], sname=f"xo{n}_{l % 2}")


    for l in range(NL):
        layer_setup(l)
        for b in range(NBLK):
            block(l, "p", b)
        if not SKIP_SAMPLE:
            block(l, "s", 0)
    S.finish("sp")
    print("ops", S.nops, "waits", S.nwaits, "sems", S.nsem + len(S.dstream))
    return nc


def _consts():
    cst = np.zeros((128, NCST), np.float32)
    i = np.arange(128)
    cst[:, 0:128] = np.eye(128)
    cst[:, 128:256] = (i[None, :] >= i[:, None])
    cst[:, 256:384] = (i[None, :] < i[:, None])
    cst[:, 384:512] = 1.0
    sc = 128.0 ** -0.5
    for h in range(4):
        g = np.float64(GAM[h])
        cst[:, 512 + h] = g ** (i + 1.0)
        cst[:, 516 + h] = g ** (-(i + 1.0)) * sc
        cst[:, 520 + h] = g ** (127.0 - i) * sc
        cst[:16, 524 + h] = g ** (15.0 - i[:16]) * sc
        cst[:, 528 + h] = g
        cst[:, 532 + h] = sc / g
        cst[:, 536 + h] = sc
    half = 64
    inv = (np.float32(10000.0) ** (-np.arange(half, dtype=np.float32) / np.float32(half))).astype(np.float32)
    rope = np.zeros((18, 128, 128), np.float32)
    for b in range(18):
        if b == 0:
            pos = np.arange(16, dtype=np.float32)
        elif b < 17:
            pos = 16 + 128 * (b - 1) + np.arange(128, dtype=np.float32)
        else:
            pos = np.full(16, 16384.0, np.float32)
        ang = (pos[:, None].astype(np.float32) * inv[None, :]).astype(np.float32)
        rope[b, :len(pos), 0:64] = np.cos(ang.astype(np.float64))
        rope[b, :len(pos), 64:128] = np.sin(ang.astype(np.float64))
    esel = np.eye(16, dtype=np.float32).reshape(1, 256)
    return cst, rope, esel


_NC_CACHE = {}


def kernel(x_prompt, x_sample, state_rglru_h, state_rglru_conv, state_ret, state_gdn_conv, state_gdn,
           meta_tokens, w_in, rg_conv_w, rg_conv_b, rg_w_a, rg_b_a, rg_w_x, rg_b_x, rg_lambda,
           ret_gn_w, ret_gn_b, gdn_conv_w, gdn_a_log, gdn_dt_bias, gdn_norm_w, w_out, ln_w, ln_b):
    f = lambda a: np.ascontiguousarray(np.asarray(a, dtype=np.float32))
    x_prompt, x_sample, meta_tokens = f(x_prompt), f(x_sample), f(meta_tokens)
    w_in, w_out = f(w_in), f(w_out)
    pf = np.zeros((NL, 128, NPF), np.float32)

    def fm(v, nch):
        return f(v).reshape(NL, nch, 128).transpose(0, 2, 1)

    pf[:, :, 0:16] = f(rg_conv_w).reshape(NL, 4, 4, 128).transpose(0, 3, 2, 1).reshape(NL, 128, 16)
    pf[:, :, 16:20] = fm(rg_conv_b, 4)
    pf[:, :, 20:24] = fm(rg_b_a, 4)
    pf[:, :, 24:28] = fm(rg_b_x, 4)
    pf[:, :, 28:32] = fm(rg_lambda, 4)
    pf[:, :, 32:36] = fm(ret_gn_w, 4)
    pf[:, :, 36:40] = fm(ret_gn_b, 4)
    pf[:, :, 40:88] = f(gdn_conv_w).reshape(NL, 4, 12, 128).transpose(0, 3, 2, 1).reshape(NL, 128, 48)
    pf[:, :, 88] = f(gdn_norm_w)
    rgw = np.zeros((NL, 128, 2, 4, 128), np.float32)
    for which, wsrc in ((0, f(rg_w_a)), (1, f(rg_w_x))):
        for n in range(8):
            c, o = n // 2, (n % 2) * 64
            rgw[:, o:o + 64, which, c, o:o + 64] = wsrc[:, n]
    rgw = rgw.reshape(NL, 128, 1024)
    rows = np.concatenate([f(ln_w), f(ln_b), f(gdn_a_log), f(gdn_dt_bias)], axis=1).reshape(NL, 1, 2056)
    cst, rope, esel = _consts()
    if "nc" not in _NC_CACHE:
        _NC_CACHE["nc"] = build_nc()
    nc = _NC_CACHE["nc"]
    in_maps = []
    for c in range(8):
        sl = slice(NS * c, NS * (c + 1))
        m = {
            "xp": np.ascontiguousarray(np.concatenate([meta_tokens, x_prompt[c]], axis=0)),
            "xs": np.ascontiguousarray(x_sample[sl, 0, :]),
            "s_h": np.ascontiguousarray(f(state_rglru_h)[:, sl].reshape(NL, NS, 4, 128).transpose(0, 3, 2, 1)),
            "s_rgc": np.ascontiguousarray(f(state_rglru_conv)[:, sl].reshape(NL, NS, 3, 4, 128).transpose(0, 4, 3, 2, 1)),
            "s_gc": np.ascontiguousarray(f(state_gdn_conv)[:, sl].reshape(NL, NS, 3, 12, 128).transpose(0, 4, 3, 2, 1)),
            "s_ret": np.ascontiguousarray(f(state_ret)[:, sl]),
            "s_gdn": np.ascontiguousarray(f(state_gdn)[:, sl]),
            "w_in": w_in, "w_out": w_out, "pf": pf, "rgw": rgw, "rows": rows,
            "cst": cst, "ropet": rope, "esel": esel,
        }
        in_maps.append(m)
    res = run_bass_kernel_spmd(nc, in_maps, core_ids=list(range(8)))
    R = res.results
    g = lambda k: [np.asarray(R[c][k], dtype=np.float32) for c in range(8)]
    y_prompt = np.stack(g("y_p"), 0)
    y_sample = np.concatenate(g("y_s"), 0)[:, None, :]
    hp = np.stack([a.transpose(0, 2, 1).reshape(NL, 512) for a in g("o_h_p")], 1)
    rgcp = np.stack([a.transpose(0, 3, 2, 1).reshape(NL, 3, 512) for a in g("o_rgc_p")], 1)
    retp = np.stack(g("o_ret_p"), 1)
    gcp = np.stack([a.transpose(0, 3, 2, 1).reshape(NL, 3, 1536) for a in g("o_gc_p")], 1)
    gdnp = np.stack(g("o_gdn_p"), 1)
    hs = np.concatenate([a.transpose(0, 3, 2, 1).reshape(NL, NS, 512) for a in g("o_h_s")], 1)
    rgcs = np.concatenate([a.transpose(0, 4, 3, 2, 1).reshape(NL, NS, 3, 512) for a in g("o_rgc_s")], 1)
    rets = np.concatenate(g("o_ret_s"), 1)
    gcs = np.concatenate([a.transpose(0, 4, 3, 2, 1).reshape(NL, NS, 3, 1536) for a in g("o_gc_s")], 1)
    gdns = np.concatenate(g("o_gdn_s"), 1)
    c = np.ascontiguousarray
    return (c(y_prompt), c(y_sample), c(hp), c(rgcp), c(retp), c(gcp), c(gdnp), c(hs), c(rgcs), c(rets), c(gcs), c(gdns))
```

```python
import numpy as np
import concourse.bass as bass
import concourse.mybir as mybir
from concourse.bass_utils import run_bass_kernel_spmd

F32 = mybir.dt.float32
BF16 = mybir.dt.bfloat16
ALU = mybir.AluOpType
AF = mybir.ActivationFunctionType
AX = mybir.AxisListType

EPOCH = 12000
RE = "dve"
DEFCOST = {"pe": 0.2, "act": 0.4, "dve": 0.3, "pool": 0.05, "sp": 0.05}
GDN_STOP = 100000
ENABLE = [True, True, True]
NET = 6
SKIP_SAMPLE = False
PER_TILE_SEMS = True
MERGE = "sim"
SAME_SYNC = False

NL = 4
DM = 1024
DIN = 5128
NTOK = 2064
NBLK = 17
NS = 16
ALPHA = 8.0 ** 0.25
EPS = 1e-6
GAM = [1.0 - 2.0 ** (-5.0 - h) for h in range(4)]
NCST = 540
NPF = 89
COL = dict(rgx=0, rgz=512, rq=1024, rk=1536, rv=2048, rz=2560, gq=3072, gk=3584, gv=4096, gz=4608, gab=5120)


class Tile:
    def __init__(self, nc, name, shape, dtype, psum=False):
        if psum:
            self.h = nc.alloc_psum_tensor("T_" + name, list(shape), dtype)
        else:
            self.h = nc.alloc_sbuf_tensor("T_" + name, list(shape), dtype)
        self.ap = self.h.ap()
        self.name = name

    def __getitem__(self, k):
        return self.ap[k]


class Sched:
    def __init__(self, nc):
        self.nc = nc
        self.eng = {"pe": nc.tensor, "act": nc.scalar, "dve": nc.vector, "pool": nc.gpsimd, "sp": nc.sync}
        self.sem = {}
        self.cnt = {}
        self.pend = {}
        self.nsem = 0
        for e in self.eng:
            self._new_sem(e)
            self.pend[e] = False
        self.lastw = {}
        self.readers = {}
        self.waited = {e: {} for e in self.eng}
        self.dstream = {}
        self.nwaits = 0
        self.nops = 0
        self.rec = None

    def _new_sem(self, e):
        self.sem[e] = self.nc.alloc_semaphore(f"s_{e}_{self.nsem}")
        self.nsem += 1
        self.cnt[e] = 0

    def _deps(self, r, w):
        evs = []
        for k in r:
            if k in self.lastw:
                evs.append(self.lastw[k] + (True,))
        for k in w:
            if k in self.lastw:
                evs.append(self.lastw[k] + (False,))
            evs.extend(v + (False,) for v in self.readers.get(k, {}).values())
        return evs

    def _do_waits(self, e, evs):
        need = {}
        for sem, val, src, raw in evs:
            if src == e and not (SAME_SYNC or raw):
                continue
            if src.startswith("dma:"):
                val = self.dstream[src[4:]][1]
            if val > need.get(sem, (0, None))[0]:
                need[sem] = (val, src)
        for sem, (val, src) in need.items():
            if self.waited[e].get(sem, 0) >= val:
                continue
            if src == e and sem is self.sem[e] and val > self.cnt[e]:
                continue
            self.eng[e].wait_ge(sem, val)
            self.waited[e][sem] = val
            self.nwaits += 1

    def _register(self, ev, r, w):
        sem = ev[0]
        for k in r:
            self.readers.setdefault(k, {})[sem] = ev
        for k in w:
            self.lastw[k] = ev
            self.readers[k] = {}

    def op(self, e, fn, r=(), w=(), inc=True, cost=None):
        if self.rec is not None:
            self.rec.append(("op", e, fn, tuple(r), tuple(w), inc, cost if cost else DEFCOST[e]))
            return None
        self._do_waits(e, self._deps(r, w))
        if self.cnt[e] >= EPOCH and not self.pend[e]:
            self._new_sem(e)
        ins = fn()
        self.nops += 1
        if inc:
            self.cnt[e] += 1
            ins.then_inc(self.sem[e], 1)
            ev = (self.sem[e], self.cnt[e], e)
            self.pend[e] = False
        else:
            ev = (self.sem[e], self.cnt[e] + 1, e)
            self.pend[e] = True
        self._register(ev, r, w)
        return ins

    def dma(self, q, out, in_, r=(), w=(), stream="d", sname=None, **kw):
        if self.rec is not None:
            self.rec.append(("dma", q, (out, in_, dict(kw, sname=sname)), tuple(r), tuple(w), True, 0.05))
            return None
        tl = [k for k in w if isinstance(k, Tile)]
        if sname is not None:
            stream = sname
        elif not PER_TILE_SEMS:
            pass
        elif tl:
            stream = "ld_" + tl[0].name
        else:
            stream = "st_" + [k for k in r if isinstance(k, Tile)][0].name
        self._do_waits(q, self._deps(r, w))
        if stream not in self.dstream:
            self.dstream[stream] = [self.nc.alloc_semaphore(f"d_{stream}"), 0]
        st = self.dstream[stream]
        ins = self.eng[q].dma_start(out=out, in_=in_, **kw)
        st[1] += 16
        ins.then_inc(st[0], 16)
        ev = (st[0], st[1], "dma:" + stream)
        self._register(ev, r, w)
        return ins

    def finish(self, e="sp"):
        for name, (sem, tot) in self.dstream.items():
            if tot > 0:
                self.eng[e].wait_ge(sem, tot)


def v3(ap, c=4):
    return ap.rearrange("p (c n) -> p c n", c=c)


def build_nc():
    nc = bass.Bass("TRN2", target_bir_lowering=False)

    def din(name, shape):
        return nc.dram_tensor(name, list(shape), F32, kind="ExternalInput").ap()

    def dout(name, shape):
        return nc.dram_tensor(name, list(shape), F32, kind="ExternalOutput").ap()

    xp_d = din("xp", [NTOK, DM])
    xs_d = din("xs", [NS, DM])
    s_h = din("s_h", [NL, 128, 4, NS])
    s_rgc = din("s_rgc", [NL, 128, 4, 3, NS])
    s_gc = din("s_gc", [NL, 128, 12, 3, NS])
    s_ret = din("s_ret", [NL, NS, 4, 128, 128])
    s_gdn = din("s_gdn", [NL, NS, 4, 128, 128])
    w_in = din("w_in", [NL, DM, DIN])
    w_out = din("w_out", [NL, 1536, DM])
    pf_d = din("pf", [NL, 128, NPF])
    rgw_d = din("rgw", [NL, 128, 2 * 4 * 128])
    rows_d = din("rows", [NL, 1, 2056])
    cst_d = din("cst", [128, NCST])
    rope_d = din("ropet", [18, 128, 128])
    esel_d = din("esel", [1, 256])

    y_p = dout("y_p", [2048, DM])
    y_s = dout("y_s", [NS, DM])
    o_h_p = dout("o_h_p", [NL, 128, 4])
    o_rgc_p = dout("o_rgc_p", [NL, 128, 4, 3])
    o_ret_p = dout("o_ret_p", [NL, 4, 128, 128])
    o_gc_p = dout("o_gc_p", [NL, 128, 12, 3])
    o_gdn_p = dout("o_gdn_p", [NL, 4, 128, 128])
    o_h_s = dout("o_h_s", [NL, 128, 4, NS])
    o_rgc_s = dout("o_rgc_s", [NL, 128, 4, 3, NS])
    o_ret_s = dout("o_ret_s", [NL, NS, 4, 128, 128])
    o_gc_s = dout("o_gc_s", [NL, 128, 12, 3, NS])
    o_gdn_s = dout("o_gdn_s", [NL, NS, 4, 128, 128])
    xscr = nc.dram_tensor("xscr", [NTOK, DM], F32, kind="Internal").ap()
    xsscr = nc.dram_tensor("xsscr", [NS, DM], F32, kind="Internal").ap()

    S = Sched(nc)

    def TL(name, shape, dt=F32):
        return Tile(nc, name, shape, dt)

    Win = TL("Win", [128, 8, DIN], BF16)
    Wout = TL("Wout", [128, 12, DM], BF16)
    cst = TL("cst", [128, NCST])
    identb = TL("identb", [128, 128], BF16)
    esel = TL("esel", [128, 16, 16])
    pft = TL("pft", [128, NPF])
    rgwt = TL("rgwt", [128, 2, 4, 128], BF16)
    rowt = TL("rowt", [128, 2056])
    nc8sp = TL("nc8sp", [128, 4])
    negA = TL("negA", [128, 4])
    ropeb = TL("ropeb", [128, 128])
    X = TL("X", [128, 4, 131])
    GX = TL("GX", [128, 12, 131])
    h0s = TL("h0s", [128, 4, NS])
    Sret = TL("Sret", [128, 512])
    Sretb = TL("Sretb", [128, 512], BF16)
    Sgdn = TL("Sgdn", [128, 512])
    hprev = TL("hprev", [128, 4])
    mix = TL("mix", [128, 12, 128], BF16)
    xt = TL("xt", [128, DM])
    xT = TL("xT", [128, 8, 128], BF16)
    ztr = TL("ztr", [128, 512])
    zte = TL("zte", [128, 512])
    ztg = TL("ztg", [128, 512])
    xcb = TL("xcb", [128, 512], BF16)
    mixr, mixe, mixg = "mixr", "mixe", "mixg"
    Vb = TL("Vb", [128, 512])
    Kbg = TL("Kbg", [128, 512])
    kd = TL("kd", [128, 512])
    qgT = TL("qgT", [128, 512])
    attT = TL("attT", [128, 512])
    Y = TL("Y", [128, 512])
    gabt = TL("gabt", [128, 8])
    gt = TL("gt", [128, 4])
    betat = TL("betat", [128, 4])
    gct = TL("gct", [128, 4])
    egt = TL("egt", [128, 4])
    eglt = TL("eglt", [128, 4])
    eglast = TL("eglast", [128, 4])
    sm1 = TL("sm1", [128, 4])
    sm2 = TL("sm2", [128, 4])
    bst = TL("bst", [128, 4, 6])
    bmv = TL("bmv", [128, 4, 2])
    ebs = TL("ebs", [128, 4, NS])
    vb = TL("vb", [128, 512], BF16)
    k2b = TL("k2b", [128, 512], BF16)
    sm3 = TL("sm3", [128, 4])
    nmr = TL("nmr", [128, 4])
    nmr2 = TL("nmr2", [128, 1])
    ngct = TL("ngct", [128, 4])
    bst2 = TL("bst2", [128, 2, 6])
    bmv2 = TL("bmv2", [128, 2])
    RT = [TL(f"rt{i}", [128, 512]) for i in range(4)]
    ET = [TL(f"et{i}", [128, 512]) for i in range(NET)]
    GT = [TL(f"gt{i}", [128, 512]) for i in range(9)]
    B = [TL(f"bb{i}", [128, 1024], BF16) for i in range(3)]
    ps = [Tile(nc, f"ps{i}", [128, 512], F32, psum=True) for i in range(8)]
    print("sbuf bytes remaining", nc.sbuf_bytes_remaining)
    GDN_W = 2

    def fs(ap):
        n = 1
        for d in ap.shape[1:]:
            n *= d
        return n

    def A(out, in_, func, r, w, bias=0.0, scale=1.0):
        S.op("act", lambda: nc.scalar.activation(out=out, in_=in_, func=func, bias=bias, scale=scale), r, w,
             cost=0.22 + fs(out) / 1000.0)

    def TT(out, a, b, op, r, w, e="dve"):
        if e == "pool":
            S.op("pool", lambda: nc.gpsimd.tensor_tensor(out=out, in0=a, in1=b, op=op), r, w, cost=0.3 + fs(out) / 500.0)
            return
        S.op("dve", lambda: nc.vector.tensor_tensor(out=out, in0=a, in1=b, op=op), r, w, cost=0.2 + fs(out) / 1000.0)

    def TS(out, a, s1, s2, op0, op1, r, w):
        if s2 is None:
            S.op("dve", lambda: nc.vector.tensor_scalar(out=out, in0=a, scalar1=s1, scalar2=None, op0=op0), r, w,
                 cost=0.2 + fs(out) / 1000.0)
        else:
            S.op("dve", lambda: nc.vector.tensor_scalar(out=out, in0=a, scalar1=s1, scalar2=s2, op0=op0, op1=op1), r, w,
                 cost=0.2 + fs(out) / 1000.0)

    def STT(out, a, s, b, op0, op1, r, w):
        S.op("dve", lambda: nc.vector.scalar_tensor_tensor(out=out, in0=a, scalar=s, in1=b, op0=op0, op1=op1), r, w,
             cost=0.2 + fs(out) / 1000.0)

    def CP(out, in_, r, w, e="dve"):
        if e == "dve":
            S.op("dve", lambda: nc.vector.tensor_copy(out=out, in_=in_), r, w, cost=0.2 + fs(out) / 1000.0)
        else:
            S.op("act", lambda: nc.scalar.activation(out=out, in_=in_, func=AF.Copy), r, w, cost=0.22 + fs(out) / 1000.0)

    def MS(out, val, w):
        S.op("dve", lambda: nc.vector.memset(out, val), (), w)

    def MM(out, lhsT, rhs, st, sp, r, w, inc=True):
        S.op("pe", lambda: nc.tensor.matmul(out, lhsT=lhsT, rhs=rhs, start=st, stop=sp), r, w, inc=inc,
             cost=(0.11 + fs(out) / 1200.0) * (2.0 if lhsT.dtype == F32 else 1.0))

    def TR(out, in_, ident, r, w, inc=True):
        S.op("pe", lambda: nc.tensor.transpose(out=out, in_=in_, identity=ident), r, w, inc=inc,
             cost=(0.11 + fs(out) / 1200.0) * (2.0 if in_.dtype == F32 else 1.0))

    def RSQ(out, in_, r, w, bias=EPS, scale=1.0):
        A(out, in_, AF.Ln, r, w, bias=bias, scale=scale)
        A(out, out, AF.Exp, w, w, scale=-0.5)

    ident = cst[:, 0:128]
    maskT = cst[:, 128:256]
    strictL = cst[:, 256:384]
    ones = cst[:, 384:512]

    S.dma("sp", cst[:], cst_d, w=[cst], stream="c")
    S.dma("sp", esel[:].rearrange("p a b -> p (a b)"), esel_d.partition_broadcast(128), w=[esel], stream="c")
    CP(identb[:], ident, [cst], [identb], e="act")

    def bc(ap2, n, L):
        return ap2.unsqueeze(2).to_broadcast([L, 4, n])

    def load_win(l):
        for kc in range(8):
            S.dma("pool", Win[:, kc, :], w_in[l, kc * 128:(kc + 1) * 128, :], w=[Win], stream="w")

    def layer_setup(l):
        if l == 0:
            load_win(0)
        for kc in range(12):
            S.dma("pool", Wout[:, kc, :], w_out[l, kc * 128:(kc + 1) * 128, :], w=[Wout], stream="w")
        S.dma("pool", rgwt[:].rearrange("p a c n -> p (a c n)"), rgw_d[l], w=[rgwt], stream="w")
        S.dma("sp", pft[:], pf_d[l], w=[pft], stream="c")
        S.dma("sp", rowt[:], rows_d[l].partition_broadcast(128), w=[rowt], stream="c")
        A(nc8sp[:], pft[:, 28:32], AF.Exp, [pft], [nc8sp], scale=-1.0)
        A(nc8sp[:], nc8sp[:], AF.Ln, [nc8sp], [nc8sp], bias=1.0)
        TS(nc8sp[:], nc8sp[:], -8.0, None, ALU.mult, None, [nc8sp], [nc8sp])
        A(negA[:], rowt[:, 2048:2052], AF.Exp, [rowt], [negA])
        TS(negA[:], negA[:], -1.0, None, ALU.mult, None, [negA], [negA])
        MS(Sret[:], 0.0, [Sret])
        MS(Sretb[:], 0.0, [Sretb])
        MS(Sgdn[:], 0.0, [Sgdn])
        MS(hprev[:], 0.0, [hprev])
        MS(X[:, :, 0:3], 0.0, [X])
        MS(GX[:, :, 0:3], 0.0, [GX])

    def xsource(l, mode, b):
        if mode == "s":
            return NS, 0, (xs_d if l == 0 else xsscr), ("xsscr", 0)
        L = 16 if b == 0 else 128
        t0 = 0 if b == 0 else 16 + 128 * (b - 1)
        return L, t0, (xp_d if l == 0 else xscr)[t0:t0 + L, :], ("xscr", b)

    head_done = set()

    def head(l, mode, b):
        L, t0, src, dkey = xsource(l, mode, b)
        S.dma("sp", xt[:L, :], src, r=[dkey], w=[xt], stream="x")
        xb = B[0]
        CP(xb[:L, :], xt[:L, :], [xt], [xb], e="act")
        pt = ps[0]
        ptb = pt.ap.bitcast(BF16)
        for kc in range(8):
            TR(ptb[:, kc * 128:kc * 128 + L], xb[:L, kc * 128:(kc + 1) * 128], identb[:L, :L], [xb, identb], [pt],
               inc=(kc == 7))
        CP(xT[:, :, :L], v3(ptb, 8)[:, :, :L], [pt], [xT], e="act")
        head_done.add((l, mode, b))

    def block(l, mode, b):
        smp = mode == "s"
        L, t0, xsrc_d, xdkey = xsource(l, mode, b)
        if (l, mode, b) not in head_done:
            head(l, mode, b)
        rb = 17 if smp else b
        S.dma("sp", ropeb[:L, :], rope_d[rb, 0:L, :], w=[ropeb], stream="x")
        mT = ident if smp else maskT
        ci = 528 if smp else 512
        qdec = cst[:L, ci:ci + 4]
        kdecp = cst[:L, ci + 4:ci + 8]
        if smp:
            k2dec = cst[:L, 536:540]
        elif L == 128:
            k2dec = cst[:L, 520:524]
        else:
            k2dec = cst[:L, 524:528]
        Xs = X.ap.rearrange("p c n -> p (c n)")[:, 0:4 * 4 * NS].rearrange("p (c j s) -> p c j s", c=4, j=4)
        GXs = GX.ap.rearrange("p c n -> p (c n)")[:, 0:12 * 4 * NS].rearrange("p (c j s) -> p c j s", c=12, j=4)

        def fm_group(bank, c0, dst_ap, dst_key, e="act"):
            b3 = v3(bank.ap)
            for c in range(4):
                for kc in range(8):
                    MM(b3[:, c, :L], Win[:, kc, c0 + c * 128:c0 + (c + 1) * 128], xT[:, kc, :L], kc == 0, kc == 7,
                       [Win, xT], [bank], inc=(c == 3 and kc == 7))
            CP(dst_ap, b3[:, :, :L], [bank], [dst_key], e=e)

        def tm_group(bank, c0, n, dst_ap, dst_key, e="dve"):
            for kc in range(8):
                MM(bank[:L, :n], xT[:, kc, :L], Win[:, kc, c0:c0 + n], kc == 0, kc == 7, [Win, xT], [bank],
                   inc=(kc == 7))
            CP(dst_ap, bank[:L, :n], [bank], [dst_key], e=e)

        def conv_tile(src_tap, wcol0, o3, dst_key, src_key, bias_col=None):
            for j in range(4):
                for c in range(4):
                    o = o3[:, c, :L]
                    wj = pft[:, wcol0 + c * 4 + j:wcol0 + c * 4 + j + 1]
                    ck_ = (dst_key, c)
                    if j == 0:
                        if bias_col is not None:
                            TS(o, src_tap(c, 0), wj, pft[:, bias_col + c:bias_col + c + 1], ALU.mult, ALU.add,
                               [src_key, pft], [ck_, dst_key])
                        else:
                            TS(o, src_tap(c, 0), wj, None, ALU.mult, None, [src_key, pft], [ck_, dst_key])
                    else:
                        STT(o, src_tap(c, j), wj, o, ALU.mult, ALU.add, [src_key, pft, ck_], [ck_])
            return [(dst_key, c) for c in range(4)]

        def conv_chunk(src_tap, wcol0, c, o, dst_key, src_key, bias_col=None):
            w0 = pft[:, wcol0 + c * 4:wcol0 + c * 4 + 1]
            if bias_col is not None:
                TS(o, src_tap(c, 0), w0, pft[:, bias_col + c:bias_col + c + 1], ALU.mult, ALU.add,
                   [src_key, pft], [dst_key])
            else:
                TS(o, src_tap(c, 0), w0, None, ALU.mult, None, [src_key, pft], [dst_key])
            for j in range(1, 4):
                STT(o, src_tap(c, j), pft[:, wcol0 + c * 4 + j:wcol0 + c * 4 + j + 1], o, ALU.mult, ALU.add,
                    [src_key, pft, dst_key], [dst_key])

        def gen_rg():
            pa, pb = ps[0], ps[1]
            if smp:
                S.dma("sp", Xs[:, :, 0:3, :], s_rgc[l], w=[X], stream="st")
                S.dma("sp", h0s[:], s_h[l], w=[h0s], stream="st")
                fm_group(pa, COL["rgx"], Xs[:, :, 3, :], X)
                tap = lambda c, j: Xs[:, c, j, :]
            else:
                fm_group(pa, COL["rgx"], X[:, :, 3:3 + L], X)
                tap = lambda c, j: X[:, c, j:j + L]
            yield
            z3 = v3(ztr.ap)
            fm_group(pb, COL["rgz"], z3[:, :, :L], ztr, e="dve")
            yield "pre"
            xc = RT[0]
            xc3 = v3(xc.ap)
            cks = conv_tile(tap, 0, xc3, xc, X, bias_col=16)
            yield
            xcb3 = v3(xcb.ap)
            CP(xcb3[:, :, :L], xc3[:, :, :L], cks, [xcb, xc], e="act")
            rt = RT[1]; it = RT[2]; at = RT[3]
            r3 = v3(rt.ap); i3 = v3(it.ap); a3 = v3(at.ap)
            for which, dst3, dkey, bcol, bank in ((0, r3, rt, 20, pa), (1, i3, it, 24, pb)):
                b3 = v3(bank.ap)
                for c in range(4):
                    MM(b3[:, c, :L], rgwt[:, which, c, :], xcb3[:, c, :L], True, True, [rgwt, xcb], [bank], inc=(c == 3))
                yield
                for c in range(4):
                    A(dst3[:, c, :L], b3[:, c, :L], AF.Sigmoid, [bank, pft], [dkey], bias=pft[:, bcol + c:bcol + c + 1])
                yield
            for c in range(4):
                A(a3[:, c, :L], r3[:, c, :L], AF.Exp, [rt, nc8sp], [at], scale=nc8sp[:, c:c + 1])
            yield
            mt = RT[1]
            m3 = v3(mt.ap)
            A(m3[:, :, :L], a3[:, :, :L], AF.Square, [at], [mt])
            A(m3[:, :, :L], m3[:, :, :L], AF.Ln, [mt], [mt], bias=1.0, scale=-1.0)
            A(m3[:, :, :L], m3[:, :, :L], AF.Exp, [mt], [mt], scale=0.5)
            yield
            TT(i3[:, :, :L], i3[:, :, :L], xc3[:, :, :L], ALU.mult, [it, xc], [it])
            yield
            TT(i3[:, :, :L], i3[:, :, :L], m3[:, :, :L], ALU.mult, [it, mt], [it])
            yield
            ht = RT[0]
            h3 = v3(ht.ap)
            if smp:
                TT(h3[:, :, :L], a3[:, :, :L], h0s[:], ALU.mult, [at, h0s], [ht])
                TT(h3[:, :, :L], h3[:, :, :L], i3[:, :, :L], ALU.add, [ht, it], [ht])
                S.dma("pool", o_h_s[l], h3[:, :, :L], r=[ht], stream="o")
                S.dma("pool", o_rgc_s[l], Xs[:, :, 1:4, :], r=[X], stream="o")
            else:
                for c in range(4):
                    S.op("dve", lambda c=c: nc.vector.tensor_tensor_scan(
                        out=h3[:, c, :L], data0=a3[:, c, :L], data1=i3[:, c, :L], initial=hprev[:, c:c + 1],
                        op0=ALU.mult, op1=ALU.add), [at, it, hprev], [ht])
                    yield
                CP(hprev[:].unsqueeze(2), h3[:, :, L - 1:L], [ht], [hprev])
                if b == NBLK - 1:
                    S.dma("pool", o_h_p[l], hprev[:], r=[hprev], stream="o")
                    S.dma("pool", o_rgc_p[l], X[:, :, L:L + 3], r=[X], stream="o")
                CP(X[:, :, 0:3], X[:, :, L:L + 3], [X], [X])
            yield
            A(z3[:, :, :L], z3[:, :, :L], AF.Silu, [ztr], [ztr])
            TT(mix[:, 0:4, :L], h3[:, :, :L], z3[:, :, :L], ALU.mult, [ht, ztr], [mixr])

        def gen_ret():
            bk = [ps[2], ps[3], ps[4]]
            rq = ET[0]; rk = ET[1]
            tm_group(bk[0], COL["rq"], 512, rq[:L, :], rq)
            yield
            tm_group(bk[1], COL["rk"], 512, rk[:L, :], rk)
            yield
            tm_group(bk[2], COL["rv"], 512, vb[:L, 0:512], vb)
            yield
            z3 = v3(zte.ap)
            fm_group(bk[0], COL["rz"], z3[:, :, :L], zte, e="dve")
            yield "pre"
            cosb = ropeb[:L, 0:64].unsqueeze(1).to_broadcast([L, 4, 64])
            sinb = ropeb[:L, 64:128].unsqueeze(1).to_broadcast([L, 4, 64])

            def rope(src, dst, tmp):
                s3 = v3(src[:L, :]); d3 = v3(dst[:L, :]); t3 = v3(tmp[:L, :])
                t1 = s3[:, :, 0:64]; t2 = s3[:, :, 64:128]
                TT(d3[:, :, 0:64], t1, cosb, ALU.mult, [src, ropeb], [dst], e=RE)
                TT(t3[:, :, 0:64], t2, sinb, ALU.mult, [src, ropeb], [tmp], e=RE)
                yield
                TT(d3[:, :, 0:64], d3[:, :, 0:64], t3[:, :, 0:64], ALU.subtract, [dst, tmp], [dst], e=RE)
                TT(d3[:, :, 64:128], t1, sinb, ALU.mult, [src, ropeb], [dst], e=RE)
                yield
                TT(t3[:, :, 64:128], t2, cosb, ALU.mult, [src, ropeb], [tmp], e=RE)
                TT(d3[:, :, 64:128], d3[:, :, 64:128], t3[:, :, 64:128], ALU.add, [dst, tmp], [dst], e=RE)
                yield

            rqr = ET[2]; rkr = ET[4]
            yield from rope(rq, rqr, ET[3])
            yield from rope(rk, rkr, ET[3])
            qkb = B[0]
            TT(v3(qkb[:L, 0:512]), v3(rqr[:L, :]), bc(qdec, 128, L), ALU.mult, [rqr, cst], [qkb], e=RE)
            TT(v3(qkb[:L, 512:1024]), v3(rkr[:L, :]), bc(kdecp, 128, L), ALU.mult, [rkr, cst], [qkb], e=RE)
            yield
            if smp:
                k2 = ET[5]
                TT(v3(k2[:L, :]), v3(rkr[:L, :]), bc(k2dec, 128, L), ALU.mult, [rkr, cst], [k2], e=RE)
            else:
                k2 = k2b
                TT(v3(k2[:L, 0:512]), v3(rkr[:L, :]), bc(k2dec, 128, L), ALU.mult, [rkr, cst], [k2], e=RE)
            yield
            pt = bk[1]
            ptb = pt.ap.bitcast(BF16)
            for j in range(8):
                TR(ptb[:, j * 128:j * 128 + L], qkb[:L, j * 128:(j + 1) * 128], identb[:L, :L], [qkb, identb], [pt],
                   inc=(j == 7))
            qkT = B[1]
            qkT3 = v3(qkT.ap, 8)
            CP(qkT3[:, :, :L], v3(ptb, 8)[:, :, :L], [pt], [qkT], e="act")
            yield
            bank = bk[2]
            b3 = v3(bank.ap)
            for h in range(4):
                MM(b3[:L, h, :L], qkT3[:, 4 + h, :L], qkT3[:, h, :L], True, True, [qkT], [bank], inc=(h == 3))
            scb = B[2]
            sc3 = v3(scb[:, 0:512])
            TT(sc3[:L, :, :L], b3[:L, :, :L], mT[:L, :L].unsqueeze(1).to_broadcast([L, 4, L]), ALU.mult, [bank, cst], [scb])
            yield
            ob = bk[0]
            ob3 = v3(ob.ap)
            for h in range(4):
                MM(ob3[:L, h, :], sc3[:L, h, :L], vb[:L, h * 128:(h + 1) * 128], True, smp, [scb, vb], [ob],
                   inc=(smp and h == 3))
                if not smp:
                    MM(ob3[:L, h, :], qkT3[:, h, :L], v3(Sretb.ap)[:, h, :], False, True, [qkT, Sretb], [ob], inc=(h == 3))
            yield
            if not smp:
                sb = bk[1]
                sb3 = v3(sb.ap)
                for h in range(4):
                    MM(sb3[:, h, :], k2[:L, h * 128:(h + 1) * 128], vb[:L, h * 128:(h + 1) * 128], True, True, [k2, vb], [sb],
                       inc=(h == 3))
                yield
                for h in range(4):
                    STT(v3(Sret.ap)[:, h, :], v3(Sret.ap)[:, h, :], float(GAM[h] ** L), sb3[:, h, :], ALU.mult, ALU.add,
                        [Sret, sb], [Sret])
                yield
                CP(Sretb[:], Sret[:], [Sret], [Sretb], e="act")
                if b == NBLK - 1:
                    S.dma("pool", o_ret_p[l].rearrange("h d v -> d h v"), v3(Sret.ap), r=[Sret], stream="o")
                osrc, okey = ob3, ob
            else:
                oacc = ET[1]
                CP(oacc[:L, :], ob[:L, :], [ob], [oacc])
                xTf = xT.ap.rearrange("p a b -> p (a b)").bitcast(F32)
                for s in range(NS):
                    St = (ET[2], Sret)[s % 2]
                    S.dma("sp", v3(St.ap), s_ret[l, s].rearrange("h d v -> d h v"), w=[St], stream="st")
                    qm = ET[3]
                    TT(v3(qm[:, 0:64], 4), qkT3[:, 0:4, :NS], esel[:, s, :].unsqueeze(1).to_broadcast([128, 4, NS]),
                       ALU.mult, [qkT, esel], [qm])
                    tb_ = bk[1]
                    tb3 = v3(tb_.ap)
                    for h in range(4):
                        MM(tb3[:NS, h, :], v3(qm[:, 0:64], 4)[:, h, :], v3(St.ap)[:, h, :], True, True, [qm, St], [tb_],
                           inc=(h == 3))
                    TT(oacc[:L, :], oacc[:L, :], tb_[:NS, :], ALU.add, [oacc, tb_], [oacc])
                    yield
                    vm = ET[4]
                    TT(vm[:NS, :], vb[:NS, 0:512], ident[:NS, s:s + 1].to_broadcast([NS, 512]), ALU.mult, [vb, cst], [vm])
                    sb = bk[2]
                    sb3 = v3(sb.ap)
                    for h in range(4):
                        MM(sb3[:, h, :], k2[:NS, h * 128:(h + 1) * 128], vm[:NS, h * 128:(h + 1) * 128], True, True,
                           [k2, vm], [sb], inc=(h == 3))
                    So, So3 = ((ET[0], v3(ET[0].ap)), (xT, v3(xTf)))[s % 2]
                    for h in range(4):
                        STT(So3[:, h, :], v3(St.ap)[:, h, :], float(GAM[h]), sb3[:, h, :], ALU.mult, ALU.add,
                            [St, sb], [So])
                    S.dma("pool", o_ret_s[l, s].rearrange("h d v -> d h v"), So3, r=[So], stream="o")
                    yield
                osrc, okey = v3(oacc.ap), oacc
            for h in range(4):
                S.op("dve", lambda h=h: nc.vector.bn_stats(out=bst[:L, h, :], in_=osrc[:L, h, :]), [okey], [bst])
            yield
            for h in range(4):
                S.op("dve", lambda h=h: nc.vector.bn_aggr(out=bmv[:L, h, :], in_=bst[:L, h, :]), [bst], [bmv])
            RSQ(sm1[:L, :], bmv[:L, :, 1], [bmv], [sm1])
            yield
            onb = B[2]
            STT(nmr[:L, :], bmv[:L, :, 0], -1.0, sm1[:L, :], ALU.mult, ALU.mult, [bmv, sm1], [nmr])
            for h in range(4):
                A(onb[:L, 512 + h * 128:512 + (h + 1) * 128], osrc[:L, h, :], AF.Identity, [okey, nmr, sm1], [onb],
                  bias=nmr[:L, h:h + 1], scale=sm1[:L, h:h + 1])
            yield
            pt = bk[1]
            ptb = pt.ap.bitcast(BF16)
            for h in range(4):
                TR(ptb[:, h * 128:h * 128 + L], onb[:L, 512 + h * 128:512 + (h + 1) * 128], identb[:L, :L], [onb, identb],
                   [pt], inc=(h == 3))
            yt = ET[0]
            y3 = v3(yt.ap)
            for h in range(4):
                A(y3[:, h, :L], ptb[:, h * 128:h * 128 + L], AF.Identity, [pt, pft], [yt], bias=pft[:, 36 + h:37 + h],
                  scale=pft[:, 32 + h:33 + h])
            yield
            A(z3[:, :, :L], z3[:, :, :L], AF.Silu, [zte], [zte])
            TT(mix[:, 4:8, :L], y3[:, :, :L], z3[:, :, :L], ALU.mult, [yt, zte], [mixe])

        def gen_gdn():
            bk = [ps[5], ps[6], ps[7]]
            nb = [0]

            def PB():
                nb[0] = (nb[0] + 1) % 3
                return bk[nb[0]]

            if smp:
                S.dma("sp", GXs[:, :, 0:3, :], s_gc[l], w=[GX], stream="st")
                for g in range(3):
                    fm_group(PB(), COL["gq"] + 512 * g, GXs[:, 4 * g:4 * g + 4, 3, :], GX)
                    yield
                gtap = lambda c, j: GXs[:, c, j, :]
            else:
                for g in range(3):
                    fm_group(PB(), COL["gq"] + 512 * g, GX[:, 4 * g:4 * g + 4, 3:3 + L], GX)
                    yield
                gtap = lambda c, j: GX[:, c, j:j + L]
            tm_group(PB(), COL["gab"], 8, gabt[:L, :], gabt)
            z3 = v3(ztg.ap)
            fm_group(PB(), COL["gz"], z3[:, :, :L], ztg, e="dve")
            yield "pre"
            TT(gt[:L, :], gabt[:L, 0:4], rowt[:L, 2052:2056], ALU.add, [gabt, rowt], [gt])
            A(gt[:L, :], gt[:L, :], AF.Exp, [gt], [gt])
            A(gt[:L, :], gt[:L, :], AF.Ln, [gt], [gt], bias=1.0)
            TT(gt[:L, :], gt[:L, :], negA[:L, :], ALU.mult, [gt, negA], [gt])
            A(betat[:L, :], gabt[:L, 4:8], AF.Sigmoid, [gabt], [betat])
            yield
            bank = PB()
            MM(bank[:L, 0:4], mT[:L, :L], gt[:L, :], True, True, [cst, gt], [bank])
            CP(gct[:L, :], bank[:L, 0:4], [bank], [gct])
            A(egt[:L, :], gct[:L, :], AF.Exp, [gct], [egt])
            yield
            Rt = GT[5]
            R3 = v3(Rt.ap)
            for h in range(4):
                A(R3[:L, h, :L], mT[:L, :L], AF.Copy, [cst, gt], [Rt], scale=gt[:L, h:h + 1])
            TS(ngct[:L, :], gct[:L, :], -1.0, None, ALU.mult, None, [gct], [ngct])
            yield
            gcB = PB()
            g3 = v3(gcB.ap)
            for h in range(4):
                MM(g3[:, h, :L], ones[:L, :], R3[:L, h, :L], True, True, [cst, Rt], [gcB], inc=(h == 3))
            EB = GT[6]
            EB3 = v3(EB.ap)
            A(EB3[:, :, :L], g3[:, :, :L], AF.Exp, [gcB], [EB])
            yield
            dT = GT[7]
            dT3 = v3(dT.ap)
            for h in range(4):
                A(dT3[:L, h, :L], g3[:L, h, :L], AF.Relu, [gcB, gct], [dT], bias=gct[:L, h:h + 1], scale=-1.0)
            yield
            if not smp:
                dl = GT[8]
                dl3 = v3(dl.ap)
                for h in range(4):
                    A(dl3[:L, h, :L], g3[:L, h, :L], AF.Relu, [gcB, ngct], [dl], bias=ngct[:L, h:h + 1], scale=1.0)
                yield
                CP(eglast[:].unsqueeze(2), EB3[:, :, L - 1:L], [EB], [eglast])
                for h in range(4):
                    A(eglt[:L, h:h + 1], g3[:L, h, L - 1:L], AF.Exp, [gcB, ngct], [eglt], bias=ngct[:L, h:h + 1])
                A(dl3[:L, :, :L], dl3[:L, :, :L], AF.Exp, [dl], [dl], scale=-1.0)
                TT(dl3[:L, :, :L], dl3[:L, :, :L], strictL[:L, :L].unsqueeze(1).to_broadcast([L, 4, L]), ALU.mult,
                   [dl, cst], [dl])
                for h in range(4):
                    A(dl3[:L, h, :L], dl3[:L, h, :L], AF.Copy, [dl, betat], [dl], scale=betat[:L, h:h + 1])
                yield
            else:
                CP(ebs[:], EB3[:, :, :NS], [EB], [ebs])
            A(dT3[:L, :, :L], dT3[:L, :, :L], AF.Exp, [dT], [dT], scale=-1.0)
            TT(dT3[:L, :, :L], dT3[:L, :, :L], mT[:L, :L].unsqueeze(1).to_broadcast([L, 4, L]), ALU.mult, [dT, cst], [dT])
            yield
            cq = GT[0]; ck = GT[1]; cv = GT[2]
            cqk = [cq, ck, cv]
            for g in range(3):
                cg3 = v3(cqk[g].ap)
                cks = conv_tile(lambda c_, j, g=g: gtap(4 * g + c_, j), 40 + 16 * g, cg3, cqk[g], GX)
                yield
                A(cg3[:, :, :L], cg3[:, :, :L], AF.Silu, cks, [cqk[g]])
            if smp:
                S.dma("pool", o_gc_s[l], GXs[:, :, 1:4, :], r=[GX], stream="o")
            else:
                if b == NBLK - 1:
                    S.dma("pool", o_gc_p[l], GX[:, :, L:L + 3], r=[GX], stream="o")
                CP(GX[:, :, 0:3], GX[:, :, L:L + 3], [GX], [GX])
            yield
            for g in range(2):
                cg3 = v3(cqk[g].ap)
                sq = GT[3]
                sq3 = v3(sq.ap)
                A(sq3[:, :, :L], cg3[:, :, :L], AF.Square, [cqk[g]], [sq])
                bank = PB()
                b3 = v3(bank.ap)
                for h in range(4):
                    MM(b3[:, h, :L], ones, sq3[:, h, :L], True, True, [cst, sq], [bank], inc=(h == 3))
                yield
                rn = GT[4]
                rn3 = v3(rn.ap)
                RSQ(rn3[:, :, :L], b3[:, :, :L], [bank], [rn])
                if g == 0:
                    STT(cg3[:, :, :L], cg3[:, :, :L], float(128 ** -0.5), rn3[:, :, :L], ALU.mult, ALU.mult, [cq, rn], [cq])
                else:
                    TT(cg3[:, :, :L], cg3[:, :, :L], rn3[:, :, :L], ALU.mult, [ck, rn], [ck])
                yield
            q3 = v3(cq.ap); k3 = v3(ck.ap); cv3 = v3(cv.ap)
            TT(v3(qgT.ap)[:, :, :L], q3[:, :, :L], EB3[:, :, :L], ALU.mult, [cq, EB], [qgT])
            bank = PB()
            b3 = v3(bank.ap)
            for h in range(4):
                MM(b3[:L, h, :L], k3[:, h, :L], q3[:, h, :L], True, True, [ck, cq], [bank], inc=(h == 3))
            at3 = v3(attT.ap)
            TT(at3[:L, :, :L], b3[:L, :, :L], dT3[:L, :, :L], ALU.mult, [bank, dT], [attT])
            yield
            kTM = GT[3]; vTM = GT[4]
            for srcT, s3_, dstT in ((ck, k3, kTM), (cv, cv3, vTM)):
                bank = PB()
                for h in range(4):
                    TR(bank[:L, h * 128:(h + 1) * 128], s3_[:, h, :L], ident, [srcT, cst], [bank], inc=(h == 3))
                CP(dstT[:L, :], bank[:L, :], [bank], [dstT], e="act")
                yield
            if not smp:
                TT(v3(kd[:L, :]), v3(kTM[:L, :]), bc(eglt[:L, :], 128, L), ALU.mult, [kTM, eglt], [kd])
            else:
                CP(kd[:L, :], kTM[:L, :], [kTM], [kd])
            TT(v3(Vb[:L, :]), v3(vTM[:L, :]), bc(betat[:L, :], 128, L), ALU.mult, [vTM, betat], [Vb])
            yield
            TT(sm2[:L, :], betat[:L, :], egt[:L, :], ALU.mult, [betat, egt], [sm2])
            TT(v3(Kbg[:L, :]), v3(kTM[:L, :]), bc(sm2[:L, :], 128, L), ALU.mult, [kTM, sm2], [Kbg])
            yield
            Y3 = v3(Y.ap)
            if smp:
                CP(Y3[:L, :, :L], ident[:L, :L].unsqueeze(1).to_broadcast([L, 4, L]), [cst], [Y])
            else:
                bank = PB()
                b3 = v3(bank.ap)
                for h in range(4):
                    MM(b3[:L, h, :L], k3[:, h, :L], k3[:, h, :L], True, True, [ck], [bank], inc=(h == 3))
                P = GT[2]
                P3 = v3(P.ap)
                TT(P3[:L, :, :L], b3[:L, :, :L], dl3[:L, :, :L], ALU.mult, [bank, dl], [P])
                yield
                bank = PB()
                b3 = v3(bank.ap)
                for h in range(4):
                    TR(b3[:L, h, :L], P3[:L, h, :L], ident[:L, :L], [P, cst], [bank], inc=(h == 3))
                Q = GT[5]
                Q3 = v3(Q.ap)
                CP(Q3[:L, :, :L], b3[:L, :, :L], [bank], [Q], e="act")
                STT(Y3[:L, :, :L], Q3[:L, :, :L], -1.0, ident[:L, :L].unsqueeze(1).to_broadcast([L, 4, L]), ALU.mult, ALU.add,
                    [Q, cst], [Y])
                yield
                nlev = 6 if L == 128 else 3
                for lev in range(nlev):
                    bq = PB(); bp = PB()
                    bq3 = v3(bq.ap); bp3 = v3(bp.ap)
                    for h in range(4):
                        MM(bq3[:L, h, :L], P3[:L, h, :L], Q3[:L, h, :L], True, True, [P, Q], [bq], inc=(h == 3))
                    for h in range(4):
                        MM(bp3[:L, h, :L], Q3[:L, h, :L], P3[:L, h, :L], True, True, [P, Q], [bp], inc=(h == 3))
                    yield
                    Pn, Qn = (GT[3], GT[4]) if lev % 2 == 0 else (GT[2], GT[5])
                    CP(v3(Qn.ap)[:L, :, :L], bq3[:L, :, :L], [bq], [Qn], e="act")
                    CP(v3(Pn.ap)[:L, :, :L], bp3[:L, :, :L], [bp], [Pn], e="dve")
                    yield
                    P, Q = Pn, Qn
                    P3, Q3 = v3(P.ap), v3(Q.ap)
                    by = PB()
                    by3 = v3(by.ap)
                    for h in range(4):
                        MM(by3[:L, h, :L], P3[:L, h, :L], Y3[:L, h, :L], True, True, [P, Y], [by], inc=(h == 3))
                    TT(Y3[:L, :, :L], Y3[:L, :, :L], by3[:L, :, :L], ALU.add, [Y, by], [Y])
                    yield
            bank = PB()
            b3 = v3(bank.ap)
            for h in range(4):
                MM(b3[:, h, :L], Kbg[:L, h * 128:(h + 1) * 128], Y3[:L, h, :L], True, True, [Kbg, Y], [bank], inc=(h == 3))
            nWT = GT[6]
            nW3 = v3(nWT.ap)
            A(nW3[:, :, :L], b3[:, :, :L], AF.Copy, [bank], [nWT], scale=-1.0)
            yield
            Sg3 = v3(Sgdn.ap)
            vnb = PB()
            vn3 = v3(vnb.ap)
            for h in range(4):
                MM(vn3[:L, h, :], Y3[:L, h, :L], Vb[:L, h * 128:(h + 1) * 128], True, smp, [Y, Vb], [vnb],
                   inc=(smp and h == 3))
                if not smp:
                    MM(vn3[:L, h, :], nW3[:, h, :L], Sg3[:, h, :], False, True, [nWT, Sgdn], [vnb], inc=(h == 3))
            vnew = GT[7]
            if not smp:
                CP(vnew[:L, :], vnb[:L, :], [vnb], [vnew], e="act")
                yield
            else:
                wacc = GT[0]; qacc = GT[1]
                CP(wacc[:L, :], vnb[:L, :], [vnb], [wacc], e="act")
                MS(qacc[:L, :], 0.0, [qacc])
                for s in range(NS):
                    St = (GT[2], Sgdn)[s % 2]
                    S.dma("sp", v3(St.ap), s_gdn[l, s].rearrange("h d v -> d h v"), w=[St], stream="st")
                    for srcT, s3_, acc in ((qgT, v3(qgT.ap), qacc), (nWT, nW3, wacc)):
                        qm = GT[3]
                        TT(v3(qm[:, 0:64], 4), s3_[:, :, :NS], esel[:, s, :].unsqueeze(1).to_broadcast([128, 4, NS]),
                           ALU.mult, [srcT, esel], [qm])
                        tb_ = PB()
                        tb3 = v3(tb_.ap)
                        for h in range(4):
                            MM(tb3[:NS, h, :], v3(qm[:, 0:64], 4)[:, h, :], v3(St.ap)[:, h, :], True, True, [qm, St], [tb_],
                               inc=(h == 3))
                        TT(acc[:L, :], acc[:L, :], tb_[:NS, :], ALU.add, [acc, tb_], [acc])
                        yield
                CP(vnew[:L, :], wacc[:L, :], [wacc], [vnew])
            ob = PB()
            ob3 = v3(ob.ap)
            for h in range(4):
                MM(ob3[:L, h, :], at3[:L, h, :L], vnew[:L, h * 128:(h + 1) * 128], True, smp, [attT, vnew], [ob],
                   inc=(smp and h == 3))
                if not smp:
                    MM(ob3[:L, h, :], v3(qgT.ap)[:, h, :L], Sg3[:, h, :], False, True, [qgT, Sgdn], [ob], inc=(h == 3))
            yield
            if smp:
                TT(qacc[:L, :], qacc[:L, :], ob[:L, :], ALU.add, [qacc, ob], [qacc])
                osrc, okey = v3(qacc.ap), qacc
                for s in range(NS):
                    St = (GT[2], Sgdn)[s % 2]
                    S.dma("sp", v3(St.ap), s_gdn[l, s].rearrange("h d v -> d h v"), w=[St], stream="st")
                    vm = GT[3]
                    TS(vm[:NS, :], vnew[:NS, :], ident[:NS, s:s + 1], None, ALU.mult, None, [vnew, cst], [vm])
                    sb = PB()
                    sb3 = v3(sb.ap)
                    for h in range(4):
                        MM(sb3[:, h, :], kd[:NS, h * 128:(h + 1) * 128], vm[:NS, h * 128:(h + 1) * 128], True, True,
                           [kd, vm], [sb], inc=(h == 3))
                    So = (GT[4], GT[8])[s % 2]
                    for h in range(4):
                        STT(v3(So.ap)[:, h, :], v3(St.ap)[:, h, :], ebs[:, h, s:s + 1], sb3[:, h, :], ALU.mult, ALU.add,
                            [St, sb, ebs], [So])
                    S.dma("pool", o_gdn_s[l, s].rearrange("h d v -> d h v"), v3(So.ap), r=[So], stream="o")
                    yield
            else:
                osrc, okey = ob3, ob
                sb = PB()
                sb3 = v3(sb.ap)
                for h in range(4):
                    MM(sb3[:, h, :], kd[:L, h * 128:(h + 1) * 128], vnew[:L, h * 128:(h + 1) * 128], True, True, [kd, vnew], [sb],
                       inc=(h == 3))
                yield
                for h in range(4):
                    STT(Sg3[:, h, :], Sg3[:, h, :], eglast[:, h:h + 1], sb3[:, h, :], ALU.mult, ALU.add,
                        [Sgdn, sb, eglast], [Sgdn])
                if b == NBLK - 1:
                    S.dma("pool", o_gdn_p[l].rearrange("h d v -> d h v"), Sg3, r=[Sgdn], stream="o")
                yield
            osq = GT[8]
            A(v3(osq.ap)[:L, :, :], osrc[:L, :, :], AF.Square, [okey], [osq])
            S.op("dve", lambda: nc.vector.reduce_sum(out=sm3[:L, :], in_=v3(osq.ap)[:L, :, :], axis=AX.X), [osq], [sm3])
            RSQ(sm3[:L, :], sm3[:L, :], [sm3], [sm3], bias=EPS, scale=1.0 / 128.0)
            yield
            on = GT[5]
            for h in range(4):
                A(on[:L, h * 128:(h + 1) * 128], osrc[:L, h, :], AF.Copy, [okey, sm3], [on], scale=sm3[:L, h:h + 1])
            yield
            bank = PB()
            b3 = v3(bank.ap)
            for h in range(4):
                TR(b3[:, h, :L], on[:L, h * 128:(h + 1) * 128], ident[:L, :L], [on, cst], [bank], inc=(h == 3))
            A(z3[:, :, :L], z3[:, :, :L], AF.Silu, [ztg], [ztg])
            STT(mix[:, 8:12, :L], b3[:, :, :L], pft[:, 88:89], z3[:, :, :L], ALU.mult, ALU.mult, [bank, pft, ztg], [mixg])

        if MERGE == "sim":
            branches = []
            live = []
            for gen in (gen_gdn, gen_ret, gen_rg):
                g = gen()
                for mark in g:
                    if mark == "pre":
                        break
                live.append(g)
            if smp and l < NL - 1:
                load_win(l + 1)
            if not smp:
                if b + 1 < NBLK:
                    head(l, "p", b + 1)
                elif not SKIP_SAMPLE:
                    head(l, "s", 0)
            for g in live:
                S.rec = []
                for _ in g:
                    pass
                ops, S.rec = S.rec, None
                units, cur = [], []
                for o in ops:
                    cur.append(o)
                    if o[5]:
                        units.append(cur)
                        cur = []
                assert not cur
                branches.append(units)
            clock = {}
            wr = {}
            rd = {}
            ptr = [0] * len(branches)
            HOP = 0.3
            while True:
                best = None
                for bi, units in enumerate(branches):
                    if ptr[bi] >= len(units):
                        continue
                    u = units[ptr[bi]]
                    e = u[0][1]
                    t = clock.get(e, 0.0)
                    for o in u:
                        for k in o[3]:
                            if k in wr:
                                t = max(t, wr[k][0] + (HOP if wr[k][1] != e else 0.0))
                        for k in o[4]:
                            if k in wr:
                                t = max(t, wr[k][0] + (HOP if wr[k][1] != e else 0.0))
                            if k in rd:
                                t = max(t, rd[k][0] + (HOP if rd[k][1] != e else 0.0))
                    if best is None or t < best[0] - 1e-9:
                        best = (t, bi)
                if best is None:
                    break
                t, bi = best
                u = branches[bi][ptr[bi]]
                ptr[bi] += 1
                e = u[0][1]
                for o in u:
                    kind, eng, fn, r_, w_, inc, cost = o
                    if kind == "dma":
                        out_, in_, kw = fn
                        S.dma(eng, out_, in_, r=r_, w=w_, **kw)
                        done = t + 2.5
                        t += cost
                        who = "dma"
                    else:
                        S.op(eng, fn, r_, w_, inc=inc)
                        t += cost
                        done = t
                        who = eng
                    for k in r_:
                        if k not in rd or rd[k][0] < done:
                            rd[k] = (done, who)
                    for k in w_:
                        wr[k] = (done, who)
                        rd.pop(k, None)
                clock[e] = t
            gens = []
        else:
            gens = [(gen_rg(), 1), (gen_ret(), 1), (gen_gdn(), GDN_W)]
        if MERGE == "seq":
            for g, _ in gens:
                for _ in g:
                    pass
            gens = []
        while gens:
            for item in list(gens):
                g, wgt = item
                for _ in range(wgt):
                    try:
                        next(g)
                    except StopIteration:
                        gens.remove(item)
                        break

        z = [RT[0], RT[1]]
        for n in range(2):
            bank = ps[n]
            for kc in range(12):
                MM(bank[:L, :], mix[:, kc, :L], Wout[:, kc, n * 512:(n + 1) * 512], kc == 0, kc == 11,
                   [mixr, mixe, mixg, Wout], [bank], inc=(kc == 11))
            S.dma("sp", z[n][:L, :], xsrc_d[:, n * 512:(n + 1) * 512], r=[xdkey], w=[z[n]], stream="x")
            STT(z[n][:L, :], z[n][:L, :], float(ALPHA), bank[:L, :], ALU.mult, ALU.add, [z[n], bank], [z[n]])
            S.op("dve", lambda n=n: nc.vector.bn_stats(out=bst2[:L, n, :], in_=z[n][:L, :]), [z[n]], [bst2])
        S.op("dve", lambda: nc.vector.bn_aggr(out=bmv2[:L, :], in_=bst2[:L, 0:2, :]), [bst2], [bmv2])
        RSQ(sm2[:L, 0:1], bmv2[:L, 1:2], [bmv2], [sm2])
        for n in range(2):
            sl = slice(n * 512, (n + 1) * 512)
            if n == 0:
                STT(nmr2[:L, :], bmv2[:L, 0:1], -1.0, sm2[:L, 0:1], ALU.mult, ALU.mult, [bmv2, sm2], [nmr2])
            A(z[n][:L, :], z[n][:L, :], AF.Identity, [z[n], nmr2, sm2], [z[n]], bias=nmr2[:L, 0:1], scale=sm2[:L, 0:1])
            TT(z[n][:L, :], z[n][:L, :], rowt[:L, sl], ALU.mult, [z[n], rowt], [z[n]])
            TT(z[n][:L, :], z[n][:L, :], rowt[:L, 1024 + n * 512:1024 + (n + 1) * 512], ALU.add, [z[n], rowt], [z[n]])
            if smp:
                if l == NL - 1:
                    S.dma("pool", y_s[:, sl], z[n][:NS, :], r=[z[n]], stream="o")
                else:
                    S.dma("pool", xsscr[:, sl], z[n][:NS, :], r=[z[n]], w=[("xsscr", 0)], sname=f"xs{n}_{l % 2}")
            else:
                if l == NL - 1:
                    if b > 0:
                        S.dma("pool", y_p[t0 - 16:t0 - 16 + L, sl], z[n][:L, :], r=[z[n]], stream="o")
                else:
                    S.dma("pool", xscr[t0:t0 + L, sl], z[n][:L, :], r=[z[n]], w=[("xscr", b)], sname=f"xo{n}_{l % 2}")


    for l in range(NL):
        layer_setup(l)
        for b in range(NBLK):
            block(l, "p", b)
        if not SKIP_SAMPLE:
            block(l, "s", 0)
    S.finish("sp")
    print("ops", S.nops, "waits", S.nwaits, "sems", S.nsem + len(S.dstream))
    return nc


def _consts():
    cst = np.zeros((128, NCST), np.float32)
    i = np.arange(128)
    cst[:, 0:128] = np.eye(128)
    cst[:, 128:256] = (i[None, :] >= i[:, None])
    cst[:, 256:384] = (i[None, :] < i[:, None])
    cst[:, 384:512] = 1.0
    sc = 128.0 ** -0.5
    for h in range(4):
        g = np.float64(GAM[h])
        cst[:, 512 + h] = g ** (i + 1.0)
        cst[:, 516 + h] = g ** (-(i + 1.0)) * sc
        cst[:, 520 + h] = g ** (127.0 - i) * sc
        cst[:16, 524 + h] = g ** (15.0 - i[:16]) * sc
        cst[:, 528 + h] = g
        cst[:, 532 + h] = sc / g
        cst[:, 536 + h] = sc
    half = 64
    inv = (np.float32(10000.0) ** (-np.arange(half, dtype=np.float32) / np.float32(half))).astype(np.float32)
    rope = np.zeros((18, 128, 128), np.float32)
    for b in range(18):
        if b == 0:
            pos = np.arange(16, dtype=np.float32)
        elif b < 17:
            pos = 16 + 128 * (b - 1) + np.arange(128, dtype=np.float32)
        else:
            pos = np.full(16, 16384.0, np.float32)
        ang = (pos[:, None].astype(np.float32) * inv[None, :]).astype(np.float32)
        rope[b, :len(pos), 0:64] = np.cos(ang.astype(np.float64))
        rope[b, :len(pos), 64:128] = np.sin(ang.astype(np.float64))
    esel = np.eye(16, dtype=np.float32).reshape(1, 256)
    return cst, rope, esel


_NC_CACHE = {}


def kernel(x_prompt, x_sample, state_rglru_h, state_rglru_conv, state_ret, state_gdn_conv, state_gdn,
           meta_tokens, w_in, rg_conv_w, rg_conv_b, rg_w_a, rg_b_a, rg_w_x, rg_b_x, rg_lambda,
           ret_gn_w, ret_gn_b, gdn_conv_w, gdn_a_log, gdn_dt_bias, gdn_norm_w, w_out, ln_w, ln_b):
    f = lambda a: np.ascontiguousarray(np.asarray(a, dtype=np.float32))
    x_prompt, x_sample, meta_tokens = f(x_prompt), f(x_sample), f(meta_tokens)
    w_in, w_out = f(w_in), f(w_out)
    pf = np.zeros((NL, 128, NPF), np.float32)

    def fm(v, nch):
        return f(v).reshape(NL, nch, 128).transpose(0, 2, 1)

    pf[:, :, 0:16] = f(rg_conv_w).reshape(NL, 4, 4, 128).transpose(0, 3, 2, 1).reshape(NL, 128, 16)
    pf[:, :, 16:20] = fm(rg_conv_b, 4)
    pf[:, :, 20:24] = fm(rg_b_a, 4)
    pf[:, :, 24:28] = fm(rg_b_x, 4)
    pf[:, :, 28:32] = fm(rg_lambda, 4)
    pf[:, :, 32:36] = fm(ret_gn_w, 4)
    pf[:, :, 36:40] = fm(ret_gn_b, 4)
    pf[:, :, 40:88] = f(gdn_conv_w).reshape(NL, 4, 12, 128).transpose(0, 3, 2, 1).reshape(NL, 128, 48)
    pf[:, :, 88] = f(gdn_norm_w)
    rgw = np.zeros((NL, 128, 2, 4, 128), np.float32)
    for which, wsrc in ((0, f(rg_w_a)), (1, f(rg_w_x))):
        for n in range(8):
            c, o = n // 2, (n % 2) * 64
            rgw[:, o:o + 64, which, c, o:o + 64] = wsrc[:, n]
    rgw = rgw.reshape(NL, 128, 1024)
    rows = np.concatenate([f(ln_w), f(ln_b), f(gdn_a_log), f(gdn_dt_bias)], axis=1).reshape(NL, 1, 2056)
    cst, rope, esel = _consts()
    if "nc" not in _NC_CACHE:
        _NC_CACHE["nc"] = build_nc()
    nc = _NC_CACHE["nc"]
    in_maps = []
    for c in range(8):
        sl = slice(NS * c, NS * (c + 1))
        m = {
            "xp": np.ascontiguousarray(np.concatenate([meta_tokens, x_prompt[c]], axis=0)),
            "xs": np.ascontiguousarray(x_sample[sl, 0, :]),
            "s_h": np.ascontiguousarray(f(state_rglru_h)[:, sl].reshape(NL, NS, 4, 128).transpose(0, 3, 2, 1)),
            "s_rgc": np.ascontiguousarray(f(state_rglru_conv)[:, sl].reshape(NL, NS, 3, 4, 128).transpose(0, 4, 3, 2, 1)),
            "s_gc": np.ascontiguousarray(f(state_gdn_conv)[:, sl].reshape(NL, NS, 3, 12, 128).transpose(0, 4, 3, 2, 1)),
            "s_ret": np.ascontiguousarray(f(state_ret)[:, sl]),
            "s_gdn": np.ascontiguousarray(f(state_gdn)[:, sl]),
            "w_in": w_in, "w_out": w_out, "pf": pf, "rgw": rgw, "rows": rows,
            "cst": cst, "ropet": rope, "esel": esel,
        }
        in_maps.append(m)
    res = run_bass_kernel_spmd(nc, in_maps, core_ids=list(range(8)))
    R = res.results
    g = lambda k: [np.asarray(R[c][k], dtype=np.float32) for c in range(8)]
    y_prompt = np.stack(g("y_p"), 0)
    y_sample = np.concatenate(g("y_s"), 0)[:, None, :]
    hp = np.stack([a.transpose(0, 2, 1).reshape(NL, 512) for a in g("o_h_p")], 1)
    rgcp = np.stack([a.transpose(0, 3, 2, 1).reshape(NL, 3, 512) for a in g("o_rgc_p")], 1)
    retp = np.stack(g("o_ret_p"), 1)
    gcp = np.stack([a.transpose(0, 3, 2, 1).reshape(NL, 3, 1536) for a in g("o_gc_p")], 1)
    gdnp = np.stack(g("o_gdn_p"), 1)
    hs = np.concatenate([a.transpose(0, 3, 2, 1).reshape(NL, NS, 512) for a in g("o_h_s")], 1)
    rgcs = np.concatenate([a.transpose(0, 4, 3, 2, 1).reshape(NL, NS, 3, 512) for a in g("o_rgc_s")], 1)
    rets = np.concatenate(g("o_ret_s"), 1)
    gcs = np.concatenate([a.transpose(0, 4, 3, 2, 1).reshape(NL, NS, 3, 1536) for a in g("o_gc_s")], 1)
    gdns = np.concatenate(g("o_gdn_s"), 1)
    c = np.ascontiguousarray
    return (c(y_prompt), c(y_sample), c(hp), c(rgcp), c(retp), c(gcp), c(gdnp), c(hs), c(rgcs), c(rets), c(gcs), c(gdns))
```
